# Optimizing a Trainium2 kernel written in Bass

```python
import numpy as np
import jax
import jax.numpy as jnp
from jax import lax

D_MODEL = 2048
BATCH = 4
SEQ = 2048
DEPTH = 2

N_MIXERS = 4
HEAD_DIM = 128
GROUP_WIDTH = D_MODEL // N_MIXERS
MIX_WIDTH = N_MIXERS * GROUP_WIDTH
N_HEADS = GROUP_WIDTH // HEAD_DIM
Q_BLOCK = 128
ROPE_THETA = 10000.0
RMS_EPS = 1e-6
NEG_INF = -1e30

CMP_LEN = 32
CMP_STRIDE = 16
CMP_HIDDEN = HEAD_DIM
SEL_LEN = 64
SEL_TOPN = 16
WINDOW = 512
FORCED_BONUS = 1e6

MLA_Q_RANK = 384
MLA_KV_RANK = 128
MLA_NOPE = 128
MLA_ROPE = 64
MLA_V = 128

IN_SPLITS = (
    ("sb_q", GROUP_WIDTH), ("sb_k", GROUP_WIDTH), ("sb_v", GROUP_WIDTH), ("sb_gate", GROUP_WIDTH),
    ("nsa_q", GROUP_WIDTH), ("nsa_k_cmp", HEAD_DIM), ("nsa_v_cmp", HEAD_DIM),
    ("nsa_k_sel", HEAD_DIM), ("nsa_v_sel", HEAD_DIM), ("nsa_k_win", HEAD_DIM), ("nsa_v_win", HEAD_DIM),
    ("nsa_branch", 3 * N_HEADS), ("nsa_gate", GROUP_WIDTH),
    ("fox_q", GROUP_WIDTH), ("fox_k", GROUP_WIDTH), ("fox_v", GROUP_WIDTH), ("fox_f", N_HEADS),
    ("fox_gate", GROUP_WIDTH),
    ("mla_cq", MLA_Q_RANK), ("mla_ckv", MLA_KV_RANK), ("mla_k_rope", MLA_ROPE), ("mla_gate", GROUP_WIDTH),
)
IN_WIDTH = sum(width for _, width in IN_SPLITS)

kernel_name = "hybrid_sb_nsa_fox_mla_layer"


def rms_norm(x, g):
    xf = x.astype(jnp.float32)
    y = xf * lax.rsqrt(jnp.mean(xf * xf, axis=-1, keepdims=True) + RMS_EPS)
    return (y * g.astype(jnp.float32)).astype(x.dtype)


def apply_rope(x, pos):
    half = x.shape[-1] // 2
    inv_freq = ROPE_THETA ** (-jnp.arange(half, dtype=jnp.float32) / half)
    ang = pos.astype(jnp.float32)[:, None] * inv_freq[None, :]
    cos = jnp.cos(ang)[:, None, :]
    sin = jnp.sin(ang)[:, None, :]
    xf = x.astype(jnp.float32)
    x1, x2 = xf[..., :half], xf[..., half:]
    return jnp.concatenate([x1 * cos - x2 * sin, x2 * cos + x1 * sin], axis=-1).astype(x.dtype)


def masked_softmax(z, mask):
    p = jax.nn.softmax(jnp.where(mask, z, NEG_INF), axis=-1)
    return jnp.where(mask, p, 0.0)


def sweep_query_blocks(block_fn, seq_len):
    out = lax.map(block_fn, jnp.arange(seq_len // Q_BLOCK))
    n_blocks, b, q, h, d = out.shape
    return jnp.moveaxis(out, 0, 1).reshape(b, n_blocks * q, h, d)


def split_columns(z):
    parts = {}
    offset = 0
    for name, width in IN_SPLITS:
        parts[name] = z[..., offset:offset + width]
        offset += width
    return parts


def stick_breaking_attention(q, k, v):
    B, S, H, d = q.shape
    scale = d ** -0.5
    kpos = jnp.arange(S)

    def block(i):
        s0 = i * Q_BLOCK
        qb = lax.dynamic_slice_in_dim(q, s0, Q_BLOCK, axis=1)
        qpos = s0 + jnp.arange(Q_BLOCK)
        z = jnp.einsum("bqhd,bshd->bhqs", qb, k).astype(jnp.float32) * scale
        earlier = kpos[None, :] < qpos[:, None]
        log_keep = jnp.where(earlier, jax.nn.log_sigmoid(-z), 0.0)
        log_after = lax.cumsum(log_keep, axis=3, reverse=True) - log_keep
        w = jnp.where(earlier, jnp.exp(jax.nn.log_sigmoid(z) + log_after), 0.0)
        return jnp.einsum("bhqs,bshd->bqhd", w.astype(v.dtype), v)

    return sweep_query_blocks(block, S)


def compress_blocks(tok, pos_emb, w1, w2):
    B, S, d = tok.shape
    n_cmp = (S - CMP_LEN) // CMP_STRIDE + 1
    gather = np.arange(n_cmp)[:, None] * CMP_STRIDE + np.arange(CMP_LEN)[None, :]
    blocks = tok[:, gather] + pos_emb
    hidden = jax.nn.silu(blocks.reshape(B, n_cmp, CMP_LEN * d) @ w1)
    return hidden @ w2


def native_sparse_attention(q, kc_tok, vc_tok, ks, vs, kw, vw, branch_gates,
                            pos_k, w1_k, w2_k, pos_v, w1_v, w2_v, pos):
    B, S, H, d = q.shape
    scale = d ** -0.5
    n_cmp = (S - CMP_LEN) // CMP_STRIDE + 1
    n_sel = S // SEL_LEN
    top_n = min(SEL_TOPN, n_sel)
    cmp_start = np.arange(n_cmp) * CMP_STRIDE
    cmp_end = jnp.asarray(cmp_start + CMP_LEN - 1, dtype=jnp.int32)
    sel_start = np.arange(n_sel) * SEL_LEN
    overlap = np.clip(np.minimum(cmp_start[:, None] + CMP_LEN, sel_start[None, :] + SEL_LEN)
                      - np.maximum(cmp_start[:, None], sel_start[None, :]), 0, None)
    cmp_to_sel = jnp.asarray(overlap / CMP_LEN, dtype=jnp.float32)

    q = apply_rope(q, pos)
    kc = compress_blocks(kc_tok, pos_k, w1_k, w2_k)
    kc = apply_rope(kc[:, :, None, :], cmp_end)[:, :, 0, :]
    vc = compress_blocks(vc_tok, pos_v, w1_v, w2_v)
    ks = apply_rope(ks[:, :, None, :], pos)[:, :, 0, :]
    kw = apply_rope(kw[:, :, None, :], pos)[:, :, 0, :]
    ks_blocks = ks.reshape(B, n_sel, SEL_LEN, d)
    vs_blocks = vs.reshape(B, n_sel, SEL_LEN, d)
    kw_pad = jnp.pad(kw, ((0, 0), (WINDOW, 0), (0, 0)))
    vw_pad = jnp.pad(vw, ((0, 0), (WINDOW, 0), (0, 0)))
    sel_ids = jnp.arange(n_sel)
    gather_blocks = jax.vmap(lambda blocks, ids: blocks[ids])

    def block(i):
        s0 = i * Q_BLOCK
        qb = lax.dynamic_slice_in_dim(q, s0, Q_BLOCK, axis=1)
        gb = lax.dynamic_slice_in_dim(branch_gates, s0, Q_BLOCK, axis=1)
        qpos = s0 + jnp.arange(Q_BLOCK)
        zc = jnp.einsum("bqhd,bnd->bhqn", qb, kc).astype(jnp.float32) * scale
        pc = masked_softmax(zc, cmp_end[None, :] <= qpos[:, None])
        o_cmp = jnp.einsum("bhqn,bnd->bqhd", pc.astype(vc.dtype), vc)
        imp = jnp.einsum("bhqn,ns->bqs", pc, cmp_to_sel)
        cur = qpos // SEL_LEN
        valid = sel_ids[None, :] <= cur[:, None]
        forced = ((sel_ids[None, :] == 0) | (sel_ids[None, :] == cur[:, None])
                  | (sel_ids[None, :] == cur[:, None] - 1))
        score = jnp.where(valid, jnp.where(forced, FORCED_BONUS, imp), NEG_INF)
        _, idx = lax.top_k(score, top_n)
        kg = gather_blocks(ks_blocks, idx)
        vg = gather_blocks(vs_blocks, idx).reshape(B, Q_BLOCK, top_n * SEL_LEN, d)
        tok = idx[..., None] * SEL_LEN + jnp.arange(SEL_LEN)
        sel_mask = (tok <= qpos[None, :, None, None]).reshape(B, 1, Q_BLOCK, top_n * SEL_LEN)
        zs = jnp.einsum("bqhd,bqnld->bhqnl", qb, kg).astype(jnp.float32)
        zs = zs.reshape(B, H, Q_BLOCK, top_n * SEL_LEN) * scale
        ps = masked_softmax(zs, sel_mask)
        o_slc = jnp.einsum("bhqm,bqmd->bqhd", ps.astype(vg.dtype), vg)
        kwb = lax.dynamic_slice_in_dim(kw_pad, s0, WINDOW + Q_BLOCK, axis=1)
        vwb = lax.dynamic_slice_in_dim(vw_pad, s0, WINDOW + Q_BLOCK, axis=1)
        kpos = s0 - WINDOW + jnp.arange(WINDOW + Q_BLOCK)
        win_mask = ((kpos[None, :] >= 0) & (kpos[None, :] <= qpos[:, None])
                    & (kpos[None, :] > qpos[:, None] - WINDOW))
        zw = jnp.einsum("bqhd,bkd->bhqk", qb, kwb).astype(jnp.float32) * scale
        pw = masked_softmax(zw, win_mask)
        o_win = jnp.einsum("bhqk,bkd->bqhd", pw.astype(vwb.dtype), vwb)
        return gb[..., 0:1] * o_cmp + gb[..., 1:2] * o_slc + gb[..., 2:3] * o_win

    return sweep_query_blocks(block, S)


def forgetting_attention(q, k, v, log_f):
    B, S, H, d = q.shape
    scale = d ** -0.5
    cum = jnp.cumsum(log_f, axis=1).transpose(0, 2, 1)
    kpos = jnp.arange(S)

    def block(i):
        s0 = i * Q_BLOCK
        qb = lax.dynamic_slice_in_dim(q, s0, Q_BLOCK, axis=1)
        cq = lax.dynamic_slice_in_dim(cum, s0, Q_BLOCK, axis=2)
        qpos = s0 + jnp.arange(Q_BLOCK)
        z = (jnp.einsum("bqhd,bshd->bhqs", qb, k).astype(jnp.float32) * scale
             + cq[..., None] - cum[:, :, None, :])
        p = masked_softmax(z, kpos[None, :] <= qpos[:, None])
        return jnp.einsum("bhqs,bshd->bqhd", p.astype(v.dtype), v)

    return sweep_query_blocks(block, S)


def latent_attention(c_q, c_kv, k_rope, q_norm_g, w_uq, kv_norm_g, w_ukv, pos):
    B, S, _ = c_q.shape
    q = (rms_norm(c_q, q_norm_g) @ w_uq).reshape(B, S, N_HEADS, MLA_NOPE + MLA_ROPE)
    q_nope = q[..., :MLA_NOPE]
    q_rot = apply_rope(q[..., MLA_NOPE:], pos)
    kv = (rms_norm(c_kv, kv_norm_g) @ w_ukv).reshape(B, S, N_HEADS, MLA_NOPE + MLA_V)
    k_nope, v = kv[..., :MLA_NOPE], kv[..., MLA_NOPE:]
    k_rot = apply_rope(k_rope[:, :, None, :], pos)[:, :, 0, :]
    scale = (MLA_NOPE + MLA_ROPE) ** -0.5
    kpos = jnp.arange(S)

    def block(i):
        s0 = i * Q_BLOCK
        qn = lax.dynamic_slice_in_dim(q_nope, s0, Q_BLOCK, axis=1)
        qr = lax.dynamic_slice_in_dim(q_rot, s0, Q_BLOCK, axis=1)
        qpos = s0 + jnp.arange(Q_BLOCK)
        z = (jnp.einsum("bqhd,bshd->bhqs", qn, k_nope)
             + jnp.einsum("bqhr,bsr->bhqs", qr, k_rot)).astype(jnp.float32) * scale
        p = masked_softmax(z, kpos[None, :] <= qpos[:, None])
        return jnp.einsum("bhqs,bshd->bqhd", p.astype(v.dtype), v)

    return sweep_query_blocks(block, S)


def hybrid_layer(x, pre_g, post_g, w_in, b_in, w_out, forget_bias,
                 pos_k, w1_k, w2_k, pos_v, w1_v, w2_v,
                 q_norm_g, w_uq, kv_norm_g, w_ukv):
    B, S, _ = x.shape
    pos = jnp.arange(S)
    h = rms_norm(x, pre_g)
    p = split_columns(h @ w_in + b_in)

    def heads(t):
        return t.reshape(B, S, N_HEADS, HEAD_DIM)

    o_sb = stick_breaking_attention(heads(p["sb_q"]), heads(p["sb_k"]), heads(p["sb_v"]))

    branch_gates = jax.nn.sigmoid(p["nsa_branch"].reshape(B, S, N_HEADS, 3))
    o_nsa = native_sparse_attention(heads(p["nsa_q"]), p["nsa_k_cmp"], p["nsa_v_cmp"],
                                    p["nsa_k_sel"], p["nsa_v_sel"], p["nsa_k_win"], p["nsa_v_win"],
                                    branch_gates, pos_k, w1_k, w2_k, pos_v, w1_v, w2_v, pos)

    log_f = jax.nn.log_sigmoid((p["fox_f"] + forget_bias).astype(jnp.float32))
    o_fox = forgetting_attention(heads(p["fox_q"]), heads(p["fox_k"]), heads(p["fox_v"]), log_f)

    o_mla = latent_attention(p["mla_cq"], p["mla_ckv"], p["mla_k_rope"],
                             q_norm_g, w_uq, kv_norm_g, w_ukv, pos)

    mix = jnp.concatenate([
        o_sb.reshape(B, S, GROUP_WIDTH) * jax.nn.silu(p["sb_gate"]),
        o_nsa.reshape(B, S, GROUP_WIDTH) * jax.nn.silu(p["nsa_gate"]),
        o_fox.reshape(B, S, GROUP_WIDTH) * jax.nn.silu(p["fox_gate"]),
        o_mla.reshape(B, S, GROUP_WIDTH) * jax.nn.silu(p["mla_gate"]),
    ], axis=-1)
    return x + rms_norm(mix @ w_out, post_g)


def setup_inputs(seed: int = 0) -> dict:
    key = jax.random.key(seed)
    ks = jax.random.split(key, 17)
    f32 = jnp.float32

    def normal(k, shape, scale):
        return jax.random.normal(k, shape, f32) * scale

    def gain(k, shape):
        return 1.0 + 0.02 * jax.random.normal(k, shape, f32)

    flat = CMP_LEN * HEAD_DIM
    return {
        "x": normal(ks[0], (BATCH, SEQ, D_MODEL), 1.0),
        "pre_norm_g": gain(ks[1], (DEPTH, D_MODEL)),
        "post_norm_g": gain(ks[2], (DEPTH, D_MODEL)),
        "w_in": normal(ks[3], (DEPTH, D_MODEL, IN_WIDTH), D_MODEL ** -0.5),
        "b_in": normal(ks[4], (DEPTH, IN_WIDTH), 0.02),
        "w_out": normal(ks[5], (DEPTH, MIX_WIDTH, D_MODEL), MIX_WIDTH ** -0.5),
        "fox_forget_bias": jax.random.uniform(ks[6], (DEPTH, N_HEADS), f32, 1.0, 4.0),
        "nsa_cmp_pos_k": normal(ks[7], (DEPTH, CMP_LEN, HEAD_DIM), 0.02),
        "nsa_cmp_w1_k": normal(ks[8], (DEPTH, flat, CMP_HIDDEN), flat ** -0.5),
        "nsa_cmp_w2_k": normal(ks[9], (DEPTH, CMP_HIDDEN, HEAD_DIM), CMP_HIDDEN ** -0.5),
        "nsa_cmp_pos_v": normal(ks[10], (DEPTH, CMP_LEN, HEAD_DIM), 0.02),
        "nsa_cmp_w1_v": normal(ks[11], (DEPTH, flat, CMP_HIDDEN), flat ** -0.5),
        "nsa_cmp_w2_v": normal(ks[12], (DEPTH, CMP_HIDDEN, HEAD_DIM), CMP_HIDDEN ** -0.5),
        "mla_q_norm_g": gain(ks[13], (DEPTH, MLA_Q_RANK)),
        "mla_w_uq": normal(ks[14], (DEPTH, MLA_Q_RANK, N_HEADS * (MLA_NOPE + MLA_ROPE)), MLA_Q_RANK ** -0.5),
        "mla_kv_norm_g": gain(ks[15], (DEPTH, MLA_KV_RANK)),
        "mla_w_ukv": normal(ks[16], (DEPTH, MLA_KV_RANK, N_HEADS * (MLA_NOPE + MLA_V)), MLA_KV_RANK ** -0.5),
    }


def reference(x, pre_norm_g, post_norm_g, w_in, b_in, w_out, fox_forget_bias,
              nsa_cmp_pos_k, nsa_cmp_w1_k, nsa_cmp_w2_k,
              nsa_cmp_pos_v, nsa_cmp_w1_v, nsa_cmp_w2_v,
              mla_q_norm_g, mla_w_uq, mla_kv_norm_g, mla_w_ukv):
    for l in range(DEPTH):
        x = hybrid_layer(x, pre_norm_g[l], post_norm_g[l], w_in[l], b_in[l], w_out[l],
                         fox_forget_bias[l],
                         nsa_cmp_pos_k[l], nsa_cmp_w1_k[l], nsa_cmp_w2_k[l],
                         nsa_cmp_pos_v[l], nsa_cmp_w1_v[l], nsa_cmp_w2_v[l],
                         mla_q_norm_g[l], mla_w_uq[l], mla_kv_norm_g[l], mla_w_ukv[l])
    return x
```

```python
import numpy as np
import ml_dtypes
from contextlib import ExitStack
import concourse.bass as bass
import concourse.mybir as mybir
from concourse.bass_utils import run_bass_kernel_spmd

F32 = mybir.dt.float32
BF16 = mybir.dt.bfloat16
AF = mybir.ActivationFunctionType
ALU = mybir.AluOpType
AX = mybir.AxisListType

D = 2048
S = 2048
NB = 4
DEPTH = 2
INW = 6992
NT = 16
NQ = 8
P = 128
EPS = 1e-6
NEG = -30000.0
NCMP = 127

OFF = {}
_o = 0
for _n, _w in (("sb_q", 512), ("sb_k", 512), ("sb_v", 512), ("sb_gate", 512),
               ("nsa_q", 512), ("nsa_k_cmp", 128), ("nsa_v_cmp", 128), ("nsa_k_sel", 128),
               ("nsa_v_sel", 128), ("nsa_k_win", 128), ("nsa_v_win", 128), ("nsa_branch", 12),
               ("nsa_gate", 512), ("fox_q", 512), ("fox_k", 512), ("fox_v", 512), ("fox_f", 4),
               ("fox_gate", 512), ("mla_cq", 384), ("mla_ckv", 128), ("mla_k_rope", 64),
               ("mla_gate", 512)):
    OFF[_n] = _o
    _o += _w
assert _o == INW


_ALL_BUFS = []


class Buf:
    __slots__ = ("lw", "rd", "rd_dma")

    def __init__(self):
        self.lw = None
        self.rd = {}
        self.rd_dma = []
        _ALL_BUFS.append(self)


class Op:
    __slots__ = ("eng", "fn", "deps", "signal", "count", "is_dma", "dsem", "dval", "dprev")


class Prog:
    ENGS = ("pe", "act", "dve", "pool", "sp")

    def __init__(self, nc, stack, n_dma_sems=12):
        self.nc = nc
        self.ops = {e: [] for e in self.ENGS}
        self.esem = {e: stack.enter_context(nc.semaphore("es_" + e)) for e in self.ENGS}
        self.dsems = {}
        self.dcount = {}
        self.drr = {}
        for e in ("sp", "pool", "act"):
            self.dsems[e] = [stack.enter_context(nc.semaphore("ds_%s%d" % (e, i))) for i in range(n_dma_sems)]
            self.dcount[e] = [0] * n_dma_sems
            self.drr[e] = 0
        self.dsems["cc"] = [stack.enter_context(nc.semaphore("cc_sem"))]
        self.dcount["cc"] = [0]

    def add(self, eng, fn, reads=(), writes=(), dma=False):
        op = Op()
        op.eng = eng
        op.fn = fn
        op.signal = False
        op.count = 0
        op.is_dma = dma
        op.dsem = None
        op.dval = 0
        op.dprev = 0
        me = (eng, len(self.ops[eng]))
        deps = set()
        for b in reads:
            if b.lw is not None:
                deps.add(b.lw)
        for b in writes:
            if b.lw is not None:
                deps.add(b.lw)
            for e2, i2 in b.rd.items():
                deps.add((e2, i2))
            for d in b.rd_dma:
                deps.add(d)
        needed = []
        for d in deps:
            if d == me:
                continue
            dop = self.ops[d[0]][d[1]]
            if dop.is_dma:
                needed.append(d)
            elif d[0] == eng and eng == "pe":
                continue
            else:
                dop.signal = True
                needed.append(d)
        op.deps = needed
        if dma == "cc":
            op.dsem = ("cc", 0)
            op.dprev = 0
            self.dcount["cc"][0] += 1
            op.dval = self.dcount["cc"][0]
        elif dma:
            k = self.drr[eng]
            self.drr[eng] = (k + 1) % len(self.dsems[eng])
            op.dsem = (eng, k)
            op.dprev = self.dcount[eng][k] * 16
            self.dcount[eng][k] += 1
            op.dval = self.dcount[eng][k] * 16
        self.ops[eng].append(op)
        for b in writes:
            b.lw = me
            b.rd = {}
            b.rd_dma = []
        wset = set(id(b) for b in writes)
        for b in reads:
            if id(b) in wset:
                continue
            if dma:
                b.rd_dma.append(me)
            else:
                b.rd[eng] = me[1]
        return me

    def barrier(self):
        deps = set()
        for b in _ALL_BUFS:
            if b.lw is not None:
                deps.add(b.lw)
            for e2, i2 in b.rd.items():
                deps.add((e2, i2))
            for d in b.rd_dma:
                deps.add(d)
        for e in self.ENGS:
            if self.ops[e]:
                last = (e, len(self.ops[e]) - 1)
                if not self.ops[e][-1].is_dma and self.ops[e][-1].fn is not None:
                    deps.add(last)
        for e in self.ENGS:
            op = Op()
            op.eng = e
            op.fn = None
            op.signal = False
            op.count = 0
            op.is_dma = False
            op.dsem = None
            op.dval = 0
            op.dprev = 0
            mx = {}
            needed = []
            for d in deps:
                dop = self.ops[d[0]][d[1]]
                if dop.is_dma:
                    needed.append(d)
                else:
                    if d[0] == e and e == "pe":
                        continue
                    mx[d[0]] = max(mx.get(d[0], -1), d[1])
            for e2, i2 in mx.items():
                self.ops[e2][i2].signal = True
                needed.append((e2, i2))
            op.deps = needed
            self.ops[e].append(op)
        for b in _ALL_BUFS:
            b.lw = None
            b.rd = {}
            b.rd_dma = []

    def emit(self, block, final_waits):
        nc = self.nc
        for e in self.ENGS:
            c = 0
            for op in self.ops[e]:
                if op.signal and not op.is_dma:
                    c += 1
                    op.count = c
        prog = self

        def run(e, engobj):
            waited = {}
            for op in prog.ops[e]:
                for d in op.deps:
                    dop = prog.ops[d[0]][d[1]]
                    if dop.is_dma:
                        key = ("d",) + dop.dsem
                        val = dop.dval
                        sem = prog.dsems[dop.dsem[0]][dop.dsem[1]]
                    else:
                        key = ("e", d[0])
                        val = dop.count
                        sem = prog.esem[d[0]]
                    if waited.get(key, 0) >= val:
                        continue
                    engobj.wait_ge(sem, val)
                    waited[key] = val
                if op.fn is None:
                    continue
                if op.is_dma:
                    key = ("d",) + op.dsem
                    sem = prog.dsems[op.dsem[0]][op.dsem[1]]
                    if op.dprev > 0 and waited.get(key, 0) < op.dprev:
                        engobj.wait_ge(sem, op.dprev)
                        waited[key] = op.dprev
                    ins = op.fn(engobj)
                    if op.dsem[0] == "cc":
                        ins.then_inc(sem)
                    else:
                        ins.then_inc(sem, 16)
                else:
                    ins = op.fn(engobj)
                    if op.signal:
                        ins.then_inc(prog.esem[e], 1)
            if e == "sp":
                for d in final_waits:
                    dop = prog.ops[d[0]][d[1]]
                    sem = prog.dsems[dop.dsem[0]][dop.dsem[1]]
                    engobj.wait_ge(sem, dop.dval)

        @block.tensor
        def _(eng):
            run("pe", eng)

        @block.scalar
        def _(eng):
            run("act", eng)

        @block.vector
        def _(eng):
            run("dve", eng)

        @block.gpsimd
        def _(eng):
            run("pool", eng)

        @block.sync
        def _(eng):
            run("sp", eng)


class T:
    def __init__(self, t, nbuf=1):
        self.t = t
        self.b = [Buf() for _ in range(nbuf)]


ARENA_BYTES = 50 * 1024
MIXERS = ("sb", "nsa", "fox", "mla")


class Arena:
    def __init__(self, base):
        self.base = base
        self.off = 0

    def reset(self, to=0):
        self.off = to

    def alloc(self, shape, dt, nbuf=1):
        n = 1
        for v in shape:
            n *= v
        nbytes = n * (4 if dt == F32 else 2)
        nbytes = (nbytes + 7) // 8 * 8
        assert self.off + nbytes <= ARENA_BYTES, ("arena overflow", self.off, nbytes)
        ap = self.base[:, self.off // 2:(self.off + nbytes) // 2]
        self.off += nbytes
        if dt == F32:
            ap = ap.bitcast(F32)
        ap = ap[:, 0:n]
        if len(shape) == 2:
            ap = ap.rearrange("p (a b) -> p a b", a=shape[0])
        elif len(shape) == 3:
            ap = ap.rearrange("p (a b c) -> p a b c", a=shape[0], b=shape[1])
        return T(ap, nbuf)


NSA_STOP = 99


def build(layers, dbg=None, mixers=MIXERS):
    del _ALL_BUFS[:]
    nc = bass.Bass("TRN2", target_bir_lowering=False)
    dr = {}

    def din(name, shape, dt=F32):
        dr[name] = nc.dram_tensor(name, list(shape), dt, kind="ExternalInput").ap()
        return dr[name]

    xa = din("xa", [S, D])
    xq = din("xq", [NQ * P, D])
    pre_g = din("pre_norm_g", [DEPTH, D])
    post_g = din("post_norm_g", [DEPTH, D])
    w_in = din("w_in", [DEPTH, D, INW])
    b_in = din("b_in", [DEPTH, INW])
    w_out = din("w_out", [DEPTH, D, D])
    fox_fb = din("fox_forget_bias", [DEPTH, 4])
    pos_kv = [din("nsa_cmp_pos_k", [DEPTH, 32, 128]), din("nsa_cmp_pos_v", [DEPTH, 32, 128])]
    w1_kv = [din("nsa_cmp_w1_k", [DEPTH, 4096, 128]), din("nsa_cmp_w1_v", [DEPTH, 4096, 128])]
    w2_kv = [din("nsa_cmp_w2_k", [DEPTH, 128, 128]), din("nsa_cmp_w2_v", [DEPTH, 128, 128])]
    qng = din("mla_q_norm_g", [DEPTH, 384])
    w_uq = din("mla_w_uq", [DEPTH, 384, 768])
    kvng = din("mla_kv_norm_g", [DEPTH, 128])
    w_ukv = din("mla_w_ukv", [DEPTH, 128, 1024])
    c_ident_bf = din("c_ident_bf", [P, P], BF16)
    c_ident_f = din("c_ident_f", [P, P])
    c_mask_c = din("c_mask_c", [P, 256], BF16)
    c_mask_s = din("c_mask_s", [P, 256], BF16)
    c_mask_w = din("c_mask_w", [P, 768], BF16)
    c_mask_cmp = din("c_mask_cmp", [P, NQ, P], BF16)
    c_cmp01 = din("c_cmp01", [P, NQ, P])
    c_selbias = din("c_selbias", [P, NQ, 32])
    c_selvalid = din("c_selvalid", [P, NQ, 32])
    c_c2s = din("c_c2s", [P, 32])
    c_e8 = din("c_e8", [8, 512], BF16)
    c_cosa = din("c_cosa", [P, NT, 64])
    c_sina = din("c_sina", [P, NT, 64])
    c_cosq = din("c_cosq", [P, NQ, 64])
    c_sinq = din("c_sinq", [P, NQ, 64])
    c_cosc = din("c_cosc", [P, 64])
    c_sinc = din("c_sinc", [P, 64])
    c_cosa32 = din("c_cosa32", [P, NT, 32])
    c_sina32 = din("c_sina32", [P, NT, 32])
    c_cosq32 = din("c_cosq32", [P, NQ, 32])
    c_sinq32 = din("c_sinq32", [P, NQ, 32])
    c_sel4 = din("c_sel4", [4, 4, P])
    yout = nc.dram_tensor("y", [NQ * P, D], F32, kind="ExternalOutput").ap()
    x1own_t = [nc.dram_tensor("x1own%d" % j, [2 * P, D], F32) for j in range(4)]
    gath_t = [nc.dram_tensor("gath%d" % j, [4 * P, D], F32) for j in range(4)]

    with ExitStack() as st:
        pg = Prog(nc, st)

        def sb(name, shape, dt, nbuf=1):
            return T(st.enter_context(nc.sbuf_tensor(name, list(shape), dt)), nbuf)

        def ps(name, shape, dt, nbuf=1):
            return T(st.enter_context(nc.psum_tensor(name, list(shape), dt)), nbuf)

        hTa = sb("hTa", [P, 16, S], BF16, 16)
        hTq = sb("hTq", [P, 16, NQ * P], BF16, 16)
        mixT = sb("mixT", [P, 16, NQ * P], BF16, 16)
        wst = [sb("wst%d" % i, [P, 16, P], F32) for i in range(2)]
        wbf = [sb("wbf%d" % i, [P, 16, P], BF16) for i in range(2)]
        ident_bf = sb("ident_bf", [P, P], BF16)
        ident_f = sb("ident_f", [P, P], F32)
        mask_c = sb("mask_c", [P, 256], BF16)
        mask_s = sb("mask_s", [P, 256], BF16)
        mask_w = sb("mask_w", [P, 768], BF16)
        small = sb("small", [P, 64], F32, 64)
        small4 = sb("small4", [P, 64], F32, 16)
        bias_fm = [sb("bias_fm%d" % i, [P, 1], F32) for i in range(3)]
        bias_bc = [sb("bias_bc%d" % i, [P, P], F32) for i in range(3)]
        arena_t = st.enter_context(nc.sbuf_tensor("arena", [P, ARENA_BYTES // 2], BF16))
        ar = Arena(arena_t)
        ps_s = ps("ps_s", [P, 2048], F32, 4)
        ps_t = [ps("ps_t%d" % i, [P, 1024], BF16) for i in range(2)]
        ps_o = [ps("ps_o%d" % i, [P, 512], F32) for i in range(2)]

        cnt = {}

        def rr(key, n):
            v = cnt.get(key, 0)
            cnt[key] = v + 1
            return v % n

        def newsmall(w=1):
            if w == 1:
                i = rr("sm", 64)
                return small.t[:, i:i + 1], small.b[i]
            i = rr("sm4", 16)
            return small4.t[:, i * 4:i * 4 + 4], small4.b[i]

        def view(tobj, ap):
            r = T(ap, 0)
            r.b = tobj.b
            return r

        def dma(out_ap, in_ap, reads, writes, q="sp"):
            return pg.add(q, lambda e: e.dma_start(out=out_ap, in_=in_ap), reads, writes, dma=True)

        for tile_, src_ in ((ident_bf, c_ident_bf), (ident_f, c_ident_f), (mask_c, c_mask_c),
                            (mask_s, c_mask_s), (mask_w, c_mask_w)):
            dma(tile_.t[:], src_, [], tile_.b)

        def add(eng, fn, reads, writes):
            return pg.add(eng, fn, reads, writes)

        def transpose_to(dst, dst_bufs, src_aps, src_bufs, evac="act", f32=False):
            n = len(src_aps)
            w = src_aps[0].shape[-1]
            rows = src_aps[0].shape[0]
            if f32:
                k = rr("pso", 2)
                pt = ps_o[k]
                idt = ident_f
                assert n <= 4
            else:
                k = rr("pst", 2)
                pt = ps_t[k]
                idt = ident_bf

            def f(e):
                ins = None
                for j, a in enumerate(src_aps):
                    ins = e.transpose(pt.t[0:w, j * P:j * P + rows], a, idt.t[0:rows, 0:rows])
                return ins
            add("pe", f, list(src_bufs) + idt.b, pt.b)
            if rows == P:
                src = pt.t[0:w, 0:n * P]
                if len(dst.shape) == 3:
                    src = src.rearrange("p (a b) -> p a b", b=P)
            elif n == 1:
                src = pt.t[0:w, 0:rows]
            else:
                src = pt.t[0:w, 0:n * P].rearrange("p (a b) -> p a b", b=P)[:, :, 0:rows]
            if evac == "act":
                add("act", lambda e: e.copy(dst, src), pt.b, dst_bufs)
            else:
                add("dve", lambda e: e.tensor_copy(dst, src), pt.b, dst_bufs)

        def load_w(src, nkc, ncols):
            s = rr("wst", 2)
            k = rr("wbf", 2)
            ws, wb = wst[s], wbf[k]
            wsv = ws.t[:, :, :].rearrange("p a b -> p (a b)")[:, 0:nkc * ncols].rearrange("p (a b) -> p a b", b=ncols)
            wbv = wb.t[:, :, :].rearrange("p a b -> p (a b)")[:, 0:nkc * ncols].rearrange("p (a b) -> p a b", b=ncols)
            dma(wsv, src.rearrange("(kc p) c -> p kc c", p=P), [], ws.b)
            if nkc >= 2:
                hk = nkc // 2
                add("dve", lambda e: e.tensor_copy(wbv[:, 0:hk, :], wsv[:, 0:hk, :]), ws.b, wb.b)
                add("act", lambda e: e.copy(wbv[:, hk:nkc, :], wsv[:, hk:nkc, :]), ws.b, wb.b)
            else:
                add("dve", lambda e: e.tensor_copy(wbv, wsv), ws.b, wb.b)
            return wbv, wb.b

        def load_bias_fm(l, c0, ncols):
            k = rr("bfm", 3)
            bt = bias_fm[k]
            dma(bt.t[0:ncols, :], b_in[l, c0:c0 + ncols].rearrange("(c o) -> c o", o=1), [], bt.b)
            return bt

        def load_bias_bc(l, c0, ncols):
            k = rr("bbc", 3)
            bt = bias_bc[k]
            dma(bt.t[:, 0:ncols], b_in[l, c0:c0 + ncols].partition_broadcast(P), [], bt.b)
            return bt

        def proj_fm(l, c0, ncols, src, ntok, dst_fn, dst_bufs):
            wbv, wbb = load_w(w_in[l, :, c0:c0 + ncols], 16, ncols)
            bt = load_bias_fm(l, c0, ncols)
            for t0 in range(0, ntok, 512):
                po = ps_o[rr("pso", 2)]

                def f(e, t0=t0, po=po):
                    ins = None
                    for kc in range(16):
                        ins = e.matmul(po.t[0:ncols, :], wbv[:, kc, :], src.t[:, kc, t0:t0 + 512],
                                       start=(kc == 0), stop=(kc == 15))
                    return ins
                add("pe", f, wbb + src.b, po.b)
                dst = dst_fn(t0, 512)
                add("act", lambda e, dst=dst, po=po: e.activation(out=dst, in_=po.t[0:ncols, :], func=AF.Identity,
                                                                   bias=bt.t[0:ncols, :], scale=1.0),
                    po.b + bt.b, dst_bufs)

        def proj_tm(l, c0, ncols, src, nblk, dst_fn, dst_bufs, blk0=0, w=None):
            if w is None:
                wbv, wbb = load_w(w_in[l, :, c0:c0 + ncols], 16, ncols)
                bt = load_bias_bc(l, c0, ncols)
            else:
                wbv, wbb, bt = w
            for b0 in range(blk0, blk0 + nblk, 4):
                po = ps_o[rr("pso", 2)]

                def f(e, b0=b0, po=po):
                    ins = None
                    for j in range(4):
                        for kc in range(16):
                            ins = e.matmul(po.t[:, j * P:j * P + ncols], src.t[:, kc, (b0 + j) * P:(b0 + j + 1) * P],
                                           wbv[:, kc, :], start=(kc == 0), stop=(kc == 15))
                    return ins
                add("pe", f, wbb + src.b, po.b)
                dst = dst_fn(b0, 4)
                pin = po.t[:, :].rearrange("p (j c) -> p j c", c=P)[:, :, 0:ncols]
                bb = bt.t[:, 0:ncols].unsqueeze(1).to_broadcast([P, 4, ncols])
                add("dve", lambda e, dst=dst, pin=pin, bb=bb: e.tensor_tensor(out=dst, in0=pin, in1=bb, op=ALU.add),
                    po.b + bt.b, dst_bufs)

        def gate_to_mixT(l, c0, ch):
            proj_tm(l, c0, P, hTq, NQ,
                    lambda b0, n: mixT.t[:, ch, b0 * P:(b0 + n) * P].rearrange("p (j c) -> p j c", c=P), [mixT.b[ch]])
            add("act", lambda e: e.activation(out=mixT.t[:, ch, :], in_=mixT.t[:, ch, :], func=AF.Silu),
                [mixT.b[ch]], [mixT.b[ch]])

        def rope_tm(x, xb, cos, sin, tb, out, ob, half, tmp):
            x1, x2 = x[:, :, 0:half], x[:, :, half:2 * half]
            o1, o2 = out[:, :, 0:half], out[:, :, half:2 * half]
            t1, t2 = tmp
            add("dve", lambda e: e.tensor_tensor(out=t1.t, in0=x1, in1=cos, op=ALU.mult), xb + tb, t1.b)
            add("pool", lambda e: e.tensor_tensor(out=t2.t, in0=x2, in1=sin, op=ALU.mult), xb + tb, t2.b)
            add("dve", lambda e: e.tensor_tensor(out=o1, in0=t1.t, in1=t2.t, op=ALU.subtract), t1.b + t2.b, ob)
            add("dve", lambda e: e.tensor_tensor(out=t1.t, in0=x2, in1=cos, op=ALU.mult), xb + tb + ob, t1.b)
            add("pool", lambda e: e.tensor_tensor(out=t2.t, in0=x1, in1=sin, op=ALU.mult), xb + tb + ob, t2.b)
            add("dve", lambda e: e.tensor_tensor(out=o2, in0=t1.t, in1=t2.t, op=ALU.add), t1.b + t2.b, ob)

        def rms_scale(ss, ssb, n):
            ms, msb = newsmall()
            add("dve", lambda e: e.tensor_scalar(out=ms, in0=ss, scalar1=1.0 / n, scalar2=EPS, op0=ALU.mult, op1=ALU.add),
                [ssb], [msb])
            sd, sdb = newsmall()
            add("act", lambda e: e.sqrt(sd, ms), [msb], [sdb])
            rs, rsb = newsmall()
            add("dve", lambda e: e.reciprocal(rs, sd), [sdb], [rsb])
            return rs, rsb

        bank_cur = [0]

        def alloc_banks(nch):
            if bank_cur[0] + nch > 4:
                bank_cur[0] = 0
            b0 = bank_cur[0]
            bank_cur[0] = (b0 + nch) % 4
            return b0

        def softmax_row(nk, scale, pb, base=0):
            nch = (nk + 511) // 512
            o0 = base * 512
            bufs = ps_s.b[base:base + nch]
            mx, mxb = newsmall()
            add("dve", lambda e: e.reduce_max(out=mx, in_=ps_s.t[:, o0:o0 + nk], axis=AX.X), bufs, [mxb])
            nm, nmb = newsmall()
            add("dve", lambda e: e.tensor_scalar(out=nm, in0=mx, scalar1=-scale, scalar2=None, op0=ALU.mult), [mxb], [nmb])
            l1, l1b = newsmall()
            add("act", lambda e: e.activation(out=pb.t[:, 0:nk], in_=ps_s.t[:, o0:o0 + nk], func=AF.Exp, bias=nm, scale=scale,
                                              accum_out=l1), bufs + [nmb], pb.b + [l1b])
            ri, rib = newsmall()
            add("dve", lambda e: e.reciprocal(ri, l1), [l1b], [rib])
            return ri, rib

        def pv_accumulate(nkb, pb, vt, po, pTs, kb_off=0):
            for g0 in range(0, nkb, 8):
                gn = min(8, nkb - g0)
                ptile = pTs[rr("pT", 2)]
                transpose_to(ptile.t[:, 0:gn, :], ptile.b,
                             [pb.t[:, (g0 + j) * P:(g0 + j + 1) * P] for j in range(gn)], pb.b,
                             evac="act" if (g0 // 8) % 2 == 0 else "dve")

                def f(e, g0=g0, gn=gn, ptile=ptile):
                    ins = None
                    for j in range(gn):
                        kb = g0 + j
                        ins = e.matmul(po.t[:, 0:P], ptile.t[:, j, :], vt.t[:, kb_off + kb, :],
                                       start=(kb == 0), stop=(kb == nkb - 1))
                    return ins
                add("pe", f, ptile.b + vt.b, po.b)

        def finish_head(i, src_ap, src_bufs, rinv, rinvb, ch, mts):
            mt = mts[rr("mixtm", 2)]
            gate = mixT.t[:, ch, i * P:(i + 1) * P]
            if rinv is not None:
                add("dve", lambda e: e.scalar_tensor_tensor(out=mt.t, in0=src_ap, scalar=rinv, in1=gate,
                                                            op0=ALU.mult, op1=ALU.mult),
                    src_bufs + [rinvb, mixT.b[ch]], mt.b)
            else:
                add("dve", lambda e: e.tensor_tensor(out=mt.t, in0=src_ap, in1=gate, op=ALU.mult),
                    src_bufs + [mixT.b[ch]], mt.b)
            transpose_to(mixT.t[:, ch, i * P:(i + 1) * P], [mixT.b[ch]], [mt.t], mt.b, evac="act")

        def alloc_attn_common():
            pbs = [ar.alloc([S], BF16) for _ in range(2)]
            pTs = [ar.alloc([8, P], BF16) for _ in range(2)]
            mts = [ar.alloc([P], BF16) for _ in range(2)]
            return pbs, pTs, mts

        def alloc_head_ops(n=2):
            return [(ar.alloc([S], BF16), ar.alloc([NT, P], BF16), ar.alloc([NQ * P], BF16)) for _ in range(n)]

        def phase_norm(l, xsrc):
            ar.reset()
            gbc = ar.alloc([D], F32)
            xin = [ar.alloc([D], F32) for _ in range(2)]
            hn = ar.alloc([D], BF16)
            dma(gbc.t, pre_g[l, :].partition_broadcast(P), [], gbc.b)
            for (which, nblk, dst) in (("a", NT, hTa), ("q", NQ, hTq)):
                for t in range(nblk):
                    xt = xin[rr("xin", 2)]
                    sap, sbufs = xsrc(which, t)
                    dma(xt.t, sap, sbufs, xt.b)
                    ss, ssb = newsmall()
                    add("act", lambda e, xt=xt, ss=ss: e.activation(out=hn.t, in_=xt.t, func=AF.Square, accum_out=ss),
                        xt.b, hn.b + [ssb])
                    rs, rsb = rms_scale(ss, ssb, D)
                    add("dve", lambda e, xt=xt, rs=rs: e.scalar_tensor_tensor(out=hn.t, in0=xt.t, scalar=rs, in1=gbc.t,
                                                                              op0=ALU.mult, op1=ALU.mult),
                        xt.b + [rsb] + gbc.b, hn.b)
                    for half in range(2):
                        c0 = half * 8
                        transpose_to(dst.t[:, c0:c0 + 8, t * P:(t + 1) * P], dst.b[c0:c0 + 8],
                                     [hn.t[:, (c0 + j) * P:(c0 + j + 1) * P] for j in range(8)], hn.b,
                                     evac="act" if half == 0 else "dve")
            pg.barrier()

        def run_pipelined(nheads, proj_items, att_items):
            for t in proj_items(0):
                t()
            for h in range(nheads):
                nxt = proj_items(h + 1) if h + 1 < nheads else []
                atts = att_items(h)
                k = 0
                for ai, a in enumerate(atts):
                    a()
                    want = (ai + 1) * len(nxt) // len(atts)
                    while k < want:
                        nxt[k]()
                        k += 1
                while k < len(nxt):
                    nxt[k]()
                    k += 1

        def mixer_sb(l):
            ar.reset()
            scale = 128 ** -0.5
            pbs, pTs, mts = alloc_attn_common()
            hops = alloc_head_ops(2)
            wk_sets = [[ar.alloc([512], F32) for _ in range(4)] for _ in range(2)]
            def proj_items(h):
                kTt, vt, qTt = hops[h % 2]
                return [
                    lambda: proj_fm(l, OFF["sb_q"] + h * P, P, hTq, NQ * P, lambda t0, n: qTt.t[:, t0:t0 + n], qTt.b),
                    lambda: proj_fm(l, OFF["sb_k"] + h * P, P, hTa, S, lambda t0, n: kTt.t[:, t0:t0 + n], kTt.b),
                    lambda: proj_tm(l, OFF["sb_v"] + h * P, P, hTa, NT, lambda b0, n: vt.t[:, b0:b0 + n, :], vt.b),
                    lambda: gate_to_mixT(l, OFF["sb_gate"] + h * P, 0 + h),
                ]

            def att_items(h):
                return [(lambda i=i: sb_att(h, i)) for i in range(NQ)]

            def sb_att(h, i):
                kTt, vt, qTt = hops[h % 2]
                if True:
                    nkb = 2 * i + 2
                    nk = nkb * P
                    nch = (nk + 511) // 512
                    pb = pbs[rr("pbf", 2)]
                    carry = None
                    for c in range(nch - 1, -1, -1):
                        k0 = c * 512
                        n = min(512, nk - k0)
                        last = (c == nch - 1)
                        bank = ps_s.b[c]
                        wk_e, wk_sp, wk_c, wk_t = wk_sets[rr("wk", 2)]

                        def f(e, i=i, k0=k0, n=n, last=last, nk=nk, qTt=qTt, kTt=kTt):
                            ins = e.matmul(ps_s.t[:, k0:k0 + n], qTt.t[:, i * P:(i + 1) * P], kTt.t[:, k0:k0 + n],
                                           start=True, stop=not last)
                            if last:
                                ins = e.matmul(ps_s.t[:, nk - 256:nk], ident_bf.t[:], mask_s.t[:, 0:256],
                                               start=False, stop=True)
                            return ins
                        add("pe", f, qTt.b + kTt.b + ident_bf.b + mask_s.b, [bank])
                        sv = ps_s.t[:, k0:k0 + n]
                        add("act", lambda e, sv=sv, n=n, wk_e=wk_e: e.activation(out=wk_e.t[:, 0:n], in_=sv, func=AF.Exp, scale=scale),
                            [bank], wk_e.b)
                        add("act", lambda e, n=n, wk_sp=wk_sp, wk_e=wk_e: e.activation(out=wk_sp.t[:, 0:n], in_=wk_e.t[:, 0:n], func=AF.Ln,
                                                                bias=1.0, scale=1.0), wk_e.b, wk_sp.b)
                        add("dve", lambda e, n=n, wk_c=wk_c, wk_sp=wk_sp: e.tensor_tensor_scan(out=wk_c.t[:, 0:n], data0=wk_sp.t[:, 0:n],
                                                                       data1=wk_sp.t[:, 0:n], initial=0.0,
                                                                       op0=ALU.add, op1=ALU.max),
                            wk_sp.b, wk_c.b)
                        add("dve", lambda e, sv=sv, n=n, wk_t=wk_t, wk_sp=wk_sp: e.scalar_tensor_tensor(out=wk_t.t[:, 0:n], in0=sv, scalar=scale,
                                                                                in1=wk_sp.t[:, 0:n], op0=ALU.mult,
                                                                                op1=ALU.subtract),
                            [bank] + wk_sp.b, wk_t.b)
                        add("pool", lambda e, n=n, wk_t=wk_t, wk_c=wk_c: e.tensor_tensor(out=wk_t.t[:, 0:n], in0=wk_t.t[:, 0:n],
                                                                   in1=wk_c.t[:, 0:n], op=ALU.add),
                            wk_t.b + wk_c.b, wk_t.b)
                        nb_, nbb = newsmall()
                        tot = wk_c.t[:, n - 1:n]
                        if carry is None:
                            add("dve", lambda e, nb_=nb_, tot=tot: e.tensor_scalar(out=nb_, in0=tot, scalar1=-1.0, scalar2=None,
                                                                                    op0=ALU.mult), wk_c.b, [nbb])
                        else:
                            cpr, cprb = carry
                            add("dve", lambda e, nb_=nb_, tot=tot, cpr=cpr: e.scalar_tensor_tensor(
                                out=nb_, in0=tot, scalar=-1.0, in1=cpr, op0=ALU.mult, op1=ALU.add), wk_c.b + [cprb], [nbb])
                        carry = (nb_, nbb)
                        add("act", lambda e, n=n, k0=k0, nb_=nb_, pb=pb, wk_t=wk_t: e.activation(out=pb.t[:, k0:k0 + n], in_=wk_t.t[:, 0:n],
                                                                                     func=AF.Exp, bias=nb_, scale=1.0),
                            wk_t.b + [nbb], pb.b)
                    po = ps_o[rr("pso", 2)]
                    pv_accumulate(nkb, pb, vt, po, pTs)
                    finish_head(i, po.t[:, 0:P], po.b, None, None, 0 + h, mts)
            run_pipelined(4, proj_items, att_items)
            pg.barrier()

        def softmax_heads_causal(i, qTt, kTt, vt, scale, pbs, pTs, mts, ch, extra=None, extra_bufs=()):
            nkb = 2 * i + 2
            nk = nkb * P
            nch = (nk + 511) // 512
            base = alloc_banks(nch)
            o0 = base * 512

            def f(e):
                ins = None
                for c in range(nch):
                    k0 = c * 512
                    n = min(512, nk - k0)
                    last = (c == nch - 1)
                    ins = e.matmul(ps_s.t[:, o0 + k0:o0 + k0 + n], qTt.t[:, i * P:(i + 1) * P], kTt.t[:, k0:k0 + n],
                                   start=True, stop=(not last) and extra is None)
                    if extra is not None:
                        ins = extra(e, ps_s.t[:, o0 + k0:o0 + k0 + n], k0, n, not last)
                    if last:
                        ins = e.matmul(ps_s.t[:, o0 + nk - 256:o0 + nk], ident_bf.t[:], mask_c.t[:, 0:256], start=False, stop=True)
                return ins
            add("pe", f, qTt.b + kTt.b + ident_bf.b + mask_c.b + list(extra_bufs), ps_s.b[base:base + nch])
            pb = pbs[rr("pbf", 2)]
            rinv, rinvb = softmax_row(nk, scale, pb, base)
            po = ps_o[rr("pso", 2)]
            pv_accumulate(nkb, pb, vt, po, pTs)
            finish_head(i, po.t[:, 0:P], po.b, rinv, rinvb, ch, mts)

        def mixer_fox(l):
            ar.reset()
            scale = 128 ** -0.5
            pbs, pTs, mts = alloc_attn_common()
            hops = alloc_head_ops(2)
            crow = ar.alloc([S], F32)
            sel4 = ar.alloc([4, P], F32)
            t_e = ar.alloc([512], F32)
            t_sp = ar.alloc([512], F32)
            ones4 = ar.alloc([512], F32)
            fb = ar.alloc([2], F32)
            dma(sel4.t[0:4, :, :], c_sel4, [], sel4.b)
            add("pool", lambda e: e.memset(ones4.t[0:4, :], 1.0), [], ones4.b)
            c0 = OFF["fox_f"]
            wbv, wbb = load_w(w_in[l, :, c0:c0 + 4], 16, 4)
            dma(fb.t[0:4, 0:1], b_in[l, c0:c0 + 4].rearrange("(c o) -> c o", o=1), [], fb.b)
            dma(fb.t[0:4, 1:2], fox_fb[l, :].rearrange("(c o) -> c o", o=1), fb.b, fb.b)
            nb_, nbb = newsmall()
            add("dve", lambda e: e.scalar_tensor_tensor(out=nb_[0:4, :], in0=fb.t[0:4, 0:1], scalar=-1.0, in1=fb.t[0:4, 1:2],
                                                        op0=ALU.mult, op1=ALU.subtract), fb.b, [nbb])
            prev = None
            for t0 in range(0, S, 512):
                po = ps_o[rr("pso", 2)]

                def f(e, t0=t0, po=po):
                    ins = None
                    for kc in range(16):
                        ins = e.matmul(po.t[0:4, :], wbv[:, kc, :], hTa.t[:, kc, t0:t0 + 512], start=(kc == 0), stop=(kc == 15))
                    return ins
                add("pe", f, wbb + hTa.b, po.b)
                add("act", lambda e, po=po: e.activation(out=t_e.t[0:4, :], in_=po.t[0:4, :], func=AF.Exp, bias=nb_[0:4, :], scale=-1.0),
                    po.b + [nbb], t_e.b)
                add("act", lambda e: e.activation(out=t_sp.t[0:4, :], in_=t_e.t[0:4, :], func=AF.Ln, bias=1.0, scale=1.0),
                    t_e.b, t_sp.b)
                init = 0.0 if prev is None else crow.t[0:4, t0 - 1:t0]
                add("dve", lambda e, t0=t0, init=init: e.tensor_tensor_scan(out=crow.t[0:4, t0:t0 + 512], data0=ones4.t[0:4, :],
                                                                             data1=t_sp.t[0:4, :], initial=init,
                                                                             op0=ALU.mult, op1=ALU.add),
                    t_sp.b + ones4.b + crow.b, crow.b)
                prev = t0
            def proj_items(h):
                kTt, vt, qTt = hops[h % 2]
                return [
                    lambda: proj_fm(l, OFF["fox_q"] + h * P, P, hTq, NQ * P, lambda t0, n: qTt.t[:, t0:t0 + n], qTt.b),
                    lambda: proj_fm(l, OFF["fox_k"] + h * P, P, hTa, S, lambda t0, n: kTt.t[:, t0:t0 + n], kTt.b),
                    lambda: proj_tm(l, OFF["fox_v"] + h * P, P, hTa, NT, lambda b0, n: vt.t[:, b0:b0 + n, :], vt.b),
                    lambda: gate_to_mixT(l, OFF["fox_gate"] + h * P, 8 + h),
                ]

            def att_items(h):
                kTt, vt, qTt = hops[h % 2]

                def extra(e, out, k0, n, stop):
                    return e.matmul(out, sel4.t[0:4, h, :], crow.t[0:4, k0:k0 + n], start=False, stop=stop)
                return [(lambda i=i: softmax_heads_causal(i, qTt, kTt, vt, scale, pbs, pTs, mts, 8 + h, extra, sel4.b + crow.b))
                        for i in range(NQ)]
            run_pipelined(4, proj_items, att_items)
            pg.barrier()

        def mixer_mla(l):
            ar.reset()
            scale = 192 ** -0.5
            cqnT = ar.alloc([3, NQ * P], BF16)
            ckvnT = ar.alloc([S], BF16)
            krT = ar.alloc([S], BF16)
            wukv_b = ar.alloc([1024], BF16)
            cosq = ar.alloc([NQ, 32], F32)
            sinq = ar.alloc([NQ, 32], F32)
            mark = ar.off
            dma(cosq.t, c_cosq32, [], cosq.b)
            dma(sinq.t, c_sinq32, [], sinq.b)
            cosa = ar.alloc([NT, 32], F32)
            sina = ar.alloc([NT, 32], F32)
            gq = ar.alloc([384], F32)
            gkv = ar.alloc([P], F32)
            xt = ar.alloc([4, 384], F32)
            xn = ar.alloc([4, 384], BF16)
            t1 = ar.alloc([4, 32], F32)
            t2 = ar.alloc([4, 32], F32)
            dma(cosa.t, c_cosa32, [], cosa.b)
            dma(sina.t, c_sina32, [], sina.b)
            dma(gq.t, qng[l, :].partition_broadcast(P), [], gq.b)
            dma(gkv.t, kvng[l, :].partition_broadcast(P), [], gkv.b)
            ws = wst[rr("wst", 2)]
            wsv = ws.t[:, :, :].rearrange("p a b -> p (a b)")[:, 0:1024]
            dma(wsv, w_ukv[l, :, :], [], ws.b)
            add("pool", lambda e, wsv=wsv: e.tensor_copy(wukv_b.t, wsv), ws.b, wukv_b.b)

            def norm_rows(x_ap, xb, width, g_ap, gb, out_ap, ob):
                ss, ssb = newsmall()
                jt, jb = junkt
                add("act", lambda e: e.activation(out=jt[:, 0:width], in_=x_ap, func=AF.Square, accum_out=ss), xb, jb + [ssb])
                rs, rsb = rms_scale(ss, ssb, width)
                add("dve", lambda e: e.scalar_tensor_tensor(out=out_ap, in0=x_ap, scalar=rs, in1=g_ap, op0=ALU.mult, op1=ALU.mult),
                    xb + [rsb] + gb, ob)
            jk = ar.alloc([384], F32)
            junkt = (jk.t, jk.b)
            for b0 in range(0, NQ, 4):
                for cg in range(3):
                    wbv, wbb = load_w(w_in[l, :, OFF["mla_cq"] + cg * P:OFF["mla_cq"] + (cg + 1) * P], 16, P)
                    bt = load_bias_bc(l, OFF["mla_cq"] + cg * P, P)
                    po = ps_o[rr("pso", 2)]

                    def f(e, b0=b0, po=po, wbv=wbv):
                        ins = None
                        for j in range(4):
                            for kc in range(16):
                                ins = e.matmul(po.t[:, j * P:(j + 1) * P], hTq.t[:, kc, (b0 + j) * P:(b0 + j + 1) * P],
                                               wbv[:, kc, :], start=(kc == 0), stop=(kc == 15))
                        return ins
                    add("pe", f, wbb + hTq.b, po.b)
                    pin = po.t[:, :].rearrange("p (j c) -> p j c", c=P)
                    bb = bt.t[:, :].unsqueeze(1).to_broadcast([P, 4, P])
                    add("dve", lambda e, cg=cg, pin=pin, bb=bb: e.tensor_tensor(out=xt.t[:, :, cg * P:(cg + 1) * P], in0=pin, in1=bb,
                                                                                op=ALU.add), po.b + bt.b, xt.b)
                for j in range(4):
                    norm_rows(xt.t[:, j, :], xt.b, 384, gq.t, gq.b, xn.t[:, j, :], xn.b)
                for cg in range(3):
                    transpose_to(cqnT.t[:, cg, b0 * P:(b0 + 4) * P], cqnT.b,
                                 [xn.t[:, j, cg * P:(cg + 1) * P] for j in range(4)], xn.b, evac="dve")
            xk = ar.alloc([4, P], F32)
            xkn = ar.alloc([4, P], BF16)
            xr = ar.alloc([4, 64], F32)
            xrb = ar.alloc([4, 64], BF16)
            wbv1, wbb1 = load_w(w_in[l, :, OFF["mla_ckv"]:OFF["mla_ckv"] + P], 16, P)
            bt1 = load_bias_bc(l, OFF["mla_ckv"], P)
            wbv2, wbb2 = load_w(w_in[l, :, OFF["mla_k_rope"]:OFF["mla_k_rope"] + 64], 16, 64)
            bt2 = load_bias_bc(l, OFF["mla_k_rope"], 64)
            for b0 in range(0, NT, 4):
                po = ps_o[rr("pso", 2)]

                def f(e, b0=b0, po=po):
                    ins = None
                    for j in range(4):
                        for kc in range(16):
                            ins = e.matmul(po.t[:, j * P:(j + 1) * P], hTa.t[:, kc, (b0 + j) * P:(b0 + j + 1) * P],
                                           wbv1[:, kc, :], start=(kc == 0), stop=(kc == 15))
                    return ins
                add("pe", f, wbb1 + hTa.b, po.b)
                pin = po.t[:, :].rearrange("p (j c) -> p j c", c=P)
                bb = bt1.t[:, :].unsqueeze(1).to_broadcast([P, 4, P])
                add("dve", lambda e, pin=pin, bb=bb: e.tensor_tensor(out=xk.t, in0=pin, in1=bb, op=ALU.add), po.b + bt1.b, xk.b)
                for j in range(4):
                    norm_rows(xk.t[:, j, :], xk.b, P, gkv.t, gkv.b, xkn.t[:, j, :], xkn.b)
                transpose_to(ckvnT.t[:, b0 * P:(b0 + 4) * P], ckvnT.b, [xkn.t[:, j, :] for j in range(4)], xkn.b, evac="dve")
                po2 = ps_o[rr("pso", 2)]

                def f2(e, b0=b0, po2=po2):
                    ins = None
                    for j in range(4):
                        for kc in range(16):
                            ins = e.matmul(po2.t[:, j * 64:(j + 1) * 64], hTa.t[:, kc, (b0 + j) * P:(b0 + j + 1) * P],
                                           wbv2[:, kc, :], start=(kc == 0), stop=(kc == 15))
                    return ins
                add("pe", f2, wbb2 + hTa.b, po2.b)
                pin2 = po2.t[:, 0:256].rearrange("p (j c) -> p j c", c=64)
                bb2 = bt2.t[:, 0:64].unsqueeze(1).to_broadcast([P, 4, 64])
                add("dve", lambda e, pin2=pin2, bb2=bb2: e.tensor_tensor(out=xr.t, in0=pin2, in1=bb2, op=ALU.add), po2.b + bt2.b, xr.b)
                rope_tm(xr.t, xr.b, cosa.t[:, b0:b0 + 4, :], sina.t[:, b0:b0 + 4, :], cosa.b + sina.b, xrb.t, xrb.b, 32, (t1, t2))
                transpose_to(krT.t[0:64, b0 * P:(b0 + 4) * P], krT.b, [xrb.t[:, j, :] for j in range(4)], xrb.b, evac="dve")
            pg.barrier()
            ar.reset(mark)
            pbs, pTs, mts = alloc_attn_common()
            kTt = ar.alloc([S], BF16)
            vt = ar.alloc([NT, P], BF16)
            qTt = ar.alloc([NQ * P], BF16)
            qrT = ar.alloc([NQ * P], BF16)
            wuqh = ar.alloc([3, 192], BF16)
            qr_f = ar.alloc([NQ, 64], F32)
            qr_b = ar.alloc([NQ, 64], BF16)
            t1 = ar.alloc([NQ, 32], F32)
            t2 = ar.alloc([NQ, 32], F32)
            for h in range(4):
                ws = wst[rr("wst", 2)]
                wsv = ws.t[:, :, :].rearrange("p a b -> p (a b)")[:, 0:576].rearrange("p (a b) -> p a b", b=192)
                dma(wsv, w_uq[l, :, h * 192:(h + 1) * 192].rearrange("(kc p) c -> p kc c", p=P), [], ws.b)
                add("pool", lambda e, wsv=wsv: e.tensor_copy(wuqh.t, wsv), ws.b, wuqh.b)
                for t0 in range(0, NQ * P, 512):
                    po = ps_o[rr("pso", 2)]

                    def f(e, t0=t0, po=po):
                        ins = None
                        for kc in range(3):
                            ins = e.matmul(po.t[:, :], wuqh.t[:, kc, 0:P], cqnT.t[:, kc, t0:t0 + 512], start=(kc == 0), stop=(kc == 2))
                        return ins
                    add("pe", f, wuqh.b + cqnT.b, po.b)
                    add("act", lambda e, t0=t0, po=po: e.copy(qTt.t[:, t0:t0 + 512], po.t[:, :]), po.b, qTt.b)
                for b0 in range(0, NQ, 4):
                    po = ps_o[rr("pso", 2)]

                    def f(e, b0=b0, po=po):
                        ins = None
                        for j in range(4):
                            for kc in range(3):
                                ins = e.matmul(po.t[:, j * 64:(j + 1) * 64], cqnT.t[:, kc, (b0 + j) * P:(b0 + j + 1) * P],
                                               wuqh.t[:, kc, P:192], start=(kc == 0), stop=(kc == 2))
                        return ins
                    add("pe", f, wuqh.b + cqnT.b, po.b)
                    add("act", lambda e, b0=b0, po=po: e.copy(qr_f.t[:, b0:b0 + 4, :], po.t[:, 0:256].rearrange("p (j c) -> p j c", c=64)),
                        po.b, qr_f.b)
                rope_tm(qr_f.t, qr_f.b, cosq.t, sinq.t, cosq.b + sinq.b, qr_b.t, qr_b.b, 32, (t1, t2))
                transpose_to(qrT.t[0:64, :], qrT.b, [qr_b.t[:, j, :] for j in range(NQ)], qr_b.b, evac="dve")
                for t0 in range(0, S, 512):
                    po = ps_o[rr("pso", 2)]
                    add("pe", lambda e, t0=t0, po=po, h=h: e.matmul(po.t[:, :], wukv_b.t[:, h * 256:h * 256 + P], ckvnT.t[:, t0:t0 + 512],
                                                                    start=True, stop=True), wukv_b.b + ckvnT.b, po.b)
                    add("act", lambda e, t0=t0, po=po: e.copy(kTt.t[:, t0:t0 + 512], po.t[:, :]), po.b, kTt.b)
                for b0 in range(0, NT, 4):
                    po = ps_o[rr("pso", 2)]

                    def f(e, b0=b0, po=po, h=h):
                        ins = None
                        for j in range(4):
                            ins = e.matmul(po.t[:, j * P:(j + 1) * P], ckvnT.t[:, (b0 + j) * P:(b0 + j + 1) * P],
                                           wukv_b.t[:, h * 256 + P:h * 256 + 256], start=True, stop=True)
                        return ins
                    add("pe", f, wukv_b.b + ckvnT.b, po.b)
                    add("dve", lambda e, b0=b0, po=po: e.tensor_copy(vt.t[:, b0:b0 + 4, :], po.t[:, :].rearrange("p (j c) -> p j c", c=P)),
                        po.b, vt.b)
                gate_to_mixT(l, OFF["mla_gate"] + h * P, 12 + h)

                for i in range(NQ):
                    def extra_i(e, out, k0, n, stop, i=i):
                        return e.matmul(out, qrT.t[0:64, i * P:(i + 1) * P], krT.t[0:64, k0:k0 + n], start=False, stop=stop)
                    softmax_heads_causal(i, qTt, kTt, vt, scale, pbs, pTs, mts, 12 + h, extra_i, qrT.b + krT.b)
            pg.barrier()

        def mixer_nsa(l):
            ar.reset()
            scale = 128 ** -0.5
            qT4 = ar.alloc([4, NQ * P], BF16)
            ksT = ar.alloc([S], BF16)
            kwT = ar.alloc([S], BF16)
            vs = ar.alloc([NT, P], BF16)
            vw = ar.alloc([NT, P], BF16)
            kcT = ar.alloc([P], BF16)
            vc = ar.alloc([P], BF16)
            bgate = ar.alloc([NQ, 12], F32)
            e8 = ar.alloc([512], BF16)
            c2s = ar.alloc([32], F32)
            mark = ar.off
            dma(e8.t[0:8, :], c_e8, [], e8.b)
            dma(c2s.t, c_c2s, [], c2s.b)
            cos_t = ar.alloc([NT, 64], F32)
            sin_t = ar.alloc([NT, 64], F32)
            xf = ar.alloc([NQ, P], F32)
            xb_ = ar.alloc([NQ, P], BF16)
            t1 = ar.alloc([NQ, 64], F32)
            t2 = ar.alloc([NQ, 64], F32)
            dma(cos_t.t[:, 0:NQ, :], c_cosq, [], cos_t.b)
            dma(sin_t.t[:, 0:NQ, :], c_sinq, [], sin_t.b)
            for h in range(4):
                proj_tm(l, OFF["nsa_q"] + h * P, P, hTq, NQ, lambda b0, n: xf.t[:, b0:b0 + n, :], xf.b)
                rope_tm(xf.t, xf.b, cos_t.t[:, 0:NQ, :], sin_t.t[:, 0:NQ, :], cos_t.b + sin_t.b, xb_.t, xb_.b, 64, (t1, t2))
                transpose_to(qT4.t[:, h, :], qT4.b, [xb_.t[:, j, :] for j in range(NQ)], xb_.b, evac="dve")
            proj_tm(l, OFF["nsa_branch"], 12, hTq, NQ, lambda b0, n: bgate.t[:, b0:b0 + n, :], bgate.b)
            add("act", lambda e: e.activation(out=bgate.t, in_=bgate.t, func=AF.Sigmoid), bgate.b, bgate.b)
            dma(cos_t.t, c_cosa, xf.b + xb_.b + t1.b + t2.b, cos_t.b)
            dma(sin_t.t, c_sina, xf.b + xb_.b + t1.b + t2.b, sin_t.b)
            for (cname, dstT) in (("nsa_k_sel", ksT), ("nsa_k_win", kwT)):
                c0_ = OFF[cname]
                wbv_, wbb_ = load_w(w_in[l, :, c0_:c0_ + P], 16, P)
                bt_ = load_bias_bc(l, c0_, P)
                for g0 in (0, 8):
                    proj_tm(l, c0_, P, hTa, 8, lambda b0, n, g0=g0: xf.t[:, b0 - g0:b0 - g0 + n, :], xf.b, blk0=g0,
                            w=(wbv_, wbb_, bt_))
                    rope_tm(xf.t, xf.b, cos_t.t[:, g0:g0 + 8, :], sin_t.t[:, g0:g0 + 8, :], cos_t.b + sin_t.b, xb_.t, xb_.b, 64,
                            (t1, t2))
                    transpose_to(dstT.t[:, g0 * P:(g0 + 8) * P], dstT.b, [xb_.t[:, j, :] for j in range(8)], xb_.b, evac="dve")
            proj_tm(l, OFF["nsa_v_sel"], P, hTa, NT, lambda b0, n: vs.t[:, b0:b0 + n, :], vs.b)
            proj_tm(l, OFF["nsa_v_win"], P, hTa, NT, lambda b0, n: vw.t[:, b0:b0 + n, :], vw.b)
            pg.barrier()
            if NSA_STOP <= 1:
                return
            ar.reset(mark)
            tokT = ar.alloc([S], BF16)
            blkT = ar.alloc([32, P], BF16)
            w1b = ar.alloc([32, P], BF16)
            w2b = ar.alloc([P], BF16)
            posr = ar.alloc([P], F32)
            posT = ar.alloc([32], F32)
            hidT = ar.alloc([P], BF16)
            kcf = ar.alloc([1, P], F32)
            kcb = ar.alloc([1, P], BF16)
            cosc = ar.alloc([1, 64], F32)
            sinc = ar.alloc([1, 64], F32)
            tc1 = ar.alloc([1, 64], F32)
            tc2 = ar.alloc([1, 64], F32)
            dma(cosc.t[:, 0, :], c_cosc, [], cosc.b)
            dma(sinc.t[:, 0, :], c_sinc, [], sinc.b)
            for which in range(2):
                cname = "nsa_k_cmp" if which == 0 else "nsa_v_cmp"
                proj_fm(l, OFF[cname], P, hTa, S, lambda t0, n: tokT.t[:, t0:t0 + n], tokT.b)
                dma(posr.t[0:32, :], pos_kv[which][l, :, :], [], posr.b)
                po = ps_o[rr("pso", 2)]
                add("pe", lambda e, po=po: e.transpose(po.t[:, 0:32], posr.t[0:32, :], ident_f.t[0:32, 0:32]), posr.b + ident_f.b, po.b)
                add("act", lambda e, po=po: e.copy(posT.t, po.t[:, 0:32]), po.b, posT.b)
                for half in range(2):
                    ws = wst[rr("wst", 2)]
                    dma(ws.t[:, :, :], w1_kv[which][l, half * 2048:(half + 1) * 2048, :].rearrange("(l d) h -> d l h", d=P), [], ws.b)
                    add("pool", lambda e, half=half, ws=ws: e.tensor_copy(w1b.t[:, half * 16:(half + 1) * 16, :], ws.t[:, :, :]),
                        ws.b, w1b.b)
                ws = wst[rr("wst", 2)]
                wsv = ws.t[:, 0, :]
                dma(wsv, w2_kv[which][l, :, :], [], ws.b)
                add("pool", lambda e, wsv=wsv: e.tensor_copy(w2b.t, wsv), ws.b, w2b.b)
                for ll in range(32):
                    src = tokT.t[:, ll:ll + 16 * (NCMP - 1) + 1:16]
                    eng = "dve" if ll % 2 == 0 else "pool"
                    add(eng, lambda e, ll=ll, src=src: e.tensor_scalar(out=blkT.t[:, ll, 0:NCMP], in0=src, scalar1=posT.t[:, ll:ll + 1],
                                                                       scalar2=None, op0=ALU.add), tokT.b + posT.b, blkT.b)
                po = ps_o[rr("pso", 2)]

                def f(e, po=po):
                    ins = None
                    for ll in range(32):
                        ins = e.matmul(po.t[:, 0:NCMP], w1b.t[:, ll, :], blkT.t[:, ll, 0:NCMP], start=(ll == 0), stop=(ll == 31))
                    return ins
                add("pe", f, w1b.b + blkT.b, po.b)
                add("act", lambda e, po=po: e.activation(out=hidT.t[:, 0:NCMP], in_=po.t[:, 0:NCMP], func=AF.Silu), po.b, hidT.b)
                po2 = ps_o[rr("pso", 2)]
                add("pe", lambda e, po2=po2: e.matmul(po2.t[0:NCMP, 0:P], hidT.t[:, 0:NCMP], w2b.t, start=True, stop=True),
                    hidT.b + w2b.b, po2.b)
                if which == 0:
                    add("act", lambda e, po2=po2: e.copy(kcf.t[0:NCMP, 0, :], po2.t[0:NCMP, 0:P]), po2.b, kcf.b)
                    rope_tm(kcf.t[0:NCMP], kcf.b, cosc.t[0:NCMP], sinc.t[0:NCMP], cosc.b + sinc.b, kcb.t[0:NCMP], kcb.b, 64,
                            (view(tc1, tc1.t[0:NCMP]), view(tc2, tc2.t[0:NCMP])))
                    add("pool", lambda e: e.memset(kcT.t, 0.0), [], kcT.b)
                    transpose_to(kcT.t[:, 0:NCMP], kcT.b, [kcb.t[0:NCMP, 0, :]], kcb.b, evac="dve")
                else:
                    add("pool", lambda e: e.memset(vc.t, 0.0), [], vc.b)
                    add("act", lambda e, po2=po2: e.copy(vc.t[0:NCMP, :], po2.t[0:NCMP, 0:P]), po2.b, vc.b)
            for h in range(4):
                gate_to_mixT(l, OFF["nsa_gate"] + h * P, 4 + h)
            pg.barrier()
            if NSA_STOP <= 2:
                return
            ar.reset(mark)
            pbs, pTs, mts = alloc_attn_common()
            mcmp2 = [ar.alloc([P], BF16) for _ in range(2)]
            cmp012 = [ar.alloc([P], F32) for _ in range(2)]
            selb2 = [ar.alloc([32], F32) for _ in range(2)]
            selv2 = [ar.alloc([32], F32) for _ in range(2)]
            ef = ar.alloc([4, P], F32)
            pcf = ef
            pcb = ar.alloc([4, P], BF16)
            ps4 = ar.alloc([P], F32)
            impA = ar.alloc([32], F32)
            imp = ar.alloc([32], F32)
            pcT_b = ar.alloc([4, P], BF16)
            ocmp = ar.alloc([4, P], F32)
            sc = ar.alloc([32], F32)
            sc2 = ar.alloc([32], F32)
            m8a = ar.alloc([8], F32)
            m8b = ar.alloc([8], F32)
            sbias = ar.alloc([32], BF16)
            selT = ar.alloc([4, P], BF16)
            acc = ar.alloc([P], F32)
            for i in range(NQ):
                nkb = 2 * i + 2
                nk = nkb * P
                nch = (nk + 511) // 512
                mcmp, cmp01, selb, selv = mcmp2[i % 2], cmp012[i % 2], selb2[i % 2], selv2[i % 2]
                dma(mcmp.t, c_mask_cmp[:, i, :], [], mcmp.b)
                dma(cmp01.t, c_cmp01[:, i, :], [], cmp01.b)
                dma(selb.t, c_selbias[:, i, :], [], selb.b)
                dma(selv.t, c_selvalid[:, i, :], [], selv.b)
                pz = ps_o[rr("pso", 2)]

                def f(e, i=i, pz=pz, mcmp=mcmp):
                    ins = None
                    for h in range(4):
                        e.matmul(pz.t[:, h * P:(h + 1) * P], qT4.t[:, h, i * P:(i + 1) * P], kcT.t[:, :], start=True, stop=False)
                        ins = e.matmul(pz.t[:, h * P:(h + 1) * P], ident_bf.t[:], mcmp.t[:, :], start=False, stop=True)
                    return ins
                add("pe", f, qT4.b + kcT.b + ident_bf.b + mcmp.b, pz.b)
                mx4, mx4b = newsmall(4)
                pz3 = pz.t[:, :].rearrange("p (h n) -> p h n", n=P)
                add("dve", lambda e, pz3=pz3, mx4=mx4: e.tensor_reduce(out=mx4, in_=pz3, axis=AX.X, op=ALU.max), pz.b, [mx4b])
                nm4, nm4b = newsmall(4)
                add("dve", lambda e, mx4=mx4, nm4=nm4: e.tensor_scalar(out=nm4, in0=mx4, scalar1=-scale, scalar2=None, op0=ALU.mult),
                    [mx4b], [nm4b])
                for h in range(4):
                    add("act", lambda e, h=h, pz=pz, nm4=nm4: e.activation(out=ef.t[:, h, :], in_=pz.t[:, h * P:(h + 1) * P], func=AF.Exp,
                                                                          bias=nm4[:, h:h + 1], scale=scale), pz.b + [nm4b], ef.b)
                m01 = cmp01.t[:, :].unsqueeze(1).to_broadcast([P, 4, P])
                add("dve", lambda e, m01=m01: e.tensor_tensor(out=ef.t, in0=ef.t, in1=m01, op=ALU.mult), ef.b + cmp01.b, ef.b)
                l4, l4b = newsmall(4)
                add("dve", lambda e, l4=l4: e.tensor_reduce(out=l4, in_=ef.t, axis=AX.X, op=ALU.add), ef.b, [l4b])
                r4, r4b = newsmall(4)
                add("dve", lambda e, l4=l4, r4=r4: e.tensor_scalar(out=r4, in0=l4, scalar1=1e-30, scalar2=None, op0=ALU.max), [l4b], [r4b])
                r4i, r4ib = newsmall(4)
                add("dve", lambda e, r4=r4, r4i=r4i: e.reciprocal(r4i, r4), [r4b], [r4ib])
                add("dve", lambda e, r4i=r4i: e.tensor_tensor(out=pcf.t, in0=ef.t, in1=r4i.unsqueeze(2).to_broadcast([P, 4, P]), op=ALU.mult),
                    ef.b + [r4ib], pcf.b)
                if NSA_STOP <= 2.1:
                    continue
                add("pool", lambda e: e.tensor_copy(pcb.t, pcf.t), pcf.b, pcb.b)
                transpose_to(pcT_b.t, pcT_b.b, [pcb.t[:, h, :] for h in range(4)], pcb.b, evac="act")
                if NSA_STOP <= 2.2:
                    continue
                add("dve", lambda e: e.tensor_reduce(out=ps4.t, in_=pcf.t.rearrange("p h n -> p n h"), axis=AX.X, op=ALU.add),
                    pcf.b, ps4.b)
                ps4v = ps4.t.rearrange("p (s j) -> p s j", j=4)
                add("dve", lambda e, ps4v=ps4v: e.tensor_reduce(out=impA.t, in_=ps4v, axis=AX.X, op=ALU.add), ps4.b, impA.b)
                v3 = ps4v[:, :, 3]
                add("dve", lambda e, v3=v3: e.scalar_tensor_tensor(out=imp.t, in0=v3, scalar=-0.5, in1=impA.t, op0=ALU.mult, op1=ALU.add),
                    ps4.b + impA.b, imp.b)
                add("dve", lambda e, v3=v3: e.scalar_tensor_tensor(out=imp.t[:, 1:32], in0=v3[:, 0:31], scalar=0.5, in1=imp.t[:, 1:32],
                                                                   op0=ALU.mult, op1=ALU.add), ps4.b + imp.b, imp.b)
                if NSA_STOP <= 2.25:
                    continue
                add("dve", lambda e, i=i, selb=selb: e.tensor_tensor(out=sc.t, in0=imp.t, in1=selb.t[:, :], op=ALU.max),
                    imp.b + selb.b, sc.b)
                add("dve", lambda e, i=i, selv=selv: e.tensor_tensor(out=sc.t, in0=sc.t, in1=selv.t[:, :], op=ALU.add), sc.b + selv.b, sc.b)
                if NSA_STOP <= 2.3:
                    continue
                add("dve", lambda e: e.max(out=m8a.t, in_=sc.t), sc.b, m8a.b)
                add("dve", lambda e: e.match_replace(out=sc2.t, in_to_replace=m8a.t, in_values=sc.t, imm_value=-3.0e38),
                    sc.b + m8a.b, sc2.b)
                add("dve", lambda e: e.max(out=m8b.t, in_=sc2.t), sc2.b, m8b.b)
                add("dve", lambda e: e.tensor_scalar(out=sc2.t, in0=sc.t, scalar1=m8b.t[:, 7:8], scalar2=1.0, op0=ALU.is_ge,
                                                     op1=ALU.subtract), sc.b + m8b.b, sc2.b)
                add("dve", lambda e: e.tensor_scalar(out=sbias.t, in0=sc2.t, scalar1=-NEG, scalar2=None, op0=ALU.mult), sc2.b, sbias.b)
                if NSA_STOP <= 2.4:
                    continue
                transpose_to(selT.t[0:8, 0:nch, :], selT.b, [sbias.t[:, c * 8:(c + 1) * 8] for c in range(nch)], sbias.b, evac="dve")
                poc = ps_o[rr("pso", 2)]

                def f(e, poc=poc):
                    ins = None
                    for h in range(4):
                        ins = e.matmul(poc.t[:, h * P:(h + 1) * P], pcT_b.t[:, h, :], vc.t[:, :], start=True, stop=True)
                    return ins
                add("pe", f, pcT_b.b + vc.b, poc.b)
                g0 = bgate.t[:, i, :].rearrange("p (h t) -> p h t", t=3)[:, :, 0:1].to_broadcast([P, 4, P])
                add("dve", lambda e, poc=poc, g0=g0: e.tensor_tensor(out=ocmp.t, in0=poc.t[:, :].rearrange("p (h n) -> p h n", n=P), in1=g0,
                                                                     op=ALU.mult), poc.b + bgate.b, ocmp.b)
                for h in range(4):
                    if NSA_STOP <= 3:
                        break
                    base = alloc_banks(nch)

                    def f(e, i=i, h=h, nk=nk, nch=nch, o0=base * 512):
                        ins = None
                        for c in range(nch):
                            k0 = c * 512
                            n = min(512, nk - k0)
                            last = (c == nch - 1)
                            e.matmul(ps_s.t[:, o0 + k0:o0 + k0 + n], qT4.t[:, h, i * P:(i + 1) * P], ksT.t[:, k0:k0 + n], start=True, stop=False)
                            ins = e.matmul(ps_s.t[:, o0 + k0:o0 + k0 + n], selT.t[0:8, c, :], e8.t[0:8, 0:n], start=False, stop=not last)
                            if last:
                                ins = e.matmul(ps_s.t[:, o0 + nk - 256:o0 + nk], ident_bf.t[:], mask_c.t[:, 0:256], start=False, stop=True)
                        return ins
                    add("pe", f, qT4.b + ksT.b + selT.b + e8.b + ident_bf.b + mask_c.b, ps_s.b[base:base + nch])
                    pb = pbs[rr("pbf", 2)]
                    rs_, rsb_ = softmax_row(nk, scale, pb, base)
                    pos_ = ps_o[rr("pso", 2)]
                    pv_accumulate(nkb, pb, vs, pos_, pTs)
                    cs, csb = newsmall()
                    add("dve", lambda e, cs=cs, rs_=rs_, i=i, h=h: e.tensor_tensor(out=cs, in0=rs_, in1=bgate.t[:, i, 3 * h + 1:3 * h + 2],
                                                                                  op=ALU.mult), [rsb_] + bgate.b, [csb])
                    add("dve", lambda e, cs=cs, pos_=pos_, h=h: e.scalar_tensor_tensor(out=acc.t, in0=pos_.t[:, 0:P], scalar=cs,
                                                                                      in1=ocmp.t[:, h, :], op0=ALU.mult, op1=ALU.add),
                        pos_.b + [csb] + ocmp.b, acc.b)
                    kb0 = max(0, 2 * i - 4)
                    nkbw = 2 * i + 2 - kb0
                    nkw = nkbw * P
                    moff = (kb0 - (2 * i - 4)) * P
                    nchw = (nkw + 511) // 512

                    basew = alloc_banks(nchw)

                    def f(e, i=i, h=h, kb0=kb0, nkw=nkw, nchw=nchw, moff=moff, o0=basew * 512):
                        ins = None
                        for c in range(nchw):
                            k0 = c * 512
                            n = min(512, nkw - k0)
                            e.matmul(ps_s.t[:, o0 + k0:o0 + k0 + n], qT4.t[:, h, i * P:(i + 1) * P], kwT.t[:, kb0 * P + k0:kb0 * P + k0 + n],
                                     start=True, stop=False)
                            ins = e.matmul(ps_s.t[:, o0 + k0:o0 + k0 + n], ident_bf.t[:], mask_w.t[:, moff + k0:moff + k0 + n], start=False, stop=True)
                        return ins
                    add("pe", f, qT4.b + kwT.b + ident_bf.b + mask_w.b, ps_s.b[basew:basew + nchw])
                    pb = pbs[rr("pbf", 2)]
                    rw_, rwb_ = softmax_row(nkw, scale, pb, basew)
                    pow_ = ps_o[rr("pso", 2)]
                    pv_accumulate(nkbw, pb, vw, pow_, pTs, kb_off=kb0)
                    cw, cwb = newsmall()
                    add("dve", lambda e, cw=cw, rw_=rw_, i=i, h=h: e.tensor_tensor(out=cw, in0=rw_, in1=bgate.t[:, i, 3 * h + 2:3 * h + 3],
                                                                                  op=ALU.mult), [rwb_] + bgate.b, [cwb])
                    add("dve", lambda e, cw=cw, pow_=pow_: e.scalar_tensor_tensor(out=acc.t, in0=pow_.t[:, 0:P], scalar=cw, in1=acc.t,
                                                                                 op0=ALU.mult, op1=ALU.add),
                        pow_.b + [cwb] + acc.b, acc.b)
                    finish_head(i, acc.t, acc.b, None, None, 4 + h, mts)
            pg.barrier()

        def post_phase(l, final, xsrc, ydst):
            ar.reset()
            gbc = ar.alloc([D], F32)
            xin = [ar.alloc([D], F32) for _ in range(2)]
            ytmp = ar.alloc([D], F32)
            junk = ar.alloc([D], BF16)
            for kc in range(16):
                ws = wst[rr("wst", 2)]
                wsv = ws.t[:, :, :].rearrange("p a b -> p (a b)")
                dma(wsv, w_out[l, kc * P:(kc + 1) * P, :], [], ws.b)
                eng = "pool" if kc % 2 == 0 else "dve"
                add(eng, lambda e, kc=kc, wsv=wsv: e.tensor_copy(hTa.t[:, kc, :], wsv), ws.b, [hTa.b[kc]])
            dma(gbc.t, post_g[l, :].partition_broadcast(P), [], gbc.b)
            for i in range(NQ):
                def f(e, i=i):
                    ins = None
                    for n0 in range(4):
                        for kc in range(16):
                            ins = e.matmul(ps_s.t[:, n0 * 512:(n0 + 1) * 512], mixT.t[:, kc, i * P:(i + 1) * P],
                                           hTa.t[:, kc, n0 * 512:(n0 + 1) * 512], start=(kc == 0), stop=(kc == 15))
                    return ins
                add("pe", f, mixT.b + hTa.b, ps_s.b)
                xt = xin[rr("xin", 2)]
                sap, sbufs = xsrc("q", i)
                dma(xt.t, sap, sbufs, xt.b)
                ss, ssb = newsmall()
                add("act", lambda e, ss=ss: e.activation(out=junk.t, in_=ps_s.t[:, :], func=AF.Square, accum_out=ss),
                    ps_s.b, junk.b + [ssb])
                rs, rsb = rms_scale(ss, ssb, D)
                add("dve", lambda e, rs=rs: e.scalar_tensor_tensor(out=ytmp.t, in0=ps_s.t[:, :], scalar=rs, in1=gbc.t,
                                                                   op0=ALU.mult, op1=ALU.mult),
                    ps_s.b + [rsb] + gbc.b, ytmp.b)
                add("pool", lambda e, xt=xt: e.tensor_tensor(out=ytmp.t, in0=ytmp.t, in1=xt.t, op=ALU.add),
                    ytmp.b + xt.b, ytmp.b)
                if ydst is None:
                    final.append(dma(yout[i * P:(i + 1) * P, :], ytmp.t, ytmp.b, []))
                else:
                    dap, dbufs = ydst(i)
                    dma(dap, ytmp.t, ytmp.b, dbufs)
                    if i % 2 == 1:
                        j = i // 2
                        add_cc(j)
            pg.barrier()

        final = []
        x1b = [Buf() for _ in range(4)]
        gab = [Buf() for _ in range(4)]

        def add_cc(j):
            pg.add("pool", lambda e: e.collective_compute("AllGather", ALU.bypass,
                                                          replica_groups=[[0, 1], [2, 3], [4, 5], [6, 7]],
                                                          ins=[x1own_t[j].ap().opt()], outs=[gath_t[j].ap().opt()]),
                   [x1b[j]], [gab[j]], dma="cc")

        def xsrc_in(which, t):
            if which == "a":
                return xa[t * P:(t + 1) * P, :], []
            return xq[t * P:(t + 1) * P, :], []

        def xsrc_mid(which, t):
            if which == "a":
                r_, i_ = t % 2, t // 2
                j_, k_ = i_ // 2, i_ % 2
                return gath_t[j_].ap()[r_ * 2 * P + k_ * P:r_ * 2 * P + (k_ + 1) * P, :], [gab[j_]]
            j_, k_ = t // 2, t % 2
            return x1own_t[j_].ap()[k_ * P:(k_ + 1) * P, :], [x1b[j_]]

        def ydst_mid(i):
            j_, k_ = i // 2, i % 2
            return x1own_t[j_].ap()[k_ * P:(k_ + 1) * P, :], [x1b[j_]]

        for li, l in enumerate(layers):
            xsrc = xsrc_in if li == 0 else xsrc_mid
            ydst = None if li == len(layers) - 1 else ydst_mid
            phase_norm(l, xsrc)
            for c in range(16):
                mname = MIXERS[c // 4]
                if mname not in mixers:
                    add("pool", lambda e, c=c: e.memset(mixT.t[:, c, :], 0.0), [], [mixT.b[c]])
            if "sb" in mixers:
                mixer_sb(l)
            if "nsa" in mixers:
                mixer_nsa(l)
            if "fox" in mixers:
                mixer_fox(l)
            if "mla" in mixers:
                mixer_mla(l)
            post_phase(l, final, xsrc, ydst)

        with nc.Block() as block:
            pg.emit(block, final)
    return nc


def _consts(r):
    bf = ml_dtypes.bfloat16
    c = {}
    c["c_ident_bf"] = np.eye(P, dtype=np.float32).astype(bf)
    c["c_ident_f"] = np.eye(P, dtype=np.float32)
    p = np.arange(P)[:, None]
    col = np.arange(256)[None, :]
    c["c_mask_c"] = np.where(col <= p + 128 * r, 0.0, NEG).astype(np.float32).astype(bf)
    c["c_mask_s"] = np.where(col < p + 128 * r, 0.0, NEG).astype(np.float32).astype(bf)
    colw = np.arange(768)[None, :]
    c["c_mask_w"] = np.where((colw <= 512 + 128 * r + p) & (colw > 128 * r + p), 0.0, NEG).astype(np.float32).astype(bf)
    qpos = (np.arange(NQ)[None, :] * 2 + r) * P + np.arange(P)[:, None]
    cmp_end = np.arange(P) * 16 + 31
    vis = (cmp_end[None, None, :] <= qpos[:, :, None]) & (np.arange(P)[None, None, :] < NCMP)
    c["c_mask_cmp"] = np.where(vis, 0.0, NEG).astype(np.float32).astype(bf)
    c["c_cmp01"] = vis.astype(np.float32)
    sel = np.arange(32)[None, None, :]
    cur = (qpos // 64)[:, :, None]
    forced = (sel == 0) | (sel == cur) | (sel == cur - 1)
    valid = sel <= cur
    c["c_selbias"] = np.where(forced, 1e6, 0.0).astype(np.float32)
    c["c_selvalid"] = np.where(valid, 0.0, -1e30).astype(np.float32)
    cmp_start = np.arange(NCMP) * 16
    sel_start = np.arange(32) * 64
    ov = np.clip(np.minimum(cmp_start[:, None] + 32, sel_start[None, :] + 64)
                 - np.maximum(cmp_start[:, None], sel_start[None, :]), 0, None)
    c2s = np.zeros((P, 32), np.float32)
    c2s[:NCMP] = (ov / 32).astype(np.float32)
    c["c_c2s"] = c2s
    c["c_e8"] = (np.arange(512)[None, :] // 64 == np.arange(8)[:, None]).astype(np.float32).astype(bf)

    def tables(pos, half):
        inv = (np.float32(10000.0) ** (-np.arange(half, dtype=np.float32) / np.float32(half))).astype(np.float32)
        ang = pos.astype(np.float32)[..., None] * inv
        return np.cos(ang).astype(np.float32), np.sin(ang).astype(np.float32)
    pos_all = np.arange(NT)[None, :] * P + np.arange(P)[:, None]
    c["c_cosa"], c["c_sina"] = tables(pos_all, 64)
    c["c_cosq"], c["c_sinq"] = tables(qpos, 64)
    c["c_cosc"], c["c_sinc"] = tables(cmp_end, 64)
    c["c_cosa32"], c["c_sina32"] = tables(pos_all, 32)
    c["c_cosq32"], c["c_sinq32"] = tables(qpos, 32)
    sel4 = np.zeros((4, 4, P), np.float32)
    for h in range(4):
        sel4[h, h, :] = np.float32(128 ** 0.5)
    c["c_sel4"] = sel4
    return c


_WNAMES = ("pre_norm_g", "post_norm_g", "w_in", "b_in", "w_out", "fox_forget_bias",
           "nsa_cmp_pos_k", "nsa_cmp_w1_k", "nsa_cmp_w2_k", "nsa_cmp_pos_v", "nsa_cmp_w1_v", "nsa_cmp_w2_v",
           "mla_q_norm_g", "mla_w_uq", "mla_kv_norm_g", "mla_w_ukv")


def _own_rows(xb, r):
    return np.ascontiguousarray(xb.reshape(NQ, 2, P, D)[:, r].reshape(NQ * P, D))


def run_layers(x, weights, layers, dbg=None, mixers=MIXERS):
    nc = build(layers, dbg, mixers)
    in_maps = []
    for c in range(8):
        b, r = c // 2, c % 2
        m = {"xa": np.ascontiguousarray(x[b]), "xq": _own_rows(x[b], r)}
        for n in _WNAMES:
            m[n] = weights[n]
        m.update(_consts(r))
        in_maps.append(m)
    res = run_bass_kernel_spmd(nc, in_maps, core_ids=list(range(8)))
    out = np.empty((NB, S, D), np.float32)
    for c in range(8):
        b, r = c // 2, c % 2
        out[b].reshape(NQ, 2, P, D)[:, r] = res.results[c]["y"].reshape(NQ, P, D)
    return out, res


def kernel(**inputs):
    x = np.ascontiguousarray(np.asarray(inputs["x"], dtype=np.float32))
    weights = {n: np.ascontiguousarray(np.asarray(inputs[n], dtype=np.float32)) for n in _WNAMES}
    x, _ = run_layers(x, weights, list(range(DEPTH)))
    return x
```

```python
import numpy as np
import ml_dtypes
from contextlib import ExitStack
import concourse.bass as bass
import concourse.mybir as mybir
from concourse.bass_utils import run_bass_kernel_spmd

F32 = mybir.dt.float32
BF16 = mybir.dt.bfloat16
AF = mybir.ActivationFunctionType
ALU = mybir.AluOpType
AX = mybir.AxisListType

D = 2048
S = 2048
NB = 4
DEPTH = 2
INW = 6992
NT = 16
NQ = 8
P = 128
EPS = 1e-6
NEG = -30000.0
NCMP = 127

OFF = {}
_o = 0
for _n, _w in (("sb_q", 512), ("sb_k", 512), ("sb_v", 512), ("sb_gate", 512),
               ("nsa_q", 512), ("nsa_k_cmp", 128), ("nsa_v_cmp", 128), ("nsa_k_sel", 128),
               ("nsa_v_sel", 128), ("nsa_k_win", 128), ("nsa_v_win", 128), ("nsa_branch", 12),
               ("nsa_gate", 512), ("fox_q", 512), ("fox_k", 512), ("fox_v", 512), ("fox_f", 4),
               ("fox_gate", 512), ("mla_cq", 384), ("mla_ckv", 128), ("mla_k_rope", 64),
               ("mla_gate", 512)):
    OFF[_n] = _o
    _o += _w
assert _o == INW


_ALL_BUFS = []


class Buf:
    __slots__ = ("lw", "rd", "rd_dma")

    def __init__(self):
        self.lw = None
        self.rd = {}
        self.rd_dma = []
        _ALL_BUFS.append(self)


class Op:
    __slots__ = ("eng", "fn", "deps", "signal", "count", "is_dma", "dsem", "dval", "dprev")


class Prog:
    ENGS = ("pe", "act", "dve", "pool", "sp")

    def __init__(self, nc, stack, n_dma_sems=12):
        self.nc = nc
        self.ops = {e: [] for e in self.ENGS}
        self.esem = {e: stack.enter_context(nc.semaphore("es_" + e)) for e in self.ENGS}
        self.dsems = {}
        self.dcount = {}
        self.drr = {}
        for e in ("sp", "pool", "act"):
            self.dsems[e] = [stack.enter_context(nc.semaphore("ds_%s%d" % (e, i))) for i in range(n_dma_sems)]
            self.dcount[e] = [0] * n_dma_sems
            self.drr[e] = 0
        self.dsems["cc"] = [stack.enter_context(nc.semaphore("cc_sem"))]
        self.dcount["cc"] = [0]

    def add(self, eng, fn, reads=(), writes=(), dma=False):
        op = Op()
        op.eng = eng
        op.fn = fn
        op.signal = False
        op.count = 0
        op.is_dma = dma
        op.dsem = None
        op.dval = 0
        op.dprev = 0
        me = (eng, len(self.ops[eng]))
        deps = set()
        for b in reads:
            if b.lw is not None:
                deps.add(b.lw)
        for b in writes:
            if b.lw is not None:
                deps.add(b.lw)
            for e2, i2 in b.rd.items():
                deps.add((e2, i2))
            for d in b.rd_dma:
                deps.add(d)
        needed = []
        for d in deps:
            if d == me:
                continue
            dop = self.ops[d[0]][d[1]]
            if dop.is_dma:
                needed.append(d)
            elif d[0] == eng and eng == "pe":
                continue
            else:
                dop.signal = True
                needed.append(d)
        op.deps = needed
        if dma == "cc":
            op.dsem = ("cc", 0)
            op.dprev = 0
            self.dcount["cc"][0] += 1
            op.dval = self.dcount["cc"][0]
        elif dma:
            k = self.drr[eng]
            self.drr[eng] = (k + 1) % len(self.dsems[eng])
            op.dsem = (eng, k)
            op.dprev = self.dcount[eng][k] * 16
            self.dcount[eng][k] += 1
            op.dval = self.dcount[eng][k] * 16
        self.ops[eng].append(op)
        for b in writes:
            b.lw = me
            b.rd = {}
            b.rd_dma = []
        wset = set(id(b) for b in writes)
        for b in reads:
            if id(b) in wset:
                continue
            if dma:
                b.rd_dma.append(me)
            else:
                b.rd[eng] = me[1]
        return me

    def barrier(self):
        deps = set()
        for b in _ALL_BUFS:
            if b.lw is not None:
                deps.add(b.lw)
            for e2, i2 in b.rd.items():
                deps.add((e2, i2))
            for d in b.rd_dma:
                deps.add(d)
        for e in self.ENGS:
            if self.ops[e]:
                last = (e, len(self.ops[e]) - 1)
                if not self.ops[e][-1].is_dma and self.ops[e][-1].fn is not None:
                    deps.add(last)
        for e in self.ENGS:
            op = Op()
            op.eng = e
            op.fn = None
            op.signal = False
            op.count = 0
            op.is_dma = False
            op.dsem = None
            op.dval = 0
            op.dprev = 0
            mx = {}
            needed = []
            for d in deps:
                dop = self.ops[d[0]][d[1]]
                if dop.is_dma:
                    needed.append(d)
                else:
                    if d[0] == e and e == "pe":
                        continue
                    mx[d[0]] = max(mx.get(d[0], -1), d[1])
            for e2, i2 in mx.items():
                self.ops[e2][i2].signal = True
                needed.append((e2, i2))
            op.deps = needed
            self.ops[e].append(op)
        for b in _ALL_BUFS:
            b.lw = None
            b.rd = {}
            b.rd_dma = []

    def emit(self, block, final_waits):
        nc = self.nc
        for e in self.ENGS:
            c = 0
            for op in self.ops[e]:
                if op.signal and not op.is_dma:
                    c += 1
                    op.count = c
        prog = self

        def run(e, engobj):
            waited = {}
            for op in prog.ops[e]:
                for d in op.deps:
                    dop = prog.ops[d[0]][d[1]]
                    if dop.is_dma:
                        key = ("d",) + dop.dsem
                        val = dop.dval
                        sem = prog.dsems[dop.dsem[0]][dop.dsem[1]]
                    else:
                        key = ("e", d[0])
                        val = dop.count
                        sem = prog.esem[d[0]]
                    if waited.get(key, 0) >= val:
                        continue
                    engobj.wait_ge(sem, val)
                    waited[key] = val
                if op.fn is None:
                    continue
                if op.is_dma:
                    key = ("d",) + op.dsem
                    sem = prog.dsems[op.dsem[0]][op.dsem[1]]
                    if op.dprev > 0 and waited.get(key, 0) < op.dprev:
                        engobj.wait_ge(sem, op.dprev)
                        waited[key] = op.dprev
                    ins = op.fn(engobj)
                    if op.dsem[0] == "cc":
                        ins.then_inc(sem)
                    else:
                        ins.then_inc(sem, 16)
                else:
                    ins = op.fn(engobj)
                    if op.signal:
                        ins.then_inc(prog.esem[e], 1)
            if e == "sp":
                for d in final_waits:
                    dop = prog.ops[d[0]][d[1]]
                    sem = prog.dsems[dop.dsem[0]][dop.dsem[1]]
                    engobj.wait_ge(sem, dop.dval)

        @block.tensor
        def _(eng):
            run("pe", eng)

        @block.scalar
        def _(eng):
            run("act", eng)

        @block.vector
        def _(eng):
            run("dve", eng)

        @block.gpsimd
        def _(eng):
            run("pool", eng)

        @block.sync
        def _(eng):
            run("sp", eng)


class T:
    def __init__(self, t, nbuf=1):
        self.t = t
        self.b = [Buf() for _ in range(nbuf)]


ARENA_BYTES = 50 * 1024
MIXERS = ("sb", "nsa", "fox", "mla")


class Arena:
    def __init__(self, base):
        self.base = base
        self.off = 0

    def reset(self, to=0):
        self.off = to

    def alloc(self, shape, dt, nbuf=1):
        n = 1
        for v in shape:
            n *= v
        nbytes = n * (4 if dt == F32 else 2)
        nbytes = (nbytes + 7) // 8 * 8
        assert self.off + nbytes <= ARENA_BYTES, ("arena overflow", self.off, nbytes)
        ap = self.base[:, self.off // 2:(self.off + nbytes) // 2]
        self.off += nbytes
        if dt == F32:
            ap = ap.bitcast(F32)
        ap = ap[:, 0:n]
        if len(shape) == 2:
            ap = ap.rearrange("p (a b) -> p a b", a=shape[0])
        elif len(shape) == 3:
            ap = ap.rearrange("p (a b c) -> p a b c", a=shape[0], b=shape[1])
        return T(ap, nbuf)


NSA_STOP = 99


def build(layers, dbg=None, mixers=MIXERS):
    del _ALL_BUFS[:]
    nc = bass.Bass("TRN2", target_bir_lowering=False)
    dr = {}

    def din(name, shape, dt=F32):
        dr[name] = nc.dram_tensor(name, list(shape), dt, kind="ExternalInput").ap()
        return dr[name]

    xa = din("xa", [S, D])
    xq = din("xq", [NQ * P, D])
    pre_g = din("pre_norm_g", [DEPTH, D])
    post_g = din("post_norm_g", [DEPTH, D])
    w_in = din("w_in", [DEPTH, D, INW])
    b_in = din("b_in", [DEPTH, INW])
    w_out = din("w_out", [DEPTH, D, D])
    fox_fb = din("fox_forget_bias", [DEPTH, 4])
    pos_kv = [din("nsa_cmp_pos_k", [DEPTH, 32, 128]), din("nsa_cmp_pos_v", [DEPTH, 32, 128])]
    w1_kv = [din("nsa_cmp_w1_k", [DEPTH, 4096, 128]), din("nsa_cmp_w1_v", [DEPTH, 4096, 128])]
    w2_kv = [din("nsa_cmp_w2_k", [DEPTH, 128, 128]), din("nsa_cmp_w2_v", [DEPTH, 128, 128])]
    qng = din("mla_q_norm_g", [DEPTH, 384])
    w_uq = din("mla_w_uq", [DEPTH, 384, 768])
    kvng = din("mla_kv_norm_g", [DEPTH, 128])
    w_ukv = din("mla_w_ukv", [DEPTH, 128, 1024])
    c_ident_bf = din("c_ident_bf", [P, P], BF16)
    c_ident_f = din("c_ident_f", [P, P])
    c_mask_c = din("c_mask_c", [P, 256], BF16)
    c_mask_s = din("c_mask_s", [P, 256], BF16)
    c_mask_w = din("c_mask_w", [P, 768], BF16)
    c_mask_cmp = din("c_mask_cmp", [P, NQ, P], BF16)
    c_cmp01 = din("c_cmp01", [P, NQ, P])
    c_selbias = din("c_selbias", [P, NQ, 32])
    c_selvalid = din("c_selvalid", [P, NQ, 32])
    c_c2s = din("c_c2s", [P, 32])
    c_e8 = din("c_e8", [8, 512], BF16)
    c_cosa = din("c_cosa", [P, NT, 64])
    c_sina = din("c_sina", [P, NT, 64])
    c_cosq = din("c_cosq", [P, NQ, 64])
    c_sinq = din("c_sinq", [P, NQ, 64])
    c_cosc = din("c_cosc", [P, 64])
    c_sinc = din("c_sinc", [P, 64])
    c_cosa32 = din("c_cosa32", [P, NT, 32])
    c_sina32 = din("c_sina32", [P, NT, 32])
    c_cosq32 = din("c_cosq32", [P, NQ, 32])
    c_sinq32 = din("c_sinq32", [P, NQ, 32])
    c_sel3 = din("c_sel3", [P, 4, P], BF16)
    yout = nc.dram_tensor("y", [NQ * P, D], F32, kind="ExternalOutput").ap()
    x1own_t = [nc.dram_tensor("x1own%d" % j, [2 * P, D], F32) for j in range(4)]
    gath_t = [nc.dram_tensor("gath%d" % j, [4 * P, D], F32) for j in range(4)]

    with ExitStack() as st:
        pg = Prog(nc, st)

        def sb(name, shape, dt, nbuf=1):
            return T(st.enter_context(nc.sbuf_tensor(name, list(shape), dt)), nbuf)

        def ps(name, shape, dt, nbuf=1):
            return T(st.enter_context(nc.psum_tensor(name, list(shape), dt)), nbuf)

        hTa = sb("hTa", [P, 16, S], BF16, 16)
        hTq = sb("hTq", [P, 16, NQ * P], BF16, 16)
        mixT = sb("mixT", [P, 16, NQ * P], BF16, 16)
        wst = [sb("wst%d" % i, [P, 16, P], F32) for i in range(2)]
        wbf = [sb("wbf%d" % i, [P, 16, P], BF16) for i in range(2)]
        ident_bf = sb("ident_bf", [P, P], BF16)
        ident_f = sb("ident_f", [P, P], F32)
        mask_c = sb("mask_c", [P, 256], BF16)
        mask_s = sb("mask_s", [P, 256], BF16)
        mask_w = sb("mask_w", [P, 768], BF16)
        small = sb("small", [P, 64], F32, 64)
        small4 = sb("small4", [P, 64], F32, 16)
        bias_fm = [sb("bias_fm%d" % i, [P, 1], F32) for i in range(3)]
        bias_bc = [sb("bias_bc%d" % i, [P, P], F32) for i in range(3)]
        arena_t = st.enter_context(nc.sbuf_tensor("arena", [P, ARENA_BYTES // 2], BF16))
        ar = Arena(arena_t)
        ps_s = ps("ps_s", [P, 2048], F32, 4)
        ps_t = [ps("ps_t%d" % i, [P, 1024], BF16) for i in range(2)]
        ps_o = [ps("ps_o%d" % i, [P, 512], F32) for i in range(2)]

        cnt = {}

        def rr(key, n):
            v = cnt.get(key, 0)
            cnt[key] = v + 1
            return v % n

        def newsmall(w=1):
            if w == 1:
                i = rr("sm", 64)
                return small.t[:, i:i + 1], small.b[i]
            i = rr("sm4", 16)
            return small4.t[:, i * 4:i * 4 + 4], small4.b[i]

        def view(tobj, ap):
            r = T(ap, 0)
            r.b = tobj.b
            return r

        def dma(out_ap, in_ap, reads, writes, q="sp"):
            return pg.add(q, lambda e: e.dma_start(out=out_ap, in_=in_ap), reads, writes, dma=True)

        for tile_, src_ in ((ident_bf, c_ident_bf), (ident_f, c_ident_f), (mask_c, c_mask_c),
                            (mask_s, c_mask_s), (mask_w, c_mask_w)):
            dma(tile_.t[:], src_, [], tile_.b)

        def add(eng, fn, reads, writes):
            return pg.add(eng, fn, reads, writes)

        def transpose_to(dst, dst_bufs, src_aps, src_bufs, evac="act", f32=False):
            n = len(src_aps)
            w = src_aps[0].shape[-1]
            rows = src_aps[0].shape[0]
            if f32:
                k = rr("pso", 2)
                pt = ps_o[k]
                idt = ident_f
                assert n <= 4
            else:
                k = rr("pst", 2)
                pt = ps_t[k]
                idt = ident_bf

            def f(e):
                ins = None
                for j, a in enumerate(src_aps):
                    ins = e.transpose(pt.t[0:w, j * P:j * P + rows], a, idt.t[0:rows, 0:rows])
                return ins
            add("pe", f, list(src_bufs) + idt.b, pt.b)
            if rows == P:
                src = pt.t[0:w, 0:n * P]
                if len(dst.shape) == 3:
                    src = src.rearrange("p (a b) -> p a b", b=P)
            elif n == 1:
                src = pt.t[0:w, 0:rows]
            else:
                src = pt.t[0:w, 0:n * P].rearrange("p (a b) -> p a b", b=P)[:, :, 0:rows]
            if evac == "act":
                add("act", lambda e: e.copy(dst, src), pt.b, dst_bufs)
            else:
                add("dve", lambda e: e.tensor_copy(dst, src), pt.b, dst_bufs)

        def load_w(src, nkc, ncols):
            s = rr("wst", 2)
            k = rr("wbf", 2)
            ws, wb = wst[s], wbf[k]
            wsv = ws.t[:, :, :].rearrange("p a b -> p (a b)")[:, 0:nkc * ncols].rearrange("p (a b) -> p a b", b=ncols)
            wbv = wb.t[:, :, :].rearrange("p a b -> p (a b)")[:, 0:nkc * ncols].rearrange("p (a b) -> p a b", b=ncols)
            dma(wsv, src.rearrange("(kc p) c -> p kc c", p=P), [], ws.b)
            if nkc >= 2:
                hk = nkc // 2
                add("dve", lambda e: e.tensor_copy(wbv[:, 0:hk, :], wsv[:, 0:hk, :]), ws.b, wb.b)
                add("act", lambda e: e.copy(wbv[:, hk:nkc, :], wsv[:, hk:nkc, :]), ws.b, wb.b)
            else:
                add("dve", lambda e: e.tensor_copy(wbv, wsv), ws.b, wb.b)
            return wbv, wb.b

        def load_bias_fm(l, c0, ncols):
            k = rr("bfm", 3)
            bt = bias_fm[k]
            dma(bt.t[0:ncols, :], b_in[l, c0:c0 + ncols].rearrange("(c o) -> c o", o=1), [], bt.b)
            return bt

        def load_bias_bc(l, c0, ncols):
            k = rr("bbc", 3)
            bt = bias_bc[k]
            dma(bt.t[:, 0:ncols], b_in[l, c0:c0 + ncols].partition_broadcast(P), [], bt.b)
            return bt

        def proj_fm(l, c0, ncols, src, ntok, dst_fn, dst_bufs):
            wbv, wbb = load_w(w_in[l, :, c0:c0 + ncols], 16, ncols)
            bt = load_bias_fm(l, c0, ncols)
            for t0 in range(0, ntok, 512):
                po = ps_o[rr("pso", 2)]

                def f(e, t0=t0, po=po):
                    ins = None
                    for kc in range(16):
                        ins = e.matmul(po.t[0:ncols, :], wbv[:, kc, :], src.t[:, kc, t0:t0 + 512],
                                       start=(kc == 0), stop=(kc == 15))
                    return ins
                add("pe", f, wbb + src.b, po.b)
                dst = dst_fn(t0, 512)
                add("act", lambda e, dst=dst, po=po: e.activation(out=dst, in_=po.t[0:ncols, :], func=AF.Identity,
                                                                   bias=bt.t[0:ncols, :], scale=1.0),
                    po.b + bt.b, dst_bufs)

        def proj_tm(l, c0, ncols, src, nblk, dst_fn, dst_bufs, blk0=0, w=None):
            if w is None:
                wbv, wbb = load_w(w_in[l, :, c0:c0 + ncols], 16, ncols)
                bt = load_bias_bc(l, c0, ncols)
            else:
                wbv, wbb, bt = w
            for b0 in range(blk0, blk0 + nblk, 4):
                po = ps_o[rr("pso", 2)]

                def f(e, b0=b0, po=po):
                    ins = None
                    for j in range(4):
                        for kc in range(16):
                            ins = e.matmul(po.t[:, j * P:j * P + ncols], src.t[:, kc, (b0 + j) * P:(b0 + j + 1) * P],
                                           wbv[:, kc, :], start=(kc == 0), stop=(kc == 15))
                    return ins
                add("pe", f, wbb + src.b, po.b)
                dst = dst_fn(b0, 4)
                pin = po.t[:, :].rearrange("p (j c) -> p j c", c=P)[:, :, 0:ncols]
                bb = bt.t[:, 0:ncols].unsqueeze(1).to_broadcast([P, 4, ncols])
                add("dve", lambda e, dst=dst, pin=pin, bb=bb: e.tensor_tensor(out=dst, in0=pin, in1=bb, op=ALU.add),
                    po.b + bt.b, dst_bufs)

        def gate_to_mixT(l, c0, ch):
            proj_tm(l, c0, P, hTq, NQ,
                    lambda b0, n: mixT.t[:, ch, b0 * P:(b0 + n) * P].rearrange("p (j c) -> p j c", c=P), [mixT.b[ch]])
            add("act", lambda e: e.activation(out=mixT.t[:, ch, :], in_=mixT.t[:, ch, :], func=AF.Silu),
                [mixT.b[ch]], [mixT.b[ch]])

        def rope_tm(x, xb, cos, sin, tb, out, ob, half, tmp):
            x1, x2 = x[:, :, 0:half], x[:, :, half:2 * half]
            o1, o2 = out[:, :, 0:half], out[:, :, half:2 * half]
            t1, t2 = tmp
            add("dve", lambda e: e.tensor_tensor(out=t1.t, in0=x1, in1=cos, op=ALU.mult), xb + tb, t1.b)
            add("pool", lambda e: e.tensor_tensor(out=t2.t, in0=x2, in1=sin, op=ALU.mult), xb + tb, t2.b)
            add("dve", lambda e: e.tensor_tensor(out=o1, in0=t1.t, in1=t2.t, op=ALU.subtract), t1.b + t2.b, ob)
            add("dve", lambda e: e.tensor_tensor(out=t1.t, in0=x2, in1=cos, op=ALU.mult), xb + tb + ob, t1.b)
            add("pool", lambda e: e.tensor_tensor(out=t2.t, in0=x1, in1=sin, op=ALU.mult), xb + tb + ob, t2.b)
            add("dve", lambda e: e.tensor_tensor(out=o2, in0=t1.t, in1=t2.t, op=ALU.add), t1.b + t2.b, ob)

        def rms_scale(ss, ssb, n):
            ms, msb = newsmall()
            add("dve", lambda e: e.tensor_scalar(out=ms, in0=ss, scalar1=1.0 / n, scalar2=EPS, op0=ALU.mult, op1=ALU.add),
                [ssb], [msb])
            sd, sdb = newsmall()
            add("act", lambda e: e.sqrt(sd, ms), [msb], [sdb])
            rs, rsb = newsmall()
            add("dve", lambda e: e.reciprocal(rs, sd), [sdb], [rsb])
            return rs, rsb

        bank_cur = [0]

        def alloc_banks(nch):
            if bank_cur[0] + nch > 4:
                bank_cur[0] = 0
            b0 = bank_cur[0]
            bank_cur[0] = (b0 + nch) % 4
            return b0

        def softmax_row(nk, scale, pb, base=0):
            nch = (nk + 511) // 512
            o0 = base * 512
            bufs = ps_s.b[base:base + nch]
            mx, mxb = newsmall()
            add("dve", lambda e: e.reduce_max(out=mx, in_=ps_s.t[:, o0:o0 + nk], axis=AX.X), bufs, [mxb])
            nm, nmb = newsmall()
            add("dve", lambda e: e.tensor_scalar(out=nm, in0=mx, scalar1=-scale, scalar2=None, op0=ALU.mult), [mxb], [nmb])
            l1, l1b = newsmall()
            add("act", lambda e: e.activation(out=pb.t[:, 0:nk], in_=ps_s.t[:, o0:o0 + nk], func=AF.Exp, bias=nm, scale=scale,
                                              accum_out=l1), bufs + [nmb], pb.b + [l1b])
            ri, rib = newsmall()
            add("dve", lambda e: e.reciprocal(ri, l1), [l1b], [rib])
            return ri, rib

        def pv_accumulate(nkb, pb, vt, po, pTs, kb_off=0):
            for g0 in range(0, nkb, 8):
                gn = min(8, nkb - g0)
                ptile = pTs[rr("pT", 2)]
                transpose_to(ptile.t[:, 0:gn, :], ptile.b,
                             [pb.t[:, (g0 + j) * P:(g0 + j + 1) * P] for j in range(gn)], pb.b,
                             evac="act" if (g0 // 8) % 2 == 0 else "dve")

                def f(e, g0=g0, gn=gn, ptile=ptile):
                    ins = None
                    for j in range(gn):
                        kb = g0 + j
                        ins = e.matmul(po.t[:, 0:P], ptile.t[:, j, :], vt.t[:, kb_off + kb, :],
                                       start=(kb == 0), stop=(kb == nkb - 1))
                    return ins
                add("pe", f, ptile.b + vt.b, po.b)

        def finish_head(i, src_ap, src_bufs, rinv, rinvb, ch, mts):
            mt = mts[rr("mixtm", 2)]
            gate = mixT.t[:, ch, i * P:(i + 1) * P]
            if rinv is not None:
                add("dve", lambda e: e.scalar_tensor_tensor(out=mt.t, in0=src_ap, scalar=rinv, in1=gate,
                                                            op0=ALU.mult, op1=ALU.mult),
                    src_bufs + [rinvb, mixT.b[ch]], mt.b)
            else:
                add("dve", lambda e: e.tensor_tensor(out=mt.t, in0=src_ap, in1=gate, op=ALU.mult),
                    src_bufs + [mixT.b[ch]], mt.b)
            transpose_to(mixT.t[:, ch, i * P:(i + 1) * P], [mixT.b[ch]], [mt.t], mt.b, evac="act")

        def alloc_attn_common():
            pbs = [ar.alloc([S], BF16) for _ in range(2)]
            pTs = [ar.alloc([8, P], BF16) for _ in range(2)]
            mts = [ar.alloc([P], BF16) for _ in range(2)]
            return pbs, pTs, mts

        def alloc_head_ops(n=2):
            return [(ar.alloc([S], BF16), ar.alloc([NT, P], BF16), ar.alloc([NQ * P], BF16)) for _ in range(n)]

        def phase_norm(l, xsrc):
            ar.reset()
            gbc = ar.alloc([D], F32)
            xin = [ar.alloc([D], F32) for _ in range(2)]
            hn = ar.alloc([D], BF16)
            dma(gbc.t, pre_g[l, :].partition_broadcast(P), [], gbc.b)
            for (which, nblk, dst) in (("a", NT, hTa), ("q", NQ, hTq)):
                for t in range(nblk):
                    xt = xin[rr("xin", 2)]
                    sap, sbufs = xsrc(which, t)
                    dma(xt.t, sap, sbufs, xt.b)
                    ss, ssb = newsmall()
                    add("act", lambda e, xt=xt, ss=ss: e.activation(out=hn.t, in_=xt.t, func=AF.Square, accum_out=ss),
                        xt.b, hn.b + [ssb])
                    rs, rsb = rms_scale(ss, ssb, D)
                    add("dve", lambda e, xt=xt, rs=rs: e.scalar_tensor_tensor(out=hn.t, in0=xt.t, scalar=rs, in1=gbc.t,
                                                                              op0=ALU.mult, op1=ALU.mult),
                        xt.b + [rsb] + gbc.b, hn.b)
                    for half in range(2):
                        c0 = half * 8
                        transpose_to(dst.t[:, c0:c0 + 8, t * P:(t + 1) * P], dst.b[c0:c0 + 8],
                                     [hn.t[:, (c0 + j) * P:(c0 + j + 1) * P] for j in range(8)], hn.b,
                                     evac="act" if half == 0 else "dve")
            pg.barrier()

        def peek_banks(nch):
            b0 = bank_cur[0] if bank_cur[0] + nch <= 4 else 0
            return set(range(b0, b0 + nch))

        def emit_rows(rows, after=None):
            cur_banks = set()
            done_a = [False] * len(rows)
            if rows:
                cur_banks = peek_banks(rows[0][2])
                rows[0][0]()
                done_a[0] = True
            for k in range(len(rows)):
                nxt_banks = set()
                if k + 1 < len(rows):
                    nxt_banks = peek_banks(rows[k + 1][2])
                    if not (nxt_banks & cur_banks):
                        rows[k + 1][0]()
                        done_a[k + 1] = True
                rows[k][1]()
                if k + 1 < len(rows) and not done_a[k + 1]:
                    nxt_banks = peek_banks(rows[k + 1][2])
                    rows[k + 1][0]()
                    done_a[k + 1] = True
                cur_banks = nxt_banks
                if after is not None:
                    after(k)

        def run_pipelined(nheads, proj_items, att_rows):
            for t in proj_items(0):
                t()
            for h in range(nheads):
                nxt = proj_items(h + 1) if h + 1 < nheads else []
                rows = att_rows(h)
                st_ = {"k": 0}

                def after(ai, nxt=nxt, rows=rows, st_=st_):
                    want = (ai + 1) * len(nxt) // len(rows)
                    while st_["k"] < want:
                        nxt[st_["k"]]()
                        st_["k"] += 1
                emit_rows(rows, after)
                while st_["k"] < len(nxt):
                    nxt[st_["k"]]()
                    st_["k"] += 1

        def mixer_sb(l):
            ar.reset()
            scale = 128 ** -0.5
            pbs, pTs, mts = alloc_attn_common()
            hops = alloc_head_ops(2)
            wk_sets = [[ar.alloc([512], F32) for _ in range(4)] for _ in range(2)]
            def proj_items(h):
                kTt, vt, qTt = hops[h % 2]
                return [
                    lambda: proj_fm(l, OFF["sb_q"] + h * P, P, hTq, NQ * P, lambda t0, n: qTt.t[:, t0:t0 + n], qTt.b),
                    lambda: proj_fm(l, OFF["sb_k"] + h * P, P, hTa, S, lambda t0, n: kTt.t[:, t0:t0 + n], kTt.b),
                    lambda: proj_tm(l, OFF["sb_v"] + h * P, P, hTa, NT, lambda b0, n: vt.t[:, b0:b0 + n, :], vt.b),
                    lambda: gate_to_mixT(l, OFF["sb_gate"] + h * P, 0 + h),
                ]

            def att_items(h):
                return [((lambda: None), (lambda i=i: sb_att(h, i)), 4) for i in range(NQ)]

            def sb_att(h, i):
                kTt, vt, qTt = hops[h % 2]
                if True:
                    nkb = 2 * i + 2
                    nk = nkb * P
                    nch = (nk + 511) // 512
                    pb = pbs[rr("pbf", 2)]
                    carry = None
                    for c in range(nch - 1, -1, -1):
                        k0 = c * 512
                        n = min(512, nk - k0)
                        last = (c == nch - 1)
                        bank = ps_s.b[c]
                        wk_e, wk_sp, wk_c, wk_t = wk_sets[rr("wk", 2)]

                        def f(e, i=i, k0=k0, n=n, last=last, nk=nk, qTt=qTt, kTt=kTt):
                            ins = e.matmul(ps_s.t[:, k0:k0 + n], qTt.t[:, i * P:(i + 1) * P], kTt.t[:, k0:k0 + n],
                                           start=True, stop=not last)
                            if last:
                                ins = e.matmul(ps_s.t[:, nk - 256:nk], ident_bf.t[:], mask_s.t[:, 0:256],
                                               start=False, stop=True)
                            return ins
                        add("pe", f, qTt.b + kTt.b + ident_bf.b + mask_s.b, [bank])
                        sv = ps_s.t[:, k0:k0 + n]
                        add("act", lambda e, sv=sv, n=n, wk_e=wk_e: e.activation(out=wk_e.t[:, 0:n], in_=sv, func=AF.Exp, scale=scale),
                            [bank], wk_e.b)
                        add("act", lambda e, n=n, wk_sp=wk_sp, wk_e=wk_e: e.activation(out=wk_sp.t[:, 0:n], in_=wk_e.t[:, 0:n], func=AF.Ln,
                                                                bias=1.0, scale=1.0), wk_e.b, wk_sp.b)
                        add("dve", lambda e, n=n, wk_c=wk_c, wk_sp=wk_sp: e.tensor_tensor_scan(out=wk_c.t[:, 0:n], data0=wk_sp.t[:, 0:n],
                                                                       data1=wk_sp.t[:, 0:n], initial=0.0,
                                                                       op0=ALU.add, op1=ALU.max),
                            wk_sp.b, wk_c.b)
                        add("dve", lambda e, sv=sv, n=n, wk_t=wk_t, wk_sp=wk_sp: e.scalar_tensor_tensor(out=wk_t.t[:, 0:n], in0=sv, scalar=scale,
                                                                                in1=wk_sp.t[:, 0:n], op0=ALU.mult,
                                                                                op1=ALU.subtract),
                            [bank] + wk_sp.b, wk_t.b)
                        add("pool", lambda e, n=n, wk_t=wk_t, wk_c=wk_c: e.tensor_tensor(out=wk_t.t[:, 0:n], in0=wk_t.t[:, 0:n],
                                                                   in1=wk_c.t[:, 0:n], op=ALU.add),
                            wk_t.b + wk_c.b, wk_t.b)
                        nb_, nbb = newsmall()
                        tot = wk_c.t[:, n - 1:n]
                        if carry is None:
                            add("dve", lambda e, nb_=nb_, tot=tot: e.tensor_scalar(out=nb_, in0=tot, scalar1=-1.0, scalar2=None,
                                                                                    op0=ALU.mult), wk_c.b, [nbb])
                        else:
                            cpr, cprb = carry
                            add("dve", lambda e, nb_=nb_, tot=tot, cpr=cpr: e.scalar_tensor_tensor(
                                out=nb_, in0=tot, scalar=-1.0, in1=cpr, op0=ALU.mult, op1=ALU.add), wk_c.b + [cprb], [nbb])
                        carry = (nb_, nbb)
                        add("act", lambda e, n=n, k0=k0, nb_=nb_, pb=pb, wk_t=wk_t: e.activation(out=pb.t[:, k0:k0 + n], in_=wk_t.t[:, 0:n],
                                                                                     func=AF.Exp, bias=nb_, scale=1.0),
                            wk_t.b + [nbb], pb.b)
                    po = ps_o[rr("pso", 2)]
                    pv_accumulate(nkb, pb, vt, po, pTs)
                    finish_head(i, po.t[:, 0:P], po.b, None, None, 0 + h, mts)
            run_pipelined(4, proj_items, att_items)
            pg.barrier()

        def softmax_heads_causal(i, qTt, kTt, vt, scale, pbs, pTs, mts, ch, extra=None, extra_bufs=()):
            nkb = 2 * i + 2
            nk = nkb * P
            nch = (nk + 511) // 512
            st_ = {}

            def A():
                base = alloc_banks(nch)
                st_["base"] = base
                o0 = base * 512

                def f(e):
                    ins = None
                    for c in range(nch):
                        k0 = c * 512
                        n = min(512, nk - k0)
                        last = (c == nch - 1)
                        ins = e.matmul(ps_s.t[:, o0 + k0:o0 + k0 + n], qTt.t[:, i * P:(i + 1) * P], kTt.t[:, k0:k0 + n],
                                       start=True, stop=(not last) and extra is None)
                        if extra is not None:
                            ins = extra(e, ps_s.t[:, o0 + k0:o0 + k0 + n], k0, n, not last)
                        if last:
                            ins = e.matmul(ps_s.t[:, o0 + nk - 256:o0 + nk], ident_bf.t[:], mask_c.t[:, 0:256], start=False, stop=True)
                    return ins
                add("pe", f, qTt.b + kTt.b + ident_bf.b + mask_c.b + list(extra_bufs), ps_s.b[base:base + nch])

            def B():
                pb = pbs[rr("pbf", 2)]
                rinv, rinvb = softmax_row(nk, scale, pb, st_["base"])
                po = ps_o[rr("pso", 2)]
                pv_accumulate(nkb, pb, vt, po, pTs)
                finish_head(i, po.t[:, 0:P], po.b, rinv, rinvb, ch, mts)
            return A, B, nch

        def mixer_fox(l):
            ar.reset()
            scale = 128 ** -0.5
            pbs, pTs, mts = alloc_attn_common()
            hops = alloc_head_ops(2)
            crow3 = ar.alloc([S], BF16)
            sel3 = ar.alloc([4, P], BF16)
            t_e = ar.alloc([512], F32)
            t_sp = ar.alloc([512], F32)
            cch = [ar.alloc([512], F32) for _ in range(2)]
            hi_t = ar.alloc([512], BF16)
            mid_t = ar.alloc([512], BF16)
            wf68 = ar.alloc([16, 68], BF16)
            fb = ar.alloc([2], F32)
            NP_ = 68
            dma(sel3.t, c_sel3, [], sel3.b)
            add("pool", lambda e: e.memset(crow3.t, 0.0), [], crow3.b)
            add("pool", lambda e: e.memset(wf68.t, 0.0), [], wf68.b)
            add("pool", lambda e: e.memset(fb.t, 0.0), [], fb.b)
            c0 = OFF["fox_f"]
            wbv, wbb = load_w(w_in[l, :, c0:c0 + 4], 16, 4)
            for rep in range(3):
                add("dve", lambda e, rep=rep: e.tensor_copy(wf68.t[:, :, rep * 32:rep * 32 + 4], wbv), wbb + wf68.b, wf68.b)
                dma(fb.t[rep * 32:rep * 32 + 4, 0:1], b_in[l, c0:c0 + 4].rearrange("(c o) -> c o", o=1), fb.b, fb.b)
                dma(fb.t[rep * 32:rep * 32 + 4, 1:2], fox_fb[l, :].rearrange("(c o) -> c o", o=1), fb.b, fb.b)
            nb_, nbb = newsmall()
            add("dve", lambda e: e.scalar_tensor_tensor(out=nb_[0:NP_, :], in0=fb.t[0:NP_, 0:1], scalar=-1.0, in1=fb.t[0:NP_, 1:2],
                                                        op0=ALU.mult, op1=ALU.subtract), fb.b, [nbb])
            prevc = None
            for ci, t0 in enumerate(range(0, S, 512)):
                po = ps_o[rr("pso", 2)]
                cc = cch[ci % 2]

                def f(e, t0=t0, po=po):
                    ins = None
                    for kc in range(16):
                        ins = e.matmul(po.t[0:NP_, :], wf68.t[:, kc, :], hTa.t[:, kc, t0:t0 + 512], start=(kc == 0), stop=(kc == 15))
                    return ins
                add("pe", f, wf68.b + hTa.b, po.b)
                add("act", lambda e, po=po: e.activation(out=t_e.t[0:NP_, :], in_=po.t[0:NP_, :], func=AF.Exp, bias=nb_[0:NP_, :],
                                                         scale=-1.0), po.b + [nbb], t_e.b)
                add("act", lambda e: e.activation(out=t_sp.t[0:NP_, :], in_=t_e.t[0:NP_, :], func=AF.Ln, bias=1.0, scale=1.0),
                    t_e.b, t_sp.b)
                init = 0.0 if prevc is None else prevc.t[0:NP_, 511:512]
                rdb = t_sp.b + ([] if prevc is None else prevc.b)
                add("dve", lambda e, cc=cc, init=init: e.tensor_tensor_scan(out=cc.t[0:NP_, :], data0=t_sp.t[0:NP_, :],
                                                                            data1=t_sp.t[0:NP_, :], initial=init,
                                                                            op0=ALU.add, op1=ALU.max), rdb, cc.b)
                add("dve", lambda e, cc=cc: e.tensor_scalar(out=t_e.t[0:NP_, :], in0=cc.t[0:NP_, :], scalar1=float(128 ** 0.5),
                                                            scalar2=None, op0=ALU.mult), cc.b, t_e.b)
                add("dve", lambda e: e.tensor_copy(hi_t.t[0:NP_, :], t_e.t[0:NP_, :]), t_e.b, hi_t.b)
                add("dve", lambda e: e.tensor_tensor(out=t_sp.t[0:NP_, :], in0=t_e.t[0:NP_, :], in1=hi_t.t[0:NP_, :], op=ALU.subtract),
                    t_e.b + hi_t.b, t_sp.b)
                add("dve", lambda e: e.tensor_copy(mid_t.t[0:NP_, :], t_sp.t[0:NP_, :]), t_sp.b, mid_t.b)
                add("dve", lambda e: e.tensor_tensor(out=t_sp.t[0:NP_, :], in0=t_sp.t[0:NP_, :], in1=mid_t.t[0:NP_, :], op=ALU.subtract),
                    t_sp.b + mid_t.b, t_sp.b)
                add("dve", lambda e, t0=t0: e.tensor_copy(crow3.t[0:4, t0:t0 + 512], hi_t.t[0:4, :]), hi_t.b, crow3.b)
                add("dve", lambda e, t0=t0: e.tensor_copy(crow3.t[32:36, t0:t0 + 512], mid_t.t[32:36, :]), mid_t.b, crow3.b)
                add("dve", lambda e, t0=t0: e.tensor_copy(crow3.t[64:68, t0:t0 + 512], t_sp.t[64:68, :]), t_sp.b, crow3.b)
                prevc = cc
            def proj_items(h):
                kTt, vt, qTt = hops[h % 2]
                return [
                    lambda: proj_fm(l, OFF["fox_q"] + h * P, P, hTq, NQ * P, lambda t0, n: qTt.t[:, t0:t0 + n], qTt.b),
                    lambda: proj_fm(l, OFF["fox_k"] + h * P, P, hTa, S, lambda t0, n: kTt.t[:, t0:t0 + n], kTt.b),
                    lambda: proj_tm(l, OFF["fox_v"] + h * P, P, hTa, NT, lambda b0, n: vt.t[:, b0:b0 + n, :], vt.b),
                    lambda: gate_to_mixT(l, OFF["fox_gate"] + h * P, 8 + h),
                ]

            def att_items(h):
                kTt, vt, qTt = hops[h % 2]

                def extra(e, out, k0, n, stop):
                    return e.matmul(out, sel3.t[:, h, :], crow3.t[:, k0:k0 + n], start=False, stop=stop)
                return [softmax_heads_causal(i, qTt, kTt, vt, scale, pbs, pTs, mts, 8 + h, extra, sel3.b + crow3.b)
                        for i in range(NQ)]
            run_pipelined(4, proj_items, att_items)
            pg.barrier()

        def mixer_mla(l):
            ar.reset()
            scale = 192 ** -0.5
            cqnT = ar.alloc([3, NQ * P], BF16)
            ckvnT = ar.alloc([S], BF16)
            krT = ar.alloc([S], BF16)
            wukv_b = ar.alloc([1024], BF16)
            cosq = ar.alloc([NQ, 32], F32)
            sinq = ar.alloc([NQ, 32], F32)
            mark = ar.off
            dma(cosq.t, c_cosq32, [], cosq.b)
            dma(sinq.t, c_sinq32, [], sinq.b)
            cosa = ar.alloc([NT, 32], F32)
            sina = ar.alloc([NT, 32], F32)
            gq = ar.alloc([384], F32)
            gkv = ar.alloc([P], F32)
            xt = ar.alloc([4, 384], F32)
            xn = ar.alloc([4, 384], BF16)
            t1 = ar.alloc([4, 32], F32)
            t2 = ar.alloc([4, 32], F32)
            dma(cosa.t, c_cosa32, [], cosa.b)
            dma(sina.t, c_sina32, [], sina.b)
            dma(gq.t, qng[l, :].partition_broadcast(P), [], gq.b)
            dma(gkv.t, kvng[l, :].partition_broadcast(P), [], gkv.b)
            ws = wst[rr("wst", 2)]
            wsv = ws.t[:, :, :].rearrange("p a b -> p (a b)")[:, 0:1024]
            dma(wsv, w_ukv[l, :, :], [], ws.b)
            add("pool", lambda e, wsv=wsv: e.tensor_copy(wukv_b.t, wsv), ws.b, wukv_b.b)

            def norm_rows(x_ap, xb, width, g_ap, gb, out_ap, ob):
                ss, ssb = newsmall()
                jt, jb = junkt
                add("act", lambda e: e.activation(out=jt[:, 0:width], in_=x_ap, func=AF.Square, accum_out=ss), xb, jb + [ssb])
                rs, rsb = rms_scale(ss, ssb, width)
                add("dve", lambda e: e.scalar_tensor_tensor(out=out_ap, in0=x_ap, scalar=rs, in1=g_ap, op0=ALU.mult, op1=ALU.mult),
                    xb + [rsb] + gb, ob)
            jk = ar.alloc([384], F32)
            junkt = (jk.t, jk.b)
            for b0 in range(0, NQ, 4):
                for cg in range(3):
                    wbv, wbb = load_w(w_in[l, :, OFF["mla_cq"] + cg * P:OFF["mla_cq"] + (cg + 1) * P], 16, P)
                    bt = load_bias_bc(l, OFF["mla_cq"] + cg * P, P)
                    po = ps_o[rr("pso", 2)]

                    def f(e, b0=b0, po=po, wbv=wbv):
                        ins = None
                        for j in range(4):
                            for kc in range(16):
                                ins = e.matmul(po.t[:, j * P:(j + 1) * P], hTq.t[:, kc, (b0 + j) * P:(b0 + j + 1) * P],
                                               wbv[:, kc, :], start=(kc == 0), stop=(kc == 15))
                        return ins
                    add("pe", f, wbb + hTq.b, po.b)
                    pin = po.t[:, :].rearrange("p (j c) -> p j c", c=P)
                    bb = bt.t[:, :].unsqueeze(1).to_broadcast([P, 4, P])
                    add("dve", lambda e, cg=cg, pin=pin, bb=bb: e.tensor_tensor(out=xt.t[:, :, cg * P:(cg + 1) * P], in0=pin, in1=bb,
                                                                                op=ALU.add), po.b + bt.b, xt.b)
                for j in range(4):
                    norm_rows(xt.t[:, j, :], xt.b, 384, gq.t, gq.b, xn.t[:, j, :], xn.b)
                for cg in range(3):
                    transpose_to(cqnT.t[:, cg, b0 * P:(b0 + 4) * P], cqnT.b,
                                 [xn.t[:, j, cg * P:(cg + 1) * P] for j in range(4)], xn.b, evac="dve")
            xk = ar.alloc([4, P], F32)
            xkn = ar.alloc([4, P], BF16)
            xr = ar.alloc([4, 64], F32)
            xrb = ar.alloc([4, 64], BF16)
            wbv1, wbb1 = load_w(w_in[l, :, OFF["mla_ckv"]:OFF["mla_ckv"] + P], 16, P)
            bt1 = load_bias_bc(l, OFF["mla_ckv"], P)
            wbv2, wbb2 = load_w(w_in[l, :, OFF["mla_k_rope"]:OFF["mla_k_rope"] + 64], 16, 64)
            bt2 = load_bias_bc(l, OFF["mla_k_rope"], 64)
            for b0 in range(0, NT, 4):
                po = ps_o[rr("pso", 2)]

                def f(e, b0=b0, po=po):
                    ins = None
                    for j in range(4):
                        for kc in range(16):
                            ins = e.matmul(po.t[:, j * P:(j + 1) * P], hTa.t[:, kc, (b0 + j) * P:(b0 + j + 1) * P],
                                           wbv1[:, kc, :], start=(kc == 0), stop=(kc == 15))
                    return ins
                add("pe", f, wbb1 + hTa.b, po.b)
                pin = po.t[:, :].rearrange("p (j c) -> p j c", c=P)
                bb = bt1.t[:, :].unsqueeze(1).to_broadcast([P, 4, P])
                add("dve", lambda e, pin=pin, bb=bb: e.tensor_tensor(out=xk.t, in0=pin, in1=bb, op=ALU.add), po.b + bt1.b, xk.b)
                for j in range(4):
                    norm_rows(xk.t[:, j, :], xk.b, P, gkv.t, gkv.b, xkn.t[:, j, :], xkn.b)
                transpose_to(ckvnT.t[:, b0 * P:(b0 + 4) * P], ckvnT.b, [xkn.t[:, j, :] for j in range(4)], xkn.b, evac="dve")
                po2 = ps_o[rr("pso", 2)]

                def f2(e, b0=b0, po2=po2):
                    ins = None
                    for j in range(4):
                        for kc in range(16):
                            ins = e.matmul(po2.t[:, j * 64:(j + 1) * 64], hTa.t[:, kc, (b0 + j) * P:(b0 + j + 1) * P],
                                           wbv2[:, kc, :], start=(kc == 0), stop=(kc == 15))
                    return ins
                add("pe", f2, wbb2 + hTa.b, po2.b)
                pin2 = po2.t[:, 0:256].rearrange("p (j c) -> p j c", c=64)
                bb2 = bt2.t[:, 0:64].unsqueeze(1).to_broadcast([P, 4, 64])
                add("dve", lambda e, pin2=pin2, bb2=bb2: e.tensor_tensor(out=xr.t, in0=pin2, in1=bb2, op=ALU.add), po2.b + bt2.b, xr.b)
                rope_tm(xr.t, xr.b, cosa.t[:, b0:b0 + 4, :], sina.t[:, b0:b0 + 4, :], cosa.b + sina.b, xrb.t, xrb.b, 32, (t1, t2))
                transpose_to(krT.t[0:64, b0 * P:(b0 + 4) * P], krT.b, [xrb.t[:, j, :] for j in range(4)], xrb.b, evac="dve")
            pg.barrier()
            ar.reset(mark)
            pbs, pTs, mts = alloc_attn_common()
            kTt = ar.alloc([S], BF16)
            vt = ar.alloc([NT, P], BF16)
            qTt = ar.alloc([NQ * P], BF16)
            qrT = ar.alloc([NQ * P], BF16)
            wuqh = ar.alloc([3, 192], BF16)
            qr_f = ar.alloc([NQ, 64], F32)
            qr_b = ar.alloc([NQ, 64], BF16)
            t1 = ar.alloc([NQ, 32], F32)
            t2 = ar.alloc([NQ, 32], F32)
            for h in range(4):
                ws = wst[rr("wst", 2)]
                wsv = ws.t[:, :, :].rearrange("p a b -> p (a b)")[:, 0:576].rearrange("p (a b) -> p a b", b=192)
                dma(wsv, w_uq[l, :, h * 192:(h + 1) * 192].rearrange("(kc p) c -> p kc c", p=P), [], ws.b)
                add("pool", lambda e, wsv=wsv: e.tensor_copy(wuqh.t, wsv), ws.b, wuqh.b)
                for t0 in range(0, NQ * P, 512):
                    po = ps_o[rr("pso", 2)]

                    def f(e, t0=t0, po=po):
                        ins = None
                        for kc in range(3):
                            ins = e.matmul(po.t[:, :], wuqh.t[:, kc, 0:P], cqnT.t[:, kc, t0:t0 + 512], start=(kc == 0), stop=(kc == 2))
                        return ins
                    add("pe", f, wuqh.b + cqnT.b, po.b)
                    add("act", lambda e, t0=t0, po=po: e.copy(qTt.t[:, t0:t0 + 512], po.t[:, :]), po.b, qTt.b)
                for b0 in range(0, NQ, 4):
                    po = ps_o[rr("pso", 2)]

                    def f(e, b0=b0, po=po):
                        ins = None
                        for j in range(4):
                            for kc in range(3):
                                ins = e.matmul(po.t[:, j * 64:(j + 1) * 64], cqnT.t[:, kc, (b0 + j) * P:(b0 + j + 1) * P],
                                               wuqh.t[:, kc, P:192], start=(kc == 0), stop=(kc == 2))
                        return ins
                    add("pe", f, wuqh.b + cqnT.b, po.b)
                    add("act", lambda e, b0=b0, po=po: e.copy(qr_f.t[:, b0:b0 + 4, :], po.t[:, 0:256].rearrange("p (j c) -> p j c", c=64)),
                        po.b, qr_f.b)
                rope_tm(qr_f.t, qr_f.b, cosq.t, sinq.t, cosq.b + sinq.b, qr_b.t, qr_b.b, 32, (t1, t2))
                transpose_to(qrT.t[0:64, :], qrT.b, [qr_b.t[:, j, :] for j in range(NQ)], qr_b.b, evac="dve")
                for t0 in range(0, S, 512):
                    po = ps_o[rr("pso", 2)]
                    add("pe", lambda e, t0=t0, po=po, h=h: e.matmul(po.t[:, :], wukv_b.t[:, h * 256:h * 256 + P], ckvnT.t[:, t0:t0 + 512],
                                                                    start=True, stop=True), wukv_b.b + ckvnT.b, po.b)
                    add("act", lambda e, t0=t0, po=po: e.copy(kTt.t[:, t0:t0 + 512], po.t[:, :]), po.b, kTt.b)
                for b0 in range(0, NT, 4):
                    po = ps_o[rr("pso", 2)]

                    def f(e, b0=b0, po=po, h=h):
                        ins = None
                        for j in range(4):
                            ins = e.matmul(po.t[:, j * P:(j + 1) * P], ckvnT.t[:, (b0 + j) * P:(b0 + j + 1) * P],
                                           wukv_b.t[:, h * 256 + P:h * 256 + 256], start=True, stop=True)
                        return ins
                    add("pe", f, wukv_b.b + ckvnT.b, po.b)
                    add("dve", lambda e, b0=b0, po=po: e.tensor_copy(vt.t[:, b0:b0 + 4, :], po.t[:, :].rearrange("p (j c) -> p j c", c=P)),
                        po.b, vt.b)
                gate_to_mixT(l, OFF["mla_gate"] + h * P, 12 + h)

                rows = []
                for i in range(NQ):
                    def extra_i(e, out, k0, n, stop, i=i):
                        return e.matmul(out, qrT.t[0:64, i * P:(i + 1) * P], krT.t[0:64, k0:k0 + n], start=False, stop=stop)
                    rows.append(softmax_heads_causal(i, qTt, kTt, vt, scale, pbs, pTs, mts, 12 + h, extra_i, qrT.b + krT.b))
                emit_rows(rows)
            pg.barrier()

        def mixer_nsa(l):
            ar.reset()
            scale = 128 ** -0.5
            qT4 = ar.alloc([4, NQ * P], BF16)
            ksT = ar.alloc([S], BF16)
            kwT = ar.alloc([S], BF16)
            vs = ar.alloc([NT, P], BF16)
            vw = ar.alloc([NT, P], BF16)
            kcT = ar.alloc([P], BF16)
            vc = ar.alloc([P], BF16)
            bgate = ar.alloc([NQ, 12], F32)
            e8 = ar.alloc([512], BF16)
            c2s = ar.alloc([32], F32)
            mark = ar.off
            dma(e8.t[0:8, :], c_e8, [], e8.b)
            dma(c2s.t, c_c2s, [], c2s.b)
            cos_t = ar.alloc([NT, 64], F32)
            sin_t = ar.alloc([NT, 64], F32)
            xf = ar.alloc([NQ, P], F32)
            xb_ = ar.alloc([NQ, P], BF16)
            t1 = ar.alloc([NQ, 64], F32)
            t2 = ar.alloc([NQ, 64], F32)
            dma(cos_t.t[:, 0:NQ, :], c_cosq, [], cos_t.b)
            dma(sin_t.t[:, 0:NQ, :], c_sinq, [], sin_t.b)
            for h in range(4):
                proj_tm(l, OFF["nsa_q"] + h * P, P, hTq, NQ, lambda b0, n: xf.t[:, b0:b0 + n, :], xf.b)
                rope_tm(xf.t, xf.b, cos_t.t[:, 0:NQ, :], sin_t.t[:, 0:NQ, :], cos_t.b + sin_t.b, xb_.t, xb_.b, 64, (t1, t2))
                transpose_to(qT4.t[:, h, :], qT4.b, [xb_.t[:, j, :] for j in range(NQ)], xb_.b, evac="dve")
            proj_tm(l, OFF["nsa_branch"], 12, hTq, NQ, lambda b0, n: bgate.t[:, b0:b0 + n, :], bgate.b)
            add("act", lambda e: e.activation(out=bgate.t, in_=bgate.t, func=AF.Sigmoid), bgate.b, bgate.b)
            dma(cos_t.t, c_cosa, xf.b + xb_.b + t1.b + t2.b, cos_t.b)
            dma(sin_t.t, c_sina, xf.b + xb_.b + t1.b + t2.b, sin_t.b)
            for (cname, dstT) in (("nsa_k_sel", ksT), ("nsa_k_win", kwT)):
                c0_ = OFF[cname]
                wbv_, wbb_ = load_w(w_in[l, :, c0_:c0_ + P], 16, P)
                bt_ = load_bias_bc(l, c0_, P)
                for g0 in (0, 8):
                    proj_tm(l, c0_, P, hTa, 8, lambda b0, n, g0=g0: xf.t[:, b0 - g0:b0 - g0 + n, :], xf.b, blk0=g0,
                            w=(wbv_, wbb_, bt_))
                    rope_tm(xf.t, xf.b, cos_t.t[:, g0:g0 + 8, :], sin_t.t[:, g0:g0 + 8, :], cos_t.b + sin_t.b, xb_.t, xb_.b, 64,
                            (t1, t2))
                    transpose_to(dstT.t[:, g0 * P:(g0 + 8) * P], dstT.b, [xb_.t[:, j, :] for j in range(8)], xb_.b, evac="dve")
            proj_tm(l, OFF["nsa_v_sel"], P, hTa, NT, lambda b0, n: vs.t[:, b0:b0 + n, :], vs.b)
            proj_tm(l, OFF["nsa_v_win"], P, hTa, NT, lambda b0, n: vw.t[:, b0:b0 + n, :], vw.b)
            pg.barrier()
            if NSA_STOP <= 1:
                return
            ar.reset(mark)
            tokT = ar.alloc([S], BF16)
            blkT = ar.alloc([32, P], BF16)
            w1b = ar.alloc([32, P], BF16)
            w2b = ar.alloc([P], BF16)
            posr = ar.alloc([P], F32)
            posT = ar.alloc([32], F32)
            hidT = ar.alloc([P], BF16)
            kcf = ar.alloc([1, P], F32)
            kcb = ar.alloc([1, P], BF16)
            cosc = ar.alloc([1, 64], F32)
            sinc = ar.alloc([1, 64], F32)
            tc1 = ar.alloc([1, 64], F32)
            tc2 = ar.alloc([1, 64], F32)
            dma(cosc.t[:, 0, :], c_cosc, [], cosc.b)
            dma(sinc.t[:, 0, :], c_sinc, [], sinc.b)
            for which in range(2):
                cname = "nsa_k_cmp" if which == 0 else "nsa_v_cmp"
                proj_fm(l, OFF[cname], P, hTa, S, lambda t0, n: tokT.t[:, t0:t0 + n], tokT.b)
                dma(posr.t[0:32, :], pos_kv[which][l, :, :], [], posr.b)
                po = ps_o[rr("pso", 2)]
                add("pe", lambda e, po=po: e.transpose(po.t[:, 0:32], posr.t[0:32, :], ident_f.t[0:32, 0:32]), posr.b + ident_f.b, po.b)
                add("act", lambda e, po=po: e.copy(posT.t, po.t[:, 0:32]), po.b, posT.b)
                for half in range(2):
                    ws = wst[rr("wst", 2)]
                    dma(ws.t[:, :, :], w1_kv[which][l, half * 2048:(half + 1) * 2048, :].rearrange("(l d) h -> d l h", d=P), [], ws.b)
                    add("pool", lambda e, half=half, ws=ws: e.tensor_copy(w1b.t[:, half * 16:(half + 1) * 16, :], ws.t[:, :, :]),
                        ws.b, w1b.b)
                ws = wst[rr("wst", 2)]
                wsv = ws.t[:, 0, :]
                dma(wsv, w2_kv[which][l, :, :], [], ws.b)
                add("pool", lambda e, wsv=wsv: e.tensor_copy(w2b.t, wsv), ws.b, w2b.b)
                for ll in range(32):
                    src = tokT.t[:, ll:ll + 16 * (NCMP - 1) + 1:16]
                    eng = "dve" if ll % 2 == 0 else "pool"
                    add(eng, lambda e, ll=ll, src=src: e.tensor_scalar(out=blkT.t[:, ll, 0:NCMP], in0=src, scalar1=posT.t[:, ll:ll + 1],
                                                                       scalar2=None, op0=ALU.add), tokT.b + posT.b, blkT.b)
                po = ps_o[rr("pso", 2)]

                def f(e, po=po):
                    ins = None
                    for ll in range(32):
                        ins = e.matmul(po.t[:, 0:NCMP], w1b.t[:, ll, :], blkT.t[:, ll, 0:NCMP], start=(ll == 0), stop=(ll == 31))
                    return ins
                add("pe", f, w1b.b + blkT.b, po.b)
                add("act", lambda e, po=po: e.activation(out=hidT.t[:, 0:NCMP], in_=po.t[:, 0:NCMP], func=AF.Silu), po.b, hidT.b)
                po2 = ps_o[rr("pso", 2)]
                add("pe", lambda e, po2=po2: e.matmul(po2.t[0:NCMP, 0:P], hidT.t[:, 0:NCMP], w2b.t, start=True, stop=True),
                    hidT.b + w2b.b, po2.b)
                if which == 0:
                    add("act", lambda e, po2=po2: e.copy(kcf.t[0:NCMP, 0, :], po2.t[0:NCMP, 0:P]), po2.b, kcf.b)
                    rope_tm(kcf.t[0:NCMP], kcf.b, cosc.t[0:NCMP], sinc.t[0:NCMP], cosc.b + sinc.b, kcb.t[0:NCMP], kcb.b, 64,
                            (view(tc1, tc1.t[0:NCMP]), view(tc2, tc2.t[0:NCMP])))
                    add("pool", lambda e: e.memset(kcT.t, 0.0), [], kcT.b)
                    transpose_to(kcT.t[:, 0:NCMP], kcT.b, [kcb.t[0:NCMP, 0, :]], kcb.b, evac="dve")
                else:
                    add("pool", lambda e: e.memset(vc.t, 0.0), [], vc.b)
                    add("act", lambda e, po2=po2: e.copy(vc.t[0:NCMP, :], po2.t[0:NCMP, 0:P]), po2.b, vc.b)
            for h in range(4):
                gate_to_mixT(l, OFF["nsa_gate"] + h * P, 4 + h)
            pg.barrier()
            if NSA_STOP <= 2:
                return
            ar.reset(mark)
            pbs, pTs, mts = alloc_attn_common()
            mcmp2 = [ar.alloc([P], BF16) for _ in range(2)]
            cmp012 = [ar.alloc([P], F32) for _ in range(2)]
            selb2 = [ar.alloc([32], F32) for _ in range(2)]
            selv2 = [ar.alloc([32], F32) for _ in range(2)]
            ef = ar.alloc([4, P], F32)
            pcf = ef
            pcb = ar.alloc([4, P], BF16)
            ps4 = ar.alloc([P], F32)
            impA = ar.alloc([32], F32)
            imp = ar.alloc([32], F32)
            pcT_b = ar.alloc([4, P], BF16)
            ocmp = ar.alloc([4, P], F32)
            sc = ar.alloc([32], F32)
            sc2 = ar.alloc([32], F32)
            m8a = ar.alloc([8], F32)
            m8b = ar.alloc([8], F32)
            sbias = ar.alloc([32], BF16)
            selT = ar.alloc([4, P], BF16)
            accs = [ar.alloc([P], F32) for _ in range(2)]
            def nsa_sel_row(i, h, nk, nkb, nch):
                st_ = {}

                def A():
                    base = alloc_banks(nch)
                    st_["base"] = base
                    o0 = base * 512

                    def f(e):
                        ins = None
                        for c in range(nch):
                            k0 = c * 512
                            n = min(512, nk - k0)
                            last = (c == nch - 1)
                            e.matmul(ps_s.t[:, o0 + k0:o0 + k0 + n], qT4.t[:, h, i * P:(i + 1) * P], ksT.t[:, k0:k0 + n], start=True, stop=False)
                            ins = e.matmul(ps_s.t[:, o0 + k0:o0 + k0 + n], selT.t[0:8, c, :], e8.t[0:8, 0:n], start=False, stop=not last)
                            if last:
                                ins = e.matmul(ps_s.t[:, o0 + nk - 256:o0 + nk], ident_bf.t[:], mask_c.t[:, 0:256], start=False, stop=True)
                        return ins
                    add("pe", f, qT4.b + ksT.b + selT.b + e8.b + ident_bf.b + mask_c.b, ps_s.b[base:base + nch])

                def B():
                    pb = pbs[rr("pbf", 2)]
                    rs_, rsb_ = softmax_row(nk, scale, pb, st_["base"])
                    pos_ = ps_o[rr("pso", 2)]
                    pv_accumulate(nkb, pb, vs, pos_, pTs)
                    cs, csb = newsmall()
                    add("dve", lambda e: e.tensor_tensor(out=cs, in0=rs_, in1=bgate.t[:, i, 3 * h + 1:3 * h + 2], op=ALU.mult),
                        [rsb_] + bgate.b, [csb])
                    a_ = accs[h % 2]
                    add("dve", lambda e: e.scalar_tensor_tensor(out=a_.t, in0=pos_.t[:, 0:P], scalar=cs, in1=ocmp.t[:, h, :],
                                                                op0=ALU.mult, op1=ALU.add), pos_.b + [csb] + ocmp.b, a_.b)
                return A, B, nch

            def nsa_win_row(i, h):
                kb0 = max(0, 2 * i - 4)
                nkbw = 2 * i + 2 - kb0
                nkw = nkbw * P
                moff = (kb0 - (2 * i - 4)) * P
                nchw = (nkw + 511) // 512
                st_ = {}

                def A():
                    basew = alloc_banks(nchw)
                    st_["base"] = basew
                    o0 = basew * 512

                    def f(e):
                        ins = None
                        for c in range(nchw):
                            k0 = c * 512
                            n = min(512, nkw - k0)
                            e.matmul(ps_s.t[:, o0 + k0:o0 + k0 + n], qT4.t[:, h, i * P:(i + 1) * P], kwT.t[:, kb0 * P + k0:kb0 * P + k0 + n],
                                     start=True, stop=False)
                            ins = e.matmul(ps_s.t[:, o0 + k0:o0 + k0 + n], ident_bf.t[:], mask_w.t[:, moff + k0:moff + k0 + n], start=False, stop=True)
                        return ins
                    add("pe", f, qT4.b + kwT.b + ident_bf.b + mask_w.b, ps_s.b[basew:basew + nchw])

                def B():
                    pb = pbs[rr("pbf", 2)]
                    rw_, rwb_ = softmax_row(nkw, scale, pb, st_["base"])
                    pow_ = ps_o[rr("pso", 2)]
                    pv_accumulate(nkbw, pb, vw, pow_, pTs, kb_off=kb0)
                    cw, cwb = newsmall()
                    add("dve", lambda e: e.tensor_tensor(out=cw, in0=rw_, in1=bgate.t[:, i, 3 * h + 2:3 * h + 3], op=ALU.mult),
                        [rwb_] + bgate.b, [cwb])
                    a_ = accs[h % 2]
                    add("dve", lambda e: e.scalar_tensor_tensor(out=a_.t, in0=pow_.t[:, 0:P], scalar=cw, in1=a_.t, op0=ALU.mult, op1=ALU.add),
                        pow_.b + [cwb] + a_.b, a_.b)
                    finish_head(i, a_.t, a_.b, None, None, 4 + h, mts)
                return A, B, nchw

            for i in range(NQ):
                nkb = 2 * i + 2
                nk = nkb * P
                nch = (nk + 511) // 512
                mcmp, cmp01, selb, selv = mcmp2[i % 2], cmp012[i % 2], selb2[i % 2], selv2[i % 2]
                dma(mcmp.t, c_mask_cmp[:, i, :], [], mcmp.b)
                dma(cmp01.t, c_cmp01[:, i, :], [], cmp01.b)
                dma(selb.t, c_selbias[:, i, :], [], selb.b)
                dma(selv.t, c_selvalid[:, i, :], [], selv.b)
                pz = ps_o[rr("pso", 2)]

                def f(e, i=i, pz=pz, mcmp=mcmp):
                    ins = None
                    for h in range(4):
                        e.matmul(pz.t[:, h * P:(h + 1) * P], qT4.t[:, h, i * P:(i + 1) * P], kcT.t[:, :], start=True, stop=False)
                        ins = e.matmul(pz.t[:, h * P:(h + 1) * P], ident_bf.t[:], mcmp.t[:, :], start=False, stop=True)
                    return ins
                add("pe", f, qT4.b + kcT.b + ident_bf.b + mcmp.b, pz.b)
                mx4, mx4b = newsmall(4)
                pz3 = pz.t[:, :].rearrange("p (h n) -> p h n", n=P)
                add("dve", lambda e, pz3=pz3, mx4=mx4: e.tensor_reduce(out=mx4, in_=pz3, axis=AX.X, op=ALU.max), pz.b, [mx4b])
                nm4, nm4b = newsmall(4)
                add("dve", lambda e, mx4=mx4, nm4=nm4: e.tensor_scalar(out=nm4, in0=mx4, scalar1=-scale, scalar2=None, op0=ALU.mult),
                    [mx4b], [nm4b])
                for h in range(4):
                    add("act", lambda e, h=h, pz=pz, nm4=nm4: e.activation(out=ef.t[:, h, :], in_=pz.t[:, h * P:(h + 1) * P], func=AF.Exp,
                                                                          bias=nm4[:, h:h + 1], scale=scale), pz.b + [nm4b], ef.b)
                m01 = cmp01.t[:, :].unsqueeze(1).to_broadcast([P, 4, P])
                add("dve", lambda e, m01=m01: e.tensor_tensor(out=ef.t, in0=ef.t, in1=m01, op=ALU.mult), ef.b + cmp01.b, ef.b)
                l4, l4b = newsmall(4)
                add("dve", lambda e, l4=l4: e.tensor_reduce(out=l4, in_=ef.t, axis=AX.X, op=ALU.add), ef.b, [l4b])
                r4, r4b = newsmall(4)
                add("dve", lambda e, l4=l4, r4=r4: e.tensor_scalar(out=r4, in0=l4, scalar1=1e-30, scalar2=None, op0=ALU.max), [l4b], [r4b])
                r4i, r4ib = newsmall(4)
                add("dve", lambda e, r4=r4, r4i=r4i: e.reciprocal(r4i, r4), [r4b], [r4ib])
                add("dve", lambda e, r4i=r4i: e.tensor_tensor(out=pcf.t, in0=ef.t, in1=r4i.unsqueeze(2).to_broadcast([P, 4, P]), op=ALU.mult),
                    ef.b + [r4ib], pcf.b)
                if NSA_STOP <= 2.1:
                    continue
                add("pool", lambda e: e.tensor_copy(pcb.t, pcf.t), pcf.b, pcb.b)
                transpose_to(pcT_b.t, pcT_b.b, [pcb.t[:, h, :] for h in range(4)], pcb.b, evac="act")
                if NSA_STOP <= 2.2:
                    continue
                add("dve", lambda e: e.tensor_reduce(out=ps4.t, in_=pcf.t.rearrange("p h n -> p n h"), axis=AX.X, op=ALU.add),
                    pcf.b, ps4.b)
                ps4v = ps4.t.rearrange("p (s j) -> p s j", j=4)
                add("dve", lambda e, ps4v=ps4v: e.tensor_reduce(out=impA.t, in_=ps4v, axis=AX.X, op=ALU.add), ps4.b, impA.b)
                v3 = ps4v[:, :, 3]
                add("dve", lambda e, v3=v3: e.scalar_tensor_tensor(out=imp.t, in0=v3, scalar=-0.5, in1=impA.t, op0=ALU.mult, op1=ALU.add),
                    ps4.b + impA.b, imp.b)
                add("dve", lambda e, v3=v3: e.scalar_tensor_tensor(out=imp.t[:, 1:32], in0=v3[:, 0:31], scalar=0.5, in1=imp.t[:, 1:32],
                                                                   op0=ALU.mult, op1=ALU.add), ps4.b + imp.b, imp.b)
                if NSA_STOP <= 2.25:
                    continue
                add("dve", lambda e, i=i, selb=selb: e.tensor_tensor(out=sc.t, in0=imp.t, in1=selb.t[:, :], op=ALU.max),
                    imp.b + selb.b, sc.b)
                add("dve", lambda e, i=i, selv=selv: e.tensor_tensor(out=sc.t, in0=sc.t, in1=selv.t[:, :], op=ALU.add), sc.b + selv.b, sc.b)
                if NSA_STOP <= 2.3:
                    continue
                add("dve", lambda e: e.max(out=m8a.t, in_=sc.t), sc.b, m8a.b)
                add("dve", lambda e: e.match_replace(out=sc2.t, in_to_replace=m8a.t, in_values=sc.t, imm_value=-3.0e38),
                    sc.b + m8a.b, sc2.b)
                add("dve", lambda e: e.max(out=m8b.t, in_=sc2.t), sc2.b, m8b.b)
                add("dve", lambda e: e.tensor_scalar(out=sc2.t, in0=sc.t, scalar1=m8b.t[:, 7:8], scalar2=1.0, op0=ALU.is_ge,
                                                     op1=ALU.subtract), sc.b + m8b.b, sc2.b)
                add("dve", lambda e: e.tensor_scalar(out=sbias.t, in0=sc2.t, scalar1=-NEG, scalar2=None, op0=ALU.mult), sc2.b, sbias.b)
                if NSA_STOP <= 2.4:
                    continue
                transpose_to(selT.t[0:8, 0:nch, :], selT.b, [sbias.t[:, c * 8:(c + 1) * 8] for c in range(nch)], sbias.b, evac="dve")
                poc = ps_o[rr("pso", 2)]

                def f(e, poc=poc):
                    ins = None
                    for h in range(4):
                        ins = e.matmul(poc.t[:, h * P:(h + 1) * P], pcT_b.t[:, h, :], vc.t[:, :], start=True, stop=True)
                    return ins
                add("pe", f, pcT_b.b + vc.b, poc.b)
                g0 = bgate.t[:, i, :].rearrange("p (h t) -> p h t", t=3)[:, :, 0:1].to_broadcast([P, 4, P])
                add("dve", lambda e, poc=poc, g0=g0: e.tensor_tensor(out=ocmp.t, in0=poc.t[:, :].rearrange("p (h n) -> p h n", n=P), in1=g0,
                                                                     op=ALU.mult), poc.b + bgate.b, ocmp.b)
                rows = []
                for h in range(4):
                    rows.append(nsa_sel_row(i, h, nk, nkb, nch))
                    rows.append(nsa_win_row(i, h))
                emit_rows(rows)
            pg.barrier()

        def post_phase(l, final, xsrc, ydst):
            ar.reset()
            gbc = ar.alloc([D], F32)
            xin = [ar.alloc([D], F32) for _ in range(2)]
            ytmp = ar.alloc([D], F32)
            junk = ar.alloc([D], BF16)
            for kc in range(16):
                ws = wst[rr("wst", 2)]
                wsv = ws.t[:, :, :].rearrange("p a b -> p (a b)")
                dma(wsv, w_out[l, kc * P:(kc + 1) * P, :], [], ws.b)
                eng = "pool" if kc % 2 == 0 else "dve"
                add(eng, lambda e, kc=kc, wsv=wsv: e.tensor_copy(hTa.t[:, kc, :], wsv), ws.b, [hTa.b[kc]])
            dma(gbc.t, post_g[l, :].partition_broadcast(P), [], gbc.b)
            for i in range(NQ):
                def f(e, i=i):
                    ins = None
                    for n0 in range(4):
                        for kc in range(16):
                            ins = e.matmul(ps_s.t[:, n0 * 512:(n0 + 1) * 512], mixT.t[:, kc, i * P:(i + 1) * P],
                                           hTa.t[:, kc, n0 * 512:(n0 + 1) * 512], start=(kc == 0), stop=(kc == 15))
                    return ins
                add("pe", f, mixT.b + hTa.b, ps_s.b)
                xt = xin[rr("xin", 2)]
                sap, sbufs = xsrc("q", i)
                dma(xt.t, sap, sbufs, xt.b)
                ss, ssb = newsmall()
                add("act", lambda e, ss=ss: e.activation(out=junk.t, in_=ps_s.t[:, :], func=AF.Square, accum_out=ss),
                    ps_s.b, junk.b + [ssb])
                rs, rsb = rms_scale(ss, ssb, D)
                add("dve", lambda e, rs=rs: e.scalar_tensor_tensor(out=ytmp.t, in0=ps_s.t[:, :], scalar=rs, in1=gbc.t,
                                                                   op0=ALU.mult, op1=ALU.mult),
                    ps_s.b + [rsb] + gbc.b, ytmp.b)
                add("pool", lambda e, xt=xt: e.tensor_tensor(out=ytmp.t, in0=ytmp.t, in1=xt.t, op=ALU.add),
                    ytmp.b + xt.b, ytmp.b)
                if ydst is None:
                    final.append(dma(yout[i * P:(i + 1) * P, :], ytmp.t, ytmp.b, []))
                else:
                    dap, dbufs = ydst(i)
                    dma(dap, ytmp.t, ytmp.b, dbufs)
                    if i % 2 == 1:
                        j = i // 2
                        add_cc(j)
            pg.barrier()

        final = []
        x1b = [Buf() for _ in range(4)]
        gab = [Buf() for _ in range(4)]

        def add_cc(j):
            pg.add("pool", lambda e: e.collective_compute("AllGather", ALU.bypass,
                                                          replica_groups=[[0, 1], [2, 3], [4, 5], [6, 7]],
                                                          ins=[x1own_t[j].ap().opt()], outs=[gath_t[j].ap().opt()]),
                   [x1b[j]], [gab[j]], dma="cc")

        def xsrc_in(which, t):
            if which == "a":
                return xa[t * P:(t + 1) * P, :], []
            return xq[t * P:(t + 1) * P, :], []

        def xsrc_mid(which, t):
            if which == "a":
                r_, i_ = t % 2, t // 2
                j_, k_ = i_ // 2, i_ % 2
                return gath_t[j_].ap()[r_ * 2 * P + k_ * P:r_ * 2 * P + (k_ + 1) * P, :], [gab[j_]]
            j_, k_ = t // 2, t % 2
            return x1own_t[j_].ap()[k_ * P:(k_ + 1) * P, :], [x1b[j_]]

        def ydst_mid(i):
            j_, k_ = i // 2, i % 2
            return x1own_t[j_].ap()[k_ * P:(k_ + 1) * P, :], [x1b[j_]]

        for li, l in enumerate(layers):
            xsrc = xsrc_in if li == 0 else xsrc_mid
            ydst = None if li == len(layers) - 1 else ydst_mid
            phase_norm(l, xsrc)
            for c in range(16):
                mname = MIXERS[c // 4]
                if mname not in mixers:
                    add("pool", lambda e, c=c: e.memset(mixT.t[:, c, :], 0.0), [], [mixT.b[c]])
            if "sb" in mixers:
                mixer_sb(l)
            if "nsa" in mixers:
                mixer_nsa(l)
            if "fox" in mixers:
                mixer_fox(l)
            if "mla" in mixers:
                mixer_mla(l)
            post_phase(l, final, xsrc, ydst)

        with nc.Block() as block:
            pg.emit(block, final)
    return nc


def _consts(r):
    bf = ml_dtypes.bfloat16
    c = {}
    c["c_ident_bf"] = np.eye(P, dtype=np.float32).astype(bf)
    c["c_ident_f"] = np.eye(P, dtype=np.float32)
    p = np.arange(P)[:, None]
    col = np.arange(256)[None, :]
    c["c_mask_c"] = np.where(col <= p + 128 * r, 0.0, NEG).astype(np.float32).astype(bf)
    c["c_mask_s"] = np.where(col < p + 128 * r, 0.0, NEG).astype(np.float32).astype(bf)
    colw = np.arange(768)[None, :]
    c["c_mask_w"] = np.where((colw <= 512 + 128 * r + p) & (colw > 128 * r + p), 0.0, NEG).astype(np.float32).astype(bf)
    qpos = (np.arange(NQ)[None, :] * 2 + r) * P + np.arange(P)[:, None]
    cmp_end = np.arange(P) * 16 + 31
    vis = (cmp_end[None, None, :] <= qpos[:, :, None]) & (np.arange(P)[None, None, :] < NCMP)
    c["c_mask_cmp"] = np.where(vis, 0.0, NEG).astype(np.float32).astype(bf)
    c["c_cmp01"] = vis.astype(np.float32)
    sel = np.arange(32)[None, None, :]
    cur = (qpos // 64)[:, :, None]
    forced = (sel == 0) | (sel == cur) | (sel == cur - 1)
    valid = sel <= cur
    c["c_selbias"] = np.where(forced, 1e6, 0.0).astype(np.float32)
    c["c_selvalid"] = np.where(valid, 0.0, -1e30).astype(np.float32)
    cmp_start = np.arange(NCMP) * 16
    sel_start = np.arange(32) * 64
    ov = np.clip(np.minimum(cmp_start[:, None] + 32, sel_start[None, :] + 64)
                 - np.maximum(cmp_start[:, None], sel_start[None, :]), 0, None)
    c2s = np.zeros((P, 32), np.float32)
    c2s[:NCMP] = (ov / 32).astype(np.float32)
    c["c_c2s"] = c2s
    c["c_e8"] = (np.arange(512)[None, :] // 64 == np.arange(8)[:, None]).astype(np.float32).astype(bf)

    def tables(pos, half):
        inv = (np.float32(10000.0) ** (-np.arange(half, dtype=np.float32) / np.float32(half))).astype(np.float32)
        ang = pos.astype(np.float32)[..., None] * inv
        return np.cos(ang).astype(np.float32), np.sin(ang).astype(np.float32)
    pos_all = np.arange(NT)[None, :] * P + np.arange(P)[:, None]
    c["c_cosa"], c["c_sina"] = tables(pos_all, 64)
    c["c_cosq"], c["c_sinq"] = tables(qpos, 64)
    c["c_cosc"], c["c_sinc"] = tables(cmp_end, 64)
    c["c_cosa32"], c["c_sina32"] = tables(pos_all, 32)
    c["c_cosq32"], c["c_sinq32"] = tables(qpos, 32)
    sel3 = np.zeros((P, 4, P), np.float32)
    for h in range(4):
        for rep in range(3):
            sel3[rep * 32 + h, h, :] = 1.0
    c["c_sel3"] = sel3.astype(bf)
    return c


_WNAMES = ("pre_norm_g", "post_norm_g", "w_in", "b_in", "w_out", "fox_forget_bias",
           "nsa_cmp_pos_k", "nsa_cmp_w1_k", "nsa_cmp_w2_k", "nsa_cmp_pos_v", "nsa_cmp_w1_v", "nsa_cmp_w2_v",
           "mla_q_norm_g", "mla_w_uq", "mla_kv_norm_g", "mla_w_ukv")


def _own_rows(xb, r):
    return np.ascontiguousarray(xb.reshape(NQ, 2, P, D)[:, r].reshape(NQ * P, D))


def run_layers(x, weights, layers, dbg=None, mixers=MIXERS):
    nc = build(layers, dbg, mixers)
    in_maps = []
    for c in range(8):
        b, r = c // 2, c % 2
        m = {"xa": np.ascontiguousarray(x[b]), "xq": _own_rows(x[b], r)}
        for n in _WNAMES:
            m[n] = weights[n]
        m.update(_consts(r))
        in_maps.append(m)
    res = run_bass_kernel_spmd(nc, in_maps, core_ids=list(range(8)))
    out = np.empty((NB, S, D), np.float32)
    for c in range(8):
        b, r = c // 2, c % 2
        out[b].reshape(NQ, 2, P, D)[:, r] = res.results[c]["y"].reshape(NQ, P, D)
    return out, res


def kernel(**inputs):
    x = np.ascontiguousarray(np.asarray(inputs["x"], dtype=np.float32))
    weights = {n: np.ascontiguousarray(np.asarray(inputs[n], dtype=np.float32)) for n in _WNAMES}
    x, _ = run_layers(x, weights, list(range(DEPTH)))
    return x
```

```python
import numpy as np
import ml_dtypes
from contextlib import ExitStack
import concourse.bass as bass
import concourse.mybir as mybir
from concourse.bass_utils import run_bass_kernel_spmd

F32 = mybir.dt.float32
BF16 = mybir.dt.bfloat16
AF = mybir.ActivationFunctionType
ALU = mybir.AluOpType
AX = mybir.AxisListType

D = 2048
S = 2048
NB = 4
DEPTH = 2
INW = 6992
NT = 16
NQ = 8
P = 128
EPS = 1e-6
NEG = -30000.0
NCMP = 127

OFF = {}
_o = 0
for _n, _w in (("sb_q", 512), ("sb_k", 512), ("sb_v", 512), ("sb_gate", 512),
               ("nsa_q", 512), ("nsa_k_cmp", 128), ("nsa_v_cmp", 128), ("nsa_k_sel", 128),
               ("nsa_v_sel", 128), ("nsa_k_win", 128), ("nsa_v_win", 128), ("nsa_branch", 12),
               ("nsa_gate", 512), ("fox_q", 512), ("fox_k", 512), ("fox_v", 512), ("fox_f", 4),
               ("fox_gate", 512), ("mla_cq", 384), ("mla_ckv", 128), ("mla_k_rope", 64),
               ("mla_gate", 512)):
    OFF[_n] = _o
    _o += _w
assert _o == INW


_ALL_BUFS = []


class Buf:
    __slots__ = ("lw", "rd", "rd_dma")

    def __init__(self):
        self.lw = None
        self.rd = {}
        self.rd_dma = []
        _ALL_BUFS.append(self)


class Op:
    __slots__ = ("eng", "fn", "deps", "signal", "count", "is_dma", "dsem", "dval", "dprev")


class Prog:
    ENGS = ("pe", "act", "dve", "pool", "sp")

    def __init__(self, nc, stack, n_dma_sems=12):
        self.nc = nc
        self.ops = {e: [] for e in self.ENGS}
        self.esem = {e: stack.enter_context(nc.semaphore("es_" + e)) for e in self.ENGS}
        self.dsems = {}
        self.dcount = {}
        self.drr = {}
        for e in ("sp", "pool", "act"):
            self.dsems[e] = [stack.enter_context(nc.semaphore("ds_%s%d" % (e, i))) for i in range(n_dma_sems)]
            self.dcount[e] = [0] * n_dma_sems
            self.drr[e] = 0
        self.dsems["cc"] = [stack.enter_context(nc.semaphore("cc_sem"))]
        self.dcount["cc"] = [0]

    def add(self, eng, fn, reads=(), writes=(), dma=False):
        op = Op()
        op.eng = eng
        op.fn = fn
        op.signal = False
        op.count = 0
        op.is_dma = dma
        op.dsem = None
        op.dval = 0
        op.dprev = 0
        me = (eng, len(self.ops[eng]))
        deps = set()
        for b in reads:
            if b.lw is not None:
                deps.add(b.lw)
        for b in writes:
            if b.lw is not None:
                deps.add(b.lw)
            for e2, i2 in b.rd.items():
                deps.add((e2, i2))
            for d in b.rd_dma:
                deps.add(d)
        needed = []
        for d in deps:
            if d == me:
                continue
            dop = self.ops[d[0]][d[1]]
            if dop.is_dma:
                needed.append(d)
            elif d[0] == eng and eng == "pe":
                continue
            else:
                dop.signal = True
                needed.append(d)
        op.deps = needed
        if dma == "cc":
            op.dsem = ("cc", 0)
            op.dprev = 0
            self.dcount["cc"][0] += 1
            op.dval = self.dcount["cc"][0]
        elif dma:
            k = self.drr[eng]
            self.drr[eng] = (k + 1) % len(self.dsems[eng])
            op.dsem = (eng, k)
            op.dprev = self.dcount[eng][k] * 16
            self.dcount[eng][k] += 1
            op.dval = self.dcount[eng][k] * 16
        self.ops[eng].append(op)
        for b in writes:
            b.lw = me
            b.rd = {}
            b.rd_dma = []
        wset = set(id(b) for b in writes)
        for b in reads:
            if id(b) in wset:
                continue
            if dma:
                b.rd_dma.append(me)
            else:
                b.rd[eng] = me[1]
        return me

    def barrier(self):
        deps = set()
        for b in _ALL_BUFS:
            if b.lw is not None:
                deps.add(b.lw)
            for e2, i2 in b.rd.items():
                deps.add((e2, i2))
            for d in b.rd_dma:
                deps.add(d)
        for e in self.ENGS:
            if self.ops[e]:
                last = (e, len(self.ops[e]) - 1)
                if not self.ops[e][-1].is_dma and self.ops[e][-1].fn is not None:
                    deps.add(last)
        for e in self.ENGS:
            op = Op()
            op.eng = e
            op.fn = None
            op.signal = False
            op.count = 0
            op.is_dma = False
            op.dsem = None
            op.dval = 0
            op.dprev = 0
            mx = {}
            needed = []
            for d in deps:
                dop = self.ops[d[0]][d[1]]
                if dop.is_dma:
                    needed.append(d)
                else:
                    if d[0] == e and e == "pe":
                        continue
                    mx[d[0]] = max(mx.get(d[0], -1), d[1])
            for e2, i2 in mx.items():
                self.ops[e2][i2].signal = True
                needed.append((e2, i2))
            op.deps = needed
            self.ops[e].append(op)
        for b in _ALL_BUFS:
            b.lw = None
            b.rd = {}
            b.rd_dma = []

    def emit(self, block, final_waits):
        nc = self.nc
        for e in self.ENGS:
            c = 0
            for op in self.ops[e]:
                if op.signal and not op.is_dma:
                    c += 1
                    op.count = c
        prog = self

        def run(e, engobj):
            waited = {}
            for op in prog.ops[e]:
                for d in op.deps:
                    dop = prog.ops[d[0]][d[1]]
                    if dop.is_dma:
                        key = ("d",) + dop.dsem
                        val = dop.dval
                        sem = prog.dsems[dop.dsem[0]][dop.dsem[1]]
                    else:
                        key = ("e", d[0])
                        val = dop.count
                        sem = prog.esem[d[0]]
                    if waited.get(key, 0) >= val:
                        continue
                    engobj.wait_ge(sem, val)
                    waited[key] = val
                if op.fn is None:
                    continue
                if op.is_dma:
                    key = ("d",) + op.dsem
                    sem = prog.dsems[op.dsem[0]][op.dsem[1]]
                    if op.dprev > 0 and waited.get(key, 0) < op.dprev:
                        engobj.wait_ge(sem, op.dprev)
                        waited[key] = op.dprev
                    ins = op.fn(engobj)
                    if op.dsem[0] == "cc":
                        ins.then_inc(sem)
                    else:
                        ins.then_inc(sem, 16)
                else:
                    ins = op.fn(engobj)
                    if op.signal:
                        ins.then_inc(prog.esem[e], 1)
            if e == "sp":
                for d in final_waits:
                    dop = prog.ops[d[0]][d[1]]
                    sem = prog.dsems[dop.dsem[0]][dop.dsem[1]]
                    engobj.wait_ge(sem, dop.dval)

        @block.tensor
        def _(eng):
            run("pe", eng)

        @block.scalar
        def _(eng):
            run("act", eng)

        @block.vector
        def _(eng):
            run("dve", eng)

        @block.gpsimd
        def _(eng):
            run("pool", eng)

        @block.sync
        def _(eng):
            run("sp", eng)


class T:
    def __init__(self, t, nbuf=1):
        self.t = t
        self.b = [Buf() for _ in range(nbuf)]


ARENA_BYTES = 50 * 1024
MIXERS = ("sb", "nsa", "fox", "mla")


class Arena:
    def __init__(self, base):
        self.base = base
        self.off = 0

    def reset(self, to=0):
        self.off = to

    def alloc(self, shape, dt, nbuf=1):
        n = 1
        for v in shape:
            n *= v
        nbytes = n * (4 if dt == F32 else 2)
        nbytes = (nbytes + 7) // 8 * 8
        assert self.off + nbytes <= ARENA_BYTES, ("arena overflow", self.off, nbytes)
        ap = self.base[:, self.off // 2:(self.off + nbytes) // 2]
        self.off += nbytes
        if dt == F32:
            ap = ap.bitcast(F32)
        ap = ap[:, 0:n]
        if len(shape) == 2:
            ap = ap.rearrange("p (a b) -> p a b", a=shape[0])
        elif len(shape) == 3:
            ap = ap.rearrange("p (a b c) -> p a b c", a=shape[0], b=shape[1])
        return T(ap, nbuf)


NSA_STOP = 99


def build(layers, dbg=None, mixers=MIXERS):
    del _ALL_BUFS[:]
    nc = bass.Bass("TRN2", target_bir_lowering=False)
    dr = {}

    def din(name, shape, dt=F32):
        dr[name] = nc.dram_tensor(name, list(shape), dt, kind="ExternalInput").ap()
        return dr[name]

    xa = din("xa", [S, D])
    xq = din("xq", [NQ * P, D])
    pre_g = din("pre_norm_g", [DEPTH, D])
    post_g = din("post_norm_g", [DEPTH, D])
    w_in = din("w_in", [DEPTH, D, INW])
    b_in = din("b_in", [DEPTH, INW])
    w_out = din("w_out", [DEPTH, D, D])
    fox_fb = din("fox_forget_bias", [DEPTH, 4])
    pos_kv = [din("nsa_cmp_pos_k", [DEPTH, 32, 128]), din("nsa_cmp_pos_v", [DEPTH, 32, 128])]
    w1_kv = [din("nsa_cmp_w1_k", [DEPTH, 4096, 128]), din("nsa_cmp_w1_v", [DEPTH, 4096, 128])]
    w2_kv = [din("nsa_cmp_w2_k", [DEPTH, 128, 128]), din("nsa_cmp_w2_v", [DEPTH, 128, 128])]
    qng = din("mla_q_norm_g", [DEPTH, 384])
    w_uq = din("mla_w_uq", [DEPTH, 384, 768])
    kvng = din("mla_kv_norm_g", [DEPTH, 128])
    w_ukv = din("mla_w_ukv", [DEPTH, 128, 1024])
    c_ident_bf = din("c_ident_bf", [P, P], BF16)
    c_ident_f = din("c_ident_f", [P, P])
    c_mask_c = din("c_mask_c", [P, 256], BF16)
    c_mask_s = din("c_mask_s", [P, 256], BF16)
    c_mask_w = din("c_mask_w", [P, 768], BF16)
    c_mask_cmp = din("c_mask_cmp", [P, NQ, P], BF16)
    c_cmp01 = din("c_cmp01", [P, NQ, P])
    c_selbias = din("c_selbias", [P, NQ, 32])
    c_selvalid = din("c_selvalid", [P, NQ, 32])
    c_c2s = din("c_c2s", [P, 32])
    c_e8 = din("c_e8", [8, 512], BF16)
    c_cosa = din("c_cosa", [P, NT, 64])
    c_sina = din("c_sina", [P, NT, 64])
    c_cosq = din("c_cosq", [P, NQ, 64])
    c_sinq = din("c_sinq", [P, NQ, 64])
    c_cosc = din("c_cosc", [P, 64])
    c_sinc = din("c_sinc", [P, 64])
    c_cosa32 = din("c_cosa32", [P, NT, 32])
    c_sina32 = din("c_sina32", [P, NT, 32])
    c_cosq32 = din("c_cosq32", [P, NQ, 32])
    c_sinq32 = din("c_sinq32", [P, NQ, 32])
    c_sel3 = din("c_sel3", [P, 4, P], BF16)
    yout = nc.dram_tensor("y", [NQ * P, D], F32, kind="ExternalOutput").ap()
    x1own_t = [nc.dram_tensor("x1own%d" % j, [2 * P, D], F32) for j in range(4)]
    gath_t = [nc.dram_tensor("gath%d" % j, [4 * P, D], F32) for j in range(4)]

    with ExitStack() as st:
        pg = Prog(nc, st)

        def sb(name, shape, dt, nbuf=1):
            return T(st.enter_context(nc.sbuf_tensor(name, list(shape), dt)), nbuf)

        def ps(name, shape, dt, nbuf=1):
            return T(st.enter_context(nc.psum_tensor(name, list(shape), dt)), nbuf)

        hTa = sb("hTa", [P, 16, S], BF16, 16)
        hTq = sb("hTq", [P, 16, NQ * P], BF16, 16)
        mixT = sb("mixT", [P, 16, NQ * P], BF16, 16)
        wst = [sb("wst%d" % i, [P, 16, P], F32) for i in range(2)]
        wbf = [sb("wbf%d" % i, [P, 16, P], BF16) for i in range(2)]
        ident_bf = sb("ident_bf", [P, P], BF16)
        ident_f = sb("ident_f", [P, P], F32)
        mask_c = sb("mask_c", [P, 256], BF16)
        mask_s = sb("mask_s", [P, 256], BF16)
        mask_w = sb("mask_w", [P, 768], BF16)
        small = sb("small", [P, 64], F32, 64)
        small4 = sb("small4", [P, 64], F32, 16)
        bias_fm = [sb("bias_fm%d" % i, [P, 1], F32) for i in range(3)]
        bias_bc = [sb("bias_bc%d" % i, [P, P], F32) for i in range(3)]
        arena_t = st.enter_context(nc.sbuf_tensor("arena", [P, ARENA_BYTES // 2], BF16))
        ar = Arena(arena_t)
        ps_s = ps("ps_s", [P, 2048], F32, 4)
        ps_t = [ps("ps_t%d" % i, [P, 1024], BF16) for i in range(2)]
        ps_o = [ps("ps_o%d" % i, [P, 512], F32) for i in range(2)]

        cnt = {}

        def rr(key, n):
            v = cnt.get(key, 0)
            cnt[key] = v + 1
            return v % n

        def newsmall(w=1):
            if w == 1:
                i = rr("sm", 64)
                return small.t[:, i:i + 1], small.b[i]
            i = rr("sm4", 16)
            return small4.t[:, i * 4:i * 4 + 4], small4.b[i]

        def view(tobj, ap):
            r = T(ap, 0)
            r.b = tobj.b
            return r

        def dma(out_ap, in_ap, reads, writes, q="sp"):
            return pg.add(q, lambda e: e.dma_start(out=out_ap, in_=in_ap), reads, writes, dma=True)

        for tile_, src_ in ((ident_bf, c_ident_bf), (ident_f, c_ident_f), (mask_c, c_mask_c),
                            (mask_s, c_mask_s), (mask_w, c_mask_w)):
            dma(tile_.t[:], src_, [], tile_.b)

        def add(eng, fn, reads, writes):
            return pg.add(eng, fn, reads, writes)

        def transpose_to(dst, dst_bufs, src_aps, src_bufs, evac="act", f32=False):
            n = len(src_aps)
            w = src_aps[0].shape[-1]
            rows = src_aps[0].shape[0]
            if f32:
                k = rr("pso", 2)
                pt = ps_o[k]
                idt = ident_f
                assert n <= 4
            else:
                k = rr("pst", 2)
                pt = ps_t[k]
                idt = ident_bf

            def f(e):
                ins = None
                for j, a in enumerate(src_aps):
                    ins = e.transpose(pt.t[0:w, j * P:j * P + rows], a, idt.t[0:rows, 0:rows])
                return ins
            add("pe", f, list(src_bufs) + idt.b, pt.b)
            if rows == P:
                src = pt.t[0:w, 0:n * P]
                if len(dst.shape) == 3:
                    src = src.rearrange("p (a b) -> p a b", b=P)
            elif n == 1:
                src = pt.t[0:w, 0:rows]
            else:
                src = pt.t[0:w, 0:n * P].rearrange("p (a b) -> p a b", b=P)[:, :, 0:rows]
            if evac == "act":
                add("act", lambda e: e.copy(dst, src), pt.b, dst_bufs)
            else:
                add("dve", lambda e: e.tensor_copy(dst, src), pt.b, dst_bufs)

        def load_w(src, nkc, ncols):
            s = rr("wst", 2)
            k = rr("wbf", 2)
            ws, wb = wst[s], wbf[k]
            wsv = ws.t[:, :, :].rearrange("p a b -> p (a b)")[:, 0:nkc * ncols].rearrange("p (a b) -> p a b", b=ncols)
            wbv = wb.t[:, :, :].rearrange("p a b -> p (a b)")[:, 0:nkc * ncols].rearrange("p (a b) -> p a b", b=ncols)
            dma(wsv, src.rearrange("(kc p) c -> p kc c", p=P), [], ws.b)
            if nkc >= 2:
                hk = nkc // 2
                add("dve", lambda e: e.tensor_copy(wbv[:, 0:hk, :], wsv[:, 0:hk, :]), ws.b, wb.b)
                add("act", lambda e: e.copy(wbv[:, hk:nkc, :], wsv[:, hk:nkc, :]), ws.b, wb.b)
            else:
                add("dve", lambda e: e.tensor_copy(wbv, wsv), ws.b, wb.b)
            return wbv, wb.b

        def load_bias_fm(l, c0, ncols):
            k = rr("bfm", 3)
            bt = bias_fm[k]
            dma(bt.t[0:ncols, :], b_in[l, c0:c0 + ncols].rearrange("(c o) -> c o", o=1), [], bt.b)
            return bt

        def load_bias_bc(l, c0, ncols):
            k = rr("bbc", 3)
            bt = bias_bc[k]
            dma(bt.t[:, 0:ncols], b_in[l, c0:c0 + ncols].partition_broadcast(P), [], bt.b)
            return bt

        def proj_fm(l, c0, ncols, src, ntok, dst_fn, dst_bufs):
            wbv, wbb = load_w(w_in[l, :, c0:c0 + ncols], 16, ncols)
            bt = load_bias_fm(l, c0, ncols)
            for t0 in range(0, ntok, 512):
                po = ps_o[rr("pso", 2)]

                def f(e, t0=t0, po=po):
                    ins = None
                    for kc in range(16):
                        ins = e.matmul(po.t[0:ncols, :], wbv[:, kc, :], src.t[:, kc, t0:t0 + 512],
                                       start=(kc == 0), stop=(kc == 15))
                    return ins
                add("pe", f, wbb + src.b, po.b)
                dst = dst_fn(t0, 512)
                add("act", lambda e, dst=dst, po=po: e.activation(out=dst, in_=po.t[0:ncols, :], func=AF.Identity,
                                                                   bias=bt.t[0:ncols, :], scale=1.0),
                    po.b + bt.b, dst_bufs)

        def proj_tm(l, c0, ncols, src, nblk, dst_fn, dst_bufs, blk0=0, w=None):
            if w is None:
                wbv, wbb = load_w(w_in[l, :, c0:c0 + ncols], 16, ncols)
                bt = load_bias_bc(l, c0, ncols)
            else:
                wbv, wbb, bt = w
            for b0 in range(blk0, blk0 + nblk, 4):
                po = ps_o[rr("pso", 2)]

                def f(e, b0=b0, po=po):
                    ins = None
                    for j in range(4):
                        for kc in range(16):
                            ins = e.matmul(po.t[:, j * P:j * P + ncols], src.t[:, kc, (b0 + j) * P:(b0 + j + 1) * P],
                                           wbv[:, kc, :], start=(kc == 0), stop=(kc == 15))
                    return ins
                add("pe", f, wbb + src.b, po.b)
                dst = dst_fn(b0, 4)
                pin = po.t[:, :].rearrange("p (j c) -> p j c", c=P)[:, :, 0:ncols]
                bb = bt.t[:, 0:ncols].unsqueeze(1).to_broadcast([P, 4, ncols])
                add("dve", lambda e, dst=dst, pin=pin, bb=bb: e.tensor_tensor(out=dst, in0=pin, in1=bb, op=ALU.add),
                    po.b + bt.b, dst_bufs)

        def gate_to_mixT(l, c0, ch):
            proj_tm(l, c0, P, hTq, NQ,
                    lambda b0, n: mixT.t[:, ch, b0 * P:(b0 + n) * P].rearrange("p (j c) -> p j c", c=P), [mixT.b[ch]])
            add("act", lambda e: e.activation(out=mixT.t[:, ch, :], in_=mixT.t[:, ch, :], func=AF.Silu),
                [mixT.b[ch]], [mixT.b[ch]])

        def rope_tm(x, xb, cos, sin, tb, out, ob, half, tmp):
            x1, x2 = x[:, :, 0:half], x[:, :, half:2 * half]
            o1, o2 = out[:, :, 0:half], out[:, :, half:2 * half]
            t1, t2 = tmp
            add("dve", lambda e: e.tensor_tensor(out=t1.t, in0=x1, in1=cos, op=ALU.mult), xb + tb, t1.b)
            add("pool", lambda e: e.tensor_tensor(out=t2.t, in0=x2, in1=sin, op=ALU.mult), xb + tb, t2.b)
            add("dve", lambda e: e.tensor_tensor(out=o1, in0=t1.t, in1=t2.t, op=ALU.subtract), t1.b + t2.b, ob)
            add("dve", lambda e: e.tensor_tensor(out=t1.t, in0=x2, in1=cos, op=ALU.mult), xb + tb + ob, t1.b)
            add("pool", lambda e: e.tensor_tensor(out=t2.t, in0=x1, in1=sin, op=ALU.mult), xb + tb + ob, t2.b)
            add("dve", lambda e: e.tensor_tensor(out=o2, in0=t1.t, in1=t2.t, op=ALU.add), t1.b + t2.b, ob)

        def rms_scale(ss, ssb, n):
            ms, msb = newsmall()
            add("dve", lambda e: e.tensor_scalar(out=ms, in0=ss, scalar1=1.0 / n, scalar2=EPS, op0=ALU.mult, op1=ALU.add),
                [ssb], [msb])
            sd, sdb = newsmall()
            add("act", lambda e: e.sqrt(sd, ms), [msb], [sdb])
            rs, rsb = newsmall()
            add("dve", lambda e: e.reciprocal(rs, sd), [sdb], [rsb])
            return rs, rsb

        bank_cur = [0]

        def alloc_banks(nch):
            if bank_cur[0] + nch > 4:
                bank_cur[0] = 0
            b0 = bank_cur[0]
            bank_cur[0] = (b0 + nch) % 4
            return b0

        def softmax_row(nk, scale, pb, base=0):
            nch = (nk + 511) // 512
            o0 = base * 512
            bufs = ps_s.b[base:base + nch]
            mx, mxb = newsmall()
            add("dve", lambda e: e.reduce_max(out=mx, in_=ps_s.t[:, o0:o0 + nk], axis=AX.X), bufs, [mxb])
            nm, nmb = newsmall()
            add("dve", lambda e: e.tensor_scalar(out=nm, in0=mx, scalar1=-scale, scalar2=None, op0=ALU.mult), [mxb], [nmb])
            l1, l1b = newsmall()
            add("act", lambda e: e.activation(out=pb.t[:, 0:nk], in_=ps_s.t[:, o0:o0 + nk], func=AF.Exp, bias=nm, scale=scale,
                                              accum_out=l1), bufs + [nmb], pb.b + [l1b])
            ri, rib = newsmall()
            add("dve", lambda e: e.reciprocal(ri, l1), [l1b], [rib])
            return ri, rib

        def pv_accumulate(nkb, pb, vt, po, pTs, kb_off=0):
            for g0 in range(0, nkb, 8):
                gn = min(8, nkb - g0)
                ptile = pTs[rr("pT", 2)]
                transpose_to(ptile.t[:, 0:gn, :], ptile.b,
                             [pb.t[:, (g0 + j) * P:(g0 + j + 1) * P] for j in range(gn)], pb.b,
                             evac="act" if (g0 // 8) % 2 == 0 else "dve")

                def f(e, g0=g0, gn=gn, ptile=ptile):
                    ins = None
                    for j in range(gn):
                        kb = g0 + j
                        ins = e.matmul(po.t[:, 0:P], ptile.t[:, j, :], vt.t[:, kb_off + kb, :],
                                       start=(kb == 0), stop=(kb == nkb - 1))
                    return ins
                add("pe", f, ptile.b + vt.b, po.b)

        def finish_head(i, src_ap, src_bufs, rinv, rinvb, ch, mts):
            mt = mts[rr("mixtm", 2)]
            gate = mixT.t[:, ch, i * P:(i + 1) * P]
            if rinv is not None:
                add("dve", lambda e: e.scalar_tensor_tensor(out=mt.t, in0=src_ap, scalar=rinv, in1=gate,
                                                            op0=ALU.mult, op1=ALU.mult),
                    src_bufs + [rinvb, mixT.b[ch]], mt.b)
            else:
                add("dve", lambda e: e.tensor_tensor(out=mt.t, in0=src_ap, in1=gate, op=ALU.mult),
                    src_bufs + [mixT.b[ch]], mt.b)
            transpose_to(mixT.t[:, ch, i * P:(i + 1) * P], [mixT.b[ch]], [mt.t], mt.b, evac="act")

        def alloc_attn_common():
            pbs = [ar.alloc([S], BF16) for _ in range(2)]
            pTs = [ar.alloc([8, P], BF16) for _ in range(2)]
            mts = [ar.alloc([P], BF16) for _ in range(2)]
            return pbs, pTs, mts

        def alloc_head_ops(n=2):
            return [(ar.alloc([S], BF16), ar.alloc([NT, P], BF16), ar.alloc([NQ * P], BF16)) for _ in range(n)]

        def phase_norm(l, xsrc):
            ar.reset()
            gbc = ar.alloc([D], F32)
            xin = [ar.alloc([D], F32) for _ in range(2)]
            hn = ar.alloc([D], BF16)
            dma(gbc.t, pre_g[l, :].partition_broadcast(P), [], gbc.b)
            for (which, nblk, dst) in (("a", NT, hTa), ("q", NQ, hTq)):
                for t in range(nblk):
                    xt = xin[rr("xin", 2)]
                    sap, sbufs = xsrc(which, t)
                    dma(xt.t, sap, sbufs, xt.b)
                    ss, ssb = newsmall()
                    add("act", lambda e, xt=xt, ss=ss: e.activation(out=hn.t, in_=xt.t, func=AF.Square, accum_out=ss),
                        xt.b, hn.b + [ssb])
                    rs, rsb = rms_scale(ss, ssb, D)
                    add("dve", lambda e, xt=xt, rs=rs: e.scalar_tensor_tensor(out=hn.t, in0=xt.t, scalar=rs, in1=gbc.t,
                                                                              op0=ALU.mult, op1=ALU.mult),
                        xt.b + [rsb] + gbc.b, hn.b)
                    for half in range(2):
                        c0 = half * 8
                        transpose_to(dst.t[:, c0:c0 + 8, t * P:(t + 1) * P], dst.b[c0:c0 + 8],
                                     [hn.t[:, (c0 + j) * P:(c0 + j + 1) * P] for j in range(8)], hn.b,
                                     evac="act" if half == 0 else "dve")
            pg.barrier()

        def peek_banks(nch):
            b0 = bank_cur[0] if bank_cur[0] + nch <= 4 else 0
            return set(range(b0, b0 + nch))

        def emit_rows(rows, after=None):
            cur_banks = set()
            done_a = [False] * len(rows)
            if rows:
                cur_banks = peek_banks(rows[0][2])
                rows[0][0]()
                done_a[0] = True
            for k in range(len(rows)):
                nxt_banks = set()
                if k + 1 < len(rows):
                    nxt_banks = peek_banks(rows[k + 1][2])
                    if not (nxt_banks & cur_banks):
                        rows[k + 1][0]()
                        done_a[k + 1] = True
                rows[k][1]()
                if k + 1 < len(rows) and not done_a[k + 1]:
                    nxt_banks = peek_banks(rows[k + 1][2])
                    rows[k + 1][0]()
                    done_a[k + 1] = True
                cur_banks = nxt_banks
                if after is not None:
                    after(k)

        def run_pipelined(nheads, proj_items, att_rows):
            for t in proj_items(0):
                t()
            for h in range(nheads):
                nxt = proj_items(h + 1) if h + 1 < nheads else []
                rows = att_rows(h)
                st_ = {"k": 0}

                def after(ai, nxt=nxt, rows=rows, st_=st_):
                    want = (ai + 1) * len(nxt) // len(rows)
                    while st_["k"] < want:
                        nxt[st_["k"]]()
                        st_["k"] += 1
                emit_rows(rows, after)
                while st_["k"] < len(nxt):
                    nxt[st_["k"]]()
                    st_["k"] += 1

        def mixer_sb(l):
            ar.reset()
            scale = 128 ** -0.5
            pbs, pTs, mts = alloc_attn_common()
            hops = alloc_head_ops(2)
            wk_sets = [[ar.alloc([512], F32) for _ in range(4)] for _ in range(2)]
            def proj_items(h):
                kTt, vt, qTt = hops[h % 2]
                return [
                    lambda: proj_fm(l, OFF["sb_q"] + h * P, P, hTq, NQ * P, lambda t0, n: qTt.t[:, t0:t0 + n], qTt.b),
                    lambda: proj_fm(l, OFF["sb_k"] + h * P, P, hTa, S, lambda t0, n: kTt.t[:, t0:t0 + n], kTt.b),
                    lambda: proj_tm(l, OFF["sb_v"] + h * P, P, hTa, NT, lambda b0, n: vt.t[:, b0:b0 + n, :], vt.b),
                    lambda: gate_to_mixT(l, OFF["sb_gate"] + h * P, 0 + h),
                ]

            def sb_steps(h, i):
                kTt, vt, qTt = hops[h % 2]
                nkb = 2 * i + 2
                nk = nkb * P
                nch = (nk + 511) // 512
                pb = pbs[rr("pbf", 2)]
                st_ = {"carry": None}
                steps = []
                for c in range(nch - 1, -1, -1):
                    k0 = c * 512
                    n = min(512, nk - k0)
                    last = (c == nch - 1)
                    loc = {}

                    def s1(c=c, k0=k0, n=n, last=last, loc=loc):
                        bank = ps_s.b[c]
                        wk_e, wk_sp, wk_c, wk_t = wk_sets[rr("wk", 2)]
                        loc["wk_t"] = wk_t

                        def f(e):
                            ins = e.matmul(ps_s.t[:, k0:k0 + n], qTt.t[:, i * P:(i + 1) * P], kTt.t[:, k0:k0 + n],
                                           start=True, stop=not last)
                            if last:
                                ins = e.matmul(ps_s.t[:, nk - 256:nk], ident_bf.t[:], mask_s.t[:, 0:256], start=False, stop=True)
                            return ins
                        add("pe", f, qTt.b + kTt.b + ident_bf.b + mask_s.b, [bank])
                        sv = ps_s.t[:, k0:k0 + n]
                        add("act", lambda e: e.activation(out=wk_e.t[:, 0:n], in_=sv, func=AF.Exp, scale=scale), [bank], wk_e.b)
                        add("act", lambda e: e.activation(out=wk_sp.t[:, 0:n], in_=wk_e.t[:, 0:n], func=AF.Ln, bias=1.0, scale=1.0),
                            wk_e.b, wk_sp.b)
                        add("dve", lambda e: e.tensor_tensor_scan(out=wk_c.t[:, 0:n], data0=wk_sp.t[:, 0:n], data1=wk_sp.t[:, 0:n],
                                                                  initial=0.0, op0=ALU.add, op1=ALU.max), wk_sp.b, wk_c.b)
                        add("dve", lambda e: e.scalar_tensor_tensor(out=wk_t.t[:, 0:n], in0=sv, scalar=scale, in1=wk_sp.t[:, 0:n],
                                                                    op0=ALU.mult, op1=ALU.subtract), [bank] + wk_sp.b, wk_t.b)
                        add("pool", lambda e: e.tensor_tensor(out=wk_t.t[:, 0:n], in0=wk_t.t[:, 0:n], in1=wk_c.t[:, 0:n], op=ALU.add),
                            wk_t.b + wk_c.b, wk_t.b)
                        nb_, nbb = newsmall()
                        tot = wk_c.t[:, n - 1:n]
                        if st_["carry"] is None:
                            add("dve", lambda e: e.tensor_scalar(out=nb_, in0=tot, scalar1=-1.0, scalar2=None, op0=ALU.mult),
                                wk_c.b, [nbb])
                        else:
                            cpr, cprb = st_["carry"]
                            add("dve", lambda e: e.scalar_tensor_tensor(out=nb_, in0=tot, scalar=-1.0, in1=cpr, op0=ALU.mult,
                                                                        op1=ALU.add), wk_c.b + [cprb], [nbb])
                        st_["carry"] = (nb_, nbb)
                        loc["nb"] = (nb_, nbb)

                    def s2(k0=k0, n=n, loc=loc):
                        wk_t = loc["wk_t"]
                        nb_, nbb = loc["nb"]
                        add("act", lambda e: e.activation(out=pb.t[:, k0:k0 + n], in_=wk_t.t[:, 0:n], func=AF.Exp, bias=nb_, scale=1.0),
                            wk_t.b + [nbb], pb.b)

                    fin = None
                    if c == 0:
                        def fin():
                            po = ps_o[rr("pso", 2)]
                            pv_accumulate(nkb, pb, vt, po, pTs)
                            finish_head(i, po.t[:, 0:P], po.b, None, None, 0 + h, mts)
                    steps.append((s1, s2, fin))
                return steps

            for t in proj_items(0):
                t()
            for h in range(4):
                nxt = proj_items(h + 1) if h + 1 < 4 else []
                kq = 0
                prev = None
                for i in range(NQ):
                    for stp in sb_steps(h, i):
                        stp[0]()
                        if prev is not None:
                            prev[1]()
                            if prev[2] is not None:
                                prev[2]()
                        prev = stp
                    want = (i + 1) * len(nxt) // NQ
                    while kq < want:
                        nxt[kq]()
                        kq += 1
                prev[1]()
                prev[2]()
                while kq < len(nxt):
                    nxt[kq]()
                    kq += 1
            pg.barrier()

        def softmax_heads_causal(i, qTt, kTt, vt, scale, pbs, pTs, mts, ch, extra=None, extra_bufs=()):
            nkb = 2 * i + 2
            nk = nkb * P
            nch = (nk + 511) // 512
            st_ = {}

            def A():
                base = alloc_banks(nch)
                st_["base"] = base
                o0 = base * 512

                def f(e):
                    ins = None
                    for c in range(nch):
                        k0 = c * 512
                        n = min(512, nk - k0)
                        last = (c == nch - 1)
                        ins = e.matmul(ps_s.t[:, o0 + k0:o0 + k0 + n], qTt.t[:, i * P:(i + 1) * P], kTt.t[:, k0:k0 + n],
                                       start=True, stop=(not last) and extra is None)
                        if extra is not None:
                            ins = extra(e, ps_s.t[:, o0 + k0:o0 + k0 + n], k0, n, not last)
                        if last:
                            ins = e.matmul(ps_s.t[:, o0 + nk - 256:o0 + nk], ident_bf.t[:], mask_c.t[:, 0:256], start=False, stop=True)
                    return ins
                add("pe", f, qTt.b + kTt.b + ident_bf.b + mask_c.b + list(extra_bufs), ps_s.b[base:base + nch])

            def B():
                pb = pbs[rr("pbf", 2)]
                rinv, rinvb = softmax_row(nk, scale, pb, st_["base"])
                po = ps_o[rr("pso", 2)]
                pv_accumulate(nkb, pb, vt, po, pTs)
                finish_head(i, po.t[:, 0:P], po.b, rinv, rinvb, ch, mts)
            return A, B, nch

        def mixer_fox(l):
            ar.reset()
            scale = 128 ** -0.5
            pbs, pTs, mts = alloc_attn_common()
            hops = alloc_head_ops(2)
            crow3 = ar.alloc([S], BF16)
            sel3 = ar.alloc([4, P], BF16)
            t_e = ar.alloc([512], F32)
            t_sp = ar.alloc([512], F32)
            cch = [ar.alloc([512], F32) for _ in range(2)]
            hi_t = ar.alloc([512], BF16)
            mid_t = ar.alloc([512], BF16)
            wf68 = ar.alloc([16, 68], BF16)
            fb = ar.alloc([2], F32)
            NP_ = 68
            dma(sel3.t, c_sel3, [], sel3.b)
            add("pool", lambda e: e.memset(crow3.t, 0.0), [], crow3.b)
            add("pool", lambda e: e.memset(wf68.t, 0.0), [], wf68.b)
            add("pool", lambda e: e.memset(fb.t, 0.0), [], fb.b)
            c0 = OFF["fox_f"]
            wbv, wbb = load_w(w_in[l, :, c0:c0 + 4], 16, 4)
            for rep in range(3):
                add("dve", lambda e, rep=rep: e.tensor_copy(wf68.t[:, :, rep * 32:rep * 32 + 4], wbv), wbb + wf68.b, wf68.b)
                dma(fb.t[rep * 32:rep * 32 + 4, 0:1], b_in[l, c0:c0 + 4].rearrange("(c o) -> c o", o=1), fb.b, fb.b)
                dma(fb.t[rep * 32:rep * 32 + 4, 1:2], fox_fb[l, :].rearrange("(c o) -> c o", o=1), fb.b, fb.b)
            nb_, nbb = newsmall()
            add("dve", lambda e: e.scalar_tensor_tensor(out=nb_[0:NP_, :], in0=fb.t[0:NP_, 0:1], scalar=-1.0, in1=fb.t[0:NP_, 1:2],
                                                        op0=ALU.mult, op1=ALU.subtract), fb.b, [nbb])
            prevc = None
            for ci, t0 in enumerate(range(0, S, 512)):
                po = ps_o[rr("pso", 2)]
                cc = cch[ci % 2]

                def f(e, t0=t0, po=po):
                    ins = None
                    for kc in range(16):
                        ins = e.matmul(po.t[0:NP_, :], wf68.t[:, kc, :], hTa.t[:, kc, t0:t0 + 512], start=(kc == 0), stop=(kc == 15))
                    return ins
                add("pe", f, wf68.b + hTa.b, po.b)
                add("act", lambda e, po=po: e.activation(out=t_e.t[0:NP_, :], in_=po.t[0:NP_, :], func=AF.Exp, bias=nb_[0:NP_, :],
                                                         scale=-1.0), po.b + [nbb], t_e.b)
                add("act", lambda e: e.activation(out=t_sp.t[0:NP_, :], in_=t_e.t[0:NP_, :], func=AF.Ln, bias=1.0, scale=1.0),
                    t_e.b, t_sp.b)
                init = 0.0 if prevc is None else prevc.t[0:NP_, 511:512]
                rdb = t_sp.b + ([] if prevc is None else prevc.b)
                add("dve", lambda e, cc=cc, init=init: e.tensor_tensor_scan(out=cc.t[0:NP_, :], data0=t_sp.t[0:NP_, :],
                                                                            data1=t_sp.t[0:NP_, :], initial=init,
                                                                            op0=ALU.add, op1=ALU.max), rdb, cc.b)
                add("dve", lambda e, cc=cc: e.tensor_scalar(out=t_e.t[0:NP_, :], in0=cc.t[0:NP_, :], scalar1=float(128 ** 0.5),
                                                            scalar2=None, op0=ALU.mult), cc.b, t_e.b)
                add("dve", lambda e: e.tensor_copy(hi_t.t[0:NP_, :], t_e.t[0:NP_, :]), t_e.b, hi_t.b)
                add("dve", lambda e: e.tensor_tensor(out=t_sp.t[0:NP_, :], in0=t_e.t[0:NP_, :], in1=hi_t.t[0:NP_, :], op=ALU.subtract),
                    t_e.b + hi_t.b, t_sp.b)
                add("dve", lambda e: e.tensor_copy(mid_t.t[0:NP_, :], t_sp.t[0:NP_, :]), t_sp.b, mid_t.b)
                add("dve", lambda e: e.tensor_tensor(out=t_sp.t[0:NP_, :], in0=t_sp.t[0:NP_, :], in1=mid_t.t[0:NP_, :], op=ALU.subtract),
                    t_sp.b + mid_t.b, t_sp.b)
                add("dve", lambda e, t0=t0: e.tensor_copy(crow3.t[0:4, t0:t0 + 512], hi_t.t[0:4, :]), hi_t.b, crow3.b)
                add("dve", lambda e, t0=t0: e.tensor_copy(crow3.t[32:36, t0:t0 + 512], mid_t.t[32:36, :]), mid_t.b, crow3.b)
                add("dve", lambda e, t0=t0: e.tensor_copy(crow3.t[64:68, t0:t0 + 512], t_sp.t[64:68, :]), t_sp.b, crow3.b)
                prevc = cc
            def proj_items(h):
                kTt, vt, qTt = hops[h % 2]
                return [
                    lambda: proj_fm(l, OFF["fox_q"] + h * P, P, hTq, NQ * P, lambda t0, n: qTt.t[:, t0:t0 + n], qTt.b),
                    lambda: proj_fm(l, OFF["fox_k"] + h * P, P, hTa, S, lambda t0, n: kTt.t[:, t0:t0 + n], kTt.b),
                    lambda: proj_tm(l, OFF["fox_v"] + h * P, P, hTa, NT, lambda b0, n: vt.t[:, b0:b0 + n, :], vt.b),
                    lambda: gate_to_mixT(l, OFF["fox_gate"] + h * P, 8 + h),
                ]

            def att_items(h):
                kTt, vt, qTt = hops[h % 2]

                def extra(e, out, k0, n, stop):
                    return e.matmul(out, sel3.t[:, h, :], crow3.t[:, k0:k0 + n], start=False, stop=stop)
                return [softmax_heads_causal(i, qTt, kTt, vt, scale, pbs, pTs, mts, 8 + h, extra, sel3.b + crow3.b)
                        for i in range(NQ)]
            run_pipelined(4, proj_items, att_items)
            pg.barrier()

        def mixer_mla(l):
            ar.reset()
            scale = 192 ** -0.5
            cqnT = ar.alloc([3, NQ * P], BF16)
            ckvnT = ar.alloc([S], BF16)
            krT = ar.alloc([S], BF16)
            wukv_b = ar.alloc([1024], BF16)
            cosq = ar.alloc([NQ, 32], F32)
            sinq = ar.alloc([NQ, 32], F32)
            mark = ar.off
            dma(cosq.t, c_cosq32, [], cosq.b)
            dma(sinq.t, c_sinq32, [], sinq.b)
            cosa = ar.alloc([NT, 32], F32)
            sina = ar.alloc([NT, 32], F32)
            gq = ar.alloc([384], F32)
            gkv = ar.alloc([P], F32)
            xt = ar.alloc([4, 384], F32)
            xn = ar.alloc([4, 384], BF16)
            t1 = ar.alloc([4, 32], F32)
            t2 = ar.alloc([4, 32], F32)
            dma(cosa.t, c_cosa32, [], cosa.b)
            dma(sina.t, c_sina32, [], sina.b)
            dma(gq.t, qng[l, :].partition_broadcast(P), [], gq.b)
            dma(gkv.t, kvng[l, :].partition_broadcast(P), [], gkv.b)
            ws = wst[rr("wst", 2)]
            wsv = ws.t[:, :, :].rearrange("p a b -> p (a b)")[:, 0:1024]
            dma(wsv, w_ukv[l, :, :], [], ws.b)
            add("pool", lambda e, wsv=wsv: e.tensor_copy(wukv_b.t, wsv), ws.b, wukv_b.b)

            def norm_rows(x_ap, xb, width, g_ap, gb, out_ap, ob):
                ss, ssb = newsmall()
                jt, jb = junkt
                add("act", lambda e: e.activation(out=jt[:, 0:width], in_=x_ap, func=AF.Square, accum_out=ss), xb, jb + [ssb])
                rs, rsb = rms_scale(ss, ssb, width)
                add("dve", lambda e: e.scalar_tensor_tensor(out=out_ap, in0=x_ap, scalar=rs, in1=g_ap, op0=ALU.mult, op1=ALU.mult),
                    xb + [rsb] + gb, ob)
            jk = ar.alloc([384], F32)
            junkt = (jk.t, jk.b)
            for b0 in range(0, NQ, 4):
                for cg in range(3):
                    wbv, wbb = load_w(w_in[l, :, OFF["mla_cq"] + cg * P:OFF["mla_cq"] + (cg + 1) * P], 16, P)
                    bt = load_bias_bc(l, OFF["mla_cq"] + cg * P, P)
                    po = ps_o[rr("pso", 2)]

                    def f(e, b0=b0, po=po, wbv=wbv):
                        ins = None
                        for j in range(4):
                            for kc in range(16):
                                ins = e.matmul(po.t[:, j * P:(j + 1) * P], hTq.t[:, kc, (b0 + j) * P:(b0 + j + 1) * P],
                                               wbv[:, kc, :], start=(kc == 0), stop=(kc == 15))
                        return ins
                    add("pe", f, wbb + hTq.b, po.b)
                    pin = po.t[:, :].rearrange("p (j c) -> p j c", c=P)
                    bb = bt.t[:, :].unsqueeze(1).to_broadcast([P, 4, P])
                    add("dve", lambda e, cg=cg, pin=pin, bb=bb: e.tensor_tensor(out=xt.t[:, :, cg * P:(cg + 1) * P], in0=pin, in1=bb,
                                                                                op=ALU.add), po.b + bt.b, xt.b)
                for j in range(4):
                    norm_rows(xt.t[:, j, :], xt.b, 384, gq.t, gq.b, xn.t[:, j, :], xn.b)
                for cg in range(3):
                    transpose_to(cqnT.t[:, cg, b0 * P:(b0 + 4) * P], cqnT.b,
                                 [xn.t[:, j, cg * P:(cg + 1) * P] for j in range(4)], xn.b, evac="dve")
            xk = ar.alloc([4, P], F32)
            xkn = ar.alloc([4, P], BF16)
            xr = ar.alloc([4, 64], F32)
            xrb = ar.alloc([4, 64], BF16)
            wbv1, wbb1 = load_w(w_in[l, :, OFF["mla_ckv"]:OFF["mla_ckv"] + P], 16, P)
            bt1 = load_bias_bc(l, OFF["mla_ckv"], P)
            wbv2, wbb2 = load_w(w_in[l, :, OFF["mla_k_rope"]:OFF["mla_k_rope"] + 64], 16, 64)
            bt2 = load_bias_bc(l, OFF["mla_k_rope"], 64)
            for b0 in range(0, NT, 4):
                po = ps_o[rr("pso", 2)]

                def f(e, b0=b0, po=po):
                    ins = None
                    for j in range(4):
                        for kc in range(16):
                            ins = e.matmul(po.t[:, j * P:(j + 1) * P], hTa.t[:, kc, (b0 + j) * P:(b0 + j + 1) * P],
                                           wbv1[:, kc, :], start=(kc == 0), stop=(kc == 15))
                    return ins
                add("pe", f, wbb1 + hTa.b, po.b)
                pin = po.t[:, :].rearrange("p (j c) -> p j c", c=P)
                bb = bt1.t[:, :].unsqueeze(1).to_broadcast([P, 4, P])
                add("dve", lambda e, pin=pin, bb=bb: e.tensor_tensor(out=xk.t, in0=pin, in1=bb, op=ALU.add), po.b + bt1.b, xk.b)
                for j in range(4):
                    norm_rows(xk.t[:, j, :], xk.b, P, gkv.t, gkv.b, xkn.t[:, j, :], xkn.b)
                transpose_to(ckvnT.t[:, b0 * P:(b0 + 4) * P], ckvnT.b, [xkn.t[:, j, :] for j in range(4)], xkn.b, evac="dve")
                po2 = ps_o[rr("pso", 2)]

                def f2(e, b0=b0, po2=po2):
                    ins = None
                    for j in range(4):
                        for kc in range(16):
                            ins = e.matmul(po2.t[:, j * 64:(j + 1) * 64], hTa.t[:, kc, (b0 + j) * P:(b0 + j + 1) * P],
                                           wbv2[:, kc, :], start=(kc == 0), stop=(kc == 15))
                    return ins
                add("pe", f2, wbb2 + hTa.b, po2.b)
                pin2 = po2.t[:, 0:256].rearrange("p (j c) -> p j c", c=64)
                bb2 = bt2.t[:, 0:64].unsqueeze(1).to_broadcast([P, 4, 64])
                add("dve", lambda e, pin2=pin2, bb2=bb2: e.tensor_tensor(out=xr.t, in0=pin2, in1=bb2, op=ALU.add), po2.b + bt2.b, xr.b)
                rope_tm(xr.t, xr.b, cosa.t[:, b0:b0 + 4, :], sina.t[:, b0:b0 + 4, :], cosa.b + sina.b, xrb.t, xrb.b, 32, (t1, t2))
                transpose_to(krT.t[0:64, b0 * P:(b0 + 4) * P], krT.b, [xrb.t[:, j, :] for j in range(4)], xrb.b, evac="dve")
            pg.barrier()
            ar.reset(mark)
            pbs, pTs, mts = alloc_attn_common()
            kTt = ar.alloc([S], BF16)
            vt = ar.alloc([NT, P], BF16)
            qTt = ar.alloc([NQ * P], BF16)
            qrT = ar.alloc([NQ * P], BF16)
            wuqh = ar.alloc([3, 192], BF16)
            qr_f = ar.alloc([NQ, 64], F32)
            qr_b = ar.alloc([NQ, 64], BF16)
            t1 = ar.alloc([NQ, 32], F32)
            t2 = ar.alloc([NQ, 32], F32)
            for h in range(4):
                ws = wst[rr("wst", 2)]
                wsv = ws.t[:, :, :].rearrange("p a b -> p (a b)")[:, 0:576].rearrange("p (a b) -> p a b", b=192)
                dma(wsv, w_uq[l, :, h * 192:(h + 1) * 192].rearrange("(kc p) c -> p kc c", p=P), [], ws.b)
                add("pool", lambda e, wsv=wsv: e.tensor_copy(wuqh.t, wsv), ws.b, wuqh.b)
                for t0 in range(0, NQ * P, 512):
                    po = ps_o[rr("pso", 2)]

                    def f(e, t0=t0, po=po):
                        ins = None
                        for kc in range(3):
                            ins = e.matmul(po.t[:, :], wuqh.t[:, kc, 0:P], cqnT.t[:, kc, t0:t0 + 512], start=(kc == 0), stop=(kc == 2))
                        return ins
                    add("pe", f, wuqh.b + cqnT.b, po.b)
                    add("act", lambda e, t0=t0, po=po: e.copy(qTt.t[:, t0:t0 + 512], po.t[:, :]), po.b, qTt.b)
                for b0 in range(0, NQ, 4):
                    po = ps_o[rr("pso", 2)]

                    def f(e, b0=b0, po=po):
                        ins = None
                        for j in range(4):
                            for kc in range(3):
                                ins = e.matmul(po.t[:, j * 64:(j + 1) * 64], cqnT.t[:, kc, (b0 + j) * P:(b0 + j + 1) * P],
                                               wuqh.t[:, kc, P:192], start=(kc == 0), stop=(kc == 2))
                        return ins
                    add("pe", f, wuqh.b + cqnT.b, po.b)
                    add("act", lambda e, b0=b0, po=po: e.copy(qr_f.t[:, b0:b0 + 4, :], po.t[:, 0:256].rearrange("p (j c) -> p j c", c=64)),
                        po.b, qr_f.b)
                rope_tm(qr_f.t, qr_f.b, cosq.t, sinq.t, cosq.b + sinq.b, qr_b.t, qr_b.b, 32, (t1, t2))
                transpose_to(qrT.t[0:64, :], qrT.b, [qr_b.t[:, j, :] for j in range(NQ)], qr_b.b, evac="dve")
                for t0 in range(0, S, 512):
                    po = ps_o[rr("pso", 2)]
                    add("pe", lambda e, t0=t0, po=po, h=h: e.matmul(po.t[:, :], wukv_b.t[:, h * 256:h * 256 + P], ckvnT.t[:, t0:t0 + 512],
                                                                    start=True, stop=True), wukv_b.b + ckvnT.b, po.b)
                    add("act", lambda e, t0=t0, po=po: e.copy(kTt.t[:, t0:t0 + 512], po.t[:, :]), po.b, kTt.b)
                for b0 in range(0, NT, 4):
                    po = ps_o[rr("pso", 2)]

                    def f(e, b0=b0, po=po, h=h):
                        ins = None
                        for j in range(4):
                            ins = e.matmul(po.t[:, j * P:(j + 1) * P], ckvnT.t[:, (b0 + j) * P:(b0 + j + 1) * P],
                                           wukv_b.t[:, h * 256 + P:h * 256 + 256], start=True, stop=True)
                        return ins
                    add("pe", f, wukv_b.b + ckvnT.b, po.b)
                    add("dve", lambda e, b0=b0, po=po: e.tensor_copy(vt.t[:, b0:b0 + 4, :], po.t[:, :].rearrange("p (j c) -> p j c", c=P)),
                        po.b, vt.b)
                gate_to_mixT(l, OFF["mla_gate"] + h * P, 12 + h)

                rows = []
                for i in range(NQ):
                    def extra_i(e, out, k0, n, stop, i=i):
                        return e.matmul(out, qrT.t[0:64, i * P:(i + 1) * P], krT.t[0:64, k0:k0 + n], start=False, stop=stop)
                    rows.append(softmax_heads_causal(i, qTt, kTt, vt, scale, pbs, pTs, mts, 12 + h, extra_i, qrT.b + krT.b))
                emit_rows(rows)
            pg.barrier()

        def mixer_nsa(l):
            ar.reset()
            scale = 128 ** -0.5
            qT4 = ar.alloc([4, NQ * P], BF16)
            ksT = ar.alloc([S], BF16)
            kwT = ar.alloc([S], BF16)
            vs = ar.alloc([NT, P], BF16)
            vw = ar.alloc([NT, P], BF16)
            kcT = ar.alloc([P], BF16)
            vc = ar.alloc([P], BF16)
            bgate = ar.alloc([NQ, 12], F32)
            e8 = ar.alloc([512], BF16)
            c2s = ar.alloc([32], F32)
            mark = ar.off
            dma(e8.t[0:8, :], c_e8, [], e8.b)
            dma(c2s.t, c_c2s, [], c2s.b)
            cos_t = ar.alloc([NT, 64], F32)
            sin_t = ar.alloc([NT, 64], F32)
            xf = ar.alloc([NQ, P], F32)
            xb_ = ar.alloc([NQ, P], BF16)
            t1 = ar.alloc([NQ, 64], F32)
            t2 = ar.alloc([NQ, 64], F32)
            dma(cos_t.t[:, 0:NQ, :], c_cosq, [], cos_t.b)
            dma(sin_t.t[:, 0:NQ, :], c_sinq, [], sin_t.b)
            for h in range(4):
                proj_tm(l, OFF["nsa_q"] + h * P, P, hTq, NQ, lambda b0, n: xf.t[:, b0:b0 + n, :], xf.b)
                rope_tm(xf.t, xf.b, cos_t.t[:, 0:NQ, :], sin_t.t[:, 0:NQ, :], cos_t.b + sin_t.b, xb_.t, xb_.b, 64, (t1, t2))
                transpose_to(qT4.t[:, h, :], qT4.b, [xb_.t[:, j, :] for j in range(NQ)], xb_.b, evac="dve")
            proj_tm(l, OFF["nsa_branch"], 12, hTq, NQ, lambda b0, n: bgate.t[:, b0:b0 + n, :], bgate.b)
            add("act", lambda e: e.activation(out=bgate.t, in_=bgate.t, func=AF.Sigmoid), bgate.b, bgate.b)
            dma(cos_t.t, c_cosa, xf.b + xb_.b + t1.b + t2.b, cos_t.b)
            dma(sin_t.t, c_sina, xf.b + xb_.b + t1.b + t2.b, sin_t.b)
            for (cname, dstT) in (("nsa_k_sel", ksT), ("nsa_k_win", kwT)):
                c0_ = OFF[cname]
                wbv_, wbb_ = load_w(w_in[l, :, c0_:c0_ + P], 16, P)
                bt_ = load_bias_bc(l, c0_, P)
                for g0 in (0, 8):
                    proj_tm(l, c0_, P, hTa, 8, lambda b0, n, g0=g0: xf.t[:, b0 - g0:b0 - g0 + n, :], xf.b, blk0=g0,
                            w=(wbv_, wbb_, bt_))
                    rope_tm(xf.t, xf.b, cos_t.t[:, g0:g0 + 8, :], sin_t.t[:, g0:g0 + 8, :], cos_t.b + sin_t.b, xb_.t, xb_.b, 64,
                            (t1, t2))
                    transpose_to(dstT.t[:, g0 * P:(g0 + 8) * P], dstT.b, [xb_.t[:, j, :] for j in range(8)], xb_.b, evac="dve")
            proj_tm(l, OFF["nsa_v_sel"], P, hTa, NT, lambda b0, n: vs.t[:, b0:b0 + n, :], vs.b)
            proj_tm(l, OFF["nsa_v_win"], P, hTa, NT, lambda b0, n: vw.t[:, b0:b0 + n, :], vw.b)
            pg.barrier()
            if NSA_STOP <= 1:
                return
            ar.reset(mark)
            tokT = ar.alloc([S], BF16)
            blkT = ar.alloc([32, P], BF16)
            w1b = ar.alloc([32, P], BF16)
            w2b = ar.alloc([P], BF16)
            posr = ar.alloc([P], F32)
            posT = ar.alloc([32], F32)
            hidT = ar.alloc([P], BF16)
            kcf = ar.alloc([1, P], F32)
            kcb = ar.alloc([1, P], BF16)
            cosc = ar.alloc([1, 64], F32)
            sinc = ar.alloc([1, 64], F32)
            tc1 = ar.alloc([1, 64], F32)
            tc2 = ar.alloc([1, 64], F32)
            dma(cosc.t[:, 0, :], c_cosc, [], cosc.b)
            dma(sinc.t[:, 0, :], c_sinc, [], sinc.b)
            for which in range(2):
                cname = "nsa_k_cmp" if which == 0 else "nsa_v_cmp"
                proj_fm(l, OFF[cname], P, hTa, S, lambda t0, n: tokT.t[:, t0:t0 + n], tokT.b)
                dma(posr.t[0:32, :], pos_kv[which][l, :, :], [], posr.b)
                po = ps_o[rr("pso", 2)]
                add("pe", lambda e, po=po: e.transpose(po.t[:, 0:32], posr.t[0:32, :], ident_f.t[0:32, 0:32]), posr.b + ident_f.b, po.b)
                add("act", lambda e, po=po: e.copy(posT.t, po.t[:, 0:32]), po.b, posT.b)
                for half in range(2):
                    ws = wst[rr("wst", 2)]
                    dma(ws.t[:, :, :], w1_kv[which][l, half * 2048:(half + 1) * 2048, :].rearrange("(l d) h -> d l h", d=P), [], ws.b)
                    add("pool", lambda e, half=half, ws=ws: e.tensor_copy(w1b.t[:, half * 16:(half + 1) * 16, :], ws.t[:, :, :]),
                        ws.b, w1b.b)
                ws = wst[rr("wst", 2)]
                wsv = ws.t[:, 0, :]
                dma(wsv, w2_kv[which][l, :, :], [], ws.b)
                add("pool", lambda e, wsv=wsv: e.tensor_copy(w2b.t, wsv), ws.b, w2b.b)
                for ll in range(32):
                    src = tokT.t[:, ll:ll + 16 * (NCMP - 1) + 1:16]
                    eng = "dve" if ll % 2 == 0 else "pool"
                    add(eng, lambda e, ll=ll, src=src: e.tensor_scalar(out=blkT.t[:, ll, 0:NCMP], in0=src, scalar1=posT.t[:, ll:ll + 1],
                                                                       scalar2=None, op0=ALU.add), tokT.b + posT.b, blkT.b)
                po = ps_o[rr("pso", 2)]

                def f(e, po=po):
                    ins = None
                    for ll in range(32):
                        ins = e.matmul(po.t[:, 0:NCMP], w1b.t[:, ll, :], blkT.t[:, ll, 0:NCMP], start=(ll == 0), stop=(ll == 31))
                    return ins
                add("pe", f, w1b.b + blkT.b, po.b)
                add("act", lambda e, po=po: e.activation(out=hidT.t[:, 0:NCMP], in_=po.t[:, 0:NCMP], func=AF.Silu), po.b, hidT.b)
                po2 = ps_o[rr("pso", 2)]
                add("pe", lambda e, po2=po2: e.matmul(po2.t[0:NCMP, 0:P], hidT.t[:, 0:NCMP], w2b.t, start=True, stop=True),
                    hidT.b + w2b.b, po2.b)
                if which == 0:
                    add("act", lambda e, po2=po2: e.copy(kcf.t[0:NCMP, 0, :], po2.t[0:NCMP, 0:P]), po2.b, kcf.b)
                    rope_tm(kcf.t[0:NCMP], kcf.b, cosc.t[0:NCMP], sinc.t[0:NCMP], cosc.b + sinc.b, kcb.t[0:NCMP], kcb.b, 64,
                            (view(tc1, tc1.t[0:NCMP]), view(tc2, tc2.t[0:NCMP])))
                    add("pool", lambda e: e.memset(kcT.t, 0.0), [], kcT.b)
                    transpose_to(kcT.t[:, 0:NCMP], kcT.b, [kcb.t[0:NCMP, 0, :]], kcb.b, evac="dve")
                else:
                    add("pool", lambda e: e.memset(vc.t, 0.0), [], vc.b)
                    add("act", lambda e, po2=po2: e.copy(vc.t[0:NCMP, :], po2.t[0:NCMP, 0:P]), po2.b, vc.b)
            for h in range(4):
                gate_to_mixT(l, OFF["nsa_gate"] + h * P, 4 + h)
            pg.barrier()
            if NSA_STOP <= 2:
                return
            ar.reset(mark)
            pbs, pTs, mts = alloc_attn_common()
            mcmp2 = [ar.alloc([P], BF16) for _ in range(2)]
            cmp012 = [ar.alloc([P], F32) for _ in range(2)]
            selb2 = [ar.alloc([32], F32) for _ in range(2)]
            selv2 = [ar.alloc([32], F32) for _ in range(2)]
            ef = ar.alloc([4, P], F32)
            pcf = ef
            pcb = ar.alloc([4, P], BF16)
            ps4 = ar.alloc([P], F32)
            impA = ar.alloc([32], F32)
            imp = ar.alloc([32], F32)
            pcT_b = ar.alloc([4, P], BF16)
            ocmp = ar.alloc([4, P], F32)
            sc = ar.alloc([32], F32)
            sc2 = ar.alloc([32], F32)
            m8a = ar.alloc([8], F32)
            m8b = ar.alloc([8], F32)
            sbias = ar.alloc([32], BF16)
            selT = ar.alloc([4, P], BF16)
            accs = [ar.alloc([P], F32) for _ in range(2)]
            def nsa_sel_row(i, h, nk, nkb, nch):
                st_ = {}

                def A():
                    base = alloc_banks(nch)
                    st_["base"] = base
                    o0 = base * 512

                    def f(e):
                        ins = None
                        for c in range(nch):
                            k0 = c * 512
                            n = min(512, nk - k0)
                            last = (c == nch - 1)
                            e.matmul(ps_s.t[:, o0 + k0:o0 + k0 + n], qT4.t[:, h, i * P:(i + 1) * P], ksT.t[:, k0:k0 + n], start=True, stop=False)
                            ins = e.matmul(ps_s.t[:, o0 + k0:o0 + k0 + n], selT.t[0:8, c, :], e8.t[0:8, 0:n], start=False, stop=not last)
                            if last:
                                ins = e.matmul(ps_s.t[:, o0 + nk - 256:o0 + nk], ident_bf.t[:], mask_c.t[:, 0:256], start=False, stop=True)
                        return ins
                    add("pe", f, qT4.b + ksT.b + selT.b + e8.b + ident_bf.b + mask_c.b, ps_s.b[base:base + nch])

                def B():
                    pb = pbs[rr("pbf", 2)]
                    rs_, rsb_ = softmax_row(nk, scale, pb, st_["base"])
                    pos_ = ps_o[rr("pso", 2)]
                    pv_accumulate(nkb, pb, vs, pos_, pTs)
                    cs, csb = newsmall()
                    add("dve", lambda e: e.tensor_tensor(out=cs, in0=rs_, in1=bgate.t[:, i, 3 * h + 1:3 * h + 2], op=ALU.mult),
                        [rsb_] + bgate.b, [csb])
                    a_ = accs[h % 2]
                    add("dve", lambda e: e.scalar_tensor_tensor(out=a_.t, in0=pos_.t[:, 0:P], scalar=cs, in1=ocmp.t[:, h, :],
                                                                op0=ALU.mult, op1=ALU.add), pos_.b + [csb] + ocmp.b, a_.b)
                return A, B, nch

            def nsa_win_row(i, h):
                kb0 = max(0, 2 * i - 4)
                nkbw = 2 * i + 2 - kb0
                nkw = nkbw * P
                moff = (kb0 - (2 * i - 4)) * P
                nchw = (nkw + 511) // 512
                st_ = {}

                def A():
                    basew = alloc_banks(nchw)
                    st_["base"] = basew
                    o0 = basew * 512

                    def f(e):
                        ins = None
                        for c in range(nchw):
                            k0 = c * 512
                            n = min(512, nkw - k0)
                            e.matmul(ps_s.t[:, o0 + k0:o0 + k0 + n], qT4.t[:, h, i * P:(i + 1) * P], kwT.t[:, kb0 * P + k0:kb0 * P + k0 + n],
                                     start=True, stop=False)
                            ins = e.matmul(ps_s.t[:, o0 + k0:o0 + k0 + n], ident_bf.t[:], mask_w.t[:, moff + k0:moff + k0 + n], start=False, stop=True)
                        return ins
                    add("pe", f, qT4.b + kwT.b + ident_bf.b + mask_w.b, ps_s.b[basew:basew + nchw])

                def B():
                    pb = pbs[rr("pbf", 2)]
                    rw_, rwb_ = softmax_row(nkw, scale, pb, st_["base"])
                    pow_ = ps_o[rr("pso", 2)]
                    pv_accumulate(nkbw, pb, vw, pow_, pTs, kb_off=kb0)
                    cw, cwb = newsmall()
                    add("dve", lambda e: e.tensor_tensor(out=cw, in0=rw_, in1=bgate.t[:, i, 3 * h + 2:3 * h + 3], op=ALU.mult),
                        [rwb_] + bgate.b, [cwb])
                    a_ = accs[h % 2]
                    add("dve", lambda e: e.scalar_tensor_tensor(out=a_.t, in0=pow_.t[:, 0:P], scalar=cw, in1=a_.t, op0=ALU.mult, op1=ALU.add),
                        pow_.b + [cwb] + a_.b, a_.b)
                    finish_head(i, a_.t, a_.b, None, None, 4 + h, mts)
                return A, B, nchw

            for i in range(NQ):
                nkb = 2 * i + 2
                nk = nkb * P
                nch = (nk + 511) // 512
                mcmp, cmp01, selb, selv = mcmp2[i % 2], cmp012[i % 2], selb2[i % 2], selv2[i % 2]
                dma(mcmp.t, c_mask_cmp[:, i, :], [], mcmp.b)
                dma(cmp01.t, c_cmp01[:, i, :], [], cmp01.b)
                dma(selb.t, c_selbias[:, i, :], [], selb.b)
                dma(selv.t, c_selvalid[:, i, :], [], selv.b)
                pz = ps_o[rr("pso", 2)]

                def f(e, i=i, pz=pz, mcmp=mcmp):
                    ins = None
                    for h in range(4):
                        e.matmul(pz.t[:, h * P:(h + 1) * P], qT4.t[:, h, i * P:(i + 1) * P], kcT.t[:, :], start=True, stop=False)
                        ins = e.matmul(pz.t[:, h * P:(h + 1) * P], ident_bf.t[:], mcmp.t[:, :], start=False, stop=True)
                    return ins
                add("pe", f, qT4.b + kcT.b + ident_bf.b + mcmp.b, pz.b)
                mx4, mx4b = newsmall(4)
                pz3 = pz.t[:, :].rearrange("p (h n) -> p h n", n=P)
                add("dve", lambda e, pz3=pz3, mx4=mx4: e.tensor_reduce(out=mx4, in_=pz3, axis=AX.X, op=ALU.max), pz.b, [mx4b])
                nm4, nm4b = newsmall(4)
                add("dve", lambda e, mx4=mx4, nm4=nm4: e.tensor_scalar(out=nm4, in0=mx4, scalar1=-scale, scalar2=None, op0=ALU.mult),
                    [mx4b], [nm4b])
                for h in range(4):
                    add("act", lambda e, h=h, pz=pz, nm4=nm4: e.activation(out=ef.t[:, h, :], in_=pz.t[:, h * P:(h + 1) * P], func=AF.Exp,
                                                                          bias=nm4[:, h:h + 1], scale=scale), pz.b + [nm4b], ef.b)
                m01 = cmp01.t[:, :].unsqueeze(1).to_broadcast([P, 4, P])
                add("dve", lambda e, m01=m01: e.tensor_tensor(out=ef.t, in0=ef.t, in1=m01, op=ALU.mult), ef.b + cmp01.b, ef.b)
                l4, l4b = newsmall(4)
                add("dve", lambda e, l4=l4: e.tensor_reduce(out=l4, in_=ef.t, axis=AX.X, op=ALU.add), ef.b, [l4b])
                r4, r4b = newsmall(4)
                add("dve", lambda e, l4=l4, r4=r4: e.tensor_scalar(out=r4, in0=l4, scalar1=1e-30, scalar2=None, op0=ALU.max), [l4b], [r4b])
                r4i, r4ib = newsmall(4)
                add("dve", lambda e, r4=r4, r4i=r4i: e.reciprocal(r4i, r4), [r4b], [r4ib])
                add("dve", lambda e, r4i=r4i: e.tensor_tensor(out=pcf.t, in0=ef.t, in1=r4i.unsqueeze(2).to_broadcast([P, 4, P]), op=ALU.mult),
                    ef.b + [r4ib], pcf.b)
                if NSA_STOP <= 2.1:
                    continue
                add("pool", lambda e: e.tensor_copy(pcb.t, pcf.t), pcf.b, pcb.b)
                transpose_to(pcT_b.t, pcT_b.b, [pcb.t[:, h, :] for h in range(4)], pcb.b, evac="act")
                if NSA_STOP <= 2.2:
                    continue
                add("dve", lambda e: e.tensor_reduce(out=ps4.t, in_=pcf.t.rearrange("p h n -> p n h"), axis=AX.X, op=ALU.add),
                    pcf.b, ps4.b)
                ps4v = ps4.t.rearrange("p (s j) -> p s j", j=4)
                add("dve", lambda e, ps4v=ps4v: e.tensor_reduce(out=impA.t, in_=ps4v, axis=AX.X, op=ALU.add), ps4.b, impA.b)
                v3 = ps4v[:, :, 3]
                add("dve", lambda e, v3=v3: e.scalar_tensor_tensor(out=imp.t, in0=v3, scalar=-0.5, in1=impA.t, op0=ALU.mult, op1=ALU.add),
                    ps4.b + impA.b, imp.b)
                add("dve", lambda e, v3=v3: e.scalar_tensor_tensor(out=imp.t[:, 1:32], in0=v3[:, 0:31], scalar=0.5, in1=imp.t[:, 1:32],
                                                                   op0=ALU.mult, op1=ALU.add), ps4.b + imp.b, imp.b)
                if NSA_STOP <= 2.25:
                    continue
                add("dve", lambda e, i=i, selb=selb: e.tensor_tensor(out=sc.t, in0=imp.t, in1=selb.t[:, :], op=ALU.max),
                    imp.b + selb.b, sc.b)
                add("dve", lambda e, i=i, selv=selv: e.tensor_tensor(out=sc.t, in0=sc.t, in1=selv.t[:, :], op=ALU.add), sc.b + selv.b, sc.b)
                if NSA_STOP <= 2.3:
                    continue
                add("dve", lambda e: e.max(out=m8a.t, in_=sc.t), sc.b, m8a.b)
                add("dve", lambda e: e.match_replace(out=sc2.t, in_to_replace=m8a.t, in_values=sc.t, imm_value=-3.0e38),
                    sc.b + m8a.b, sc2.b)
                add("dve", lambda e: e.max(out=m8b.t, in_=sc2.t), sc2.b, m8b.b)
                add("dve", lambda e: e.tensor_scalar(out=sc2.t, in0=sc.t, scalar1=m8b.t[:, 7:8], scalar2=1.0, op0=ALU.is_ge,
                                                     op1=ALU.subtract), sc.b + m8b.b, sc2.b)
                add("dve", lambda e: e.tensor_scalar(out=sbias.t, in0=sc2.t, scalar1=-NEG, scalar2=None, op0=ALU.mult), sc2.b, sbias.b)
                if NSA_STOP <= 2.4:
                    continue
                transpose_to(selT.t[0:8, 0:nch, :], selT.b, [sbias.t[:, c * 8:(c + 1) * 8] for c in range(nch)], sbias.b, evac="dve")
                poc = ps_o[rr("pso", 2)]

                def f(e, poc=poc):
                    ins = None
                    for h in range(4):
                        ins = e.matmul(poc.t[:, h * P:(h + 1) * P], pcT_b.t[:, h, :], vc.t[:, :], start=True, stop=True)
                    return ins
                add("pe", f, pcT_b.b + vc.b, poc.b)
                g0 = bgate.t[:, i, :].rearrange("p (h t) -> p h t", t=3)[:, :, 0:1].to_broadcast([P, 4, P])
                add("dve", lambda e, poc=poc, g0=g0: e.tensor_tensor(out=ocmp.t, in0=poc.t[:, :].rearrange("p (h n) -> p h n", n=P), in1=g0,
                                                                     op=ALU.mult), poc.b + bgate.b, ocmp.b)
                rows = []
                for h in range(4):
                    rows.append(nsa_sel_row(i, h, nk, nkb, nch))
                    rows.append(nsa_win_row(i, h))
                emit_rows(rows)
            pg.barrier()

        def post_phase(l, final, xsrc, ydst):
            ar.reset()
            gbc = ar.alloc([D], F32)
            xin = [ar.alloc([D], F32) for _ in range(2)]
            ytmp = ar.alloc([D], F32)
            junk = ar.alloc([D], BF16)
            for kc in range(16):
                ws = wst[rr("wst", 2)]
                wsv = ws.t[:, :, :].rearrange("p a b -> p (a b)")
                dma(wsv, w_out[l, kc * P:(kc + 1) * P, :], [], ws.b)
                eng = "pool" if kc % 2 == 0 else "dve"
                add(eng, lambda e, kc=kc, wsv=wsv: e.tensor_copy(hTa.t[:, kc, :], wsv), ws.b, [hTa.b[kc]])
            dma(gbc.t, post_g[l, :].partition_broadcast(P), [], gbc.b)
            for i in range(NQ):
                def f(e, i=i):
                    ins = None
                    for n0 in range(4):
                        for kc in range(16):
                            ins = e.matmul(ps_s.t[:, n0 * 512:(n0 + 1) * 512], mixT.t[:, kc, i * P:(i + 1) * P],
                                           hTa.t[:, kc, n0 * 512:(n0 + 1) * 512], start=(kc == 0), stop=(kc == 15))
                    return ins
                add("pe", f, mixT.b + hTa.b, ps_s.b)
                xt = xin[rr("xin", 2)]
                sap, sbufs = xsrc("q", i)
                dma(xt.t, sap, sbufs, xt.b)
                ss, ssb = newsmall()
                add("act", lambda e, ss=ss: e.activation(out=junk.t, in_=ps_s.t[:, :], func=AF.Square, accum_out=ss),
                    ps_s.b, junk.b + [ssb])
                rs, rsb = rms_scale(ss, ssb, D)
                add("dve", lambda e, rs=rs: e.scalar_tensor_tensor(out=ytmp.t, in0=ps_s.t[:, :], scalar=rs, in1=gbc.t,
                                                                   op0=ALU.mult, op1=ALU.mult),
                    ps_s.b + [rsb] + gbc.b, ytmp.b)
                add("pool", lambda e, xt=xt: e.tensor_tensor(out=ytmp.t, in0=ytmp.t, in1=xt.t, op=ALU.add),
                    ytmp.b + xt.b, ytmp.b)
                if ydst is None:
                    final.append(dma(yout[i * P:(i + 1) * P, :], ytmp.t, ytmp.b, []))
                else:
                    dap, dbufs = ydst(i)
                    dma(dap, ytmp.t, ytmp.b, dbufs)
                    if i % 2 == 1:
                        j = i // 2
                        add_cc(j)
            pg.barrier()

        final = []
        x1b = [Buf() for _ in range(4)]
        gab = [Buf() for _ in range(4)]

        def add_cc(j):
            pg.add("pool", lambda e: e.collective_compute("AllGather", ALU.bypass,
                                                          replica_groups=[[0, 1], [2, 3], [4, 5], [6, 7]],
                                                          ins=[x1own_t[j].ap().opt()], outs=[gath_t[j].ap().opt()]),
                   [x1b[j]], [gab[j]], dma="cc")

        def xsrc_in(which, t):
            if which == "a":
                return xa[t * P:(t + 1) * P, :], []
            return xq[t * P:(t + 1) * P, :], []

        def xsrc_mid(which, t):
            if which == "a":
                r_, i_ = t % 2, t // 2
                j_, k_ = i_ // 2, i_ % 2
                return gath_t[j_].ap()[r_ * 2 * P + k_ * P:r_ * 2 * P + (k_ + 1) * P, :], [gab[j_]]
            j_, k_ = t // 2, t % 2
            return x1own_t[j_].ap()[k_ * P:(k_ + 1) * P, :], [x1b[j_]]

        def ydst_mid(i):
            j_, k_ = i // 2, i % 2
            return x1own_t[j_].ap()[k_ * P:(k_ + 1) * P, :], [x1b[j_]]

        for li, l in enumerate(layers):
            xsrc = xsrc_in if li == 0 else xsrc_mid
            ydst = None if li == len(layers) - 1 else ydst_mid
            phase_norm(l, xsrc)
            for c in range(16):
                mname = MIXERS[c // 4]
                if mname not in mixers:
                    add("pool", lambda e, c=c: e.memset(mixT.t[:, c, :], 0.0), [], [mixT.b[c]])
            if "sb" in mixers:
                mixer_sb(l)
            if "nsa" in mixers:
                mixer_nsa(l)
            if "fox" in mixers:
                mixer_fox(l)
            if "mla" in mixers:
                mixer_mla(l)
            post_phase(l, final, xsrc, ydst)

        with nc.Block() as block:
            pg.emit(block, final)
    return nc


def _consts(r):
    bf = ml_dtypes.bfloat16
    c = {}
    c["c_ident_bf"] = np.eye(P, dtype=np.float32).astype(bf)
    c["c_ident_f"] = np.eye(P, dtype=np.float32)
    p = np.arange(P)[:, None]
    col = np.arange(256)[None, :]
    c["c_mask_c"] = np.where(col <= p + 128 * r, 0.0, NEG).astype(np.float32).astype(bf)
    c["c_mask_s"] = np.where(col < p + 128 * r, 0.0, NEG).astype(np.float32).astype(bf)
    colw = np.arange(768)[None, :]
    c["c_mask_w"] = np.where((colw <= 512 + 128 * r + p) & (colw > 128 * r + p), 0.0, NEG).astype(np.float32).astype(bf)
    qpos = (np.arange(NQ)[None, :] * 2 + r) * P + np.arange(P)[:, None]
    cmp_end = np.arange(P) * 16 + 31
    vis = (cmp_end[None, None, :] <= qpos[:, :, None]) & (np.arange(P)[None, None, :] < NCMP)
    c["c_mask_cmp"] = np.where(vis, 0.0, NEG).astype(np.float32).astype(bf)
    c["c_cmp01"] = vis.astype(np.float32)
    sel = np.arange(32)[None, None, :]
    cur = (qpos // 64)[:, :, None]
    forced = (sel == 0) | (sel == cur) | (sel == cur - 1)
    valid = sel <= cur
    c["c_selbias"] = np.where(forced, 1e6, 0.0).astype(np.float32)
    c["c_selvalid"] = np.where(valid, 0.0, -1e30).astype(np.float32)
    cmp_start = np.arange(NCMP) * 16
    sel_start = np.arange(32) * 64
    ov = np.clip(np.minimum(cmp_start[:, None] + 32, sel_start[None, :] + 64)
                 - np.maximum(cmp_start[:, None], sel_start[None, :]), 0, None)
    c2s = np.zeros((P, 32), np.float32)
    c2s[:NCMP] = (ov / 32).astype(np.float32)
    c["c_c2s"] = c2s
    c["c_e8"] = (np.arange(512)[None, :] // 64 == np.arange(8)[:, None]).astype(np.float32).astype(bf)

    def tables(pos, half):
        inv = (np.float32(10000.0) ** (-np.arange(half, dtype=np.float32) / np.float32(half))).astype(np.float32)
        ang = pos.astype(np.float32)[..., None] * inv
        return np.cos(ang).astype(np.float32), np.sin(ang).astype(np.float32)
    pos_all = np.arange(NT)[None, :] * P + np.arange(P)[:, None]
    c["c_cosa"], c["c_sina"] = tables(pos_all, 64)
    c["c_cosq"], c["c_sinq"] = tables(qpos, 64)
    c["c_cosc"], c["c_sinc"] = tables(cmp_end, 64)
    c["c_cosa32"], c["c_sina32"] = tables(pos_all, 32)
    c["c_cosq32"], c["c_sinq32"] = tables(qpos, 32)
    sel3 = np.zeros((P, 4, P), np.float32)
    for h in range(4):
        for rep in range(3):
            sel3[rep * 32 + h, h, :] = 1.0
    c["c_sel3"] = sel3.astype(bf)
    return c


_WNAMES = ("pre_norm_g", "post_norm_g", "w_in", "b_in", "w_out", "fox_forget_bias",
           "nsa_cmp_pos_k", "nsa_cmp_w1_k", "nsa_cmp_w2_k", "nsa_cmp_pos_v", "nsa_cmp_w1_v", "nsa_cmp_w2_v",
           "mla_q_norm_g", "mla_w_uq", "mla_kv_norm_g", "mla_w_ukv")


def _own_rows(xb, r):
    return np.ascontiguousarray(xb.reshape(NQ, 2, P, D)[:, r].reshape(NQ * P, D))


def run_layers(x, weights, layers, dbg=None, mixers=MIXERS):
    nc = build(layers, dbg, mixers)
    in_maps = []
    for c in range(8):
        b, r = c // 2, c % 2
        m = {"xa": np.ascontiguousarray(x[b]), "xq": _own_rows(x[b], r)}
        for n in _WNAMES:
            m[n] = weights[n]
        m.update(_consts(r))
        in_maps.append(m)
    res = run_bass_kernel_spmd(nc, in_maps, core_ids=list(range(8)))
    out = np.empty((NB, S, D), np.float32)
    for c in range(8):
        b, r = c // 2, c % 2
        out[b].reshape(NQ, 2, P, D)[:, r] = res.results[c]["y"].reshape(NQ, P, D)
    return out, res


def kernel(**inputs):
    x = np.ascontiguousarray(np.asarray(inputs["x"], dtype=np.float32))
    weights = {n: np.ascontiguousarray(np.asarray(inputs[n], dtype=np.float32)) for n in _WNAMES}
    x, _ = run_layers(x, weights, list(range(DEPTH)))
    return x
```

```python
import numpy as np
import ml_dtypes
from contextlib import ExitStack
import concourse.bass as bass
import concourse.mybir as mybir
from concourse.bass_utils import run_bass_kernel_spmd

F32 = mybir.dt.float32
BF16 = mybir.dt.bfloat16
AF = mybir.ActivationFunctionType
ALU = mybir.AluOpType
AX = mybir.AxisListType

D = 2048
S = 2048
NB = 4
DEPTH = 2
INW = 6992
NT = 16
NQ = 8
P = 128
EPS = 1e-6
NEG = -30000.0
NCMP = 127

OFF = {}
_o = 0
for _n, _w in (("sb_q", 512), ("sb_k", 512), ("sb_v", 512), ("sb_gate", 512),
               ("nsa_q", 512), ("nsa_k_cmp", 128), ("nsa_v_cmp", 128), ("nsa_k_sel", 128),
               ("nsa_v_sel", 128), ("nsa_k_win", 128), ("nsa_v_win", 128), ("nsa_branch", 12),
               ("nsa_gate", 512), ("fox_q", 512), ("fox_k", 512), ("fox_v", 512), ("fox_f", 4),
               ("fox_gate", 512), ("mla_cq", 384), ("mla_ckv", 128), ("mla_k_rope", 64),
               ("mla_gate", 512)):
    OFF[_n] = _o
    _o += _w
assert _o == INW


_ALL_BUFS = []


class Buf:
    __slots__ = ("lw", "rd", "rd_dma")

    def __init__(self):
        self.lw = None
        self.rd = {}
        self.rd_dma = []
        _ALL_BUFS.append(self)


class Op:
    __slots__ = ("eng", "fn", "deps", "signal", "count", "is_dma", "dsem", "dval", "dprev")


class Prog:
    ENGS = ("pe", "act", "dve", "pool", "sp")

    def __init__(self, nc, stack, n_dma_sems=12):
        self.nc = nc
        self.ops = {e: [] for e in self.ENGS}
        self.esem = {e: stack.enter_context(nc.semaphore("es_" + e)) for e in self.ENGS}
        self.dsems = {}
        self.dcount = {}
        self.drr = {}
        for e in ("sp", "pool", "act"):
            self.dsems[e] = [stack.enter_context(nc.semaphore("ds_%s%d" % (e, i))) for i in range(n_dma_sems)]
            self.dcount[e] = [0] * n_dma_sems
            self.drr[e] = 0
        self.dsems["cc"] = [stack.enter_context(nc.semaphore("cc_sem"))]
        self.dcount["cc"] = [0]

    def add(self, eng, fn, reads=(), writes=(), dma=False):
        op = Op()
        op.eng = eng
        op.fn = fn
        op.signal = False
        op.count = 0
        op.is_dma = dma
        op.dsem = None
        op.dval = 0
        op.dprev = 0
        me = (eng, len(self.ops[eng]))
        deps = set()
        for b in reads:
            if b.lw is not None:
                deps.add(b.lw)
        for b in writes:
            if b.lw is not None:
                deps.add(b.lw)
            for e2, i2 in b.rd.items():
                deps.add((e2, i2))
            for d in b.rd_dma:
                deps.add(d)
        needed = []
        for d in deps:
            if d == me:
                continue
            dop = self.ops[d[0]][d[1]]
            if dop.is_dma:
                needed.append(d)
            elif d[0] == eng and eng == "pe":
                continue
            else:
                dop.signal = True
                needed.append(d)
        op.deps = needed
        if dma == "cc":
            op.dsem = ("cc", 0)
            op.dprev = 0
            self.dcount["cc"][0] += 1
            op.dval = self.dcount["cc"][0]
        elif dma:
            k = self.drr[eng]
            self.drr[eng] = (k + 1) % len(self.dsems[eng])
            op.dsem = (eng, k)
            op.dprev = self.dcount[eng][k] * 16
            self.dcount[eng][k] += 1
            op.dval = self.dcount[eng][k] * 16
        self.ops[eng].append(op)
        for b in writes:
            b.lw = me
            b.rd = {}
            b.rd_dma = []
        wset = set(id(b) for b in writes)
        for b in reads:
            if id(b) in wset:
                continue
            if dma:
                b.rd_dma.append(me)
            else:
                b.rd[eng] = me[1]
        return me

    def barrier(self):
        deps = set()
        for b in _ALL_BUFS:
            if b.lw is not None:
                deps.add(b.lw)
            for e2, i2 in b.rd.items():
                deps.add((e2, i2))
            for d in b.rd_dma:
                deps.add(d)
        for e in self.ENGS:
            if self.ops[e]:
                last = (e, len(self.ops[e]) - 1)
                if not self.ops[e][-1].is_dma and self.ops[e][-1].fn is not None:
                    deps.add(last)
        for e in self.ENGS:
            op = Op()
            op.eng = e
            op.fn = None
            op.signal = False
            op.count = 0
            op.is_dma = False
            op.dsem = None
            op.dval = 0
            op.dprev = 0
            mx = {}
            needed = []
            for d in deps:
                dop = self.ops[d[0]][d[1]]
                if dop.is_dma:
                    needed.append(d)
                else:
                    if d[0] == e and e == "pe":
                        continue
                    mx[d[0]] = max(mx.get(d[0], -1), d[1])
            for e2, i2 in mx.items():
                self.ops[e2][i2].signal = True
                needed.append((e2, i2))
            op.deps = needed
            self.ops[e].append(op)
        for b in _ALL_BUFS:
            b.lw = None
            b.rd = {}
            b.rd_dma = []

    def emit(self, block, final_waits):
        nc = self.nc
        for e in self.ENGS:
            c = 0
            for op in self.ops[e]:
                if op.signal and not op.is_dma:
                    c += 1
                    op.count = c
        prog = self

        def run(e, engobj):
            waited = {}
            for op in prog.ops[e]:
                for d in op.deps:
                    dop = prog.ops[d[0]][d[1]]
                    if dop.is_dma:
                        key = ("d",) + dop.dsem
                        val = dop.dval
                        sem = prog.dsems[dop.dsem[0]][dop.dsem[1]]
                    else:
                        key = ("e", d[0])
                        val = dop.count
                        sem = prog.esem[d[0]]
                    if waited.get(key, 0) >= val:
                        continue
                    engobj.wait_ge(sem, val)
                    waited[key] = val
                if op.fn is None:
                    continue
                if op.is_dma:
                    key = ("d",) + op.dsem
                    sem = prog.dsems[op.dsem[0]][op.dsem[1]]
                    if op.dprev > 0 and waited.get(key, 0) < op.dprev:
                        engobj.wait_ge(sem, op.dprev)
                        waited[key] = op.dprev
                    ins = op.fn(engobj)
                    if op.dsem[0] == "cc":
                        ins.then_inc(sem)
                    else:
                        ins.then_inc(sem, 16)
                else:
                    ins = op.fn(engobj)
                    if op.signal:
                        ins.then_inc(prog.esem[e], 1)
            if e == "sp":
                for d in final_waits:
                    dop = prog.ops[d[0]][d[1]]
                    sem = prog.dsems[dop.dsem[0]][dop.dsem[1]]
                    engobj.wait_ge(sem, dop.dval)

        @block.tensor
        def _(eng):
            run("pe", eng)

        @block.scalar
        def _(eng):
            run("act", eng)

        @block.vector
        def _(eng):
            run("dve", eng)

        @block.gpsimd
        def _(eng):
            run("pool", eng)

        @block.sync
        def _(eng):
            run("sp", eng)


class T:
    def __init__(self, t, nbuf=1):
        self.t = t
        self.b = [Buf() for _ in range(nbuf)]


ARENA_BYTES = 50 * 1024
MIXERS = ("sb", "nsa", "fox", "mla")


class Arena:
    def __init__(self, base):
        self.base = base
        self.off = 0

    def reset(self, to=0):
        self.off = to

    def alloc(self, shape, dt, nbuf=1):
        n = 1
        for v in shape:
            n *= v
        nbytes = n * (4 if dt == F32 else 2)
        nbytes = (nbytes + 7) // 8 * 8
        assert self.off + nbytes <= ARENA_BYTES, ("arena overflow", self.off, nbytes)
        ap = self.base[:, self.off // 2:(self.off + nbytes) // 2]
        self.off += nbytes
        if dt == F32:
            ap = ap.bitcast(F32)
        ap = ap[:, 0:n]
        if len(shape) == 2:
            ap = ap.rearrange("p (a b) -> p a b", a=shape[0])
        elif len(shape) == 3:
            ap = ap.rearrange("p (a b c) -> p a b c", a=shape[0], b=shape[1])
        return T(ap, nbuf)


NSA_STOP = 99


def build(layers, dbg=None, mixers=MIXERS):
    del _ALL_BUFS[:]
    nc = bass.Bass("TRN2", target_bir_lowering=False)
    dr = {}

    def din(name, shape, dt=F32):
        dr[name] = nc.dram_tensor(name, list(shape), dt, kind="ExternalInput").ap()
        return dr[name]

    xa = din("xa", [S, D])
    xq = din("xq", [NQ * P, D])
    pre_g = din("pre_norm_g", [DEPTH, D])
    post_g = din("post_norm_g", [DEPTH, D])
    w_in = din("w_in", [DEPTH, D, INW])
    b_in = din("b_in", [DEPTH, INW])
    w_out = din("w_out", [DEPTH, D, D])
    fox_fb = din("fox_forget_bias", [DEPTH, 4])
    pos_kv = [din("nsa_cmp_pos_k", [DEPTH, 32, 128]), din("nsa_cmp_pos_v", [DEPTH, 32, 128])]
    w1_kv = [din("nsa_cmp_w1_k", [DEPTH, 4096, 128]), din("nsa_cmp_w1_v", [DEPTH, 4096, 128])]
    w2_kv = [din("nsa_cmp_w2_k", [DEPTH, 128, 128]), din("nsa_cmp_w2_v", [DEPTH, 128, 128])]
    qng = din("mla_q_norm_g", [DEPTH, 384])
    w_uq = din("mla_w_uq", [DEPTH, 384, 768])
    kvng = din("mla_kv_norm_g", [DEPTH, 128])
    w_ukv = din("mla_w_ukv", [DEPTH, 128, 1024])
    c_ident_bf = din("c_ident_bf", [P, P], BF16)
    c_ident_f = din("c_ident_f", [P, P])
    c_mask_c = din("c_mask_c", [P, 256], BF16)
    c_mask_s = din("c_mask_s", [P, 256], BF16)
    c_mask_w = din("c_mask_w", [P, 768], BF16)
    c_mask_cmp = din("c_mask_cmp", [P, NQ, P], BF16)
    c_cmp01 = din("c_cmp01", [P, NQ, P])
    c_selbias = din("c_selbias", [P, NQ, 32])
    c_selvalid = din("c_selvalid", [P, NQ, 32])
    c_c2s = din("c_c2s", [P, 32])
    c_e8 = din("c_e8", [8, 512], BF16)
    c_cosa = din("c_cosa", [P, NT, 64])
    c_sina = din("c_sina", [P, NT, 64])
    c_cosq = din("c_cosq", [P, NQ, 64])
    c_sinq = din("c_sinq", [P, NQ, 64])
    c_cosc = din("c_cosc", [P, 64])
    c_sinc = din("c_sinc", [P, 64])
    c_cosa32 = din("c_cosa32", [P, NT, 32])
    c_sina32 = din("c_sina32", [P, NT, 32])
    c_cosq32 = din("c_cosq32", [P, NQ, 32])
    c_sinq32 = din("c_sinq32", [P, NQ, 32])
    c_sel3 = din("c_sel3", [P, 4, P], BF16)
    yout = nc.dram_tensor("y", [NQ * P, D], F32, kind="ExternalOutput").ap()
    x1own_t = [nc.dram_tensor("x1own%d" % j, [2 * P, D], F32) for j in range(4)]
    gath_t = [nc.dram_tensor("gath%d" % j, [4 * P, D], F32) for j in range(4)]

    with ExitStack() as st:
        pg = Prog(nc, st)

        def sb(name, shape, dt, nbuf=1):
            return T(st.enter_context(nc.sbuf_tensor(name, list(shape), dt)), nbuf)

        def ps(name, shape, dt, nbuf=1):
            return T(st.enter_context(nc.psum_tensor(name, list(shape), dt)), nbuf)

        hTa = sb("hTa", [P, 16, S], BF16, 16)
        hTq = sb("hTq", [P, 16, NQ * P], BF16, 16)
        mixT = sb("mixT", [P, 16, NQ * P], BF16, 16)
        wst = [sb("wst%d" % i, [P, 16, P], F32) for i in range(2)]
        wbf = [sb("wbf%d" % i, [P, 16, P], BF16) for i in range(2)]
        ident_bf = sb("ident_bf", [P, P], BF16)
        ident_f = sb("ident_f", [P, P], F32)
        mask_c = sb("mask_c", [P, 256], BF16)
        mask_s = sb("mask_s", [P, 256], BF16)
        mask_w = sb("mask_w", [P, 768], BF16)
        small = sb("small", [P, 64], F32, 64)
        small4 = sb("small4", [P, 64], F32, 16)
        bias_fm = [sb("bias_fm%d" % i, [P, 1], F32) for i in range(3)]
        bias_bc = [sb("bias_bc%d" % i, [P, P], F32) for i in range(3)]
        arena_t = st.enter_context(nc.sbuf_tensor("arena", [P, ARENA_BYTES // 2], BF16))
        ar = Arena(arena_t)
        ps_s = ps("ps_s", [P, 2048], F32, 4)
        ps_t = [ps("ps_t%d" % i, [P, 1024], BF16) for i in range(2)]
        ps_o = [ps("ps_o%d" % i, [P, 512], F32) for i in range(2)]

        cnt = {}

        def rr(key, n):
            v = cnt.get(key, 0)
            cnt[key] = v + 1
            return v % n

        def newsmall(w=1):
            if w == 1:
                i = rr("sm", 64)
                return small.t[:, i:i + 1], small.b[i]
            i = rr("sm4", 16)
            return small4.t[:, i * 4:i * 4 + 4], small4.b[i]

        def view(tobj, ap):
            r = T(ap, 0)
            r.b = tobj.b
            return r

        def dma(out_ap, in_ap, reads, writes, q="sp"):
            return pg.add(q, lambda e: e.dma_start(out=out_ap, in_=in_ap), reads, writes, dma=True)

        for tile_, src_ in ((ident_bf, c_ident_bf), (ident_f, c_ident_f), (mask_c, c_mask_c),
                            (mask_s, c_mask_s), (mask_w, c_mask_w)):
            dma(tile_.t[:], src_, [], tile_.b)

        def add(eng, fn, reads, writes):
            return pg.add(eng, fn, reads, writes)

        def transpose_to(dst, dst_bufs, src_aps, src_bufs, evac="act", f32=False):
            n = len(src_aps)
            w = src_aps[0].shape[-1]
            rows = src_aps[0].shape[0]
            if f32:
                k = rr("pso", 2)
                pt = ps_o[k]
                idt = ident_f
                assert n <= 4
            else:
                k = rr("pst", 2)
                pt = ps_t[k]
                idt = ident_bf

            def f(e):
                ins = None
                for j, a in enumerate(src_aps):
                    ins = e.transpose(pt.t[0:w, j * P:j * P + rows], a, idt.t[0:rows, 0:rows])
                return ins
            add("pe", f, list(src_bufs) + idt.b, pt.b)
            if rows == P:
                src = pt.t[0:w, 0:n * P]
                if len(dst.shape) == 3:
                    src = src.rearrange("p (a b) -> p a b", b=P)
            elif n == 1:
                src = pt.t[0:w, 0:rows]
            else:
                src = pt.t[0:w, 0:n * P].rearrange("p (a b) -> p a b", b=P)[:, :, 0:rows]
            if evac == "act":
                add("act", lambda e: e.copy(dst, src), pt.b, dst_bufs)
            else:
                add("dve", lambda e: e.tensor_copy(dst, src), pt.b, dst_bufs)

        def load_w(src, nkc, ncols):
            s = rr("wst", 2)
            k = rr("wbf", 2)
            ws, wb = wst[s], wbf[k]
            wsv = ws.t[:, :, :].rearrange("p a b -> p (a b)")[:, 0:nkc * ncols].rearrange("p (a b) -> p a b", b=ncols)
            wbv = wb.t[:, :, :].rearrange("p a b -> p (a b)")[:, 0:nkc * ncols].rearrange("p (a b) -> p a b", b=ncols)
            dma(wsv, src.rearrange("(kc p) c -> p kc c", p=P), [], ws.b)
            if nkc >= 2:
                hk = nkc // 2
                add("dve", lambda e: e.tensor_copy(wbv[:, 0:hk, :], wsv[:, 0:hk, :]), ws.b, wb.b)
                add("act", lambda e: e.copy(wbv[:, hk:nkc, :], wsv[:, hk:nkc, :]), ws.b, wb.b)
            else:
                add("dve", lambda e: e.tensor_copy(wbv, wsv), ws.b, wb.b)
            return wbv, wb.b

        def load_bias_fm(l, c0, ncols):
            k = rr("bfm", 3)
            bt = bias_fm[k]
            dma(bt.t[0:ncols, :], b_in[l, c0:c0 + ncols].rearrange("(c o) -> c o", o=1), [], bt.b)
            return bt

        def load_bias_bc(l, c0, ncols):
            k = rr("bbc", 3)
            bt = bias_bc[k]
            dma(bt.t[:, 0:ncols], b_in[l, c0:c0 + ncols].partition_broadcast(P), [], bt.b)
            return bt

        def proj_fm(l, c0, ncols, src, ntok, dst_fn, dst_bufs):
            wbv, wbb = load_w(w_in[l, :, c0:c0 + ncols], 16, ncols)
            bt = load_bias_fm(l, c0, ncols)
            for t0 in range(0, ntok, 512):
                po = ps_o[rr("pso", 2)]

                def f(e, t0=t0, po=po):
                    ins = None
                    for kc in range(16):
                        ins = e.matmul(po.t[0:ncols, :], wbv[:, kc, :], src.t[:, kc, t0:t0 + 512],
                                       start=(kc == 0), stop=(kc == 15))
                    return ins
                add("pe", f, wbb + src.b, po.b)
                dst = dst_fn(t0, 512)
                add("act", lambda e, dst=dst, po=po: e.activation(out=dst, in_=po.t[0:ncols, :], func=AF.Identity,
                                                                   bias=bt.t[0:ncols, :], scale=1.0),
                    po.b + bt.b, dst_bufs)

        def proj_tm(l, c0, ncols, src, nblk, dst_fn, dst_bufs, blk0=0, w=None):
            if w is None:
                wbv, wbb = load_w(w_in[l, :, c0:c0 + ncols], 16, ncols)
                bt = load_bias_bc(l, c0, ncols)
            else:
                wbv, wbb, bt = w
            for b0 in range(blk0, blk0 + nblk, 4):
                po = ps_o[rr("pso", 2)]

                def f(e, b0=b0, po=po):
                    ins = None
                    for j in range(4):
                        for kc in range(16):
                            ins = e.matmul(po.t[:, j * P:j * P + ncols], src.t[:, kc, (b0 + j) * P:(b0 + j + 1) * P],
                                           wbv[:, kc, :], start=(kc == 0), stop=(kc == 15))
                    return ins
                add("pe", f, wbb + src.b, po.b)
                dst = dst_fn(b0, 4)
                pin = po.t[:, :].rearrange("p (j c) -> p j c", c=P)[:, :, 0:ncols]
                bb = bt.t[:, 0:ncols].unsqueeze(1).to_broadcast([P, 4, ncols])
                add("dve", lambda e, dst=dst, pin=pin, bb=bb: e.tensor_tensor(out=dst, in0=pin, in1=bb, op=ALU.add),
                    po.b + bt.b, dst_bufs)

        def gate_to_mixT(l, c0, ch):
            proj_tm(l, c0, P, hTq, NQ,
                    lambda b0, n: mixT.t[:, ch, b0 * P:(b0 + n) * P].rearrange("p (j c) -> p j c", c=P), [mixT.b[ch]])
            add("act", lambda e: e.activation(out=mixT.t[:, ch, :], in_=mixT.t[:, ch, :], func=AF.Silu),
                [mixT.b[ch]], [mixT.b[ch]])

        def rope_tm(x, xb, cos, sin, tb, out, ob, half, tmp):
            x1, x2 = x[:, :, 0:half], x[:, :, half:2 * half]
            o1, o2 = out[:, :, 0:half], out[:, :, half:2 * half]
            t1, t2 = tmp
            add("dve", lambda e: e.tensor_tensor(out=t1.t, in0=x1, in1=cos, op=ALU.mult), xb + tb, t1.b)
            add("pool", lambda e: e.tensor_tensor(out=t2.t, in0=x2, in1=sin, op=ALU.mult), xb + tb, t2.b)
            add("dve", lambda e: e.tensor_tensor(out=o1, in0=t1.t, in1=t2.t, op=ALU.subtract), t1.b + t2.b, ob)
            add("dve", lambda e: e.tensor_tensor(out=t1.t, in0=x2, in1=cos, op=ALU.mult), xb + tb + ob, t1.b)
            add("pool", lambda e: e.tensor_tensor(out=t2.t, in0=x1, in1=sin, op=ALU.mult), xb + tb + ob, t2.b)
            add("dve", lambda e: e.tensor_tensor(out=o2, in0=t1.t, in1=t2.t, op=ALU.add), t1.b + t2.b, ob)

        def rms_scale(ss, ssb, n):
            ms, msb = newsmall()
            add("dve", lambda e: e.tensor_scalar(out=ms, in0=ss, scalar1=1.0 / n, scalar2=EPS, op0=ALU.mult, op1=ALU.add),
                [ssb], [msb])
            sd, sdb = newsmall()
            add("act", lambda e: e.sqrt(sd, ms), [msb], [sdb])
            rs, rsb = newsmall()
            add("dve", lambda e: e.reciprocal(rs, sd), [sdb], [rsb])
            return rs, rsb

        bank_cur = [0]

        def alloc_banks(nch):
            if bank_cur[0] + nch > 4:
                bank_cur[0] = 0
            b0 = bank_cur[0]
            bank_cur[0] = (b0 + nch) % 4
            return b0

        def softmax_row(nk, scale, pb, base=0):
            nch = (nk + 511) // 512
            o0 = base * 512
            bufs = ps_s.b[base:base + nch]
            mx, mxb = newsmall()
            add("dve", lambda e: e.reduce_max(out=mx, in_=ps_s.t[:, o0:o0 + nk], axis=AX.X), bufs, [mxb])
            nm, nmb = newsmall()
            add("dve", lambda e: e.tensor_scalar(out=nm, in0=mx, scalar1=-scale, scalar2=None, op0=ALU.mult), [mxb], [nmb])
            l1, l1b = newsmall()
            add("act", lambda e: e.activation(out=pb.t[:, 0:nk], in_=ps_s.t[:, o0:o0 + nk], func=AF.Exp, bias=nm, scale=scale,
                                              accum_out=l1), bufs + [nmb], pb.b + [l1b])
            ri, rib = newsmall()
            add("dve", lambda e: e.reciprocal(ri, l1), [l1b], [rib])
            return ri, rib

        def pv_accumulate(nkb, pb, vt, po, pTs, kb_off=0):
            for g0 in range(0, nkb, 8):
                gn = min(8, nkb - g0)
                ptile = pTs[rr("pT", 2)]
                transpose_to(ptile.t[:, 0:gn, :], ptile.b,
                             [pb.t[:, (g0 + j) * P:(g0 + j + 1) * P] for j in range(gn)], pb.b,
                             evac="act" if (g0 // 8) % 2 == 0 else "dve")

                def f(e, g0=g0, gn=gn, ptile=ptile):
                    ins = None
                    for j in range(gn):
                        kb = g0 + j
                        ins = e.matmul(po.t[:, 0:P], ptile.t[:, j, :], vt.t[:, kb_off + kb, :],
                                       start=(kb == 0), stop=(kb == nkb - 1))
                    return ins
                add("pe", f, ptile.b + vt.b, po.b)

        def finish_head(i, src_ap, src_bufs, rinv, rinvb, ch, mts):
            mt = mts[rr("mixtm", 2)]
            gate = mixT.t[:, ch, i * P:(i + 1) * P]
            if rinv is not None:
                add("dve", lambda e: e.scalar_tensor_tensor(out=mt.t, in0=src_ap, scalar=rinv, in1=gate,
                                                            op0=ALU.mult, op1=ALU.mult),
                    src_bufs + [rinvb, mixT.b[ch]], mt.b)
            else:
                add("dve", lambda e: e.tensor_tensor(out=mt.t, in0=src_ap, in1=gate, op=ALU.mult),
                    src_bufs + [mixT.b[ch]], mt.b)
            transpose_to(mixT.t[:, ch, i * P:(i + 1) * P], [mixT.b[ch]], [mt.t], mt.b, evac="act")

        def alloc_attn_common():
            pbs = [ar.alloc([S], BF16) for _ in range(2)]
            pTs = [ar.alloc([8, P], BF16) for _ in range(2)]
            mts = [ar.alloc([P], BF16) for _ in range(2)]
            return pbs, pTs, mts

        def alloc_head_ops(n=2):
            return [(ar.alloc([S], BF16), ar.alloc([NT, P], BF16), ar.alloc([NQ * P], BF16)) for _ in range(n)]

        def phase_norm(l, xsrc):
            ar.reset()
            gbc = ar.alloc([D], F32)
            xin = [ar.alloc([D], F32) for _ in range(2)]
            hn = ar.alloc([D], BF16)
            dma(gbc.t, pre_g[l, :].partition_broadcast(P), [], gbc.b)
            for (which, nblk, dst) in (("a", NT, hTa), ("q", NQ, hTq)):
                for t in range(nblk):
                    xt = xin[rr("xin", 2)]
                    sap, sbufs = xsrc(which, t)
                    dma(xt.t, sap, sbufs, xt.b)
                    ss, ssb = newsmall()
                    add("act", lambda e, xt=xt, ss=ss: e.activation(out=hn.t, in_=xt.t, func=AF.Square, accum_out=ss),
                        xt.b, hn.b + [ssb])
                    rs, rsb = rms_scale(ss, ssb, D)
                    add("dve", lambda e, xt=xt, rs=rs: e.scalar_tensor_tensor(out=hn.t, in0=xt.t, scalar=rs, in1=gbc.t,
                                                                              op0=ALU.mult, op1=ALU.mult),
                        xt.b + [rsb] + gbc.b, hn.b)
                    for half in range(2):
                        c0 = half * 8
                        transpose_to(dst.t[:, c0:c0 + 8, t * P:(t + 1) * P], dst.b[c0:c0 + 8],
                                     [hn.t[:, (c0 + j) * P:(c0 + j + 1) * P] for j in range(8)], hn.b,
                                     evac="act" if half == 0 else "dve")
            pg.barrier()

        def peek_banks(nch):
            b0 = bank_cur[0] if bank_cur[0] + nch <= 4 else 0
            return set(range(b0, b0 + nch))

        def emit_rows(rows, after=None):
            cur_banks = set()
            done_a = [False] * len(rows)
            if rows:
                cur_banks = peek_banks(rows[0][2])
                rows[0][0]()
                done_a[0] = True
            for k in range(len(rows)):
                nxt_banks = set()
                if k + 1 < len(rows):
                    nxt_banks = peek_banks(rows[k + 1][2])
                    if not (nxt_banks & cur_banks):
                        rows[k + 1][0]()
                        done_a[k + 1] = True
                rows[k][1]()
                if k + 1 < len(rows) and not done_a[k + 1]:
                    nxt_banks = peek_banks(rows[k + 1][2])
                    rows[k + 1][0]()
                    done_a[k + 1] = True
                cur_banks = nxt_banks
                if after is not None:
                    after(k)

        def run_pipelined(nheads, proj_items, att_rows):
            for t in proj_items(0):
                t()
            for h in range(nheads):
                nxt = proj_items(h + 1) if h + 1 < nheads else []
                rows = att_rows(h)
                st_ = {"k": 0}

                def after(ai, nxt=nxt, rows=rows, st_=st_):
                    want = (ai + 1) * len(nxt) // len(rows)
                    while st_["k"] < want:
                        nxt[st_["k"]]()
                        st_["k"] += 1
                emit_rows(rows, after)
                while st_["k"] < len(nxt):
                    nxt[st_["k"]]()
                    st_["k"] += 1

        def mixer_sb(l):
            ar.reset()
            scale = 128 ** -0.5
            pbs, pTs, mts = alloc_attn_common()
            hops = alloc_head_ops(2)
            wk_sets = [[ar.alloc([512], F32) for _ in range(4)] for _ in range(2)]
            def proj_items(h):
                kTt, vt, qTt = hops[h % 2]
                return [
                    lambda: proj_fm(l, OFF["sb_q"] + h * P, P, hTq, NQ * P, lambda t0, n: qTt.t[:, t0:t0 + n], qTt.b),
                    lambda: proj_fm(l, OFF["sb_k"] + h * P, P, hTa, S, lambda t0, n: kTt.t[:, t0:t0 + n], kTt.b),
                    lambda: proj_tm(l, OFF["sb_v"] + h * P, P, hTa, NT, lambda b0, n: vt.t[:, b0:b0 + n, :], vt.b),
                    lambda: gate_to_mixT(l, OFF["sb_gate"] + h * P, 0 + h),
                ]

            def sb_steps(h, i):
                kTt, vt, qTt = hops[h % 2]
                nkb = 2 * i + 2
                nk = nkb * P
                nch = (nk + 511) // 512
                pb = pbs[rr("pbf", 2)]
                st_ = {"carry": None}
                steps = []
                for c in range(nch - 1, -1, -1):
                    k0 = c * 512
                    n = min(512, nk - k0)
                    last = (c == nch - 1)
                    loc = {}

                    def s1(c=c, k0=k0, n=n, last=last, loc=loc):
                        bank = ps_s.b[c]
                        wk_e, wk_sp, wk_c, wk_t = wk_sets[rr("wk", 2)]
                        loc["wk_t"] = wk_t

                        def f(e):
                            ins = e.matmul(ps_s.t[:, k0:k0 + n], qTt.t[:, i * P:(i + 1) * P], kTt.t[:, k0:k0 + n],
                                           start=True, stop=not last)
                            if last:
                                ins = e.matmul(ps_s.t[:, nk - 256:nk], ident_bf.t[:], mask_s.t[:, 0:256], start=False, stop=True)
                            return ins
                        add("pe", f, qTt.b + kTt.b + ident_bf.b + mask_s.b, [bank])
                        sv = ps_s.t[:, k0:k0 + n]
                        add("act", lambda e: e.activation(out=wk_e.t[:, 0:n], in_=sv, func=AF.Exp, scale=scale), [bank], wk_e.b)
                        add("act", lambda e: e.activation(out=wk_sp.t[:, 0:n], in_=wk_e.t[:, 0:n], func=AF.Ln, bias=1.0, scale=1.0),
                            wk_e.b, wk_sp.b)
                        add("dve", lambda e: e.tensor_tensor_scan(out=wk_c.t[:, 0:n], data0=wk_sp.t[:, 0:n], data1=wk_sp.t[:, 0:n],
                                                                  initial=0.0, op0=ALU.add, op1=ALU.max), wk_sp.b, wk_c.b)
                        add("dve", lambda e: e.scalar_tensor_tensor(out=wk_t.t[:, 0:n], in0=sv, scalar=scale, in1=wk_sp.t[:, 0:n],
                                                                    op0=ALU.mult, op1=ALU.subtract), [bank] + wk_sp.b, wk_t.b)
                        add("pool", lambda e: e.tensor_tensor(out=wk_t.t[:, 0:n], in0=wk_t.t[:, 0:n], in1=wk_c.t[:, 0:n], op=ALU.add),
                            wk_t.b + wk_c.b, wk_t.b)
                        nb_, nbb = newsmall()
                        tot = wk_c.t[:, n - 1:n]
                        if st_["carry"] is None:
                            add("dve", lambda e: e.tensor_scalar(out=nb_, in0=tot, scalar1=-1.0, scalar2=None, op0=ALU.mult),
                                wk_c.b, [nbb])
                        else:
                            cpr, cprb = st_["carry"]
                            add("dve", lambda e: e.scalar_tensor_tensor(out=nb_, in0=tot, scalar=-1.0, in1=cpr, op0=ALU.mult,
                                                                        op1=ALU.add), wk_c.b + [cprb], [nbb])
                        st_["carry"] = (nb_, nbb)
                        loc["nb"] = (nb_, nbb)

                    def s2(k0=k0, n=n, loc=loc):
                        wk_t = loc["wk_t"]
                        nb_, nbb = loc["nb"]
                        add("act", lambda e: e.activation(out=pb.t[:, k0:k0 + n], in_=wk_t.t[:, 0:n], func=AF.Exp, bias=nb_, scale=1.0),
                            wk_t.b + [nbb], pb.b)

                    fin = None
                    if c == 0:
                        def fin():
                            po = ps_o[rr("pso", 2)]
                            pv_accumulate(nkb, pb, vt, po, pTs)
                            finish_head(i, po.t[:, 0:P], po.b, None, None, 0 + h, mts)
                    steps.append((s1, s2, fin))
                return steps

            for t in proj_items(0):
                t()
            for h in range(4):
                nxt = proj_items(h + 1) if h + 1 < 4 else []
                kq = 0
                prev = None
                for i in range(NQ):
                    for stp in sb_steps(h, i):
                        stp[0]()
                        if prev is not None:
                            prev[1]()
                            if prev[2] is not None:
                                prev[2]()
                        prev = stp
                    want = (i + 1) * len(nxt) // NQ
                    while kq < want:
                        nxt[kq]()
                        kq += 1
                prev[1]()
                prev[2]()
                while kq < len(nxt):
                    nxt[kq]()
                    kq += 1
            pg.barrier()

        def softmax_heads_causal(i, qTt, kTt, vt, scale, pbs, pTs, mts, ch, extra=None, extra_bufs=()):
            nkb = 2 * i + 2
            nk = nkb * P
            nch = (nk + 511) // 512
            st_ = {}

            def A():
                base = alloc_banks(nch)
                st_["base"] = base
                o0 = base * 512

                def f(e):
                    ins = None
                    for c in range(nch):
                        k0 = c * 512
                        n = min(512, nk - k0)
                        last = (c == nch - 1)
                        ins = e.matmul(ps_s.t[:, o0 + k0:o0 + k0 + n], qTt.t[:, i * P:(i + 1) * P], kTt.t[:, k0:k0 + n],
                                       start=True, stop=(not last) and extra is None)
                        if extra is not None:
                            ins = extra(e, ps_s.t[:, o0 + k0:o0 + k0 + n], k0, n, not last)
                        if last:
                            ins = e.matmul(ps_s.t[:, o0 + nk - 256:o0 + nk], ident_bf.t[:], mask_c.t[:, 0:256], start=False, stop=True)
                    return ins
                add("pe", f, qTt.b + kTt.b + ident_bf.b + mask_c.b + list(extra_bufs), ps_s.b[base:base + nch])

            def B():
                pb = pbs[rr("pbf", 2)]
                rinv, rinvb = softmax_row(nk, scale, pb, st_["base"])
                po = ps_o[rr("pso", 2)]
                pv_accumulate(nkb, pb, vt, po, pTs)
                finish_head(i, po.t[:, 0:P], po.b, rinv, rinvb, ch, mts)
            return A, B, nch

        def mixer_fox(l):
            ar.reset()
            scale = 128 ** -0.5
            pbs, pTs, mts = alloc_attn_common()
            hops = alloc_head_ops(2)
            crow3 = ar.alloc([S], BF16)
            sel3 = ar.alloc([4, P], BF16)
            t_e = ar.alloc([512], F32)
            t_sp = ar.alloc([512], F32)
            cch = [ar.alloc([512], F32) for _ in range(2)]
            hi_t = ar.alloc([512], BF16)
            mid_t = ar.alloc([512], BF16)
            wf68 = ar.alloc([16, 68], BF16)
            fb = ar.alloc([2], F32)
            NP_ = 68
            dma(sel3.t, c_sel3, [], sel3.b)
            add("pool", lambda e: e.memset(crow3.t, 0.0), [], crow3.b)
            add("pool", lambda e: e.memset(wf68.t, 0.0), [], wf68.b)
            add("pool", lambda e: e.memset(fb.t, 0.0), [], fb.b)
            c0 = OFF["fox_f"]
            wbv, wbb = load_w(w_in[l, :, c0:c0 + 4], 16, 4)
            for rep in range(3):
                add("dve", lambda e, rep=rep: e.tensor_copy(wf68.t[:, :, rep * 32:rep * 32 + 4], wbv), wbb + wf68.b, wf68.b)
                dma(fb.t[rep * 32:rep * 32 + 4, 0:1], b_in[l, c0:c0 + 4].rearrange("(c o) -> c o", o=1), fb.b, fb.b)
                dma(fb.t[rep * 32:rep * 32 + 4, 1:2], fox_fb[l, :].rearrange("(c o) -> c o", o=1), fb.b, fb.b)
            nb_, nbb = newsmall()
            add("dve", lambda e: e.scalar_tensor_tensor(out=nb_[0:NP_, :], in0=fb.t[0:NP_, 0:1], scalar=-1.0, in1=fb.t[0:NP_, 1:2],
                                                        op0=ALU.mult, op1=ALU.subtract), fb.b, [nbb])
            prevc = None
            for ci, t0 in enumerate(range(0, S, 512)):
                po = ps_o[rr("pso", 2)]
                cc = cch[ci % 2]

                def f(e, t0=t0, po=po):
                    ins = None
                    for kc in range(16):
                        ins = e.matmul(po.t[0:NP_, :], wf68.t[:, kc, :], hTa.t[:, kc, t0:t0 + 512], start=(kc == 0), stop=(kc == 15))
                    return ins
                add("pe", f, wf68.b + hTa.b, po.b)
                add("act", lambda e, po=po: e.activation(out=t_e.t[0:NP_, :], in_=po.t[0:NP_, :], func=AF.Exp, bias=nb_[0:NP_, :],
                                                         scale=-1.0), po.b + [nbb], t_e.b)
                add("act", lambda e: e.activation(out=t_sp.t[0:NP_, :], in_=t_e.t[0:NP_, :], func=AF.Ln, bias=1.0, scale=1.0),
                    t_e.b, t_sp.b)
                init = 0.0 if prevc is None else prevc.t[0:NP_, 511:512]
                rdb = t_sp.b + ([] if prevc is None else prevc.b)
                add("dve", lambda e, cc=cc, init=init: e.tensor_tensor_scan(out=cc.t[0:NP_, :], data0=t_sp.t[0:NP_, :],
                                                                            data1=t_sp.t[0:NP_, :], initial=init,
                                                                            op0=ALU.add, op1=ALU.max), rdb, cc.b)
                add("dve", lambda e, cc=cc: e.tensor_scalar(out=t_e.t[0:NP_, :], in0=cc.t[0:NP_, :], scalar1=float(128 ** 0.5),
                                                            scalar2=None, op0=ALU.mult), cc.b, t_e.b)
                add("dve", lambda e: e.tensor_copy(hi_t.t[0:NP_, :], t_e.t[0:NP_, :]), t_e.b, hi_t.b)
                add("dve", lambda e: e.tensor_tensor(out=t_sp.t[0:NP_, :], in0=t_e.t[0:NP_, :], in1=hi_t.t[0:NP_, :], op=ALU.subtract),
                    t_e.b + hi_t.b, t_sp.b)
                add("dve", lambda e: e.tensor_copy(mid_t.t[0:NP_, :], t_sp.t[0:NP_, :]), t_sp.b, mid_t.b)
                add("dve", lambda e: e.tensor_tensor(out=t_sp.t[0:NP_, :], in0=t_sp.t[0:NP_, :], in1=mid_t.t[0:NP_, :], op=ALU.subtract),
                    t_sp.b + mid_t.b, t_sp.b)
                add("dve", lambda e, t0=t0: e.tensor_copy(crow3.t[0:4, t0:t0 + 512], hi_t.t[0:4, :]), hi_t.b, crow3.b)
                add("dve", lambda e, t0=t0: e.tensor_copy(crow3.t[32:36, t0:t0 + 512], mid_t.t[32:36, :]), mid_t.b, crow3.b)
                add("dve", lambda e, t0=t0: e.tensor_copy(crow3.t[64:68, t0:t0 + 512], t_sp.t[64:68, :]), t_sp.b, crow3.b)
                prevc = cc
            def proj_items(h):
                kTt, vt, qTt = hops[h % 2]
                return [
                    lambda: proj_fm(l, OFF["fox_q"] + h * P, P, hTq, NQ * P, lambda t0, n: qTt.t[:, t0:t0 + n], qTt.b),
                    lambda: proj_fm(l, OFF["fox_k"] + h * P, P, hTa, S, lambda t0, n: kTt.t[:, t0:t0 + n], kTt.b),
                    lambda: proj_tm(l, OFF["fox_v"] + h * P, P, hTa, NT, lambda b0, n: vt.t[:, b0:b0 + n, :], vt.b),
                    lambda: gate_to_mixT(l, OFF["fox_gate"] + h * P, 8 + h),
                ]

            def att_items(h):
                kTt, vt, qTt = hops[h % 2]

                def extra(e, out, k0, n, stop):
                    return e.matmul(out, sel3.t[:, h, :], crow3.t[:, k0:k0 + n], start=False, stop=stop)
                return [softmax_heads_causal(i, qTt, kTt, vt, scale, pbs, pTs, mts, 8 + h, extra, sel3.b + crow3.b)
                        for i in range(NQ)]
            run_pipelined(4, proj_items, att_items)
            pg.barrier()

        wout_done = {}

        def load_wout_chunk(l, kc):
            if (l, kc) in wout_done:
                return
            wout_done[(l, kc)] = True
            ws = wst[rr("wst", 2)]
            wsv = ws.t[:, :, :].rearrange("p a b -> p (a b)")
            dma(wsv, w_out[l, kc * P:(kc + 1) * P, :], [], ws.b)
            eng = "pool" if kc % 2 == 0 else "dve"
            add(eng, lambda e: e.tensor_copy(hTa.t[:, kc, :], wsv), ws.b, [hTa.b[kc]])

        def mixer_mla(l):
            ar.reset()
            scale = 192 ** -0.5
            cqnT = ar.alloc([3, NQ * P], BF16)
            ckvnT = ar.alloc([S], BF16)
            krT = ar.alloc([S], BF16)
            wukv_b = ar.alloc([1024], BF16)
            cosq = ar.alloc([NQ, 32], F32)
            sinq = ar.alloc([NQ, 32], F32)
            mark = ar.off
            dma(cosq.t, c_cosq32, [], cosq.b)
            dma(sinq.t, c_sinq32, [], sinq.b)
            cosa = ar.alloc([NT, 32], F32)
            sina = ar.alloc([NT, 32], F32)
            gq = ar.alloc([384], F32)
            gkv = ar.alloc([P], F32)
            xt = ar.alloc([4, 384], F32)
            xn = ar.alloc([4, 384], BF16)
            t1 = ar.alloc([4, 32], F32)
            t2 = ar.alloc([4, 32], F32)
            dma(cosa.t, c_cosa32, [], cosa.b)
            dma(sina.t, c_sina32, [], sina.b)
            dma(gq.t, qng[l, :].partition_broadcast(P), [], gq.b)
            dma(gkv.t, kvng[l, :].partition_broadcast(P), [], gkv.b)
            ws = wst[rr("wst", 2)]
            wsv = ws.t[:, :, :].rearrange("p a b -> p (a b)")[:, 0:1024]
            dma(wsv, w_ukv[l, :, :], [], ws.b)
            add("pool", lambda e, wsv=wsv: e.tensor_copy(wukv_b.t, wsv), ws.b, wukv_b.b)

            def norm_rows(x_ap, xb, width, g_ap, gb, out_ap, ob):
                ss, ssb = newsmall()
                jt, jb = junkt
                add("act", lambda e: e.activation(out=jt[:, 0:width], in_=x_ap, func=AF.Square, accum_out=ss), xb, jb + [ssb])
                rs, rsb = rms_scale(ss, ssb, width)
                add("dve", lambda e: e.scalar_tensor_tensor(out=out_ap, in0=x_ap, scalar=rs, in1=g_ap, op0=ALU.mult, op1=ALU.mult),
                    xb + [rsb] + gb, ob)
            jk = ar.alloc([384], F32)
            junkt = (jk.t, jk.b)
            for b0 in range(0, NQ, 4):
                for cg in range(3):
                    wbv, wbb = load_w(w_in[l, :, OFF["mla_cq"] + cg * P:OFF["mla_cq"] + (cg + 1) * P], 16, P)
                    bt = load_bias_bc(l, OFF["mla_cq"] + cg * P, P)
                    po = ps_o[rr("pso", 2)]

                    def f(e, b0=b0, po=po, wbv=wbv):
                        ins = None
                        for j in range(4):
                            for kc in range(16):
                                ins = e.matmul(po.t[:, j * P:(j + 1) * P], hTq.t[:, kc, (b0 + j) * P:(b0 + j + 1) * P],
                                               wbv[:, kc, :], start=(kc == 0), stop=(kc == 15))
                        return ins
                    add("pe", f, wbb + hTq.b, po.b)
                    pin = po.t[:, :].rearrange("p (j c) -> p j c", c=P)
                    bb = bt.t[:, :].unsqueeze(1).to_broadcast([P, 4, P])
                    add("dve", lambda e, cg=cg, pin=pin, bb=bb: e.tensor_tensor(out=xt.t[:, :, cg * P:(cg + 1) * P], in0=pin, in1=bb,
                                                                                op=ALU.add), po.b + bt.b, xt.b)
                for j in range(4):
                    norm_rows(xt.t[:, j, :], xt.b, 384, gq.t, gq.b, xn.t[:, j, :], xn.b)
                for cg in range(3):
                    transpose_to(cqnT.t[:, cg, b0 * P:(b0 + 4) * P], cqnT.b,
                                 [xn.t[:, j, cg * P:(cg + 1) * P] for j in range(4)], xn.b, evac="dve")
            xk = ar.alloc([4, P], F32)
            xkn = ar.alloc([4, P], BF16)
            xr = ar.alloc([4, 64], F32)
            xrb = ar.alloc([4, 64], BF16)
            wbv1, wbb1 = load_w(w_in[l, :, OFF["mla_ckv"]:OFF["mla_ckv"] + P], 16, P)
            bt1 = load_bias_bc(l, OFF["mla_ckv"], P)
            wbv2, wbb2 = load_w(w_in[l, :, OFF["mla_k_rope"]:OFF["mla_k_rope"] + 64], 16, 64)
            bt2 = load_bias_bc(l, OFF["mla_k_rope"], 64)
            for b0 in range(0, NT, 4):
                po = ps_o[rr("pso", 2)]

                def f(e, b0=b0, po=po):
                    ins = None
                    for j in range(4):
                        for kc in range(16):
                            ins = e.matmul(po.t[:, j * P:(j + 1) * P], hTa.t[:, kc, (b0 + j) * P:(b0 + j + 1) * P],
                                           wbv1[:, kc, :], start=(kc == 0), stop=(kc == 15))
                    return ins
                add("pe", f, wbb1 + hTa.b, po.b)
                pin = po.t[:, :].rearrange("p (j c) -> p j c", c=P)
                bb = bt1.t[:, :].unsqueeze(1).to_broadcast([P, 4, P])
                add("dve", lambda e, pin=pin, bb=bb: e.tensor_tensor(out=xk.t, in0=pin, in1=bb, op=ALU.add), po.b + bt1.b, xk.b)
                for j in range(4):
                    norm_rows(xk.t[:, j, :], xk.b, P, gkv.t, gkv.b, xkn.t[:, j, :], xkn.b)
                transpose_to(ckvnT.t[:, b0 * P:(b0 + 4) * P], ckvnT.b, [xkn.t[:, j, :] for j in range(4)], xkn.b, evac="dve")
                po2 = ps_o[rr("pso", 2)]

                def f2(e, b0=b0, po2=po2):
                    ins = None
                    for j in range(4):
                        for kc in range(16):
                            ins = e.matmul(po2.t[:, j * 64:(j + 1) * 64], hTa.t[:, kc, (b0 + j) * P:(b0 + j + 1) * P],
                                           wbv2[:, kc, :], start=(kc == 0), stop=(kc == 15))
                    return ins
                add("pe", f2, wbb2 + hTa.b, po2.b)
                pin2 = po2.t[:, 0:256].rearrange("p (j c) -> p j c", c=64)
                bb2 = bt2.t[:, 0:64].unsqueeze(1).to_broadcast([P, 4, 64])
                add("dve", lambda e, pin2=pin2, bb2=bb2: e.tensor_tensor(out=xr.t, in0=pin2, in1=bb2, op=ALU.add), po2.b + bt2.b, xr.b)
                rope_tm(xr.t, xr.b, cosa.t[:, b0:b0 + 4, :], sina.t[:, b0:b0 + 4, :], cosa.b + sina.b, xrb.t, xrb.b, 32, (t1, t2))
                transpose_to(krT.t[0:64, b0 * P:(b0 + 4) * P], krT.b, [xrb.t[:, j, :] for j in range(4)], xrb.b, evac="dve")
            pg.barrier()
            ar.reset(mark)
            pbs, pTs, mts = alloc_attn_common()
            kTt = ar.alloc([S], BF16)
            vt = ar.alloc([NT, P], BF16)
            qTt = ar.alloc([NQ * P], BF16)
            qrT = ar.alloc([NQ * P], BF16)
            wuqh = ar.alloc([3, 192], BF16)
            qr_f = ar.alloc([NQ, 64], F32)
            qr_b = ar.alloc([NQ, 64], BF16)
            t1 = ar.alloc([NQ, 32], F32)
            t2 = ar.alloc([NQ, 32], F32)
            for h in range(4):
                ws = wst[rr("wst", 2)]
                wsv = ws.t[:, :, :].rearrange("p a b -> p (a b)")[:, 0:576].rearrange("p (a b) -> p a b", b=192)
                dma(wsv, w_uq[l, :, h * 192:(h + 1) * 192].rearrange("(kc p) c -> p kc c", p=P), [], ws.b)
                add("pool", lambda e, wsv=wsv: e.tensor_copy(wuqh.t, wsv), ws.b, wuqh.b)
                for t0 in range(0, NQ * P, 512):
                    po = ps_o[rr("pso", 2)]

                    def f(e, t0=t0, po=po):
                        ins = None
                        for kc in range(3):
                            ins = e.matmul(po.t[:, :], wuqh.t[:, kc, 0:P], cqnT.t[:, kc, t0:t0 + 512], start=(kc == 0), stop=(kc == 2))
                        return ins
                    add("pe", f, wuqh.b + cqnT.b, po.b)
                    add("act", lambda e, t0=t0, po=po: e.copy(qTt.t[:, t0:t0 + 512], po.t[:, :]), po.b, qTt.b)
                for b0 in range(0, NQ, 4):
                    po = ps_o[rr("pso", 2)]

                    def f(e, b0=b0, po=po):
                        ins = None
                        for j in range(4):
                            for kc in range(3):
                                ins = e.matmul(po.t[:, j * 64:(j + 1) * 64], cqnT.t[:, kc, (b0 + j) * P:(b0 + j + 1) * P],
                                               wuqh.t[:, kc, P:192], start=(kc == 0), stop=(kc == 2))
                        return ins
                    add("pe", f, wuqh.b + cqnT.b, po.b)
                    add("act", lambda e, b0=b0, po=po: e.copy(qr_f.t[:, b0:b0 + 4, :], po.t[:, 0:256].rearrange("p (j c) -> p j c", c=64)),
                        po.b, qr_f.b)
                rope_tm(qr_f.t, qr_f.b, cosq.t, sinq.t, cosq.b + sinq.b, qr_b.t, qr_b.b, 32, (t1, t2))
                transpose_to(qrT.t[0:64, :], qrT.b, [qr_b.t[:, j, :] for j in range(NQ)], qr_b.b, evac="dve")
                for t0 in range(0, S, 512):
                    po = ps_o[rr("pso", 2)]
                    add("pe", lambda e, t0=t0, po=po, h=h: e.matmul(po.t[:, :], wukv_b.t[:, h * 256:h * 256 + P], ckvnT.t[:, t0:t0 + 512],
                                                                    start=True, stop=True), wukv_b.b + ckvnT.b, po.b)
                    add("act", lambda e, t0=t0, po=po: e.copy(kTt.t[:, t0:t0 + 512], po.t[:, :]), po.b, kTt.b)
                for b0 in range(0, NT, 4):
                    po = ps_o[rr("pso", 2)]

                    def f(e, b0=b0, po=po, h=h):
                        ins = None
                        for j in range(4):
                            ins = e.matmul(po.t[:, j * P:(j + 1) * P], ckvnT.t[:, (b0 + j) * P:(b0 + j + 1) * P],
                                           wukv_b.t[:, h * 256 + P:h * 256 + 256], start=True, stop=True)
                        return ins
                    add("pe", f, wukv_b.b + ckvnT.b, po.b)
                    add("dve", lambda e, b0=b0, po=po: e.tensor_copy(vt.t[:, b0:b0 + 4, :], po.t[:, :].rearrange("p (j c) -> p j c", c=P)),
                        po.b, vt.b)
                gate_to_mixT(l, OFF["mla_gate"] + h * P, 12 + h)

                rows = []
                for i in range(NQ):
                    def extra_i(e, out, k0, n, stop, i=i):
                        return e.matmul(out, qrT.t[0:64, i * P:(i + 1) * P], krT.t[0:64, k0:k0 + n], start=False, stop=stop)
                    rows.append(softmax_heads_causal(i, qTt, kTt, vt, scale, pbs, pTs, mts, 12 + h, extra_i, qrT.b + krT.b))

                def after_row(k, h=h):
                    if k % 2 == 1:
                        load_wout_chunk(l, (h * NQ + k) // 2)
                emit_rows(rows, after_row)
            pg.barrier()

        def mixer_nsa(l):
            ar.reset()
            scale = 128 ** -0.5
            qT4 = ar.alloc([4, NQ * P], BF16)
            ksT = ar.alloc([S], BF16)
            kwT = ar.alloc([S], BF16)
            vs = ar.alloc([NT, P], BF16)
            vw = ar.alloc([NT, P], BF16)
            kcT = ar.alloc([P], BF16)
            vc = ar.alloc([P], BF16)
            bgate = ar.alloc([NQ, 12], F32)
            e8 = ar.alloc([512], BF16)
            c2s = ar.alloc([32], F32)
            mark = ar.off
            dma(e8.t[0:8, :], c_e8, [], e8.b)
            dma(c2s.t, c_c2s, [], c2s.b)
            cos_t = ar.alloc([NT, 64], F32)
            sin_t = ar.alloc([NT, 64], F32)
            xf = ar.alloc([NQ, P], F32)
            xb_ = ar.alloc([NQ, P], BF16)
            t1 = ar.alloc([NQ, 64], F32)
            t2 = ar.alloc([NQ, 64], F32)
            dma(cos_t.t[:, 0:NQ, :], c_cosq, [], cos_t.b)
            dma(sin_t.t[:, 0:NQ, :], c_sinq, [], sin_t.b)
            for h in range(4):
                proj_tm(l, OFF["nsa_q"] + h * P, P, hTq, NQ, lambda b0, n: xf.t[:, b0:b0 + n, :], xf.b)
                rope_tm(xf.t, xf.b, cos_t.t[:, 0:NQ, :], sin_t.t[:, 0:NQ, :], cos_t.b + sin_t.b, xb_.t, xb_.b, 64, (t1, t2))
                transpose_to(qT4.t[:, h, :], qT4.b, [xb_.t[:, j, :] for j in range(NQ)], xb_.b, evac="dve")
            proj_tm(l, OFF["nsa_branch"], 12, hTq, NQ, lambda b0, n: bgate.t[:, b0:b0 + n, :], bgate.b)
            add("act", lambda e: e.activation(out=bgate.t, in_=bgate.t, func=AF.Sigmoid), bgate.b, bgate.b)
            dma(cos_t.t, c_cosa, xf.b + xb_.b + t1.b + t2.b, cos_t.b)
            dma(sin_t.t, c_sina, xf.b + xb_.b + t1.b + t2.b, sin_t.b)
            for (cname, dstT) in (("nsa_k_sel", ksT), ("nsa_k_win", kwT)):
                c0_ = OFF[cname]
                wbv_, wbb_ = load_w(w_in[l, :, c0_:c0_ + P], 16, P)
                bt_ = load_bias_bc(l, c0_, P)
                for g0 in (0, 8):
                    proj_tm(l, c0_, P, hTa, 8, lambda b0, n, g0=g0: xf.t[:, b0 - g0:b0 - g0 + n, :], xf.b, blk0=g0,
                            w=(wbv_, wbb_, bt_))
                    rope_tm(xf.t, xf.b, cos_t.t[:, g0:g0 + 8, :], sin_t.t[:, g0:g0 + 8, :], cos_t.b + sin_t.b, xb_.t, xb_.b, 64,
                            (t1, t2))
                    transpose_to(dstT.t[:, g0 * P:(g0 + 8) * P], dstT.b, [xb_.t[:, j, :] for j in range(8)], xb_.b, evac="dve")
            proj_tm(l, OFF["nsa_v_sel"], P, hTa, NT, lambda b0, n: vs.t[:, b0:b0 + n, :], vs.b)
            proj_tm(l, OFF["nsa_v_win"], P, hTa, NT, lambda b0, n: vw.t[:, b0:b0 + n, :], vw.b)
            pg.barrier()
            if NSA_STOP <= 1:
                return
            ar.reset(mark)
            tokT = ar.alloc([S], BF16)
            blkT = ar.alloc([32, P], BF16)
            w1b = ar.alloc([32, P], BF16)
            w2b = ar.alloc([P], BF16)
            posr = ar.alloc([P], F32)
            posT = ar.alloc([32], F32)
            hidT = ar.alloc([P], BF16)
            kcf = ar.alloc([1, P], F32)
            kcb = ar.alloc([1, P], BF16)
            cosc = ar.alloc([1, 64], F32)
            sinc = ar.alloc([1, 64], F32)
            tc1 = ar.alloc([1, 64], F32)
            tc2 = ar.alloc([1, 64], F32)
            dma(cosc.t[:, 0, :], c_cosc, [], cosc.b)
            dma(sinc.t[:, 0, :], c_sinc, [], sinc.b)
            for which in range(2):
                cname = "nsa_k_cmp" if which == 0 else "nsa_v_cmp"
                proj_fm(l, OFF[cname], P, hTa, S, lambda t0, n: tokT.t[:, t0:t0 + n], tokT.b)
                dma(posr.t[0:32, :], pos_kv[which][l, :, :], [], posr.b)
                po = ps_o[rr("pso", 2)]
                add("pe", lambda e, po=po: e.transpose(po.t[:, 0:32], posr.t[0:32, :], ident_f.t[0:32, 0:32]), posr.b + ident_f.b, po.b)
                add("act", lambda e, po=po: e.copy(posT.t, po.t[:, 0:32]), po.b, posT.b)
                for half in range(2):
                    ws = wst[rr("wst", 2)]
                    dma(ws.t[:, :, :], w1_kv[which][l, half * 2048:(half + 1) * 2048, :].rearrange("(l d) h -> d l h", d=P), [], ws.b)
                    add("pool", lambda e, half=half, ws=ws: e.tensor_copy(w1b.t[:, half * 16:(half + 1) * 16, :], ws.t[:, :, :]),
                        ws.b, w1b.b)
                ws = wst[rr("wst", 2)]
                wsv = ws.t[:, 0, :]
                dma(wsv, w2_kv[which][l, :, :], [], ws.b)
                add("pool", lambda e, wsv=wsv: e.tensor_copy(w2b.t, wsv), ws.b, w2b.b)
                for ll in range(32):
                    src = tokT.t[:, ll:ll + 16 * (NCMP - 1) + 1:16]
                    eng = "dve" if ll % 2 == 0 else "pool"
                    add(eng, lambda e, ll=ll, src=src: e.tensor_scalar(out=blkT.t[:, ll, 0:NCMP], in0=src, scalar1=posT.t[:, ll:ll + 1],
                                                                       scalar2=None, op0=ALU.add), tokT.b + posT.b, blkT.b)
                po = ps_o[rr("pso", 2)]

                def f(e, po=po):
                    ins = None
                    for ll in range(32):
                        ins = e.matmul(po.t[:, 0:NCMP], w1b.t[:, ll, :], blkT.t[:, ll, 0:NCMP], start=(ll == 0), stop=(ll == 31))
                    return ins
                add("pe", f, w1b.b + blkT.b, po.b)
                add("act", lambda e, po=po: e.activation(out=hidT.t[:, 0:NCMP], in_=po.t[:, 0:NCMP], func=AF.Silu), po.b, hidT.b)
                po2 = ps_o[rr("pso", 2)]
                add("pe", lambda e, po2=po2: e.matmul(po2.t[0:NCMP, 0:P], hidT.t[:, 0:NCMP], w2b.t, start=True, stop=True),
                    hidT.b + w2b.b, po2.b)
                if which == 0:
                    add("act", lambda e, po2=po2: e.copy(kcf.t[0:NCMP, 0, :], po2.t[0:NCMP, 0:P]), po2.b, kcf.b)
                    rope_tm(kcf.t[0:NCMP], kcf.b, cosc.t[0:NCMP], sinc.t[0:NCMP], cosc.b + sinc.b, kcb.t[0:NCMP], kcb.b, 64,
                            (view(tc1, tc1.t[0:NCMP]), view(tc2, tc2.t[0:NCMP])))
                    add("pool", lambda e: e.memset(kcT.t, 0.0), [], kcT.b)
                    transpose_to(kcT.t[:, 0:NCMP], kcT.b, [kcb.t[0:NCMP, 0, :]], kcb.b, evac="dve")
                else:
                    add("pool", lambda e: e.memset(vc.t, 0.0), [], vc.b)
                    add("act", lambda e, po2=po2: e.copy(vc.t[0:NCMP, :], po2.t[0:NCMP, 0:P]), po2.b, vc.b)
            for h in range(4):
                gate_to_mixT(l, OFF["nsa_gate"] + h * P, 4 + h)
            pg.barrier()
            if NSA_STOP <= 2:
                return
            ar.reset(mark)
            pbs, pTs, mts = alloc_attn_common()
            mcmp2 = [ar.alloc([P], BF16) for _ in range(2)]
            cmp012 = [ar.alloc([P], F32) for _ in range(2)]
            selb2 = [ar.alloc([32], F32) for _ in range(2)]
            selv2 = [ar.alloc([32], F32) for _ in range(2)]
            ef = ar.alloc([4, P], F32)
            pcf = ef
            pcb = ar.alloc([4, P], BF16)
            ps4 = ar.alloc([P], F32)
            impA = ar.alloc([32], F32)
            imp = ar.alloc([32], F32)
            pcT_b = ar.alloc([4, P], BF16)
            ocmp = ar.alloc([4, P], F32)
            sc = ar.alloc([32], F32)
            sc2 = ar.alloc([32], F32)
            m8a = ar.alloc([8], F32)
            m8b = ar.alloc([8], F32)
            sbias = ar.alloc([32], BF16)
            selT = ar.alloc([4, P], BF16)
            accs = [ar.alloc([P], F32) for _ in range(2)]
            def nsa_sel_row(i, h, nk, nkb, nch):
                st_ = {}

                def A():
                    base = alloc_banks(nch)
                    st_["base"] = base
                    o0 = base * 512

                    def f(e):
                        ins = None
                        for c in range(nch):
                            k0 = c * 512
                            n = min(512, nk - k0)
                            last = (c == nch - 1)
                            e.matmul(ps_s.t[:, o0 + k0:o0 + k0 + n], qT4.t[:, h, i * P:(i + 1) * P], ksT.t[:, k0:k0 + n], start=True, stop=False)
                            ins = e.matmul(ps_s.t[:, o0 + k0:o0 + k0 + n], selT.t[0:8, c, :], e8.t[0:8, 0:n], start=False, stop=not last)
                            if last:
                                ins = e.matmul(ps_s.t[:, o0 + nk - 256:o0 + nk], ident_bf.t[:], mask_c.t[:, 0:256], start=False, stop=True)
                        return ins
                    add("pe", f, qT4.b + ksT.b + selT.b + e8.b + ident_bf.b + mask_c.b, ps_s.b[base:base + nch])

                def B():
                    pb = pbs[rr("pbf", 2)]
                    rs_, rsb_ = softmax_row(nk, scale, pb, st_["base"])
                    pos_ = ps_o[rr("pso", 2)]
                    pv_accumulate(nkb, pb, vs, pos_, pTs)
                    cs, csb = newsmall()
                    add("dve", lambda e: e.tensor_tensor(out=cs, in0=rs_, in1=bgate.t[:, i, 3 * h + 1:3 * h + 2], op=ALU.mult),
                        [rsb_] + bgate.b, [csb])
                    a_ = accs[h % 2]
                    add("dve", lambda e: e.scalar_tensor_tensor(out=a_.t, in0=pos_.t[:, 0:P], scalar=cs, in1=ocmp.t[:, h, :],
                                                                op0=ALU.mult, op1=ALU.add), pos_.b + [csb] + ocmp.b, a_.b)
                return A, B, nch

            def nsa_win_row(i, h):
                kb0 = max(0, 2 * i - 4)
                nkbw = 2 * i + 2 - kb0
                nkw = nkbw * P
                moff = (kb0 - (2 * i - 4)) * P
                nchw = (nkw + 511) // 512
                st_ = {}

                def A():
                    basew = alloc_banks(nchw)
                    st_["base"] = basew
                    o0 = basew * 512

                    def f(e):
                        ins = None
                        for c in range(nchw):
                            k0 = c * 512
                            n = min(512, nkw - k0)
                            e.matmul(ps_s.t[:, o0 + k0:o0 + k0 + n], qT4.t[:, h, i * P:(i + 1) * P], kwT.t[:, kb0 * P + k0:kb0 * P + k0 + n],
                                     start=True, stop=False)
                            ins = e.matmul(ps_s.t[:, o0 + k0:o0 + k0 + n], ident_bf.t[:], mask_w.t[:, moff + k0:moff + k0 + n], start=False, stop=True)
                        return ins
                    add("pe", f, qT4.b + kwT.b + ident_bf.b + mask_w.b, ps_s.b[basew:basew + nchw])

                def B():
                    pb = pbs[rr("pbf", 2)]
                    rw_, rwb_ = softmax_row(nkw, scale, pb, st_["base"])
                    pow_ = ps_o[rr("pso", 2)]
                    pv_accumulate(nkbw, pb, vw, pow_, pTs, kb_off=kb0)
                    cw, cwb = newsmall()
                    add("dve", lambda e: e.tensor_tensor(out=cw, in0=rw_, in1=bgate.t[:, i, 3 * h + 2:3 * h + 3], op=ALU.mult),
                        [rwb_] + bgate.b, [cwb])
                    a_ = accs[h % 2]
                    add("dve", lambda e: e.scalar_tensor_tensor(out=a_.t, in0=pow_.t[:, 0:P], scalar=cw, in1=a_.t, op0=ALU.mult, op1=ALU.add),
                        pow_.b + [cwb] + a_.b, a_.b)
                    finish_head(i, a_.t, a_.b, None, None, 4 + h, mts)
                return A, B, nchw

            for i in range(NQ):
                nkb = 2 * i + 2
                nk = nkb * P
                nch = (nk + 511) // 512
                mcmp, cmp01, selb, selv = mcmp2[i % 2], cmp012[i % 2], selb2[i % 2], selv2[i % 2]
                dma(mcmp.t, c_mask_cmp[:, i, :], [], mcmp.b)
                dma(cmp01.t, c_cmp01[:, i, :], [], cmp01.b)
                dma(selb.t, c_selbias[:, i, :], [], selb.b)
                dma(selv.t, c_selvalid[:, i, :], [], selv.b)
                pz = ps_o[rr("pso", 2)]

                def f(e, i=i, pz=pz, mcmp=mcmp):
                    ins = None
                    for h in range(4):
                        e.matmul(pz.t[:, h * P:(h + 1) * P], qT4.t[:, h, i * P:(i + 1) * P], kcT.t[:, :], start=True, stop=False)
                        ins = e.matmul(pz.t[:, h * P:(h + 1) * P], ident_bf.t[:], mcmp.t[:, :], start=False, stop=True)
                    return ins
                add("pe", f, qT4.b + kcT.b + ident_bf.b + mcmp.b, pz.b)
                mx4, mx4b = newsmall(4)
                pz3 = pz.t[:, :].rearrange("p (h n) -> p h n", n=P)
                add("dve", lambda e, pz3=pz3, mx4=mx4: e.tensor_reduce(out=mx4, in_=pz3, axis=AX.X, op=ALU.max), pz.b, [mx4b])
                nm4, nm4b = newsmall(4)
                add("dve", lambda e, mx4=mx4, nm4=nm4: e.tensor_scalar(out=nm4, in0=mx4, scalar1=-scale, scalar2=None, op0=ALU.mult),
                    [mx4b], [nm4b])
                for h in range(4):
                    add("act", lambda e, h=h, pz=pz, nm4=nm4: e.activation(out=ef.t[:, h, :], in_=pz.t[:, h * P:(h + 1) * P], func=AF.Exp,
                                                                          bias=nm4[:, h:h + 1], scale=scale), pz.b + [nm4b], ef.b)
                m01 = cmp01.t[:, :].unsqueeze(1).to_broadcast([P, 4, P])
                add("dve", lambda e, m01=m01: e.tensor_tensor(out=ef.t, in0=ef.t, in1=m01, op=ALU.mult), ef.b + cmp01.b, ef.b)
                l4, l4b = newsmall(4)
                add("dve", lambda e, l4=l4: e.tensor_reduce(out=l4, in_=ef.t, axis=AX.X, op=ALU.add), ef.b, [l4b])
                r4, r4b = newsmall(4)
                add("dve", lambda e, l4=l4, r4=r4: e.tensor_scalar(out=r4, in0=l4, scalar1=1e-30, scalar2=None, op0=ALU.max), [l4b], [r4b])
                r4i, r4ib = newsmall(4)
                add("dve", lambda e, r4=r4, r4i=r4i: e.reciprocal(r4i, r4), [r4b], [r4ib])
                add("dve", lambda e, r4i=r4i: e.tensor_tensor(out=pcf.t, in0=ef.t, in1=r4i.unsqueeze(2).to_broadcast([P, 4, P]), op=ALU.mult),
                    ef.b + [r4ib], pcf.b)
                if NSA_STOP <= 2.1:
                    continue
                add("pool", lambda e: e.tensor_copy(pcb.t, pcf.t), pcf.b, pcb.b)
                transpose_to(pcT_b.t, pcT_b.b, [pcb.t[:, h, :] for h in range(4)], pcb.b, evac="act")
                if NSA_STOP <= 2.2:
                    continue
                add("dve", lambda e: e.tensor_reduce(out=ps4.t, in_=pcf.t.rearrange("p h n -> p n h"), axis=AX.X, op=ALU.add),
                    pcf.b, ps4.b)
                ps4v = ps4.t.rearrange("p (s j) -> p s j", j=4)
                add("dve", lambda e, ps4v=ps4v: e.tensor_reduce(out=impA.t, in_=ps4v, axis=AX.X, op=ALU.add), ps4.b, impA.b)
                v3 = ps4v[:, :, 3]
                add("dve", lambda e, v3=v3: e.scalar_tensor_tensor(out=imp.t, in0=v3, scalar=-0.5, in1=impA.t, op0=ALU.mult, op1=ALU.add),
                    ps4.b + impA.b, imp.b)
                add("dve", lambda e, v3=v3: e.scalar_tensor_tensor(out=imp.t[:, 1:32], in0=v3[:, 0:31], scalar=0.5, in1=imp.t[:, 1:32],
                                                                   op0=ALU.mult, op1=ALU.add), ps4.b + imp.b, imp.b)
                if NSA_STOP <= 2.25:
                    continue
                add("dve", lambda e, i=i, selb=selb: e.tensor_tensor(out=sc.t, in0=imp.t, in1=selb.t[:, :], op=ALU.max),
                    imp.b + selb.b, sc.b)
                add("dve", lambda e, i=i, selv=selv: e.tensor_tensor(out=sc.t, in0=sc.t, in1=selv.t[:, :], op=ALU.add), sc.b + selv.b, sc.b)
                if NSA_STOP <= 2.3:
                    continue
                add("dve", lambda e: e.max(out=m8a.t, in_=sc.t), sc.b, m8a.b)
                add("dve", lambda e: e.match_replace(out=sc2.t, in_to_replace=m8a.t, in_values=sc.t, imm_value=-3.0e38),
                    sc.b + m8a.b, sc2.b)
                add("dve", lambda e: e.max(out=m8b.t, in_=sc2.t), sc2.b, m8b.b)
                add("dve", lambda e: e.tensor_scalar(out=sc2.t, in0=sc.t, scalar1=m8b.t[:, 7:8], scalar2=1.0, op0=ALU.is_ge,
                                                     op1=ALU.subtract), sc.b + m8b.b, sc2.b)
                add("dve", lambda e: e.tensor_scalar(out=sbias.t, in0=sc2.t, scalar1=-NEG, scalar2=None, op0=ALU.mult), sc2.b, sbias.b)
                if NSA_STOP <= 2.4:
                    continue
                transpose_to(selT.t[0:8, 0:nch, :], selT.b, [sbias.t[:, c * 8:(c + 1) * 8] for c in range(nch)], sbias.b, evac="dve")
                poc = ps_o[rr("pso", 2)]

                def f(e, poc=poc):
                    ins = None
                    for h in range(4):
                        ins = e.matmul(poc.t[:, h * P:(h + 1) * P], pcT_b.t[:, h, :], vc.t[:, :], start=True, stop=True)
                    return ins
                add("pe", f, pcT_b.b + vc.b, poc.b)
                g0 = bgate.t[:, i, :].rearrange("p (h t) -> p h t", t=3)[:, :, 0:1].to_broadcast([P, 4, P])
                add("dve", lambda e, poc=poc, g0=g0: e.tensor_tensor(out=ocmp.t, in0=poc.t[:, :].rearrange("p (h n) -> p h n", n=P), in1=g0,
                                                                     op=ALU.mult), poc.b + bgate.b, ocmp.b)
                rows = []
                for h in range(4):
                    rows.append(nsa_sel_row(i, h, nk, nkb, nch))
                    rows.append(nsa_win_row(i, h))
                emit_rows(rows)
            pg.barrier()

        def post_phase(l, final, xsrc, ydst):
            ar.reset()
            gbc = ar.alloc([D], F32)
            xin = [ar.alloc([D], F32) for _ in range(2)]
            ytmp = ar.alloc([D], F32)
            junk = ar.alloc([D], BF16)
            for kc in range(16):
                load_wout_chunk(l, kc)
            dma(gbc.t, post_g[l, :].partition_broadcast(P), [], gbc.b)
            for i in range(NQ):
                def f(e, i=i):
                    ins = None
                    for n0 in range(4):
                        for kc in range(16):
                            ins = e.matmul(ps_s.t[:, n0 * 512:(n0 + 1) * 512], mixT.t[:, kc, i * P:(i + 1) * P],
                                           hTa.t[:, kc, n0 * 512:(n0 + 1) * 512], start=(kc == 0), stop=(kc == 15))
                    return ins
                add("pe", f, mixT.b + hTa.b, ps_s.b)
                xt = xin[rr("xin", 2)]
                sap, sbufs = xsrc("q", i)
                dma(xt.t, sap, sbufs, xt.b)
                ss, ssb = newsmall()
                add("act", lambda e, ss=ss: e.activation(out=junk.t, in_=ps_s.t[:, :], func=AF.Square, accum_out=ss),
                    ps_s.b, junk.b + [ssb])
                rs, rsb = rms_scale(ss, ssb, D)
                add("dve", lambda e, rs=rs: e.scalar_tensor_tensor(out=ytmp.t, in0=ps_s.t[:, :], scalar=rs, in1=gbc.t,
                                                                   op0=ALU.mult, op1=ALU.mult),
                    ps_s.b + [rsb] + gbc.b, ytmp.b)
                add("pool", lambda e, xt=xt: e.tensor_tensor(out=ytmp.t, in0=ytmp.t, in1=xt.t, op=ALU.add),
                    ytmp.b + xt.b, ytmp.b)
                if ydst is None:
                    final.append(dma(yout[i * P:(i + 1) * P, :], ytmp.t, ytmp.b, []))
                else:
                    dap, dbufs = ydst(i)
                    dma(dap, ytmp.t, ytmp.b, dbufs)
                    if i % 2 == 1:
                        j = i // 2
                        add_cc(j)
            pg.barrier()

        final = []
        x1b = [Buf() for _ in range(4)]
        gab = [Buf() for _ in range(4)]

        def add_cc(j):
            pg.add("pool", lambda e: e.collective_compute("AllGather", ALU.bypass,
                                                          replica_groups=[[0, 1], [2, 3], [4, 5], [6, 7]],
                                                          ins=[x1own_t[j].ap().opt()], outs=[gath_t[j].ap().opt()]),
                   [x1b[j]], [gab[j]], dma="cc")

        def xsrc_in(which, t):
            if which == "a":
                return xa[t * P:(t + 1) * P, :], []
            return xq[t * P:(t + 1) * P, :], []

        def xsrc_mid(which, t):
            if which == "a":
                r_, i_ = t % 2, t // 2
                j_, k_ = i_ // 2, i_ % 2
                return gath_t[j_].ap()[r_ * 2 * P + k_ * P:r_ * 2 * P + (k_ + 1) * P, :], [gab[j_]]
            j_, k_ = t // 2, t % 2
            return x1own_t[j_].ap()[k_ * P:(k_ + 1) * P, :], [x1b[j_]]

        def ydst_mid(i):
            j_, k_ = i // 2, i % 2
            return x1own_t[j_].ap()[k_ * P:(k_ + 1) * P, :], [x1b[j_]]

        for li, l in enumerate(layers):
            xsrc = xsrc_in if li == 0 else xsrc_mid
            ydst = None if li == len(layers) - 1 else ydst_mid
            phase_norm(l, xsrc)
            for c in range(16):
                mname = MIXERS[c // 4]
                if mname not in mixers:
                    add("pool", lambda e, c=c: e.memset(mixT.t[:, c, :], 0.0), [], [mixT.b[c]])
            if "sb" in mixers:
                mixer_sb(l)
            if "nsa" in mixers:
                mixer_nsa(l)
            if "fox" in mixers:
                mixer_fox(l)
            if "mla" in mixers:
                mixer_mla(l)
            post_phase(l, final, xsrc, ydst)

        with nc.Block() as block:
            pg.emit(block, final)
    return nc


def _consts(r):
    bf = ml_dtypes.bfloat16
    c = {}
    c["c_ident_bf"] = np.eye(P, dtype=np.float32).astype(bf)
    c["c_ident_f"] = np.eye(P, dtype=np.float32)
    p = np.arange(P)[:, None]
    col = np.arange(256)[None, :]
    c["c_mask_c"] = np.where(col <= p + 128 * r, 0.0, NEG).astype(np.float32).astype(bf)
    c["c_mask_s"] = np.where(col < p + 128 * r, 0.0, NEG).astype(np.float32).astype(bf)
    colw = np.arange(768)[None, :]
    c["c_mask_w"] = np.where((colw <= 512 + 128 * r + p) & (colw > 128 * r + p), 0.0, NEG).astype(np.float32).astype(bf)
    qpos = (np.arange(NQ)[None, :] * 2 + r) * P + np.arange(P)[:, None]
    cmp_end = np.arange(P) * 16 + 31
    vis = (cmp_end[None, None, :] <= qpos[:, :, None]) & (np.arange(P)[None, None, :] < NCMP)
    c["c_mask_cmp"] = np.where(vis, 0.0, NEG).astype(np.float32).astype(bf)
    c["c_cmp01"] = vis.astype(np.float32)
    sel = np.arange(32)[None, None, :]
    cur = (qpos // 64)[:, :, None]
    forced = (sel == 0) | (sel == cur) | (sel == cur - 1)
    valid = sel <= cur
    c["c_selbias"] = np.where(forced, 1e6, 0.0).astype(np.float32)
    c["c_selvalid"] = np.where(valid, 0.0, -1e30).astype(np.float32)
    cmp_start = np.arange(NCMP) * 16
    sel_start = np.arange(32) * 64
    ov = np.clip(np.minimum(cmp_start[:, None] + 32, sel_start[None, :] + 64)
                 - np.maximum(cmp_start[:, None], sel_start[None, :]), 0, None)
    c2s = np.zeros((P, 32), np.float32)
    c2s[:NCMP] = (ov / 32).astype(np.float32)
    c["c_c2s"] = c2s
    c["c_e8"] = (np.arange(512)[None, :] // 64 == np.arange(8)[:, None]).astype(np.float32).astype(bf)

    def tables(pos, half):
        inv = (np.float32(10000.0) ** (-np.arange(half, dtype=np.float32) / np.float32(half))).astype(np.float32)
        ang = pos.astype(np.float32)[..., None] * inv
        return np.cos(ang).astype(np.float32), np.sin(ang).astype(np.float32)
    pos_all = np.arange(NT)[None, :] * P + np.arange(P)[:, None]
    c["c_cosa"], c["c_sina"] = tables(pos_all, 64)
    c["c_cosq"], c["c_sinq"] = tables(qpos, 64)
    c["c_cosc"], c["c_sinc"] = tables(cmp_end, 64)
    c["c_cosa32"], c["c_sina32"] = tables(pos_all, 32)
    c["c_cosq32"], c["c_sinq32"] = tables(qpos, 32)
    sel3 = np.zeros((P, 4, P), np.float32)
    for h in range(4):
        for rep in range(3):
            sel3[rep * 32 + h, h, :] = 1.0
    c["c_sel3"] = sel3.astype(bf)
    return c


_WNAMES = ("pre_norm_g", "post_norm_g", "w_in", "b_in", "w_out", "fox_forget_bias",
           "nsa_cmp_pos_k", "nsa_cmp_w1_k", "nsa_cmp_w2_k", "nsa_cmp_pos_v", "nsa_cmp_w1_v", "nsa_cmp_w2_v",
           "mla_q_norm_g", "mla_w_uq", "mla_kv_norm_g", "mla_w_ukv")


def _own_rows(xb, r):
    return np.ascontiguousarray(xb.reshape(NQ, 2, P, D)[:, r].reshape(NQ * P, D))


def run_layers(x, weights, layers, dbg=None, mixers=MIXERS):
    nc = build(layers, dbg, mixers)
    in_maps = []
    for c in range(8):
        b, r = c // 2, c % 2
        m = {"xa": np.ascontiguousarray(x[b]), "xq": _own_rows(x[b], r)}
        for n in _WNAMES:
            m[n] = weights[n]
        m.update(_consts(r))
        in_maps.append(m)
    res = run_bass_kernel_spmd(nc, in_maps, core_ids=list(range(8)))
    out = np.empty((NB, S, D), np.float32)
    for c in range(8):
        b, r = c // 2, c % 2
        out[b].reshape(NQ, 2, P, D)[:, r] = res.results[c]["y"].reshape(NQ, P, D)
    return out, res


def kernel(**inputs):
    x = np.ascontiguousarray(np.asarray(inputs["x"], dtype=np.float32))
    weights = {n: np.ascontiguousarray(np.asarray(inputs[n], dtype=np.float32)) for n in _WNAMES}
    x, _ = run_layers(x, weights, list(range(DEPTH)))
    return x
```

```python
import numpy as np
import ml_dtypes
from contextlib import ExitStack
import concourse.bass as bass
import concourse.mybir as mybir
from concourse.bass_utils import run_bass_kernel_spmd

F32 = mybir.dt.float32
BF16 = mybir.dt.bfloat16
AF = mybir.ActivationFunctionType
ALU = mybir.AluOpType
AX = mybir.AxisListType

D = 2048
S = 2048
NB = 4
DEPTH = 2
INW = 6992
NT = 16
NQ = 8
P = 128
EPS = 1e-6
NEG = -30000.0
NCMP = 127

OFF = {}
_o = 0
for _n, _w in (("sb_q", 512), ("sb_k", 512), ("sb_v", 512), ("sb_gate", 512),
               ("nsa_q", 512), ("nsa_k_cmp", 128), ("nsa_v_cmp", 128), ("nsa_k_sel", 128),
               ("nsa_v_sel", 128), ("nsa_k_win", 128), ("nsa_v_win", 128), ("nsa_branch", 12),
               ("nsa_gate", 512), ("fox_q", 512), ("fox_k", 512), ("fox_v", 512), ("fox_f", 4),
               ("fox_gate", 512), ("mla_cq", 384), ("mla_ckv", 128), ("mla_k_rope", 64),
               ("mla_gate", 512)):
    OFF[_n] = _o
    _o += _w
assert _o == INW


_ALL_BUFS = []


class Buf:
    __slots__ = ("lw", "rd", "rd_dma")

    def __init__(self):
        self.lw = None
        self.rd = {}
        self.rd_dma = []
        _ALL_BUFS.append(self)


class Op:
    __slots__ = ("eng", "fn", "deps", "signal", "count", "is_dma", "dsem", "dval", "dprev")


class Prog:
    ENGS = ("pe", "act", "dve", "pool", "sp")

    def __init__(self, nc, stack, n_dma_sems=12):
        self.nc = nc
        self.ops = {e: [] for e in self.ENGS}
        self.esem = {e: stack.enter_context(nc.semaphore("es_" + e)) for e in self.ENGS}
        self.dsems = {}
        self.dcount = {}
        self.drr = {}
        for e in ("sp", "pool", "act"):
            self.dsems[e] = [stack.enter_context(nc.semaphore("ds_%s%d" % (e, i))) for i in range(n_dma_sems)]
            self.dcount[e] = [0] * n_dma_sems
            self.drr[e] = 0
        self.dsems["cc"] = [stack.enter_context(nc.semaphore("cc_sem"))]
        self.dcount["cc"] = [0]

    def add(self, eng, fn, reads=(), writes=(), dma=False):
        op = Op()
        op.eng = eng
        op.fn = fn
        op.signal = False
        op.count = 0
        op.is_dma = dma
        op.dsem = None
        op.dval = 0
        op.dprev = 0
        me = (eng, len(self.ops[eng]))
        deps = set()
        for b in reads:
            if b.lw is not None:
                deps.add(b.lw)
        for b in writes:
            if b.lw is not None:
                deps.add(b.lw)
            for e2, i2 in b.rd.items():
                deps.add((e2, i2))
            for d in b.rd_dma:
                deps.add(d)
        needed = []
        for d in deps:
            if d == me:
                continue
            dop = self.ops[d[0]][d[1]]
            if dop.is_dma:
                needed.append(d)
            elif d[0] == eng and eng == "pe":
                continue
            else:
                dop.signal = True
                needed.append(d)
        op.deps = needed
        if dma == "cc":
            op.dsem = ("cc", 0)
            op.dprev = 0
            self.dcount["cc"][0] += 1
            op.dval = self.dcount["cc"][0]
        elif dma:
            k = self.drr[eng]
            self.drr[eng] = (k + 1) % len(self.dsems[eng])
            op.dsem = (eng, k)
            op.dprev = self.dcount[eng][k] * 16
            self.dcount[eng][k] += 1
            op.dval = self.dcount[eng][k] * 16
        self.ops[eng].append(op)
        for b in writes:
            b.lw = me
            b.rd = {}
            b.rd_dma = []
        wset = set(id(b) for b in writes)
        for b in reads:
            if id(b) in wset:
                continue
            if dma:
                b.rd_dma.append(me)
            else:
                b.rd[eng] = me[1]
        return me

    def barrier(self):
        deps = set()
        for b in _ALL_BUFS:
            if b.lw is not None:
                deps.add(b.lw)
            for e2, i2 in b.rd.items():
                deps.add((e2, i2))
            for d in b.rd_dma:
                deps.add(d)
        for e in self.ENGS:
            if self.ops[e]:
                last = (e, len(self.ops[e]) - 1)
                if not self.ops[e][-1].is_dma and self.ops[e][-1].fn is not None:
                    deps.add(last)
        for e in self.ENGS:
            op = Op()
            op.eng = e
            op.fn = None
            op.signal = False
            op.count = 0
            op.is_dma = False
            op.dsem = None
            op.dval = 0
            op.dprev = 0
            mx = {}
            needed = []
            for d in deps:
                dop = self.ops[d[0]][d[1]]
                if dop.is_dma:
                    needed.append(d)
                else:
                    if d[0] == e and e == "pe":
                        continue
                    mx[d[0]] = max(mx.get(d[0], -1), d[1])
            for e2, i2 in mx.items():
                self.ops[e2][i2].signal = True
                needed.append((e2, i2))
            op.deps = needed
            self.ops[e].append(op)
        for b in _ALL_BUFS:
            b.lw = None
            b.rd = {}
            b.rd_dma = []

    def emit(self, block, final_waits):
        nc = self.nc
        for e in self.ENGS:
            c = 0
            for op in self.ops[e]:
                if op.signal and not op.is_dma:
                    c += 1
                    op.count = c
        prog = self

        def run(e, engobj):
            waited = {}
            for op in prog.ops[e]:
                for d in op.deps:
                    dop = prog.ops[d[0]][d[1]]
                    if dop.is_dma:
                        key = ("d",) + dop.dsem
                        val = dop.dval
                        sem = prog.dsems[dop.dsem[0]][dop.dsem[1]]
                    else:
                        key = ("e", d[0])
                        val = dop.count
                        sem = prog.esem[d[0]]
                    if waited.get(key, 0) >= val:
                        continue
                    engobj.wait_ge(sem, val)
                    waited[key] = val
                if op.fn is None:
                    continue
                if op.is_dma:
                    key = ("d",) + op.dsem
                    sem = prog.dsems[op.dsem[0]][op.dsem[1]]
                    if op.dprev > 0 and waited.get(key, 0) < op.dprev:
                        engobj.wait_ge(sem, op.dprev)
                        waited[key] = op.dprev
                    ins = op.fn(engobj)
                    if op.dsem[0] == "cc":
                        ins.then_inc(sem)
                    else:
                        ins.then_inc(sem, 16)
                else:
                    ins = op.fn(engobj)
                    if op.signal:
                        ins.then_inc(prog.esem[e], 1)
            if e == "sp":
                for d in final_waits:
                    dop = prog.ops[d[0]][d[1]]
                    sem = prog.dsems[dop.dsem[0]][dop.dsem[1]]
                    engobj.wait_ge(sem, dop.dval)

        @block.tensor
        def _(eng):
            run("pe", eng)

        @block.scalar
        def _(eng):
            run("act", eng)

        @block.vector
        def _(eng):
            run("dve", eng)

        @block.gpsimd
        def _(eng):
            run("pool", eng)

        @block.sync
        def _(eng):
            run("sp", eng)


class T:
    def __init__(self, t, nbuf=1):
        self.t = t
        self.b = [Buf() for _ in range(nbuf)]


ARENA_BYTES = 50 * 1024
MIXERS = ("sb", "nsa", "fox", "mla")


class Arena:
    def __init__(self, base):
        self.base = base
        self.off = 0

    def reset(self, to=0):
        self.off = to

    def alloc(self, shape, dt, nbuf=1):
        n = 1
        for v in shape:
            n *= v
        nbytes = n * (4 if dt == F32 else 2)
        nbytes = (nbytes + 7) // 8 * 8
        assert self.off + nbytes <= ARENA_BYTES, ("arena overflow", self.off, nbytes)
        ap = self.base[:, self.off // 2:(self.off + nbytes) // 2]
        self.off += nbytes
        if dt == F32:
            ap = ap.bitcast(F32)
        ap = ap[:, 0:n]
        if len(shape) == 2:
            ap = ap.rearrange("p (a b) -> p a b", a=shape[0])
        elif len(shape) == 3:
            ap = ap.rearrange("p (a b c) -> p a b c", a=shape[0], b=shape[1])
        return T(ap, nbuf)


NSA_STOP = 99


def build(layers, dbg=None, mixers=MIXERS):
    del _ALL_BUFS[:]
    nc = bass.Bass("TRN2", target_bir_lowering=False)
    dr = {}

    def din(name, shape, dt=F32):
        dr[name] = nc.dram_tensor(name, list(shape), dt, kind="ExternalInput").ap()
        return dr[name]

    xa = din("xa", [S, D])
    xq = din("xq", [NQ * P, D])
    pre_g = din("pre_norm_g", [DEPTH, D])
    post_g = din("post_norm_g", [DEPTH, D])
    w_in = din("w_in", [DEPTH, D, INW])
    b_in = din("b_in", [DEPTH, INW])
    w_out = din("w_out", [DEPTH, D, D])
    fox_fb = din("fox_forget_bias", [DEPTH, 4])
    pos_kv = [din("nsa_cmp_pos_k", [DEPTH, 32, 128]), din("nsa_cmp_pos_v", [DEPTH, 32, 128])]
    w1_kv = [din("nsa_cmp_w1_k", [DEPTH, 4096, 128]), din("nsa_cmp_w1_v", [DEPTH, 4096, 128])]
    w2_kv = [din("nsa_cmp_w2_k", [DEPTH, 128, 128]), din("nsa_cmp_w2_v", [DEPTH, 128, 128])]
    qng = din("mla_q_norm_g", [DEPTH, 384])
    w_uq = din("mla_w_uq", [DEPTH, 384, 768])
    kvng = din("mla_kv_norm_g", [DEPTH, 128])
    w_ukv = din("mla_w_ukv", [DEPTH, 128, 1024])
    c_ident_bf = din("c_ident_bf", [P, P], BF16)
    c_ident_f = din("c_ident_f", [P, P])
    c_mask_c = din("c_mask_c", [P, 256], BF16)
    c_mask_s = din("c_mask_s", [P, 256], BF16)
    c_mask_w = din("c_mask_w", [P, 768], BF16)
    c_mask_cmp = din("c_mask_cmp", [P, NQ, P], BF16)
    c_cmp01 = din("c_cmp01", [P, NQ, P])
    c_selbias = din("c_selbias", [P, NQ, 32])
    c_selvalid = din("c_selvalid", [P, NQ, 32])
    c_c2s = din("c_c2s", [P, 32])
    c_e8 = din("c_e8", [8, 512], BF16)
    c_cosa = din("c_cosa", [P, NT, 64])
    c_sina = din("c_sina", [P, NT, 64])
    c_cosq = din("c_cosq", [P, NQ, 64])
    c_sinq = din("c_sinq", [P, NQ, 64])
    c_cosc = din("c_cosc", [P, 64])
    c_sinc = din("c_sinc", [P, 64])
    c_cosa32 = din("c_cosa32", [P, NT, 32])
    c_sina32 = din("c_sina32", [P, NT, 32])
    c_cosq32 = din("c_cosq32", [P, NQ, 32])
    c_sinq32 = din("c_sinq32", [P, NQ, 32])
    c_sel3 = din("c_sel3", [P, 4, P], BF16)
    yout = nc.dram_tensor("y", [NQ * P, D], F32, kind="ExternalOutput").ap()
    x1own_t = [nc.dram_tensor("x1own%d" % j, [2 * P, D], F32) for j in range(4)]
    gath_t = [nc.dram_tensor("gath%d" % j, [4 * P, D], F32) for j in range(4)]

    with ExitStack() as st:
        pg = Prog(nc, st)

        def sb(name, shape, dt, nbuf=1):
            return T(st.enter_context(nc.sbuf_tensor(name, list(shape), dt)), nbuf)

        def ps(name, shape, dt, nbuf=1):
            return T(st.enter_context(nc.psum_tensor(name, list(shape), dt)), nbuf)

        hTa = sb("hTa", [P, 16, S], BF16, 16)
        hTq = sb("hTq", [P, 16, NQ * P], BF16, 16)
        mixT = sb("mixT", [P, 16, NQ * P], BF16, 16)
        wst = [sb("wst%d" % i, [P, 16, P], F32) for i in range(2)]
        wbf = [sb("wbf%d" % i, [P, 16, P], BF16) for i in range(2)]
        ident_bf = sb("ident_bf", [P, P], BF16)
        ident_f = sb("ident_f", [P, P], F32)
        mask_c = sb("mask_c", [P, 256], BF16)
        mask_s = sb("mask_s", [P, 256], BF16)
        mask_w = sb("mask_w", [P, 768], BF16)
        small = sb("small", [P, 64], F32, 64)
        small4 = sb("small4", [P, 64], F32, 16)
        bias_fm = [sb("bias_fm%d" % i, [P, 1], F32) for i in range(3)]
        bias_bc = [sb("bias_bc%d" % i, [P, P], F32) for i in range(3)]
        arena_t = st.enter_context(nc.sbuf_tensor("arena", [P, ARENA_BYTES // 2], BF16))
        ar = Arena(arena_t)
        ps_s = ps("ps_s", [P, 2048], F32, 4)
        ps_t = [ps("ps_t%d" % i, [P, 1024], BF16) for i in range(2)]
        ps_o = [ps("ps_o%d" % i, [P, 512], F32) for i in range(2)]

        cnt = {}

        def rr(key, n):
            v = cnt.get(key, 0)
            cnt[key] = v + 1
            return v % n

        def newsmall(w=1):
            if w == 1:
                i = rr("sm", 64)
                return small.t[:, i:i + 1], small.b[i]
            i = rr("sm4", 16)
            return small4.t[:, i * 4:i * 4 + 4], small4.b[i]

        def view(tobj, ap):
            r = T(ap, 0)
            r.b = tobj.b
            return r

        def dma(out_ap, in_ap, reads, writes, q="sp"):
            return pg.add(q, lambda e: e.dma_start(out=out_ap, in_=in_ap), reads, writes, dma=True)

        for tile_, src_ in ((ident_bf, c_ident_bf), (ident_f, c_ident_f), (mask_c, c_mask_c),
                            (mask_s, c_mask_s), (mask_w, c_mask_w)):
            dma(tile_.t[:], src_, [], tile_.b)

        def add(eng, fn, reads, writes):
            return pg.add(eng, fn, reads, writes)

        def transpose_to(dst, dst_bufs, src_aps, src_bufs, evac="act", f32=False):
            n = len(src_aps)
            w = src_aps[0].shape[-1]
            rows = src_aps[0].shape[0]
            if f32:
                k = rr("pso", 2)
                pt = ps_o[k]
                idt = ident_f
                assert n <= 4
            else:
                k = rr("pst", 2)
                pt = ps_t[k]
                idt = ident_bf

            def f(e):
                ins = None
                for j, a in enumerate(src_aps):
                    ins = e.transpose(pt.t[0:w, j * P:j * P + rows], a, idt.t[0:rows, 0:rows])
                return ins
            add("pe", f, list(src_bufs) + idt.b, pt.b)
            if rows == P:
                src = pt.t[0:w, 0:n * P]
                if len(dst.shape) == 3:
                    src = src.rearrange("p (a b) -> p a b", b=P)
            elif n == 1:
                src = pt.t[0:w, 0:rows]
            else:
                src = pt.t[0:w, 0:n * P].rearrange("p (a b) -> p a b", b=P)[:, :, 0:rows]
            if evac == "act":
                add("act", lambda e: e.copy(dst, src), pt.b, dst_bufs)
            else:
                add("dve", lambda e: e.tensor_copy(dst, src), pt.b, dst_bufs)

        def load_w(src, nkc, ncols):
            s = rr("wst", 2)
            k = rr("wbf", 2)
            ws, wb = wst[s], wbf[k]
            wsv = ws.t[:, :, :].rearrange("p a b -> p (a b)")[:, 0:nkc * ncols].rearrange("p (a b) -> p a b", b=ncols)
            wbv = wb.t[:, :, :].rearrange("p a b -> p (a b)")[:, 0:nkc * ncols].rearrange("p (a b) -> p a b", b=ncols)
            dma(wsv, src.rearrange("(kc p) c -> p kc c", p=P), [], ws.b)
            if nkc >= 2:
                hk = nkc // 2
                add("dve", lambda e: e.tensor_copy(wbv[:, 0:hk, :], wsv[:, 0:hk, :]), ws.b, wb.b)
                add("act", lambda e: e.copy(wbv[:, hk:nkc, :], wsv[:, hk:nkc, :]), ws.b, wb.b)
            else:
                add("dve", lambda e: e.tensor_copy(wbv, wsv), ws.b, wb.b)
            return wbv, wb.b

        def load_bias_fm(l, c0, ncols):
            k = rr("bfm", 3)
            bt = bias_fm[k]
            dma(bt.t[0:ncols, :], b_in[l, c0:c0 + ncols].rearrange("(c o) -> c o", o=1), [], bt.b)
            return bt

        def load_bias_bc(l, c0, ncols):
            k = rr("bbc", 3)
            bt = bias_bc[k]
            dma(bt.t[:, 0:ncols], b_in[l, c0:c0 + ncols].partition_broadcast(P), [], bt.b)
            return bt

        def proj_fm(l, c0, ncols, src, ntok, dst_fn, dst_bufs):
            wbv, wbb = load_w(w_in[l, :, c0:c0 + ncols], 16, ncols)
            bt = load_bias_fm(l, c0, ncols)
            for t0 in range(0, ntok, 512):
                po = ps_o[rr("pso", 2)]

                def f(e, t0=t0, po=po):
                    ins = None
                    for kc in range(16):
                        ins = e.matmul(po.t[0:ncols, :], wbv[:, kc, :], src.t[:, kc, t0:t0 + 512],
                                       start=(kc == 0), stop=(kc == 15))
                    return ins
                add("pe", f, wbb + src.b, po.b)
                dst = dst_fn(t0, 512)
                add("act", lambda e, dst=dst, po=po: e.activation(out=dst, in_=po.t[0:ncols, :], func=AF.Identity,
                                                                   bias=bt.t[0:ncols, :], scale=1.0),
                    po.b + bt.b, dst_bufs)

        def proj_tm(l, c0, ncols, src, nblk, dst_fn, dst_bufs, blk0=0, w=None):
            if w is None:
                wbv, wbb = load_w(w_in[l, :, c0:c0 + ncols], 16, ncols)
                bt = load_bias_bc(l, c0, ncols)
            else:
                wbv, wbb, bt = w
            for b0 in range(blk0, blk0 + nblk, 4):
                po = ps_o[rr("pso", 2)]

                def f(e, b0=b0, po=po):
                    ins = None
                    for j in range(4):
                        for kc in range(16):
                            ins = e.matmul(po.t[:, j * P:j * P + ncols], src.t[:, kc, (b0 + j) * P:(b0 + j + 1) * P],
                                           wbv[:, kc, :], start=(kc == 0), stop=(kc == 15))
                    return ins
                add("pe", f, wbb + src.b, po.b)
                dst = dst_fn(b0, 4)
                pin = po.t[:, :].rearrange("p (j c) -> p j c", c=P)[:, :, 0:ncols]
                bb = bt.t[:, 0:ncols].unsqueeze(1).to_broadcast([P, 4, ncols])
                add("dve", lambda e, dst=dst, pin=pin, bb=bb: e.tensor_tensor(out=dst, in0=pin, in1=bb, op=ALU.add),
                    po.b + bt.b, dst_bufs)

        def gate_to_mixT(l, c0, ch):
            proj_tm(l, c0, P, hTq, NQ,
                    lambda b0, n: mixT.t[:, ch, b0 * P:(b0 + n) * P].rearrange("p (j c) -> p j c", c=P), [mixT.b[ch]])
            add("act", lambda e: e.activation(out=mixT.t[:, ch, :], in_=mixT.t[:, ch, :], func=AF.Silu),
                [mixT.b[ch]], [mixT.b[ch]])

        def rope_tm(x, xb, cos, sin, tb, out, ob, half, tmp):
            x1, x2 = x[:, :, 0:half], x[:, :, half:2 * half]
            o1, o2 = out[:, :, 0:half], out[:, :, half:2 * half]
            t1, t2 = tmp
            add("dve", lambda e: e.tensor_tensor(out=t1.t, in0=x1, in1=cos, op=ALU.mult), xb + tb, t1.b)
            add("pool", lambda e: e.tensor_tensor(out=t2.t, in0=x2, in1=sin, op=ALU.mult), xb + tb, t2.b)
            add("dve", lambda e: e.tensor_tensor(out=o1, in0=t1.t, in1=t2.t, op=ALU.subtract), t1.b + t2.b, ob)
            add("dve", lambda e: e.tensor_tensor(out=t1.t, in0=x2, in1=cos, op=ALU.mult), xb + tb + ob, t1.b)
            add("pool", lambda e: e.tensor_tensor(out=t2.t, in0=x1, in1=sin, op=ALU.mult), xb + tb + ob, t2.b)
            add("dve", lambda e: e.tensor_tensor(out=o2, in0=t1.t, in1=t2.t, op=ALU.add), t1.b + t2.b, ob)

        def rms_scale(ss, ssb, n):
            ms, msb = newsmall()
            add("dve", lambda e: e.tensor_scalar(out=ms, in0=ss, scalar1=1.0 / n, scalar2=EPS, op0=ALU.mult, op1=ALU.add),
                [ssb], [msb])
            sd, sdb = newsmall()
            add("act", lambda e: e.sqrt(sd, ms), [msb], [sdb])
            rs, rsb = newsmall()
            add("dve", lambda e: e.reciprocal(rs, sd), [sdb], [rsb])
            return rs, rsb

        bank_cur = [0]

        def alloc_banks(nch):
            if bank_cur[0] + nch > 4:
                bank_cur[0] = 0
            b0 = bank_cur[0]
            bank_cur[0] = (b0 + nch) % 4
            return b0

        def softmax_row(nk, scale, pb, base=0):
            nch = (nk + 511) // 512
            o0 = base * 512
            bufs = ps_s.b[base:base + nch]
            mx, mxb = newsmall()
            add("dve", lambda e: e.reduce_max(out=mx, in_=ps_s.t[:, o0:o0 + nk], axis=AX.X), bufs, [mxb])
            nm, nmb = newsmall()
            add("dve", lambda e: e.tensor_scalar(out=nm, in0=mx, scalar1=-scale, scalar2=None, op0=ALU.mult), [mxb], [nmb])
            l1, l1b = newsmall()
            add("act", lambda e: e.activation(out=pb.t[:, 0:nk], in_=ps_s.t[:, o0:o0 + nk], func=AF.Exp, bias=nm, scale=scale,
                                              accum_out=l1), bufs + [nmb], pb.b + [l1b])
            ri, rib = newsmall()
            add("dve", lambda e: e.reciprocal(ri, l1), [l1b], [rib])
            return ri, rib

        def pv_accumulate(nkb, pb, vt, po, pTs, kb_off=0):
            for g0 in range(0, nkb, 8):
                gn = min(8, nkb - g0)
                ptile = pTs[rr("pT", 2)]
                transpose_to(ptile.t[:, 0:gn, :], ptile.b,
                             [pb.t[:, (g0 + j) * P:(g0 + j + 1) * P] for j in range(gn)], pb.b,
                             evac="act" if (g0 // 8) % 2 == 0 else "dve")

                def f(e, g0=g0, gn=gn, ptile=ptile):
                    ins = None
                    for j in range(gn):
                        kb = g0 + j
                        ins = e.matmul(po.t[:, 0:P], ptile.t[:, j, :], vt.t[:, kb_off + kb, :],
                                       start=(kb == 0), stop=(kb == nkb - 1))
                    return ins
                add("pe", f, ptile.b + vt.b, po.b)

        def finish_head(i, src_ap, src_bufs, rinv, rinvb, ch, mts):
            mt = mts[rr("mixtm", 2)]
            gate = mixT.t[:, ch, i * P:(i + 1) * P]
            if rinv is not None:
                add("dve", lambda e: e.scalar_tensor_tensor(out=mt.t, in0=src_ap, scalar=rinv, in1=gate,
                                                            op0=ALU.mult, op1=ALU.mult),
                    src_bufs + [rinvb, mixT.b[ch]], mt.b)
            else:
                add("dve", lambda e: e.tensor_tensor(out=mt.t, in0=src_ap, in1=gate, op=ALU.mult),
                    src_bufs + [mixT.b[ch]], mt.b)
            transpose_to(mixT.t[:, ch, i * P:(i + 1) * P], [mixT.b[ch]], [mt.t], mt.b, evac="act")

        def alloc_attn_common():
            pbs = [ar.alloc([S], BF16) for _ in range(2)]
            pTs = [ar.alloc([8, P], BF16) for _ in range(2)]
            mts = [ar.alloc([P], BF16) for _ in range(2)]
            return pbs, pTs, mts

        def alloc_head_ops(n=2):
            return [(ar.alloc([S], BF16), ar.alloc([NT, P], BF16), ar.alloc([NQ * P], BF16)) for _ in range(n)]

        def phase_norm(l, xsrc):
            ar.reset()
            gbc = ar.alloc([D], F32)
            xin = [ar.alloc([D], F32) for _ in range(2)]
            hn = ar.alloc([D], BF16)
            dma(gbc.t, pre_g[l, :].partition_broadcast(P), [], gbc.b)
            for (which, nblk, dst) in (("a", NT, hTa), ("q", NQ, hTq)):
                for t in range(nblk):
                    xt = xin[rr("xin", 2)]
                    sap, sbufs = xsrc(which, t)
                    dma(xt.t, sap, sbufs, xt.b)
                    ss, ssb = newsmall()
                    add("act", lambda e, xt=xt, ss=ss: e.activation(out=hn.t, in_=xt.t, func=AF.Square, accum_out=ss),
                        xt.b, hn.b + [ssb])
                    rs, rsb = rms_scale(ss, ssb, D)
                    add("dve", lambda e, xt=xt, rs=rs: e.scalar_tensor_tensor(out=hn.t, in0=xt.t, scalar=rs, in1=gbc.t,
                                                                              op0=ALU.mult, op1=ALU.mult),
                        xt.b + [rsb] + gbc.b, hn.b)
                    for half in range(2):
                        c0 = half * 8
                        transpose_to(dst.t[:, c0:c0 + 8, t * P:(t + 1) * P], dst.b[c0:c0 + 8],
                                     [hn.t[:, (c0 + j) * P:(c0 + j + 1) * P] for j in range(8)], hn.b,
                                     evac="act" if half == 0 else "dve")
            pg.barrier()

        def peek_banks(nch):
            b0 = bank_cur[0] if bank_cur[0] + nch <= 4 else 0
            return set(range(b0, b0 + nch))

        def emit_rows(rows, after=None):
            cur_banks = set()
            if rows:
                cur_banks = peek_banks(rows[0][3])
                rows[0][0]()
            for k in range(len(rows)):
                nxt_banks = set()
                early = False
                if k + 1 < len(rows):
                    nxt_banks = peek_banks(rows[k + 1][3])
                    if not (nxt_banks & cur_banks):
                        rows[k + 1][0]()
                        early = True
                rows[k][1]()
                if k + 1 < len(rows) and not early:
                    nxt_banks = peek_banks(rows[k + 1][3])
                    rows[k + 1][0]()
                rows[k][2]()
                cur_banks = nxt_banks
                if after is not None:
                    after(k)

        def run_pipelined(nheads, proj_items, att_rows):
            for t in proj_items(0):
                t()
            for h in range(nheads):
                nxt = proj_items(h + 1) if h + 1 < nheads else []
                rows = att_rows(h)
                st_ = {"k": 0}

                def after(ai, nxt=nxt, rows=rows, st_=st_):
                    want = (ai + 1) * len(nxt) // len(rows)
                    while st_["k"] < want:
                        nxt[st_["k"]]()
                        st_["k"] += 1
                emit_rows(rows, after)
                while st_["k"] < len(nxt):
                    nxt[st_["k"]]()
                    st_["k"] += 1

        def mixer_sb(l):
            ar.reset()
            scale = 128 ** -0.5
            pbs, pTs, mts = alloc_attn_common()
            hops = alloc_head_ops(2)
            wk_sets = [[ar.alloc([512], F32) for _ in range(4)] for _ in range(2)]
            def proj_items(h):
                kTt, vt, qTt = hops[h % 2]
                return [
                    lambda: proj_fm(l, OFF["sb_q"] + h * P, P, hTq, NQ * P, lambda t0, n: qTt.t[:, t0:t0 + n], qTt.b),
                    lambda: proj_fm(l, OFF["sb_k"] + h * P, P, hTa, S, lambda t0, n: kTt.t[:, t0:t0 + n], kTt.b),
                    lambda: proj_tm(l, OFF["sb_v"] + h * P, P, hTa, NT, lambda b0, n: vt.t[:, b0:b0 + n, :], vt.b),
                    lambda: gate_to_mixT(l, OFF["sb_gate"] + h * P, 0 + h),
                ]

            def sb_steps(h, i):
                kTt, vt, qTt = hops[h % 2]
                nkb = 2 * i + 2
                nk = nkb * P
                nch = (nk + 511) // 512
                pb = pbs[rr("pbf", 2)]
                st_ = {"carry": None}
                steps = []
                for c in range(nch - 1, -1, -1):
                    k0 = c * 512
                    n = min(512, nk - k0)
                    last = (c == nch - 1)
                    loc = {}

                    def s1(c=c, k0=k0, n=n, last=last, loc=loc):
                        bank = ps_s.b[c]
                        wk_e, wk_sp, wk_c, wk_t = wk_sets[rr("wk", 2)]
                        loc["wk_t"] = wk_t

                        def f(e):
                            ins = e.matmul(ps_s.t[:, k0:k0 + n], qTt.t[:, i * P:(i + 1) * P], kTt.t[:, k0:k0 + n],
                                           start=True, stop=not last)
                            if last:
                                ins = e.matmul(ps_s.t[:, nk - 256:nk], ident_bf.t[:], mask_s.t[:, 0:256], start=False, stop=True)
                            return ins
                        add("pe", f, qTt.b + kTt.b + ident_bf.b + mask_s.b, [bank])
                        sv = ps_s.t[:, k0:k0 + n]
                        add("act", lambda e: e.activation(out=wk_e.t[:, 0:n], in_=sv, func=AF.Exp, scale=scale), [bank], wk_e.b)
                        add("act", lambda e: e.activation(out=wk_sp.t[:, 0:n], in_=wk_e.t[:, 0:n], func=AF.Ln, bias=1.0, scale=1.0),
                            wk_e.b, wk_sp.b)
                        add("dve", lambda e: e.tensor_tensor_scan(out=wk_c.t[:, 0:n], data0=wk_sp.t[:, 0:n], data1=wk_sp.t[:, 0:n],
                                                                  initial=0.0, op0=ALU.add, op1=ALU.max), wk_sp.b, wk_c.b)
                        add("dve", lambda e: e.scalar_tensor_tensor(out=wk_t.t[:, 0:n], in0=sv, scalar=scale, in1=wk_sp.t[:, 0:n],
                                                                    op0=ALU.mult, op1=ALU.subtract), [bank] + wk_sp.b, wk_t.b)
                        add("pool", lambda e: e.tensor_tensor(out=wk_t.t[:, 0:n], in0=wk_t.t[:, 0:n], in1=wk_c.t[:, 0:n], op=ALU.add),
                            wk_t.b + wk_c.b, wk_t.b)
                        nb_, nbb = newsmall()
                        tot = wk_c.t[:, n - 1:n]
                        if st_["carry"] is None:
                            add("dve", lambda e: e.tensor_scalar(out=nb_, in0=tot, scalar1=-1.0, scalar2=None, op0=ALU.mult),
                                wk_c.b, [nbb])
                        else:
                            cpr, cprb = st_["carry"]
                            add("dve", lambda e: e.scalar_tensor_tensor(out=nb_, in0=tot, scalar=-1.0, in1=cpr, op0=ALU.mult,
                                                                        op1=ALU.add), wk_c.b + [cprb], [nbb])
                        st_["carry"] = (nb_, nbb)
                        loc["nb"] = (nb_, nbb)

                    def s2(k0=k0, n=n, loc=loc):
                        wk_t = loc["wk_t"]
                        nb_, nbb = loc["nb"]
                        add("act", lambda e: e.activation(out=pb.t[:, k0:k0 + n], in_=wk_t.t[:, 0:n], func=AF.Exp, bias=nb_, scale=1.0),
                            wk_t.b + [nbb], pb.b)

                    fin = None
                    if c == 0:
                        def fin():
                            po = ps_o[rr("pso", 2)]
                            pv_accumulate(nkb, pb, vt, po, pTs)
                            finish_head(i, po.t[:, 0:P], po.b, None, None, 0 + h, mts)
                    steps.append((s1, s2, fin))
                return steps

            for t in proj_items(0):
                t()
            for h in range(4):
                nxt = proj_items(h + 1) if h + 1 < 4 else []
                kq = 0
                prev = None
                for i in range(NQ):
                    for stp in sb_steps(h, i):
                        stp[0]()
                        if prev is not None:
                            prev[1]()
                            if prev[2] is not None:
                                prev[2]()
                        prev = stp
                    want = (i + 1) * len(nxt) // NQ
                    while kq < want:
                        nxt[kq]()
                        kq += 1
                prev[1]()
                prev[2]()
                while kq < len(nxt):
                    nxt[kq]()
                    kq += 1
            pg.barrier()

        def softmax_heads_causal(i, qTt, kTt, vt, scale, pbs, pTs, mts, ch, extra=None, extra_bufs=()):
            nkb = 2 * i + 2
            nk = nkb * P
            nch = (nk + 511) // 512
            st_ = {}

            def A():
                base = alloc_banks(nch)
                st_["base"] = base
                o0 = base * 512

                def f(e):
                    ins = None
                    for c in range(nch):
                        k0 = c * 512
                        n = min(512, nk - k0)
                        last = (c == nch - 1)
                        ins = e.matmul(ps_s.t[:, o0 + k0:o0 + k0 + n], qTt.t[:, i * P:(i + 1) * P], kTt.t[:, k0:k0 + n],
                                       start=True, stop=(not last) and extra is None)
                        if extra is not None:
                            ins = extra(e, ps_s.t[:, o0 + k0:o0 + k0 + n], k0, n, not last)
                        if last:
                            ins = e.matmul(ps_s.t[:, o0 + nk - 256:o0 + nk], ident_bf.t[:], mask_c.t[:, 0:256], start=False, stop=True)
                    return ins
                add("pe", f, qTt.b + kTt.b + ident_bf.b + mask_c.b + list(extra_bufs), ps_s.b[base:base + nch])

            def B1():
                pb = pbs[rr("pbf", 2)]
                st_["pb"] = pb
                st_["r"] = softmax_row(nk, scale, pb, st_["base"])

            def B2():
                pb = st_["pb"]
                rinv, rinvb = st_["r"]
                po = ps_o[rr("pso", 2)]
                pv_accumulate(nkb, pb, vt, po, pTs)
                finish_head(i, po.t[:, 0:P], po.b, rinv, rinvb, ch, mts)
            return A, B1, B2, nch

        def mixer_fox(l):
            ar.reset()
            scale = 128 ** -0.5
            pbs, pTs, mts = alloc_attn_common()
            hops = alloc_head_ops(2)
            crow3 = ar.alloc([S], BF16)
            sel3 = ar.alloc([4, P], BF16)
            t_e = ar.alloc([512], F32)
            t_sp = ar.alloc([512], F32)
            cch = [ar.alloc([512], F32) for _ in range(2)]
            hi_t = ar.alloc([512], BF16)
            mid_t = ar.alloc([512], BF16)
            wf68 = ar.alloc([16, 68], BF16)
            fb = ar.alloc([2], F32)
            NP_ = 68
            dma(sel3.t, c_sel3, [], sel3.b)
            add("pool", lambda e: e.memset(crow3.t, 0.0), [], crow3.b)
            add("pool", lambda e: e.memset(wf68.t, 0.0), [], wf68.b)
            add("pool", lambda e: e.memset(fb.t, 0.0), [], fb.b)
            c0 = OFF["fox_f"]
            wbv, wbb = load_w(w_in[l, :, c0:c0 + 4], 16, 4)
            for rep in range(3):
                add("dve", lambda e, rep=rep: e.tensor_copy(wf68.t[:, :, rep * 32:rep * 32 + 4], wbv), wbb + wf68.b, wf68.b)
                dma(fb.t[rep * 32:rep * 32 + 4, 0:1], b_in[l, c0:c0 + 4].rearrange("(c o) -> c o", o=1), fb.b, fb.b)
                dma(fb.t[rep * 32:rep * 32 + 4, 1:2], fox_fb[l, :].rearrange("(c o) -> c o", o=1), fb.b, fb.b)
            nb_, nbb = newsmall()
            add("dve", lambda e: e.scalar_tensor_tensor(out=nb_[0:NP_, :], in0=fb.t[0:NP_, 0:1], scalar=-1.0, in1=fb.t[0:NP_, 1:2],
                                                        op0=ALU.mult, op1=ALU.subtract), fb.b, [nbb])
            prevc = None
            for ci, t0 in enumerate(range(0, S, 512)):
                po = ps_o[rr("pso", 2)]
                cc = cch[ci % 2]

                def f(e, t0=t0, po=po):
                    ins = None
                    for kc in range(16):
                        ins = e.matmul(po.t[0:NP_, :], wf68.t[:, kc, :], hTa.t[:, kc, t0:t0 + 512], start=(kc == 0), stop=(kc == 15))
                    return ins
                add("pe", f, wf68.b + hTa.b, po.b)
                add("act", lambda e, po=po: e.activation(out=t_e.t[0:NP_, :], in_=po.t[0:NP_, :], func=AF.Exp, bias=nb_[0:NP_, :],
                                                         scale=-1.0), po.b + [nbb], t_e.b)
                add("act", lambda e: e.activation(out=t_sp.t[0:NP_, :], in_=t_e.t[0:NP_, :], func=AF.Ln, bias=1.0, scale=1.0),
                    t_e.b, t_sp.b)
                init = 0.0 if prevc is None else prevc.t[0:NP_, 511:512]
                rdb = t_sp.b + ([] if prevc is None else prevc.b)
                add("dve", lambda e, cc=cc, init=init: e.tensor_tensor_scan(out=cc.t[0:NP_, :], data0=t_sp.t[0:NP_, :],
                                                                            data1=t_sp.t[0:NP_, :], initial=init,
                                                                            op0=ALU.add, op1=ALU.max), rdb, cc.b)
                add("dve", lambda e, cc=cc: e.tensor_scalar(out=t_e.t[0:NP_, :], in0=cc.t[0:NP_, :], scalar1=float(128 ** 0.5),
                                                            scalar2=None, op0=ALU.mult), cc.b, t_e.b)
                add("dve", lambda e: e.tensor_copy(hi_t.t[0:NP_, :], t_e.t[0:NP_, :]), t_e.b, hi_t.b)
                add("dve", lambda e: e.tensor_tensor(out=t_sp.t[0:NP_, :], in0=t_e.t[0:NP_, :], in1=hi_t.t[0:NP_, :], op=ALU.subtract),
                    t_e.b + hi_t.b, t_sp.b)
                add("dve", lambda e: e.tensor_copy(mid_t.t[0:NP_, :], t_sp.t[0:NP_, :]), t_sp.b, mid_t.b)
                add("dve", lambda e: e.tensor_tensor(out=t_sp.t[0:NP_, :], in0=t_sp.t[0:NP_, :], in1=mid_t.t[0:NP_, :], op=ALU.subtract),
                    t_sp.b + mid_t.b, t_sp.b)
                add("dve", lambda e, t0=t0: e.tensor_copy(crow3.t[0:4, t0:t0 + 512], hi_t.t[0:4, :]), hi_t.b, crow3.b)
                add("dve", lambda e, t0=t0: e.tensor_copy(crow3.t[32:36, t0:t0 + 512], mid_t.t[32:36, :]), mid_t.b, crow3.b)
                add("dve", lambda e, t0=t0: e.tensor_copy(crow3.t[64:68, t0:t0 + 512], t_sp.t[64:68, :]), t_sp.b, crow3.b)
                prevc = cc
            def proj_items(h):
                kTt, vt, qTt = hops[h % 2]
                return [
                    lambda: proj_fm(l, OFF["fox_q"] + h * P, P, hTq, NQ * P, lambda t0, n: qTt.t[:, t0:t0 + n], qTt.b),
                    lambda: proj_fm(l, OFF["fox_k"] + h * P, P, hTa, S, lambda t0, n: kTt.t[:, t0:t0 + n], kTt.b),
                    lambda: proj_tm(l, OFF["fox_v"] + h * P, P, hTa, NT, lambda b0, n: vt.t[:, b0:b0 + n, :], vt.b),
                    lambda: gate_to_mixT(l, OFF["fox_gate"] + h * P, 8 + h),
                ]

            def att_items(h):
                kTt, vt, qTt = hops[h % 2]

                def extra(e, out, k0, n, stop):
                    return e.matmul(out, sel3.t[:, h, :], crow3.t[:, k0:k0 + n], start=False, stop=stop)
                return [softmax_heads_causal(i, qTt, kTt, vt, scale, pbs, pTs, mts, 8 + h, extra, sel3.b + crow3.b)
                        for i in range(NQ)]
            run_pipelined(4, proj_items, att_items)
            pg.barrier()

        wout_done = {}

        def load_wout_chunk(l, kc):
            if (l, kc) in wout_done:
                return
            wout_done[(l, kc)] = True
            ws = wst[rr("wst", 2)]
            wsv = ws.t[:, :, :].rearrange("p a b -> p (a b)")
            dma(wsv, w_out[l, kc * P:(kc + 1) * P, :], [], ws.b)
            eng = "pool" if kc % 2 == 0 else "dve"
            add(eng, lambda e: e.tensor_copy(hTa.t[:, kc, :], wsv), ws.b, [hTa.b[kc]])

        def mixer_mla(l):
            ar.reset()
            scale = 192 ** -0.5
            cqnT = ar.alloc([3, NQ * P], BF16)
            ckvnT = ar.alloc([S], BF16)
            krT = ar.alloc([S], BF16)
            wukv_b = ar.alloc([1024], BF16)
            cosq = ar.alloc([NQ, 32], F32)
            sinq = ar.alloc([NQ, 32], F32)
            mark = ar.off
            dma(cosq.t, c_cosq32, [], cosq.b)
            dma(sinq.t, c_sinq32, [], sinq.b)
            cosa = ar.alloc([NT, 32], F32)
            sina = ar.alloc([NT, 32], F32)
            gq = ar.alloc([384], F32)
            gkv = ar.alloc([P], F32)
            xt = ar.alloc([4, 384], F32)
            xn = ar.alloc([4, 384], BF16)
            t1 = ar.alloc([4, 32], F32)
            t2 = ar.alloc([4, 32], F32)
            dma(cosa.t, c_cosa32, [], cosa.b)
            dma(sina.t, c_sina32, [], sina.b)
            dma(gq.t, qng[l, :].partition_broadcast(P), [], gq.b)
            dma(gkv.t, kvng[l, :].partition_broadcast(P), [], gkv.b)
            ws = wst[rr("wst", 2)]
            wsv = ws.t[:, :, :].rearrange("p a b -> p (a b)")[:, 0:1024]
            dma(wsv, w_ukv[l, :, :], [], ws.b)
            add("pool", lambda e, wsv=wsv: e.tensor_copy(wukv_b.t, wsv), ws.b, wukv_b.b)

            def norm_rows(x_ap, xb, width, g_ap, gb, out_ap, ob):
                ss, ssb = newsmall()
                jt, jb = junkt
                add("act", lambda e: e.activation(out=jt[:, 0:width], in_=x_ap, func=AF.Square, accum_out=ss), xb, jb + [ssb])
                rs, rsb = rms_scale(ss, ssb, width)
                add("dve", lambda e: e.scalar_tensor_tensor(out=out_ap, in0=x_ap, scalar=rs, in1=g_ap, op0=ALU.mult, op1=ALU.mult),
                    xb + [rsb] + gb, ob)
            jk = ar.alloc([384], F32)
            junkt = (jk.t, jk.b)
            for b0 in range(0, NQ, 4):
                for cg in range(3):
                    wbv, wbb = load_w(w_in[l, :, OFF["mla_cq"] + cg * P:OFF["mla_cq"] + (cg + 1) * P], 16, P)
                    bt = load_bias_bc(l, OFF["mla_cq"] + cg * P, P)
                    po = ps_o[rr("pso", 2)]

                    def f(e, b0=b0, po=po, wbv=wbv):
                        ins = None
                        for j in range(4):
                            for kc in range(16):
                                ins = e.matmul(po.t[:, j * P:(j + 1) * P], hTq.t[:, kc, (b0 + j) * P:(b0 + j + 1) * P],
                                               wbv[:, kc, :], start=(kc == 0), stop=(kc == 15))
                        return ins
                    add("pe", f, wbb + hTq.b, po.b)
                    pin = po.t[:, :].rearrange("p (j c) -> p j c", c=P)
                    bb = bt.t[:, :].unsqueeze(1).to_broadcast([P, 4, P])
                    add("dve", lambda e, cg=cg, pin=pin, bb=bb: e.tensor_tensor(out=xt.t[:, :, cg * P:(cg + 1) * P], in0=pin, in1=bb,
                                                                                op=ALU.add), po.b + bt.b, xt.b)
                for j in range(4):
                    norm_rows(xt.t[:, j, :], xt.b, 384, gq.t, gq.b, xn.t[:, j, :], xn.b)
                for cg in range(3):
                    transpose_to(cqnT.t[:, cg, b0 * P:(b0 + 4) * P], cqnT.b,
                                 [xn.t[:, j, cg * P:(cg + 1) * P] for j in range(4)], xn.b, evac="dve")
            xk = ar.alloc([4, P], F32)
            xkn = ar.alloc([4, P], BF16)
            xr = ar.alloc([4, 64], F32)
            xrb = ar.alloc([4, 64], BF16)
            wbv1, wbb1 = load_w(w_in[l, :, OFF["mla_ckv"]:OFF["mla_ckv"] + P], 16, P)
            bt1 = load_bias_bc(l, OFF["mla_ckv"], P)
            wbv2, wbb2 = load_w(w_in[l, :, OFF["mla_k_rope"]:OFF["mla_k_rope"] + 64], 16, 64)
            bt2 = load_bias_bc(l, OFF["mla_k_rope"], 64)
            for b0 in range(0, NT, 4):
                po = ps_o[rr("pso", 2)]

                def f(e, b0=b0, po=po):
                    ins = None
                    for j in range(4):
                        for kc in range(16):
                            ins = e.matmul(po.t[:, j * P:(j + 1) * P], hTa.t[:, kc, (b0 + j) * P:(b0 + j + 1) * P],
                                           wbv1[:, kc, :], start=(kc == 0), stop=(kc == 15))
                    return ins
                add("pe", f, wbb1 + hTa.b, po.b)
                pin = po.t[:, :].rearrange("p (j c) -> p j c", c=P)
                bb = bt1.t[:, :].unsqueeze(1).to_broadcast([P, 4, P])
                add("dve", lambda e, pin=pin, bb=bb: e.tensor_tensor(out=xk.t, in0=pin, in1=bb, op=ALU.add), po.b + bt1.b, xk.b)
                for j in range(4):
                    norm_rows(xk.t[:, j, :], xk.b, P, gkv.t, gkv.b, xkn.t[:, j, :], xkn.b)
                transpose_to(ckvnT.t[:, b0 * P:(b0 + 4) * P], ckvnT.b, [xkn.t[:, j, :] for j in range(4)], xkn.b, evac="dve")
                po2 = ps_o[rr("pso", 2)]

                def f2(e, b0=b0, po2=po2):
                    ins = None
                    for j in range(4):
                        for kc in range(16):
                            ins = e.matmul(po2.t[:, j * 64:(j + 1) * 64], hTa.t[:, kc, (b0 + j) * P:(b0 + j + 1) * P],
                                           wbv2[:, kc, :], start=(kc == 0), stop=(kc == 15))
                    return ins
                add("pe", f2, wbb2 + hTa.b, po2.b)
                pin2 = po2.t[:, 0:256].rearrange("p (j c) -> p j c", c=64)
                bb2 = bt2.t[:, 0:64].unsqueeze(1).to_broadcast([P, 4, 64])
                add("dve", lambda e, pin2=pin2, bb2=bb2: e.tensor_tensor(out=xr.t, in0=pin2, in1=bb2, op=ALU.add), po2.b + bt2.b, xr.b)
                rope_tm(xr.t, xr.b, cosa.t[:, b0:b0 + 4, :], sina.t[:, b0:b0 + 4, :], cosa.b + sina.b, xrb.t, xrb.b, 32, (t1, t2))
                transpose_to(krT.t[0:64, b0 * P:(b0 + 4) * P], krT.b, [xrb.t[:, j, :] for j in range(4)], xrb.b, evac="dve")
            pg.barrier()
            ar.reset(mark)
            pbs, pTs, mts = alloc_attn_common()
            kTt = ar.alloc([S], BF16)
            vt = ar.alloc([NT, P], BF16)
            qTt = ar.alloc([NQ * P], BF16)
            qrT = ar.alloc([NQ * P], BF16)
            wuqh = ar.alloc([3, 192], BF16)
            qr_f = ar.alloc([NQ, 64], F32)
            qr_b = ar.alloc([NQ, 64], BF16)
            t1 = ar.alloc([NQ, 32], F32)
            t2 = ar.alloc([NQ, 32], F32)
            for h in range(4):
                ws = wst[rr("wst", 2)]
                wsv = ws.t[:, :, :].rearrange("p a b -> p (a b)")[:, 0:576].rearrange("p (a b) -> p a b", b=192)
                dma(wsv, w_uq[l, :, h * 192:(h + 1) * 192].rearrange("(kc p) c -> p kc c", p=P), [], ws.b)
                add("pool", lambda e, wsv=wsv: e.tensor_copy(wuqh.t, wsv), ws.b, wuqh.b)
                for t0 in range(0, NQ * P, 512):
                    po = ps_o[rr("pso", 2)]

                    def f(e, t0=t0, po=po):
                        ins = None
                        for kc in range(3):
                            ins = e.matmul(po.t[:, :], wuqh.t[:, kc, 0:P], cqnT.t[:, kc, t0:t0 + 512], start=(kc == 0), stop=(kc == 2))
                        return ins
                    add("pe", f, wuqh.b + cqnT.b, po.b)
                    add("act", lambda e, t0=t0, po=po: e.copy(qTt.t[:, t0:t0 + 512], po.t[:, :]), po.b, qTt.b)
                for b0 in range(0, NQ, 4):
                    po = ps_o[rr("pso", 2)]

                    def f(e, b0=b0, po=po):
                        ins = None
                        for j in range(4):
                            for kc in range(3):
                                ins = e.matmul(po.t[:, j * 64:(j + 1) * 64], cqnT.t[:, kc, (b0 + j) * P:(b0 + j + 1) * P],
                                               wuqh.t[:, kc, P:192], start=(kc == 0), stop=(kc == 2))
                        return ins
                    add("pe", f, wuqh.b + cqnT.b, po.b)
                    add("act", lambda e, b0=b0, po=po: e.copy(qr_f.t[:, b0:b0 + 4, :], po.t[:, 0:256].rearrange("p (j c) -> p j c", c=64)),
                        po.b, qr_f.b)
                rope_tm(qr_f.t, qr_f.b, cosq.t, sinq.t, cosq.b + sinq.b, qr_b.t, qr_b.b, 32, (t1, t2))
                transpose_to(qrT.t[0:64, :], qrT.b, [qr_b.t[:, j, :] for j in range(NQ)], qr_b.b, evac="dve")
                for t0 in range(0, S, 512):
                    po = ps_o[rr("pso", 2)]
                    add("pe", lambda e, t0=t0, po=po, h=h: e.matmul(po.t[:, :], wukv_b.t[:, h * 256:h * 256 + P], ckvnT.t[:, t0:t0 + 512],
                                                                    start=True, stop=True), wukv_b.b + ckvnT.b, po.b)
                    add("act", lambda e, t0=t0, po=po: e.copy(kTt.t[:, t0:t0 + 512], po.t[:, :]), po.b, kTt.b)
                for b0 in range(0, NT, 4):
                    po = ps_o[rr("pso", 2)]

                    def f(e, b0=b0, po=po, h=h):
                        ins = None
                        for j in range(4):
                            ins = e.matmul(po.t[:, j * P:(j + 1) * P], ckvnT.t[:, (b0 + j) * P:(b0 + j + 1) * P],
                                           wukv_b.t[:, h * 256 + P:h * 256 + 256], start=True, stop=True)
                        return ins
                    add("pe", f, wukv_b.b + ckvnT.b, po.b)
                    add("dve", lambda e, b0=b0, po=po: e.tensor_copy(vt.t[:, b0:b0 + 4, :], po.t[:, :].rearrange("p (j c) -> p j c", c=P)),
                        po.b, vt.b)
                gate_to_mixT(l, OFF["mla_gate"] + h * P, 12 + h)

                rows = []
                for i in range(NQ):
                    def extra_i(e, out, k0, n, stop, i=i):
                        return e.matmul(out, qrT.t[0:64, i * P:(i + 1) * P], krT.t[0:64, k0:k0 + n], start=False, stop=stop)
                    rows.append(softmax_heads_causal(i, qTt, kTt, vt, scale, pbs, pTs, mts, 12 + h, extra_i, qrT.b + krT.b))

                def after_row(k, h=h):
                    if k % 2 == 1:
                        load_wout_chunk(l, (h * NQ + k) // 2)
                emit_rows(rows, after_row)
            pg.barrier()

        def mixer_nsa(l):
            ar.reset()
            scale = 128 ** -0.5
            qT4 = ar.alloc([4, NQ * P], BF16)
            ksT = ar.alloc([S], BF16)
            kwT = ar.alloc([S], BF16)
            vs = ar.alloc([NT, P], BF16)
            vw = ar.alloc([NT, P], BF16)
            kcT = ar.alloc([P], BF16)
            vc = ar.alloc([P], BF16)
            bgate = ar.alloc([NQ, 12], F32)
            e8 = ar.alloc([512], BF16)
            c2s = ar.alloc([32], F32)
            mark = ar.off
            dma(e8.t[0:8, :], c_e8, [], e8.b)
            dma(c2s.t, c_c2s, [], c2s.b)
            cos_t = ar.alloc([NT, 64], F32)
            sin_t = ar.alloc([NT, 64], F32)
            xf = ar.alloc([NQ, P], F32)
            xb_ = ar.alloc([NQ, P], BF16)
            t1 = ar.alloc([NQ, 64], F32)
            t2 = ar.alloc([NQ, 64], F32)
            dma(cos_t.t[:, 0:NQ, :], c_cosq, [], cos_t.b)
            dma(sin_t.t[:, 0:NQ, :], c_sinq, [], sin_t.b)
            for h in range(4):
                proj_tm(l, OFF["nsa_q"] + h * P, P, hTq, NQ, lambda b0, n: xf.t[:, b0:b0 + n, :], xf.b)
                rope_tm(xf.t, xf.b, cos_t.t[:, 0:NQ, :], sin_t.t[:, 0:NQ, :], cos_t.b + sin_t.b, xb_.t, xb_.b, 64, (t1, t2))
                transpose_to(qT4.t[:, h, :], qT4.b, [xb_.t[:, j, :] for j in range(NQ)], xb_.b, evac="dve")
            proj_tm(l, OFF["nsa_branch"], 12, hTq, NQ, lambda b0, n: bgate.t[:, b0:b0 + n, :], bgate.b)
            add("act", lambda e: e.activation(out=bgate.t, in_=bgate.t, func=AF.Sigmoid), bgate.b, bgate.b)
            dma(cos_t.t, c_cosa, xf.b + xb_.b + t1.b + t2.b, cos_t.b)
            dma(sin_t.t, c_sina, xf.b + xb_.b + t1.b + t2.b, sin_t.b)
            for (cname, dstT) in (("nsa_k_sel", ksT), ("nsa_k_win", kwT)):
                c0_ = OFF[cname]
                wbv_, wbb_ = load_w(w_in[l, :, c0_:c0_ + P], 16, P)
                bt_ = load_bias_bc(l, c0_, P)
                for g0 in (0, 8):
                    proj_tm(l, c0_, P, hTa, 8, lambda b0, n, g0=g0: xf.t[:, b0 - g0:b0 - g0 + n, :], xf.b, blk0=g0,
                            w=(wbv_, wbb_, bt_))
                    rope_tm(xf.t, xf.b, cos_t.t[:, g0:g0 + 8, :], sin_t.t[:, g0:g0 + 8, :], cos_t.b + sin_t.b, xb_.t, xb_.b, 64,
                            (t1, t2))
                    transpose_to(dstT.t[:, g0 * P:(g0 + 8) * P], dstT.b, [xb_.t[:, j, :] for j in range(8)], xb_.b, evac="dve")
            proj_tm(l, OFF["nsa_v_sel"], P, hTa, NT, lambda b0, n: vs.t[:, b0:b0 + n, :], vs.b)
            proj_tm(l, OFF["nsa_v_win"], P, hTa, NT, lambda b0, n: vw.t[:, b0:b0 + n, :], vw.b)
            pg.barrier()
            if NSA_STOP <= 1:
                return
            ar.reset(mark)
            tokT = ar.alloc([S], BF16)
            blkT = ar.alloc([32, P], BF16)
            w1b = ar.alloc([32, P], BF16)
            w2b = ar.alloc([P], BF16)
            posr = ar.alloc([P], F32)
            posT = ar.alloc([32], F32)
            hidT = ar.alloc([P], BF16)
            kcf = ar.alloc([1, P], F32)
            kcb = ar.alloc([1, P], BF16)
            cosc = ar.alloc([1, 64], F32)
            sinc = ar.alloc([1, 64], F32)
            tc1 = ar.alloc([1, 64], F32)
            tc2 = ar.alloc([1, 64], F32)
            dma(cosc.t[:, 0, :], c_cosc, [], cosc.b)
            dma(sinc.t[:, 0, :], c_sinc, [], sinc.b)
            for which in range(2):
                cname = "nsa_k_cmp" if which == 0 else "nsa_v_cmp"
                proj_fm(l, OFF[cname], P, hTa, S, lambda t0, n: tokT.t[:, t0:t0 + n], tokT.b)
                dma(posr.t[0:32, :], pos_kv[which][l, :, :], [], posr.b)
                po = ps_o[rr("pso", 2)]
                add("pe", lambda e, po=po: e.transpose(po.t[:, 0:32], posr.t[0:32, :], ident_f.t[0:32, 0:32]), posr.b + ident_f.b, po.b)
                add("act", lambda e, po=po: e.copy(posT.t, po.t[:, 0:32]), po.b, posT.b)
                for half in range(2):
                    ws = wst[rr("wst", 2)]
                    dma(ws.t[:, :, :], w1_kv[which][l, half * 2048:(half + 1) * 2048, :].rearrange("(l d) h -> d l h", d=P), [], ws.b)
                    add("pool", lambda e, half=half, ws=ws: e.tensor_copy(w1b.t[:, half * 16:(half + 1) * 16, :], ws.t[:, :, :]),
                        ws.b, w1b.b)
                ws = wst[rr("wst", 2)]
                wsv = ws.t[:, 0, :]
                dma(wsv, w2_kv[which][l, :, :], [], ws.b)
                add("pool", lambda e, wsv=wsv: e.tensor_copy(w2b.t, wsv), ws.b, w2b.b)
                for ll in range(32):
                    src = tokT.t[:, ll:ll + 16 * (NCMP - 1) + 1:16]
                    eng = "dve" if ll % 2 == 0 else "pool"
                    add(eng, lambda e, ll=ll, src=src: e.tensor_scalar(out=blkT.t[:, ll, 0:NCMP], in0=src, scalar1=posT.t[:, ll:ll + 1],
                                                                       scalar2=None, op0=ALU.add), tokT.b + posT.b, blkT.b)
                po = ps_o[rr("pso", 2)]

                def f(e, po=po):
                    ins = None
                    for ll in range(32):
                        ins = e.matmul(po.t[:, 0:NCMP], w1b.t[:, ll, :], blkT.t[:, ll, 0:NCMP], start=(ll == 0), stop=(ll == 31))
                    return ins
                add("pe", f, w1b.b + blkT.b, po.b)
                add("act", lambda e, po=po: e.activation(out=hidT.t[:, 0:NCMP], in_=po.t[:, 0:NCMP], func=AF.Silu), po.b, hidT.b)
                po2 = ps_o[rr("pso", 2)]
                add("pe", lambda e, po2=po2: e.matmul(po2.t[0:NCMP, 0:P], hidT.t[:, 0:NCMP], w2b.t, start=True, stop=True),
                    hidT.b + w2b.b, po2.b)
                if which == 0:
                    add("act", lambda e, po2=po2: e.copy(kcf.t[0:NCMP, 0, :], po2.t[0:NCMP, 0:P]), po2.b, kcf.b)
                    rope_tm(kcf.t[0:NCMP], kcf.b, cosc.t[0:NCMP], sinc.t[0:NCMP], cosc.b + sinc.b, kcb.t[0:NCMP], kcb.b, 64,
                            (view(tc1, tc1.t[0:NCMP]), view(tc2, tc2.t[0:NCMP])))
                    add("pool", lambda e: e.memset(kcT.t, 0.0), [], kcT.b)
                    transpose_to(kcT.t[:, 0:NCMP], kcT.b, [kcb.t[0:NCMP, 0, :]], kcb.b, evac="dve")
                else:
                    add("pool", lambda e: e.memset(vc.t, 0.0), [], vc.b)
                    add("act", lambda e, po2=po2: e.copy(vc.t[0:NCMP, :], po2.t[0:NCMP, 0:P]), po2.b, vc.b)
            for h in range(4):
                gate_to_mixT(l, OFF["nsa_gate"] + h * P, 4 + h)
            pg.barrier()
            if NSA_STOP <= 2:
                return
            ar.reset(mark)
            pbs, pTs, mts = alloc_attn_common()
            mcmp2 = [ar.alloc([P], BF16) for _ in range(2)]
            cmp012 = [ar.alloc([P], F32) for _ in range(2)]
            selb2 = [ar.alloc([32], F32) for _ in range(2)]
            selv2 = [ar.alloc([32], F32) for _ in range(2)]
            ef = ar.alloc([4, P], F32)
            pcf = ef
            pcb = ar.alloc([4, P], BF16)
            ps4 = ar.alloc([P], F32)
            impA = ar.alloc([32], F32)
            imp = ar.alloc([32], F32)
            pcT_b = ar.alloc([4, P], BF16)
            ocmp = ar.alloc([4, P], F32)
            sc = ar.alloc([32], F32)
            sc2 = ar.alloc([32], F32)
            m8a = ar.alloc([8], F32)
            m8b = ar.alloc([8], F32)
            sbias = ar.alloc([32], BF16)
            selT = ar.alloc([4, P], BF16)
            accs = [ar.alloc([P], F32) for _ in range(2)]
            def nsa_sel_row(i, h, nk, nkb, nch):
                st_ = {}

                def A():
                    base = alloc_banks(nch)
                    st_["base"] = base
                    o0 = base * 512

                    def f(e):
                        ins = None
                        for c in range(nch):
                            k0 = c * 512
                            n = min(512, nk - k0)
                            last = (c == nch - 1)
                            e.matmul(ps_s.t[:, o0 + k0:o0 + k0 + n], qT4.t[:, h, i * P:(i + 1) * P], ksT.t[:, k0:k0 + n], start=True, stop=False)
                            ins = e.matmul(ps_s.t[:, o0 + k0:o0 + k0 + n], selT.t[0:8, c, :], e8.t[0:8, 0:n], start=False, stop=not last)
                            if last:
                                ins = e.matmul(ps_s.t[:, o0 + nk - 256:o0 + nk], ident_bf.t[:], mask_c.t[:, 0:256], start=False, stop=True)
                        return ins
                    add("pe", f, qT4.b + ksT.b + selT.b + e8.b + ident_bf.b + mask_c.b, ps_s.b[base:base + nch])

                def B1():
                    pb = pbs[rr("pbf", 2)]
                    st_["pb"] = pb
                    st_["r"] = softmax_row(nk, scale, pb, st_["base"])

                def B2():
                    pb = st_["pb"]
                    rs_, rsb_ = st_["r"]
                    pos_ = ps_o[rr("pso", 2)]
                    pv_accumulate(nkb, pb, vs, pos_, pTs)
                    cs, csb = newsmall()
                    add("dve", lambda e: e.tensor_tensor(out=cs, in0=rs_, in1=bgate.t[:, i, 3 * h + 1:3 * h + 2], op=ALU.mult),
                        [rsb_] + bgate.b, [csb])
                    a_ = accs[h % 2]
                    add("dve", lambda e: e.scalar_tensor_tensor(out=a_.t, in0=pos_.t[:, 0:P], scalar=cs, in1=ocmp.t[:, h, :],
                                                                op0=ALU.mult, op1=ALU.add), pos_.b + [csb] + ocmp.b, a_.b)
                return A, B1, B2, nch

            def nsa_win_row(i, h):
                kb0 = max(0, 2 * i - 4)
                nkbw = 2 * i + 2 - kb0
                nkw = nkbw * P
                moff = (kb0 - (2 * i - 4)) * P
                nchw = (nkw + 511) // 512
                st_ = {}

                def A():
                    basew = alloc_banks(nchw)
                    st_["base"] = basew
                    o0 = basew * 512

                    def f(e):
                        ins = None
                        for c in range(nchw):
                            k0 = c * 512
                            n = min(512, nkw - k0)
                            e.matmul(ps_s.t[:, o0 + k0:o0 + k0 + n], qT4.t[:, h, i * P:(i + 1) * P], kwT.t[:, kb0 * P + k0:kb0 * P + k0 + n],
                                     start=True, stop=False)
                            ins = e.matmul(ps_s.t[:, o0 + k0:o0 + k0 + n], ident_bf.t[:], mask_w.t[:, moff + k0:moff + k0 + n], start=False, stop=True)
                        return ins
                    add("pe", f, qT4.b + kwT.b + ident_bf.b + mask_w.b, ps_s.b[basew:basew + nchw])

                def B1():
                    pb = pbs[rr("pbf", 2)]
                    st_["pb"] = pb
                    st_["r"] = softmax_row(nkw, scale, pb, st_["base"])

                def B2():
                    pb = st_["pb"]
                    rw_, rwb_ = st_["r"]
                    pow_ = ps_o[rr("pso", 2)]
                    pv_accumulate(nkbw, pb, vw, pow_, pTs, kb_off=kb0)
                    cw, cwb = newsmall()
                    add("dve", lambda e: e.tensor_tensor(out=cw, in0=rw_, in1=bgate.t[:, i, 3 * h + 2:3 * h + 3], op=ALU.mult),
                        [rwb_] + bgate.b, [cwb])
                    a_ = accs[h % 2]
                    add("dve", lambda e: e.scalar_tensor_tensor(out=a_.t, in0=pow_.t[:, 0:P], scalar=cw, in1=a_.t, op0=ALU.mult, op1=ALU.add),
                        pow_.b + [cwb] + a_.b, a_.b)
                    finish_head(i, a_.t, a_.b, None, None, 4 + h, mts)
                return A, B1, B2, nchw

            for i in range(NQ):
                nkb = 2 * i + 2
                nk = nkb * P
                nch = (nk + 511) // 512
                mcmp, cmp01, selb, selv = mcmp2[i % 2], cmp012[i % 2], selb2[i % 2], selv2[i % 2]
                dma(mcmp.t, c_mask_cmp[:, i, :], [], mcmp.b)
                dma(cmp01.t, c_cmp01[:, i, :], [], cmp01.b)
                dma(selb.t, c_selbias[:, i, :], [], selb.b)
                dma(selv.t, c_selvalid[:, i, :], [], selv.b)
                pz = ps_o[rr("pso", 2)]

                def f(e, i=i, pz=pz, mcmp=mcmp):
                    ins = None
                    for h in range(4):
                        e.matmul(pz.t[:, h * P:(h + 1) * P], qT4.t[:, h, i * P:(i + 1) * P], kcT.t[:, :], start=True, stop=False)
                        ins = e.matmul(pz.t[:, h * P:(h + 1) * P], ident_bf.t[:], mcmp.t[:, :], start=False, stop=True)
                    return ins
                add("pe", f, qT4.b + kcT.b + ident_bf.b + mcmp.b, pz.b)
                mx4, mx4b = newsmall(4)
                pz3 = pz.t[:, :].rearrange("p (h n) -> p h n", n=P)
                add("dve", lambda e, pz3=pz3, mx4=mx4: e.tensor_reduce(out=mx4, in_=pz3, axis=AX.X, op=ALU.max), pz.b, [mx4b])
                nm4, nm4b = newsmall(4)
                add("dve", lambda e, mx4=mx4, nm4=nm4: e.tensor_scalar(out=nm4, in0=mx4, scalar1=-scale, scalar2=None, op0=ALU.mult),
                    [mx4b], [nm4b])
                for h in range(4):
                    add("act", lambda e, h=h, pz=pz, nm4=nm4: e.activation(out=ef.t[:, h, :], in_=pz.t[:, h * P:(h + 1) * P], func=AF.Exp,
                                                                          bias=nm4[:, h:h + 1], scale=scale), pz.b + [nm4b], ef.b)
                m01 = cmp01.t[:, :].unsqueeze(1).to_broadcast([P, 4, P])
                add("dve", lambda e, m01=m01: e.tensor_tensor(out=ef.t, in0=ef.t, in1=m01, op=ALU.mult), ef.b + cmp01.b, ef.b)
                l4, l4b = newsmall(4)
                add("dve", lambda e, l4=l4: e.tensor_reduce(out=l4, in_=ef.t, axis=AX.X, op=ALU.add), ef.b, [l4b])
                r4, r4b = newsmall(4)
                add("dve", lambda e, l4=l4, r4=r4: e.tensor_scalar(out=r4, in0=l4, scalar1=1e-30, scalar2=None, op0=ALU.max), [l4b], [r4b])
                r4i, r4ib = newsmall(4)
                add("dve", lambda e, r4=r4, r4i=r4i: e.reciprocal(r4i, r4), [r4b], [r4ib])
                add("dve", lambda e, r4i=r4i: e.tensor_tensor(out=pcf.t, in0=ef.t, in1=r4i.unsqueeze(2).to_broadcast([P, 4, P]), op=ALU.mult),
                    ef.b + [r4ib], pcf.b)
                if NSA_STOP <= 2.1:
                    continue
                add("pool", lambda e: e.tensor_copy(pcb.t, pcf.t), pcf.b, pcb.b)
                transpose_to(pcT_b.t, pcT_b.b, [pcb.t[:, h, :] for h in range(4)], pcb.b, evac="act")
                if NSA_STOP <= 2.2:
                    continue
                add("dve", lambda e: e.tensor_reduce(out=ps4.t, in_=pcf.t.rearrange("p h n -> p n h"), axis=AX.X, op=ALU.add),
                    pcf.b, ps4.b)
                ps4v = ps4.t.rearrange("p (s j) -> p s j", j=4)
                add("dve", lambda e, ps4v=ps4v: e.tensor_reduce(out=impA.t, in_=ps4v, axis=AX.X, op=ALU.add), ps4.b, impA.b)
                v3 = ps4v[:, :, 3]
                add("dve", lambda e, v3=v3: e.scalar_tensor_tensor(out=imp.t, in0=v3, scalar=-0.5, in1=impA.t, op0=ALU.mult, op1=ALU.add),
                    ps4.b + impA.b, imp.b)
                add("dve", lambda e, v3=v3: e.scalar_tensor_tensor(out=imp.t[:, 1:32], in0=v3[:, 0:31], scalar=0.5, in1=imp.t[:, 1:32],
                                                                   op0=ALU.mult, op1=ALU.add), ps4.b + imp.b, imp.b)
                if NSA_STOP <= 2.25:
                    continue
                add("dve", lambda e, i=i, selb=selb: e.tensor_tensor(out=sc.t, in0=imp.t, in1=selb.t[:, :], op=ALU.max),
                    imp.b + selb.b, sc.b)
                add("dve", lambda e, i=i, selv=selv: e.tensor_tensor(out=sc.t, in0=sc.t, in1=selv.t[:, :], op=ALU.add), sc.b + selv.b, sc.b)
                if NSA_STOP <= 2.3:
                    continue
                add("dve", lambda e: e.max(out=m8a.t, in_=sc.t), sc.b, m8a.b)
                add("dve", lambda e: e.match_replace(out=sc2.t, in_to_replace=m8a.t, in_values=sc.t, imm_value=-3.0e38),
                    sc.b + m8a.b, sc2.b)
                add("dve", lambda e: e.max(out=m8b.t, in_=sc2.t), sc2.b, m8b.b)
                add("dve", lambda e: e.tensor_scalar(out=sc2.t, in0=sc.t, scalar1=m8b.t[:, 7:8], scalar2=1.0, op0=ALU.is_ge,
                                                     op1=ALU.subtract), sc.b + m8b.b, sc2.b)
                add("dve", lambda e: e.tensor_scalar(out=sbias.t, in0=sc2.t, scalar1=-NEG, scalar2=None, op0=ALU.mult), sc2.b, sbias.b)
                if NSA_STOP <= 2.4:
                    continue
                transpose_to(selT.t[0:8, 0:nch, :], selT.b, [sbias.t[:, c * 8:(c + 1) * 8] for c in range(nch)], sbias.b, evac="dve")
                poc = ps_o[rr("pso", 2)]

                def f(e, poc=poc):
                    ins = None
                    for h in range(4):
                        ins = e.matmul(poc.t[:, h * P:(h + 1) * P], pcT_b.t[:, h, :], vc.t[:, :], start=True, stop=True)
                    return ins
                add("pe", f, pcT_b.b + vc.b, poc.b)
                g0 = bgate.t[:, i, :].rearrange("p (h t) -> p h t", t=3)[:, :, 0:1].to_broadcast([P, 4, P])
                add("dve", lambda e, poc=poc, g0=g0: e.tensor_tensor(out=ocmp.t, in0=poc.t[:, :].rearrange("p (h n) -> p h n", n=P), in1=g0,
                                                                     op=ALU.mult), poc.b + bgate.b, ocmp.b)
                rows = []
                for h in range(4):
                    rows.append(nsa_sel_row(i, h, nk, nkb, nch))
                    rows.append(nsa_win_row(i, h))
                emit_rows(rows)
            pg.barrier()

        def post_phase(l, final, xsrc, ydst):
            ar.reset()
            gbc = ar.alloc([D], F32)
            xin = [ar.alloc([D], F32) for _ in range(2)]
            ytmp = ar.alloc([D], F32)
            junk = ar.alloc([D], BF16)
            for kc in range(16):
                load_wout_chunk(l, kc)
            dma(gbc.t, post_g[l, :].partition_broadcast(P), [], gbc.b)
            for i in range(NQ):
                def f(e, i=i):
                    ins = None
                    for n0 in range(4):
                        for kc in range(16):
                            ins = e.matmul(ps_s.t[:, n0 * 512:(n0 + 1) * 512], mixT.t[:, kc, i * P:(i + 1) * P],
                                           hTa.t[:, kc, n0 * 512:(n0 + 1) * 512], start=(kc == 0), stop=(kc == 15))
                    return ins
                add("pe", f, mixT.b + hTa.b, ps_s.b)
                xt = xin[rr("xin", 2)]
                sap, sbufs = xsrc("q", i)
                dma(xt.t, sap, sbufs, xt.b)
                ss, ssb = newsmall()
                add("act", lambda e, ss=ss: e.activation(out=junk.t, in_=ps_s.t[:, :], func=AF.Square, accum_out=ss),
                    ps_s.b, junk.b + [ssb])
                rs, rsb = rms_scale(ss, ssb, D)
                add("dve", lambda e, rs=rs: e.scalar_tensor_tensor(out=ytmp.t, in0=ps_s.t[:, :], scalar=rs, in1=gbc.t,
                                                                   op0=ALU.mult, op1=ALU.mult),
                    ps_s.b + [rsb] + gbc.b, ytmp.b)
                add("pool", lambda e, xt=xt: e.tensor_tensor(out=ytmp.t, in0=ytmp.t, in1=xt.t, op=ALU.add),
                    ytmp.b + xt.b, ytmp.b)
                if ydst is None:
                    final.append(dma(yout[i * P:(i + 1) * P, :], ytmp.t, ytmp.b, []))
                else:
                    dap, dbufs = ydst(i)
                    dma(dap, ytmp.t, ytmp.b, dbufs)
                    if i % 2 == 1:
                        j = i // 2
                        add_cc(j)
            pg.barrier()

        final = []
        x1b = [Buf() for _ in range(4)]
        gab = [Buf() for _ in range(4)]

        def add_cc(j):
            pg.add("pool", lambda e: e.collective_compute("AllGather", ALU.bypass,
                                                          replica_groups=[[0, 1], [2, 3], [4, 5], [6, 7]],
                                                          ins=[x1own_t[j].ap().opt()], outs=[gath_t[j].ap().opt()]),
                   [x1b[j]], [gab[j]], dma="cc")

        def xsrc_in(which, t):
            if which == "a":
                return xa[t * P:(t + 1) * P, :], []
            return xq[t * P:(t + 1) * P, :], []

        def xsrc_mid(which, t):
            if which == "a":
                r_, i_ = t % 2, t // 2
                j_, k_ = i_ // 2, i_ % 2
                return gath_t[j_].ap()[r_ * 2 * P + k_ * P:r_ * 2 * P + (k_ + 1) * P, :], [gab[j_]]
            j_, k_ = t // 2, t % 2
            return x1own_t[j_].ap()[k_ * P:(k_ + 1) * P, :], [x1b[j_]]

        def ydst_mid(i):
            j_, k_ = i // 2, i % 2
            return x1own_t[j_].ap()[k_ * P:(k_ + 1) * P, :], [x1b[j_]]

        for li, l in enumerate(layers):
            xsrc = xsrc_in if li == 0 else xsrc_mid
            ydst = None if li == len(layers) - 1 else ydst_mid
            phase_norm(l, xsrc)
            for c in range(16):
                mname = MIXERS[c // 4]
                if mname not in mixers:
                    add("pool", lambda e, c=c: e.memset(mixT.t[:, c, :], 0.0), [], [mixT.b[c]])
            if "sb" in mixers:
                mixer_sb(l)
            if "nsa" in mixers:
                mixer_nsa(l)
            if "fox" in mixers:
                mixer_fox(l)
            if "mla" in mixers:
                mixer_mla(l)
            post_phase(l, final, xsrc, ydst)

        with nc.Block() as block:
            pg.emit(block, final)
    return nc


def _consts(r):
    bf = ml_dtypes.bfloat16
    c = {}
    c["c_ident_bf"] = np.eye(P, dtype=np.float32).astype(bf)
    c["c_ident_f"] = np.eye(P, dtype=np.float32)
    p = np.arange(P)[:, None]
    col = np.arange(256)[None, :]
    c["c_mask_c"] = np.where(col <= p + 128 * r, 0.0, NEG).astype(np.float32).astype(bf)
    c["c_mask_s"] = np.where(col < p + 128 * r, 0.0, NEG).astype(np.float32).astype(bf)
    colw = np.arange(768)[None, :]
    c["c_mask_w"] = np.where((colw <= 512 + 128 * r + p) & (colw > 128 * r + p), 0.0, NEG).astype(np.float32).astype(bf)
    qpos = (np.arange(NQ)[None, :] * 2 + r) * P + np.arange(P)[:, None]
    cmp_end = np.arange(P) * 16 + 31
    vis = (cmp_end[None, None, :] <= qpos[:, :, None]) & (np.arange(P)[None, None, :] < NCMP)
    c["c_mask_cmp"] = np.where(vis, 0.0, NEG).astype(np.float32).astype(bf)
    c["c_cmp01"] = vis.astype(np.float32)
    sel = np.arange(32)[None, None, :]
    cur = (qpos // 64)[:, :, None]
    forced = (sel == 0) | (sel == cur) | (sel == cur - 1)
    valid = sel <= cur
    c["c_selbias"] = np.where(forced, 1e6, 0.0).astype(np.float32)
    c["c_selvalid"] = np.where(valid, 0.0, -1e30).astype(np.float32)
    cmp_start = np.arange(NCMP) * 16
    sel_start = np.arange(32) * 64
    ov = np.clip(np.minimum(cmp_start[:, None] + 32, sel_start[None, :] + 64)
                 - np.maximum(cmp_start[:, None], sel_start[None, :]), 0, None)
    c2s = np.zeros((P, 32), np.float32)
    c2s[:NCMP] = (ov / 32).astype(np.float32)
    c["c_c2s"] = c2s
    c["c_e8"] = (np.arange(512)[None, :] // 64 == np.arange(8)[:, None]).astype(np.float32).astype(bf)

    def tables(pos, half):
        inv = (np.float32(10000.0) ** (-np.arange(half, dtype=np.float32) / np.float32(half))).astype(np.float32)
        ang = pos.astype(np.float32)[..., None] * inv
        return np.cos(ang).astype(np.float32), np.sin(ang).astype(np.float32)
    pos_all = np.arange(NT)[None, :] * P + np.arange(P)[:, None]
    c["c_cosa"], c["c_sina"] = tables(pos_all, 64)
    c["c_cosq"], c["c_sinq"] = tables(qpos, 64)
    c["c_cosc"], c["c_sinc"] = tables(cmp_end, 64)
    c["c_cosa32"], c["c_sina32"] = tables(pos_all, 32)
    c["c_cosq32"], c["c_sinq32"] = tables(qpos, 32)
    sel3 = np.zeros((P, 4, P), np.float32)
    for h in range(4):
        for rep in range(3):
            sel3[rep * 32 + h, h, :] = 1.0
    c["c_sel3"] = sel3.astype(bf)
    return c


_WNAMES = ("pre_norm_g", "post_norm_g", "w_in", "b_in", "w_out", "fox_forget_bias",
           "nsa_cmp_pos_k", "nsa_cmp_w1_k", "nsa_cmp_w2_k", "nsa_cmp_pos_v", "nsa_cmp_w1_v", "nsa_cmp_w2_v",
           "mla_q_norm_g", "mla_w_uq", "mla_kv_norm_g", "mla_w_ukv")


def _own_rows(xb, r):
    return np.ascontiguousarray(xb.reshape(NQ, 2, P, D)[:, r].reshape(NQ * P, D))


def run_layers(x, weights, layers, dbg=None, mixers=MIXERS):
    nc = build(layers, dbg, mixers)
    in_maps = []
    for c in range(8):
        b, r = c // 2, c % 2
        m = {"xa": np.ascontiguousarray(x[b]), "xq": _own_rows(x[b], r)}
        for n in _WNAMES:
            m[n] = weights[n]
        m.update(_consts(r))
        in_maps.append(m)
    res = run_bass_kernel_spmd(nc, in_maps, core_ids=list(range(8)))
    out = np.empty((NB, S, D), np.float32)
    for c in range(8):
        b, r = c // 2, c % 2
        out[b].reshape(NQ, 2, P, D)[:, r] = res.results[c]["y"].reshape(NQ, P, D)
    return out, res


def kernel(**inputs):
    x = np.ascontiguousarray(np.asarray(inputs["x"], dtype=np.float32))
    weights = {n: np.ascontiguousarray(np.asarray(inputs[n], dtype=np.float32)) for n in _WNAMES}
    x, _ = run_layers(x, weights, list(range(DEPTH)))
    return x
```

```python
import numpy as np
import ml_dtypes
from contextlib import ExitStack
import concourse.bass as bass
import concourse.mybir as mybir
from concourse.bass_utils import run_bass_kernel_spmd

F32 = mybir.dt.float32
BF16 = mybir.dt.bfloat16
AF = mybir.ActivationFunctionType
ALU = mybir.AluOpType
AX = mybir.AxisListType

D = 2048
S = 2048
NB = 4
DEPTH = 2
INW = 6992
NT = 16
NQ = 8
P = 128
EPS = 1e-6
NEG = -30000.0
NCMP = 127

OFF = {}
_o = 0
for _n, _w in (("sb_q", 512), ("sb_k", 512), ("sb_v", 512), ("sb_gate", 512),
               ("nsa_q", 512), ("nsa_k_cmp", 128), ("nsa_v_cmp", 128), ("nsa_k_sel", 128),
               ("nsa_v_sel", 128), ("nsa_k_win", 128), ("nsa_v_win", 128), ("nsa_branch", 12),
               ("nsa_gate", 512), ("fox_q", 512), ("fox_k", 512), ("fox_v", 512), ("fox_f", 4),
               ("fox_gate", 512), ("mla_cq", 384), ("mla_ckv", 128), ("mla_k_rope", 64),
               ("mla_gate", 512)):
    OFF[_n] = _o
    _o += _w
assert _o == INW


_ALL_BUFS = []


class Buf:
    __slots__ = ("lw", "rd", "rd_dma")

    def __init__(self):
        self.lw = None
        self.rd = {}
        self.rd_dma = []
        _ALL_BUFS.append(self)


class Op:
    __slots__ = ("eng", "fn", "deps", "signal", "count", "is_dma", "dsem", "dval", "dprev")


class Prog:
    ENGS = ("pe", "act", "dve", "pool", "sp")

    def __init__(self, nc, stack, n_dma_sems=12):
        self.nc = nc
        self.ops = {e: [] for e in self.ENGS}
        self.esem = {e: stack.enter_context(nc.semaphore("es_" + e)) for e in self.ENGS}
        self.dsems = {}
        self.dcount = {}
        self.drr = {}
        for e in ("sp", "pool", "act"):
            self.dsems[e] = [stack.enter_context(nc.semaphore("ds_%s%d" % (e, i))) for i in range(n_dma_sems)]
            self.dcount[e] = [0] * n_dma_sems
            self.drr[e] = 0
        self.dsems["cc"] = [stack.enter_context(nc.semaphore("cc_sem"))]
        self.dcount["cc"] = [0]

    def add(self, eng, fn, reads=(), writes=(), dma=False):
        op = Op()
        op.eng = eng
        op.fn = fn
        op.signal = False
        op.count = 0
        op.is_dma = dma
        op.dsem = None
        op.dval = 0
        op.dprev = 0
        me = (eng, len(self.ops[eng]))
        deps = set()
        for b in reads:
            if b.lw is not None:
                deps.add(b.lw)
        for b in writes:
            if b.lw is not None:
                deps.add(b.lw)
            for e2, i2 in b.rd.items():
                deps.add((e2, i2))
            for d in b.rd_dma:
                deps.add(d)
        needed = []
        for d in deps:
            if d == me:
                continue
            dop = self.ops[d[0]][d[1]]
            if dop.is_dma:
                needed.append(d)
            elif d[0] == eng and eng == "pe":
                continue
            else:
                dop.signal = True
                needed.append(d)
        op.deps = needed
        if dma == "cc":
            op.dsem = ("cc", 0)
            op.dprev = 0
            self.dcount["cc"][0] += 1
            op.dval = self.dcount["cc"][0]
        elif dma:
            k = self.drr[eng]
            self.drr[eng] = (k + 1) % len(self.dsems[eng])
            op.dsem = (eng, k)
            op.dprev = self.dcount[eng][k] * 16
            self.dcount[eng][k] += 1
            op.dval = self.dcount[eng][k] * 16
        self.ops[eng].append(op)
        for b in writes:
            b.lw = me
            b.rd = {}
            b.rd_dma = []
        wset = set(id(b) for b in writes)
        for b in reads:
            if id(b) in wset:
                continue
            if dma:
                b.rd_dma.append(me)
            else:
                b.rd[eng] = me[1]
        return me

    def barrier(self):
        deps = set()
        for b in _ALL_BUFS:
            if b.lw is not None:
                deps.add(b.lw)
            for e2, i2 in b.rd.items():
                deps.add((e2, i2))
            for d in b.rd_dma:
                deps.add(d)
        for e in self.ENGS:
            if self.ops[e]:
                last = (e, len(self.ops[e]) - 1)
                if not self.ops[e][-1].is_dma and self.ops[e][-1].fn is not None:
                    deps.add(last)
        for e in self.ENGS:
            op = Op()
            op.eng = e
            op.fn = None
            op.signal = False
            op.count = 0
            op.is_dma = False
            op.dsem = None
            op.dval = 0
            op.dprev = 0
            mx = {}
            needed = []
            for d in deps:
                dop = self.ops[d[0]][d[1]]
                if dop.is_dma:
                    needed.append(d)
                else:
                    if d[0] == e and e == "pe":
                        continue
                    mx[d[0]] = max(mx.get(d[0], -1), d[1])
            for e2, i2 in mx.items():
                self.ops[e2][i2].signal = True
                needed.append((e2, i2))
            op.deps = needed
            self.ops[e].append(op)
        for b in _ALL_BUFS:
            b.lw = None
            b.rd = {}
            b.rd_dma = []

    def emit(self, block, final_waits):
        nc = self.nc
        for e in self.ENGS:
            c = 0
            for op in self.ops[e]:
                if op.signal and not op.is_dma:
                    c += 1
                    op.count = c
        prog = self

        def run(e, engobj):
            waited = {}
            for op in prog.ops[e]:
                for d in op.deps:
                    dop = prog.ops[d[0]][d[1]]
                    if dop.is_dma:
                        key = ("d",) + dop.dsem
                        val = dop.dval
                        sem = prog.dsems[dop.dsem[0]][dop.dsem[1]]
                    else:
                        key = ("e", d[0])
                        val = dop.count
                        sem = prog.esem[d[0]]
                    if waited.get(key, 0) >= val:
                        continue
                    engobj.wait_ge(sem, val)
                    waited[key] = val
                if op.fn is None:
                    continue
                if op.is_dma:
                    key = ("d",) + op.dsem
                    sem = prog.dsems[op.dsem[0]][op.dsem[1]]
                    if op.dprev > 0 and waited.get(key, 0) < op.dprev:
                        engobj.wait_ge(sem, op.dprev)
                        waited[key] = op.dprev
                    ins = op.fn(engobj)
                    if op.dsem[0] == "cc":
                        ins.then_inc(sem)
                    else:
                        ins.then_inc(sem, 16)
                else:
                    ins = op.fn(engobj)
                    if op.signal:
                        ins.then_inc(prog.esem[e], 1)
            if e == "sp":
                for d in final_waits:
                    dop = prog.ops[d[0]][d[1]]
                    sem = prog.dsems[dop.dsem[0]][dop.dsem[1]]
                    engobj.wait_ge(sem, dop.dval)

        @block.tensor
        def _(eng):
            run("pe", eng)

        @block.scalar
        def _(eng):
            run("act", eng)

        @block.vector
        def _(eng):
            run("dve", eng)

        @block.gpsimd
        def _(eng):
            run("pool", eng)

        @block.sync
        def _(eng):
            run("sp", eng)


class T:
    def __init__(self, t, nbuf=1):
        self.t = t
        self.b = [Buf() for _ in range(nbuf)]


ARENA_BYTES = 50 * 1024 + 512
MIXERS = ("sb", "nsa", "fox", "mla")


class Arena:
    def __init__(self, base):
        self.base = base
        self.off = 0

    def reset(self, to=0):
        self.off = to

    def alloc(self, shape, dt, nbuf=1):
        n = 1
        for v in shape:
            n *= v
        nbytes = n * (4 if dt == F32 else 2)
        nbytes = (nbytes + 7) // 8 * 8
        assert self.off + nbytes <= ARENA_BYTES, ("arena overflow", self.off, nbytes)
        ap = self.base[:, self.off // 2:(self.off + nbytes) // 2]
        self.off += nbytes
        if dt == F32:
            ap = ap.bitcast(F32)
        ap = ap[:, 0:n]
        if len(shape) == 2:
            ap = ap.rearrange("p (a b) -> p a b", a=shape[0])
        elif len(shape) == 3:
            ap = ap.rearrange("p (a b c) -> p a b c", a=shape[0], b=shape[1])
        return T(ap, nbuf)


NSA_STOP = 99


def build(layers, dbg=None, mixers=MIXERS):
    del _ALL_BUFS[:]
    nc = bass.Bass("TRN2", target_bir_lowering=False)
    dr = {}

    def din(name, shape, dt=F32):
        dr[name] = nc.dram_tensor(name, list(shape), dt, kind="ExternalInput").ap()
        return dr[name]

    xa = din("xa", [S, D])
    xq = din("xq", [NQ * P, D])
    pre_g = din("pre_norm_g", [DEPTH, D])
    post_g = din("post_norm_g", [DEPTH, D])
    w_in = din("w_in", [DEPTH, D, INW])
    b_in = din("b_in", [DEPTH, INW])
    w_out = din("w_out", [DEPTH, D, D])
    fox_fb = din("fox_forget_bias", [DEPTH, 4])
    pos_kv = [din("nsa_cmp_pos_k", [DEPTH, 32, 128]), din("nsa_cmp_pos_v", [DEPTH, 32, 128])]
    w1_kv = [din("nsa_cmp_w1_k", [DEPTH, 4096, 128]), din("nsa_cmp_w1_v", [DEPTH, 4096, 128])]
    w2_kv = [din("nsa_cmp_w2_k", [DEPTH, 128, 128]), din("nsa_cmp_w2_v", [DEPTH, 128, 128])]
    qng = din("mla_q_norm_g", [DEPTH, 384])
    w_uq = din("mla_w_uq", [DEPTH, 384, 768])
    kvng = din("mla_kv_norm_g", [DEPTH, 128])
    w_ukv = din("mla_w_ukv", [DEPTH, 128, 1024])
    c_ident_bf = din("c_ident_bf", [P, P], BF16)
    c_ident_f = din("c_ident_f", [P, P])
    c_mask_c = din("c_mask_c", [P, 256], BF16)
    c_mask_s = din("c_mask_s", [P, 256], BF16)
    c_mask_w = din("c_mask_w", [P, 768], BF16)
    c_mask_cmp = din("c_mask_cmp", [P, NQ, P], BF16)
    c_cmp01 = din("c_cmp01", [P, NQ, P])
    c_selbias = din("c_selbias", [P, NQ, 32])
    c_selvalid = din("c_selvalid", [P, NQ, 32])
    c_c2s = din("c_c2s", [P, 32])
    c_e8 = din("c_e8", [8, 512], BF16)
    c_cosa = din("c_cosa", [P, NT, 64])
    c_sina = din("c_sina", [P, NT, 64])
    c_cosq = din("c_cosq", [P, NQ, 64])
    c_sinq = din("c_sinq", [P, NQ, 64])
    c_cosc = din("c_cosc", [P, 64])
    c_sinc = din("c_sinc", [P, 64])
    c_cosa32 = din("c_cosa32", [P, NT, 32])
    c_sina32 = din("c_sina32", [P, NT, 32])
    c_cosq32 = din("c_cosq32", [P, NQ, 32])
    c_sinq32 = din("c_sinq32", [P, NQ, 32])
    c_sel3 = din("c_sel3", [P, 4, P], BF16)
    yout = nc.dram_tensor("y", [NQ * P, D], F32, kind="ExternalOutput").ap()
    x1own_t = [nc.dram_tensor("x1own%d" % j, [2 * P, D], F32) for j in range(4)]
    gath_t = [nc.dram_tensor("gath%d" % j, [4 * P, D], F32) for j in range(4)]

    with ExitStack() as st:
        pg = Prog(nc, st)

        def sb(name, shape, dt, nbuf=1):
            return T(st.enter_context(nc.sbuf_tensor(name, list(shape), dt)), nbuf)

        def ps(name, shape, dt, nbuf=1):
            return T(st.enter_context(nc.psum_tensor(name, list(shape), dt)), nbuf)

        hTa = sb("hTa", [P, 16, S], BF16, 16)
        hTq = sb("hTq", [P, 16, NQ * P], BF16, 16)
        mixT = sb("mixT", [P, 16, NQ * P], BF16, 16)
        wst = [sb("wst%d" % i, [P, 16, P], F32) for i in range(2)]
        wbf = [sb("wbf%d" % i, [P, 16, P], BF16) for i in range(2)]
        ident_bf = sb("ident_bf", [P, P], BF16)
        ident_f = sb("ident_f", [P, P], F32)
        mask_c = sb("mask_c", [P, 256], BF16)
        mask_s = sb("mask_s", [P, 256], BF16)
        mask_w = sb("mask_w", [P, 768], BF16)
        small = sb("small", [P, 64], F32, 64)
        small4 = sb("small4", [P, 64], F32, 16)
        bias_fm = [sb("bias_fm%d" % i, [P, 1], F32) for i in range(3)]
        bias_bc = [sb("bias_bc%d" % i, [P, P], F32) for i in range(2)]
        arena_t = st.enter_context(nc.sbuf_tensor("arena", [P, ARENA_BYTES // 2], BF16))
        ar = Arena(arena_t)
        ps_s = ps("ps_s", [P, 2048], F32, 4)
        ps_t = [ps("ps_t%d" % i, [P, 1024], BF16) for i in range(2)]
        ps_o = [ps("ps_o%d" % i, [P, 512], F32) for i in range(2)]

        cnt = {}

        def rr(key, n):
            v = cnt.get(key, 0)
            cnt[key] = v + 1
            return v % n

        def newsmall(w=1):
            if w == 1:
                i = rr("sm", 64)
                return small.t[:, i:i + 1], small.b[i]
            i = rr("sm4", 16)
            return small4.t[:, i * 4:i * 4 + 4], small4.b[i]

        def view(tobj, ap):
            r = T(ap, 0)
            r.b = tobj.b
            return r

        def dma(out_ap, in_ap, reads, writes, q="sp"):
            return pg.add(q, lambda e: e.dma_start(out=out_ap, in_=in_ap), reads, writes, dma=True)

        for tile_, src_ in ((ident_bf, c_ident_bf), (ident_f, c_ident_f), (mask_c, c_mask_c),
                            (mask_s, c_mask_s), (mask_w, c_mask_w)):
            dma(tile_.t[:], src_, [], tile_.b)

        def add(eng, fn, reads, writes):
            return pg.add(eng, fn, reads, writes)

        def transpose_to(dst, dst_bufs, src_aps, src_bufs, evac="act", f32=False):
            n = len(src_aps)
            w = src_aps[0].shape[-1]
            rows = src_aps[0].shape[0]
            if f32:
                k = rr("pso", 2)
                pt = ps_o[k]
                idt = ident_f
                assert n <= 4
            else:
                k = rr("pst", 2)
                pt = ps_t[k]
                idt = ident_bf

            def f(e):
                ins = None
                for j, a in enumerate(src_aps):
                    ins = e.transpose(pt.t[0:w, j * P:j * P + rows], a, idt.t[0:rows, 0:rows])
                return ins
            add("pe", f, list(src_bufs) + idt.b, pt.b)
            if rows == P:
                src = pt.t[0:w, 0:n * P]
                if len(dst.shape) == 3:
                    src = src.rearrange("p (a b) -> p a b", b=P)
            elif n == 1:
                src = pt.t[0:w, 0:rows]
            else:
                src = pt.t[0:w, 0:n * P].rearrange("p (a b) -> p a b", b=P)[:, :, 0:rows]
            if evac == "act":
                add("act", lambda e: e.copy(dst, src), pt.b, dst_bufs)
            else:
                add("dve", lambda e: e.tensor_copy(dst, src), pt.b, dst_bufs)

        def load_w(src, nkc, ncols):
            s = rr("wst", 2)
            k = rr("wbf", 2)
            ws, wb = wst[s], wbf[k]
            wsv = ws.t[:, :, :].rearrange("p a b -> p (a b)")[:, 0:nkc * ncols].rearrange("p (a b) -> p a b", b=ncols)
            wbv = wb.t[:, :, :].rearrange("p a b -> p (a b)")[:, 0:nkc * ncols].rearrange("p (a b) -> p a b", b=ncols)
            dma(wsv, src.rearrange("(kc p) c -> p kc c", p=P), [], ws.b)
            if nkc >= 2:
                hk = nkc // 2
                add("dve", lambda e: e.tensor_copy(wbv[:, 0:hk, :], wsv[:, 0:hk, :]), ws.b, wb.b)
                add("act", lambda e: e.copy(wbv[:, hk:nkc, :], wsv[:, hk:nkc, :]), ws.b, wb.b)
            else:
                add("dve", lambda e: e.tensor_copy(wbv, wsv), ws.b, wb.b)
            return wbv, wb.b

        def load_bias_fm(l, c0, ncols):
            k = rr("bfm", 3)
            bt = bias_fm[k]
            dma(bt.t[0:ncols, :], b_in[l, c0:c0 + ncols].rearrange("(c o) -> c o", o=1), [], bt.b)
            return bt

        def load_bias_bc(l, c0, ncols):
            k = rr("bbc", 2)
            bt = bias_bc[k]
            dma(bt.t[:, 0:ncols], b_in[l, c0:c0 + ncols].partition_broadcast(P), [], bt.b)
            return bt

        def proj_fm(l, c0, ncols, src, ntok, dst_fn, dst_bufs):
            wbv, wbb = load_w(w_in[l, :, c0:c0 + ncols], 16, ncols)
            bt = load_bias_fm(l, c0, ncols)
            for t0 in range(0, ntok, 512):
                po = ps_o[rr("pso", 2)]

                def f(e, t0=t0, po=po):
                    ins = None
                    for kc in range(16):
                        ins = e.matmul(po.t[0:ncols, :], wbv[:, kc, :], src.t[:, kc, t0:t0 + 512],
                                       start=(kc == 0), stop=(kc == 15))
                    return ins
                add("pe", f, wbb + src.b, po.b)
                dst = dst_fn(t0, 512)
                add("act", lambda e, dst=dst, po=po: e.activation(out=dst, in_=po.t[0:ncols, :], func=AF.Identity,
                                                                   bias=bt.t[0:ncols, :], scale=1.0),
                    po.b + bt.b, dst_bufs)

        def proj_tm(l, c0, ncols, src, nblk, dst_fn, dst_bufs, blk0=0, w=None):
            if w is None:
                wbv, wbb = load_w(w_in[l, :, c0:c0 + ncols], 16, ncols)
                bt = load_bias_bc(l, c0, ncols)
            else:
                wbv, wbb, bt = w
            for b0 in range(blk0, blk0 + nblk, 4):
                po = ps_o[rr("pso", 2)]

                def f(e, b0=b0, po=po):
                    ins = None
                    for j in range(4):
                        for kc in range(16):
                            ins = e.matmul(po.t[:, j * P:j * P + ncols], src.t[:, kc, (b0 + j) * P:(b0 + j + 1) * P],
                                           wbv[:, kc, :], start=(kc == 0), stop=(kc == 15))
                    return ins
                add("pe", f, wbb + src.b, po.b)
                dst = dst_fn(b0, 4)
                pin = po.t[:, :].rearrange("p (j c) -> p j c", c=P)[:, :, 0:ncols]
                bb = bt.t[:, 0:ncols].unsqueeze(1).to_broadcast([P, 4, ncols])
                add("dve", lambda e, dst=dst, pin=pin, bb=bb: e.tensor_tensor(out=dst, in0=pin, in1=bb, op=ALU.add),
                    po.b + bt.b, dst_bufs)

        def gate_to_mixT(l, c0, ch):
            proj_tm(l, c0, P, hTq, NQ,
                    lambda b0, n: mixT.t[:, ch, b0 * P:(b0 + n) * P].rearrange("p (j c) -> p j c", c=P), [mixT.b[ch]])
            add("act", lambda e: e.activation(out=mixT.t[:, ch, :], in_=mixT.t[:, ch, :], func=AF.Silu),
                [mixT.b[ch]], [mixT.b[ch]])

        def rope_tm(x, xb, cos, sin, tb, out, ob, half, tmp):
            x1, x2 = x[:, :, 0:half], x[:, :, half:2 * half]
            o1, o2 = out[:, :, 0:half], out[:, :, half:2 * half]
            t1, t2 = tmp
            add("dve", lambda e: e.tensor_tensor(out=t1.t, in0=x1, in1=cos, op=ALU.mult), xb + tb, t1.b)
            add("pool", lambda e: e.tensor_tensor(out=t2.t, in0=x2, in1=sin, op=ALU.mult), xb + tb, t2.b)
            add("dve", lambda e: e.tensor_tensor(out=o1, in0=t1.t, in1=t2.t, op=ALU.subtract), t1.b + t2.b, ob)
            add("dve", lambda e: e.tensor_tensor(out=t1.t, in0=x2, in1=cos, op=ALU.mult), xb + tb + ob, t1.b)
            add("pool", lambda e: e.tensor_tensor(out=t2.t, in0=x1, in1=sin, op=ALU.mult), xb + tb + ob, t2.b)
            add("dve", lambda e: e.tensor_tensor(out=o2, in0=t1.t, in1=t2.t, op=ALU.add), t1.b + t2.b, ob)

        def rms_scale(ss, ssb, n):
            ms, msb = newsmall()
            add("dve", lambda e: e.tensor_scalar(out=ms, in0=ss, scalar1=1.0 / n, scalar2=EPS, op0=ALU.mult, op1=ALU.add),
                [ssb], [msb])
            sd, sdb = newsmall()
            add("act", lambda e: e.sqrt(sd, ms), [msb], [sdb])
            rs, rsb = newsmall()
            add("dve", lambda e: e.reciprocal(rs, sd), [sdb], [rsb])
            return rs, rsb

        bank_cur = [0]

        def alloc_banks(nch):
            if bank_cur[0] + nch > 4:
                bank_cur[0] = 0
            b0 = bank_cur[0]
            bank_cur[0] = (b0 + nch) % 4
            return b0

        def softmax_row(nk, scale, pb, base=0):
            nch = (nk + 511) // 512
            o0 = base * 512
            bufs = ps_s.b[base:base + nch]
            mx, mxb = newsmall()
            add("dve", lambda e: e.reduce_max(out=mx, in_=ps_s.t[:, o0:o0 + nk], axis=AX.X), bufs, [mxb])
            nm, nmb = newsmall()
            add("dve", lambda e: e.tensor_scalar(out=nm, in0=mx, scalar1=-scale, scalar2=None, op0=ALU.mult), [mxb], [nmb])
            l1, l1b = newsmall()
            add("act", lambda e: e.activation(out=pb.t[:, 0:nk], in_=ps_s.t[:, o0:o0 + nk], func=AF.Exp, bias=nm, scale=scale,
                                              accum_out=l1), bufs + [nmb], pb.b + [l1b])
            ri, rib = newsmall()
            add("dve", lambda e: e.reciprocal(ri, l1), [l1b], [rib])
            return ri, rib

        def pv_accumulate(nkb, pb, vt, po, pTs, kb_off=0):
            for g0 in range(0, nkb, 8):
                gn = min(8, nkb - g0)
                ptile = pTs[rr("pT", 2)]
                transpose_to(ptile.t[:, 0:gn, :], ptile.b,
                             [pb.t[:, (g0 + j) * P:(g0 + j + 1) * P] for j in range(gn)], pb.b,
                             evac="act" if (g0 // 8) % 2 == 0 else "dve")

                def f(e, g0=g0, gn=gn, ptile=ptile):
                    ins = None
                    for j in range(gn):
                        kb = g0 + j
                        ins = e.matmul(po.t[:, 0:P], ptile.t[:, j, :], vt.t[:, kb_off + kb, :],
                                       start=(kb == 0), stop=(kb == nkb - 1))
                    return ins
                add("pe", f, ptile.b + vt.b, po.b)

        def finish_head(i, src_ap, src_bufs, rinv, rinvb, ch, mts):
            mt = mts[rr("mixtm", 2)]
            gate = mixT.t[:, ch, i * P:(i + 1) * P]
            if rinv is not None:
                add("dve", lambda e: e.scalar_tensor_tensor(out=mt.t, in0=src_ap, scalar=rinv, in1=gate,
                                                            op0=ALU.mult, op1=ALU.mult),
                    src_bufs + [rinvb, mixT.b[ch]], mt.b)
            else:
                add("dve", lambda e: e.tensor_tensor(out=mt.t, in0=src_ap, in1=gate, op=ALU.mult),
                    src_bufs + [mixT.b[ch]], mt.b)
            transpose_to(mixT.t[:, ch, i * P:(i + 1) * P], [mixT.b[ch]], [mt.t], mt.b, evac="act")

        def alloc_attn_common():
            pbs = [ar.alloc([S], BF16) for _ in range(2)]
            pTs = [ar.alloc([8, P], BF16) for _ in range(2)]
            mts = [ar.alloc([P], BF16) for _ in range(2)]
            return pbs, pTs, mts

        def alloc_head_ops(n=2):
            return [(ar.alloc([S], BF16), ar.alloc([NT, P], BF16), ar.alloc([NQ * P], BF16)) for _ in range(n)]

        def phase_norm(l, xsrc):
            ar.reset()
            gbc = ar.alloc([D], F32)
            xin = [ar.alloc([D], F32) for _ in range(2)]
            hn = ar.alloc([D], BF16)
            dma(gbc.t, pre_g[l, :].partition_broadcast(P), [], gbc.b)
            for (which, nblk, dst) in (("a", NT, hTa), ("q", NQ, hTq)):
                for t in range(nblk):
                    xt = xin[rr("xin", 2)]
                    sap, sbufs = xsrc(which, t)
                    dma(xt.t, sap, sbufs, xt.b)
                    ss, ssb = newsmall()
                    add("act", lambda e, xt=xt, ss=ss: e.activation(out=hn.t, in_=xt.t, func=AF.Square, accum_out=ss),
                        xt.b, hn.b + [ssb])
                    rs, rsb = rms_scale(ss, ssb, D)
                    add("dve", lambda e, xt=xt, rs=rs: e.scalar_tensor_tensor(out=hn.t, in0=xt.t, scalar=rs, in1=gbc.t,
                                                                              op0=ALU.mult, op1=ALU.mult),
                        xt.b + [rsb] + gbc.b, hn.b)
                    for half in range(2):
                        c0 = half * 8
                        transpose_to(dst.t[:, c0:c0 + 8, t * P:(t + 1) * P], dst.b[c0:c0 + 8],
                                     [hn.t[:, (c0 + j) * P:(c0 + j + 1) * P] for j in range(8)], hn.b,
                                     evac="act" if half == 0 else "dve")
            pg.barrier()

        def peek_banks(nch):
            b0 = bank_cur[0] if bank_cur[0] + nch <= 4 else 0
            return set(range(b0, b0 + nch))

        def emit_rows(rows, after=None):
            cur_banks = set()
            if rows:
                cur_banks = peek_banks(rows[0][3])
                rows[0][0]()
            for k in range(len(rows)):
                nxt_banks = set()
                early = False
                if k + 1 < len(rows):
                    nxt_banks = peek_banks(rows[k + 1][3])
                    if not (nxt_banks & cur_banks):
                        rows[k + 1][0]()
                        early = True
                rows[k][1]()
                if k + 1 < len(rows) and not early:
                    nxt_banks = peek_banks(rows[k + 1][3])
                    rows[k + 1][0]()
                rows[k][2]()
                cur_banks = nxt_banks
                if after is not None:
                    after(k)

        def run_pipelined(nheads, proj_items, att_rows):
            for t in proj_items(0):
                t()
            for h in range(nheads):
                nxt = proj_items(h + 1) if h + 1 < nheads else []
                rows = att_rows(h)
                st_ = {"k": 0}

                def after(ai, nxt=nxt, rows=rows, st_=st_):
                    want = (ai + 1) * len(nxt) // len(rows)
                    while st_["k"] < want:
                        nxt[st_["k"]]()
                        st_["k"] += 1
                emit_rows(rows, after)
                while st_["k"] < len(nxt):
                    nxt[st_["k"]]()
                    st_["k"] += 1

        def mixer_sb(l):
            ar.reset()
            scale = 128 ** -0.5
            pbs, pTs, mts = alloc_attn_common()
            hops = alloc_head_ops(2)
            wk_sets = []
            for _ in range(3):
                a_, b_, c_ = [ar.alloc([512], F32) for _ in range(3)]
                wk_sets.append([a_, b_, c_, a_])
            def proj_items(h):
                kTt, vt, qTt = hops[h % 2]
                return [
                    lambda: proj_fm(l, OFF["sb_q"] + h * P, P, hTq, NQ * P, lambda t0, n: qTt.t[:, t0:t0 + n], qTt.b),
                    lambda: proj_fm(l, OFF["sb_k"] + h * P, P, hTa, S, lambda t0, n: kTt.t[:, t0:t0 + n], kTt.b),
                    lambda: proj_tm(l, OFF["sb_v"] + h * P, P, hTa, NT, lambda b0, n: vt.t[:, b0:b0 + n, :], vt.b),
                    lambda: gate_to_mixT(l, OFF["sb_gate"] + h * P, 0 + h),
                ]

            def sb_steps(h, i):
                kTt, vt, qTt = hops[h % 2]
                nkb = 2 * i + 2
                nk = nkb * P
                nch = (nk + 511) // 512
                pb = pbs[rr("pbf", 2)]
                st_ = {"carry": None}
                steps = []
                for c in range(nch - 1, -1, -1):
                    k0 = c * 512
                    n = min(512, nk - k0)
                    last = (c == nch - 1)
                    loc = {}

                    def s1(c=c, k0=k0, n=n, last=last, loc=loc):
                        bank = ps_s.b[c]
                        wk_e, wk_sp, wk_c, wk_t = wk_sets[rr("wk", 3)]
                        loc["wk_t"] = wk_t

                        def f(e):
                            ins = e.matmul(ps_s.t[:, k0:k0 + n], qTt.t[:, i * P:(i + 1) * P], kTt.t[:, k0:k0 + n],
                                           start=True, stop=not last)
                            if last:
                                ins = e.matmul(ps_s.t[:, nk - 256:nk], ident_bf.t[:], mask_s.t[:, 0:256], start=False, stop=True)
                            return ins
                        add("pe", f, qTt.b + kTt.b + ident_bf.b + mask_s.b, [bank])
                        sv = ps_s.t[:, k0:k0 + n]
                        add("act", lambda e: e.activation(out=wk_e.t[:, 0:n], in_=sv, func=AF.Exp, scale=scale), [bank], wk_e.b)
                        add("act", lambda e: e.activation(out=wk_sp.t[:, 0:n], in_=wk_e.t[:, 0:n], func=AF.Ln, bias=1.0, scale=1.0),
                            wk_e.b, wk_sp.b)
                        add("dve", lambda e: e.tensor_tensor_scan(out=wk_c.t[:, 0:n], data0=wk_sp.t[:, 0:n], data1=wk_sp.t[:, 0:n],
                                                                  initial=0.0, op0=ALU.add, op1=ALU.max), wk_sp.b, wk_c.b)
                        add("dve", lambda e: e.scalar_tensor_tensor(out=wk_t.t[:, 0:n], in0=sv, scalar=scale, in1=wk_sp.t[:, 0:n],
                                                                    op0=ALU.mult, op1=ALU.subtract), [bank] + wk_sp.b, wk_t.b)
                        add("pool", lambda e: e.tensor_tensor(out=wk_t.t[:, 0:n], in0=wk_t.t[:, 0:n], in1=wk_c.t[:, 0:n], op=ALU.add),
                            wk_t.b + wk_c.b, wk_t.b)
                        nb_, nbb = newsmall()
                        tot = wk_c.t[:, n - 1:n]
                        if st_["carry"] is None:
                            add("dve", lambda e: e.tensor_scalar(out=nb_, in0=tot, scalar1=-1.0, scalar2=None, op0=ALU.mult),
                                wk_c.b, [nbb])
                        else:
                            cpr, cprb = st_["carry"]
                            add("dve", lambda e: e.scalar_tensor_tensor(out=nb_, in0=tot, scalar=-1.0, in1=cpr, op0=ALU.mult,
                                                                        op1=ALU.add), wk_c.b + [cprb], [nbb])
                        st_["carry"] = (nb_, nbb)
                        loc["nb"] = (nb_, nbb)

                    def s2(k0=k0, n=n, loc=loc):
                        wk_t = loc["wk_t"]
                        nb_, nbb = loc["nb"]
                        add("act", lambda e: e.activation(out=pb.t[:, k0:k0 + n], in_=wk_t.t[:, 0:n], func=AF.Exp, bias=nb_, scale=1.0),
                            wk_t.b + [nbb], pb.b)

                    fin = None
                    if c == 0:
                        def fin():
                            po = ps_o[rr("pso", 2)]
                            pv_accumulate(nkb, pb, vt, po, pTs)
                            finish_head(i, po.t[:, 0:P], po.b, None, None, 0 + h, mts)
                    steps.append((s1, s2, fin))
                return steps

            for t in proj_items(0):
                t()
            for h in range(4):
                nxt = proj_items(h + 1) if h + 1 < 4 else []
                kq = 0
                pend = []
                for i in range(NQ):
                    for stp in sb_steps(h, i):
                        stp[0]()
                        pend.append(stp)
                        if len(pend) > 2:
                            p0 = pend.pop(0)
                            p0[1]()
                            if p0[2] is not None:
                                p0[2]()
                    want = (i + 1) * len(nxt) // NQ
                    while kq < want:
                        nxt[kq]()
                        kq += 1
                for p0 in pend:
                    p0[1]()
                    if p0[2] is not None:
                        p0[2]()
                while kq < len(nxt):
                    nxt[kq]()
                    kq += 1
            pg.barrier()

        def softmax_heads_causal(i, qTt, kTt, vt, scale, pbs, pTs, mts, ch, extra=None, extra_bufs=()):
            nkb = 2 * i + 2
            nk = nkb * P
            nch = (nk + 511) // 512
            st_ = {}

            def A():
                base = alloc_banks(nch)
                st_["base"] = base
                o0 = base * 512

                def f(e):
                    ins = None
                    for c in range(nch):
                        k0 = c * 512
                        n = min(512, nk - k0)
                        last = (c == nch - 1)
                        ins = e.matmul(ps_s.t[:, o0 + k0:o0 + k0 + n], qTt.t[:, i * P:(i + 1) * P], kTt.t[:, k0:k0 + n],
                                       start=True, stop=(not last) and extra is None)
                        if extra is not None:
                            ins = extra(e, ps_s.t[:, o0 + k0:o0 + k0 + n], k0, n, not last)
                        if last:
                            ins = e.matmul(ps_s.t[:, o0 + nk - 256:o0 + nk], ident_bf.t[:], mask_c.t[:, 0:256], start=False, stop=True)
                    return ins
                add("pe", f, qTt.b + kTt.b + ident_bf.b + mask_c.b + list(extra_bufs), ps_s.b[base:base + nch])

            def B1():
                pb = pbs[rr("pbf", 2)]
                st_["pb"] = pb
                st_["r"] = softmax_row(nk, scale, pb, st_["base"])

            def B2():
                pb = st_["pb"]
                rinv, rinvb = st_["r"]
                po = ps_o[rr("pso", 2)]
                pv_accumulate(nkb, pb, vt, po, pTs)
                finish_head(i, po.t[:, 0:P], po.b, rinv, rinvb, ch, mts)
            return A, B1, B2, nch

        def mixer_fox(l):
            ar.reset()
            scale = 128 ** -0.5
            pbs, pTs, mts = alloc_attn_common()
            hops = alloc_head_ops(2)
            crow3 = ar.alloc([S], BF16)
            sel3 = ar.alloc([4, P], BF16)
            t_e = ar.alloc([512], F32)
            t_sp = ar.alloc([512], F32)
            cch = [ar.alloc([512], F32) for _ in range(2)]
            hi_t = ar.alloc([512], BF16)
            mid_t = ar.alloc([512], BF16)
            wf68 = ar.alloc([16, 68], BF16)
            fb = ar.alloc([2], F32)
            NP_ = 68
            dma(sel3.t, c_sel3, [], sel3.b)
            add("pool", lambda e: e.memset(crow3.t, 0.0), [], crow3.b)
            add("pool", lambda e: e.memset(wf68.t, 0.0), [], wf68.b)
            add("pool", lambda e: e.memset(fb.t, 0.0), [], fb.b)
            c0 = OFF["fox_f"]
            wbv, wbb = load_w(w_in[l, :, c0:c0 + 4], 16, 4)
            for rep in range(3):
                add("dve", lambda e, rep=rep: e.tensor_copy(wf68.t[:, :, rep * 32:rep * 32 + 4], wbv), wbb + wf68.b, wf68.b)
                dma(fb.t[rep * 32:rep * 32 + 4, 0:1], b_in[l, c0:c0 + 4].rearrange("(c o) -> c o", o=1), fb.b, fb.b)
                dma(fb.t[rep * 32:rep * 32 + 4, 1:2], fox_fb[l, :].rearrange("(c o) -> c o", o=1), fb.b, fb.b)
            nb_, nbb = newsmall()
            add("dve", lambda e: e.scalar_tensor_tensor(out=nb_[0:NP_, :], in0=fb.t[0:NP_, 0:1], scalar=-1.0, in1=fb.t[0:NP_, 1:2],
                                                        op0=ALU.mult, op1=ALU.subtract), fb.b, [nbb])
            prevc = None
            for ci, t0 in enumerate(range(0, S, 512)):
                po = ps_o[rr("pso", 2)]
                cc = cch[ci % 2]

                def f(e, t0=t0, po=po):
                    ins = None
                    for kc in range(16):
                        ins = e.matmul(po.t[0:NP_, :], wf68.t[:, kc, :], hTa.t[:, kc, t0:t0 + 512], start=(kc == 0), stop=(kc == 15))
                    return ins
                add("pe", f, wf68.b + hTa.b, po.b)
                add("act", lambda e, po=po: e.activation(out=t_e.t[0:NP_, :], in_=po.t[0:NP_, :], func=AF.Exp, bias=nb_[0:NP_, :],
                                                         scale=-1.0), po.b + [nbb], t_e.b)
                add("act", lambda e: e.activation(out=t_sp.t[0:NP_, :], in_=t_e.t[0:NP_, :], func=AF.Ln, bias=1.0, scale=1.0),
                    t_e.b, t_sp.b)
                init = 0.0 if prevc is None else prevc.t[0:NP_, 511:512]
                rdb = t_sp.b + ([] if prevc is None else prevc.b)
                add("dve", lambda e, cc=cc, init=init: e.tensor_tensor_scan(out=cc.t[0:NP_, :], data0=t_sp.t[0:NP_, :],
                                                                            data1=t_sp.t[0:NP_, :], initial=init,
                                                                            op0=ALU.add, op1=ALU.max), rdb, cc.b)
                add("dve", lambda e, cc=cc: e.tensor_scalar(out=t_e.t[0:NP_, :], in0=cc.t[0:NP_, :], scalar1=float(128 ** 0.5),
                                                            scalar2=None, op0=ALU.mult), cc.b, t_e.b)
                add("dve", lambda e: e.tensor_copy(hi_t.t[0:NP_, :], t_e.t[0:NP_, :]), t_e.b, hi_t.b)
                add("dve", lambda e: e.tensor_tensor(out=t_sp.t[0:NP_, :], in0=t_e.t[0:NP_, :], in1=hi_t.t[0:NP_, :], op=ALU.subtract),
                    t_e.b + hi_t.b, t_sp.b)
                add("dve", lambda e: e.tensor_copy(mid_t.t[0:NP_, :], t_sp.t[0:NP_, :]), t_sp.b, mid_t.b)
                add("dve", lambda e: e.tensor_tensor(out=t_sp.t[0:NP_, :], in0=t_sp.t[0:NP_, :], in1=mid_t.t[0:NP_, :], op=ALU.subtract),
                    t_sp.b + mid_t.b, t_sp.b)
                add("dve", lambda e, t0=t0: e.tensor_copy(crow3.t[0:4, t0:t0 + 512], hi_t.t[0:4, :]), hi_t.b, crow3.b)
                add("dve", lambda e, t0=t0: e.tensor_copy(crow3.t[32:36, t0:t0 + 512], mid_t.t[32:36, :]), mid_t.b, crow3.b)
                add("dve", lambda e, t0=t0: e.tensor_copy(crow3.t[64:68, t0:t0 + 512], t_sp.t[64:68, :]), t_sp.b, crow3.b)
                prevc = cc
            def proj_items(h):
                kTt, vt, qTt = hops[h % 2]
                return [
                    lambda: proj_fm(l, OFF["fox_q"] + h * P, P, hTq, NQ * P, lambda t0, n: qTt.t[:, t0:t0 + n], qTt.b),
                    lambda: proj_fm(l, OFF["fox_k"] + h * P, P, hTa, S, lambda t0, n: kTt.t[:, t0:t0 + n], kTt.b),
                    lambda: proj_tm(l, OFF["fox_v"] + h * P, P, hTa, NT, lambda b0, n: vt.t[:, b0:b0 + n, :], vt.b),
                    lambda: gate_to_mixT(l, OFF["fox_gate"] + h * P, 8 + h),
                ]

            def att_items(h):
                kTt, vt, qTt = hops[h % 2]

                def extra(e, out, k0, n, stop):
                    return e.matmul(out, sel3.t[:, h, :], crow3.t[:, k0:k0 + n], start=False, stop=stop)
                return [softmax_heads_causal(i, qTt, kTt, vt, scale, pbs, pTs, mts, 8 + h, extra, sel3.b + crow3.b)
                        for i in range(NQ)]
            run_pipelined(4, proj_items, att_items)
            pg.barrier()

        wout_done = {}

        def load_wout_chunk(l, kc):
            if (l, kc) in wout_done:
                return
            wout_done[(l, kc)] = True
            ws = wst[rr("wst", 2)]
            wsv = ws.t[:, :, :].rearrange("p a b -> p (a b)")
            dma(wsv, w_out[l, kc * P:(kc + 1) * P, :], [], ws.b)
            eng = "pool" if kc % 2 == 0 else "dve"
            add(eng, lambda e: e.tensor_copy(hTa.t[:, kc, :], wsv), ws.b, [hTa.b[kc]])

        def mixer_mla(l):
            ar.reset()
            scale = 192 ** -0.5
            cqnT = ar.alloc([3, NQ * P], BF16)
            ckvnT = ar.alloc([S], BF16)
            krT = ar.alloc([S], BF16)
            wukv_b = ar.alloc([1024], BF16)
            cosq = ar.alloc([NQ, 32], F32)
            sinq = ar.alloc([NQ, 32], F32)
            mark = ar.off
            dma(cosq.t, c_cosq32, [], cosq.b)
            dma(sinq.t, c_sinq32, [], sinq.b)
            cosa = ar.alloc([NT, 32], F32)
            sina = ar.alloc([NT, 32], F32)
            gq = ar.alloc([384], F32)
            gkv = ar.alloc([P], F32)
            xt = ar.alloc([4, 384], F32)
            xn = ar.alloc([4, 384], BF16)
            t1 = ar.alloc([4, 32], F32)
            t2 = ar.alloc([4, 32], F32)
            dma(cosa.t, c_cosa32, [], cosa.b)
            dma(sina.t, c_sina32, [], sina.b)
            dma(gq.t, qng[l, :].partition_broadcast(P), [], gq.b)
            dma(gkv.t, kvng[l, :].partition_broadcast(P), [], gkv.b)
            ws = wst[rr("wst", 2)]
            wsv = ws.t[:, :, :].rearrange("p a b -> p (a b)")[:, 0:1024]
            dma(wsv, w_ukv[l, :, :], [], ws.b)
            add("pool", lambda e, wsv=wsv: e.tensor_copy(wukv_b.t, wsv), ws.b, wukv_b.b)

            def norm_rows(x_ap, xb, width, g_ap, gb, out_ap, ob):
                ss, ssb = newsmall()
                jt, jb = junkt
                add("act", lambda e: e.activation(out=jt[:, 0:width], in_=x_ap, func=AF.Square, accum_out=ss), xb, jb + [ssb])
                rs, rsb = rms_scale(ss, ssb, width)
                add("dve", lambda e: e.scalar_tensor_tensor(out=out_ap, in0=x_ap, scalar=rs, in1=g_ap, op0=ALU.mult, op1=ALU.mult),
                    xb + [rsb] + gb, ob)
            jk = ar.alloc([384], F32)
            junkt = (jk.t, jk.b)
            for b0 in range(0, NQ, 4):
                for cg in range(3):
                    wbv, wbb = load_w(w_in[l, :, OFF["mla_cq"] + cg * P:OFF["mla_cq"] + (cg + 1) * P], 16, P)
                    bt = load_bias_bc(l, OFF["mla_cq"] + cg * P, P)
                    po = ps_o[rr("pso", 2)]

                    def f(e, b0=b0, po=po, wbv=wbv):
                        ins = None
                        for j in range(4):
                            for kc in range(16):
                                ins = e.matmul(po.t[:, j * P:(j + 1) * P], hTq.t[:, kc, (b0 + j) * P:(b0 + j + 1) * P],
                                               wbv[:, kc, :], start=(kc == 0), stop=(kc == 15))
                        return ins
                    add("pe", f, wbb + hTq.b, po.b)
                    pin = po.t[:, :].rearrange("p (j c) -> p j c", c=P)
                    bb = bt.t[:, :].unsqueeze(1).to_broadcast([P, 4, P])
                    add("dve", lambda e, cg=cg, pin=pin, bb=bb: e.tensor_tensor(out=xt.t[:, :, cg * P:(cg + 1) * P], in0=pin, in1=bb,
                                                                                op=ALU.add), po.b + bt.b, xt.b)
                for j in range(4):
                    norm_rows(xt.t[:, j, :], xt.b, 384, gq.t, gq.b, xn.t[:, j, :], xn.b)
                for cg in range(3):
                    transpose_to(cqnT.t[:, cg, b0 * P:(b0 + 4) * P], cqnT.b,
                                 [xn.t[:, j, cg * P:(cg + 1) * P] for j in range(4)], xn.b, evac="dve")
            xk = ar.alloc([4, P], F32)
            xkn = ar.alloc([4, P], BF16)
            xr = ar.alloc([4, 64], F32)
            xrb = ar.alloc([4, 64], BF16)
            wbv1, wbb1 = load_w(w_in[l, :, OFF["mla_ckv"]:OFF["mla_ckv"] + P], 16, P)
            bt1 = load_bias_bc(l, OFF["mla_ckv"], P)
            wbv2, wbb2 = load_w(w_in[l, :, OFF["mla_k_rope"]:OFF["mla_k_rope"] + 64], 16, 64)
            bt2 = load_bias_bc(l, OFF["mla_k_rope"], 64)
            for b0 in range(0, NT, 4):
                po = ps_o[rr("pso", 2)]

                def f(e, b0=b0, po=po):
                    ins = None
                    for j in range(4):
                        for kc in range(16):
                            ins = e.matmul(po.t[:, j * P:(j + 1) * P], hTa.t[:, kc, (b0 + j) * P:(b0 + j + 1) * P],
                                           wbv1[:, kc, :], start=(kc == 0), stop=(kc == 15))
                    return ins
                add("pe", f, wbb1 + hTa.b, po.b)
                pin = po.t[:, :].rearrange("p (j c) -> p j c", c=P)
                bb = bt1.t[:, :].unsqueeze(1).to_broadcast([P, 4, P])
                add("dve", lambda e, pin=pin, bb=bb: e.tensor_tensor(out=xk.t, in0=pin, in1=bb, op=ALU.add), po.b + bt1.b, xk.b)
                for j in range(4):
                    norm_rows(xk.t[:, j, :], xk.b, P, gkv.t, gkv.b, xkn.t[:, j, :], xkn.b)
                transpose_to(ckvnT.t[:, b0 * P:(b0 + 4) * P], ckvnT.b, [xkn.t[:, j, :] for j in range(4)], xkn.b, evac="dve")
                po2 = ps_o[rr("pso", 2)]

                def f2(e, b0=b0, po2=po2):
                    ins = None
                    for j in range(4):
                        for kc in range(16):
                            ins = e.matmul(po2.t[:, j * 64:(j + 1) * 64], hTa.t[:, kc, (b0 + j) * P:(b0 + j + 1) * P],
                                           wbv2[:, kc, :], start=(kc == 0), stop=(kc == 15))
                    return ins
                add("pe", f2, wbb2 + hTa.b, po2.b)
                pin2 = po2.t[:, 0:256].rearrange("p (j c) -> p j c", c=64)
                bb2 = bt2.t[:, 0:64].unsqueeze(1).to_broadcast([P, 4, 64])
                add("dve", lambda e, pin2=pin2, bb2=bb2: e.tensor_tensor(out=xr.t, in0=pin2, in1=bb2, op=ALU.add), po2.b + bt2.b, xr.b)
                rope_tm(xr.t, xr.b, cosa.t[:, b0:b0 + 4, :], sina.t[:, b0:b0 + 4, :], cosa.b + sina.b, xrb.t, xrb.b, 32, (t1, t2))
                transpose_to(krT.t[0:64, b0 * P:(b0 + 4) * P], krT.b, [xrb.t[:, j, :] for j in range(4)], xrb.b, evac="dve")
            pg.barrier()
            ar.reset(mark)
            pbs, pTs, mts = alloc_attn_common()
            kTt = ar.alloc([S], BF16)
            vt = ar.alloc([NT, P], BF16)
            qTt = ar.alloc([NQ * P], BF16)
            qrT = ar.alloc([NQ * P], BF16)
            wuqh = ar.alloc([3, 192], BF16)
            qr_f = ar.alloc([NQ, 64], F32)
            qr_b = ar.alloc([NQ, 64], BF16)
            t1 = ar.alloc([NQ, 32], F32)
            t2 = ar.alloc([NQ, 32], F32)
            for h in range(4):
                ws = wst[rr("wst", 2)]
                wsv = ws.t[:, :, :].rearrange("p a b -> p (a b)")[:, 0:576].rearrange("p (a b) -> p a b", b=192)
                dma(wsv, w_uq[l, :, h * 192:(h + 1) * 192].rearrange("(kc p) c -> p kc c", p=P), [], ws.b)
                add("pool", lambda e, wsv=wsv: e.tensor_copy(wuqh.t, wsv), ws.b, wuqh.b)
                for t0 in range(0, NQ * P, 512):
                    po = ps_o[rr("pso", 2)]

                    def f(e, t0=t0, po=po):
                        ins = None
                        for kc in range(3):
                            ins = e.matmul(po.t[:, :], wuqh.t[:, kc, 0:P], cqnT.t[:, kc, t0:t0 + 512], start=(kc == 0), stop=(kc == 2))
                        return ins
                    add("pe", f, wuqh.b + cqnT.b, po.b)
                    add("act", lambda e, t0=t0, po=po: e.copy(qTt.t[:, t0:t0 + 512], po.t[:, :]), po.b, qTt.b)
                for b0 in range(0, NQ, 4):
                    po = ps_o[rr("pso", 2)]

                    def f(e, b0=b0, po=po):
                        ins = None
                        for j in range(4):
                            for kc in range(3):
                                ins = e.matmul(po.t[:, j * 64:(j + 1) * 64], cqnT.t[:, kc, (b0 + j) * P:(b0 + j + 1) * P],
                                               wuqh.t[:, kc, P:192], start=(kc == 0), stop=(kc == 2))
                        return ins
                    add("pe", f, wuqh.b + cqnT.b, po.b)
                    add("act", lambda e, b0=b0, po=po: e.copy(qr_f.t[:, b0:b0 + 4, :], po.t[:, 0:256].rearrange("p (j c) -> p j c", c=64)),
                        po.b, qr_f.b)
                rope_tm(qr_f.t, qr_f.b, cosq.t, sinq.t, cosq.b + sinq.b, qr_b.t, qr_b.b, 32, (t1, t2))
                transpose_to(qrT.t[0:64, :], qrT.b, [qr_b.t[:, j, :] for j in range(NQ)], qr_b.b, evac="dve")
                for t0 in range(0, S, 512):
                    po = ps_o[rr("pso", 2)]
                    add("pe", lambda e, t0=t0, po=po, h=h: e.matmul(po.t[:, :], wukv_b.t[:, h * 256:h * 256 + P], ckvnT.t[:, t0:t0 + 512],
                                                                    start=True, stop=True), wukv_b.b + ckvnT.b, po.b)
                    add("act", lambda e, t0=t0, po=po: e.copy(kTt.t[:, t0:t0 + 512], po.t[:, :]), po.b, kTt.b)
                for b0 in range(0, NT, 4):
                    po = ps_o[rr("pso", 2)]

                    def f(e, b0=b0, po=po, h=h):
                        ins = None
                        for j in range(4):
                            ins = e.matmul(po.t[:, j * P:(j + 1) * P], ckvnT.t[:, (b0 + j) * P:(b0 + j + 1) * P],
                                           wukv_b.t[:, h * 256 + P:h * 256 + 256], start=True, stop=True)
                        return ins
                    add("pe", f, wukv_b.b + ckvnT.b, po.b)
                    add("dve", lambda e, b0=b0, po=po: e.tensor_copy(vt.t[:, b0:b0 + 4, :], po.t[:, :].rearrange("p (j c) -> p j c", c=P)),
                        po.b, vt.b)
                gate_to_mixT(l, OFF["mla_gate"] + h * P, 12 + h)

                rows = []
                for i in range(NQ):
                    def extra_i(e, out, k0, n, stop, i=i):
                        return e.matmul(out, qrT.t[0:64, i * P:(i + 1) * P], krT.t[0:64, k0:k0 + n], start=False, stop=stop)
                    rows.append(softmax_heads_causal(i, qTt, kTt, vt, scale, pbs, pTs, mts, 12 + h, extra_i, qrT.b + krT.b))

                def after_row(k, h=h):
                    if k % 2 == 1:
                        load_wout_chunk(l, (h * NQ + k) // 2)
                emit_rows(rows, after_row)
            pg.barrier()

        def mixer_nsa(l):
            ar.reset()
            scale = 128 ** -0.5
            qT4 = ar.alloc([4, NQ * P], BF16)
            ksT = ar.alloc([S], BF16)
            kwT = ar.alloc([S], BF16)
            vs = ar.alloc([NT, P], BF16)
            vw = ar.alloc([NT, P], BF16)
            kcT = ar.alloc([P], BF16)
            vc = ar.alloc([P], BF16)
            bgate = ar.alloc([NQ, 12], F32)
            e8 = ar.alloc([512], BF16)
            c2s = ar.alloc([32], F32)
            mark = ar.off
            dma(e8.t[0:8, :], c_e8, [], e8.b)
            dma(c2s.t, c_c2s, [], c2s.b)
            cos_t = ar.alloc([NT, 64], F32)
            sin_t = ar.alloc([NT, 64], F32)
            xf = ar.alloc([NQ, P], F32)
            xb_ = ar.alloc([NQ, P], BF16)
            t1 = ar.alloc([NQ, 64], F32)
            t2 = ar.alloc([NQ, 64], F32)
            dma(cos_t.t[:, 0:NQ, :], c_cosq, [], cos_t.b)
            dma(sin_t.t[:, 0:NQ, :], c_sinq, [], sin_t.b)
            for h in range(4):
                proj_tm(l, OFF["nsa_q"] + h * P, P, hTq, NQ, lambda b0, n: xf.t[:, b0:b0 + n, :], xf.b)
                rope_tm(xf.t, xf.b, cos_t.t[:, 0:NQ, :], sin_t.t[:, 0:NQ, :], cos_t.b + sin_t.b, xb_.t, xb_.b, 64, (t1, t2))
                transpose_to(qT4.t[:, h, :], qT4.b, [xb_.t[:, j, :] for j in range(NQ)], xb_.b, evac="dve")
            proj_tm(l, OFF["nsa_branch"], 12, hTq, NQ, lambda b0, n: bgate.t[:, b0:b0 + n, :], bgate.b)
            add("act", lambda e: e.activation(out=bgate.t, in_=bgate.t, func=AF.Sigmoid), bgate.b, bgate.b)
            dma(cos_t.t, c_cosa, xf.b + xb_.b + t1.b + t2.b, cos_t.b)
            dma(sin_t.t, c_sina, xf.b + xb_.b + t1.b + t2.b, sin_t.b)
            for (cname, dstT) in (("nsa_k_sel", ksT), ("nsa_k_win", kwT)):
                c0_ = OFF[cname]
                wbv_, wbb_ = load_w(w_in[l, :, c0_:c0_ + P], 16, P)
                bt_ = load_bias_bc(l, c0_, P)
                for g0 in (0, 8):
                    proj_tm(l, c0_, P, hTa, 8, lambda b0, n, g0=g0: xf.t[:, b0 - g0:b0 - g0 + n, :], xf.b, blk0=g0,
                            w=(wbv_, wbb_, bt_))
                    rope_tm(xf.t, xf.b, cos_t.t[:, g0:g0 + 8, :], sin_t.t[:, g0:g0 + 8, :], cos_t.b + sin_t.b, xb_.t, xb_.b, 64,
                            (t1, t2))
                    transpose_to(dstT.t[:, g0 * P:(g0 + 8) * P], dstT.b, [xb_.t[:, j, :] for j in range(8)], xb_.b, evac="dve")
            proj_tm(l, OFF["nsa_v_sel"], P, hTa, NT, lambda b0, n: vs.t[:, b0:b0 + n, :], vs.b)
            proj_tm(l, OFF["nsa_v_win"], P, hTa, NT, lambda b0, n: vw.t[:, b0:b0 + n, :], vw.b)
            pg.barrier()
            if NSA_STOP <= 1:
                return
            ar.reset(mark)
            tokT = ar.alloc([S], BF16)
            blkT = ar.alloc([32, P], BF16)
            w1b = ar.alloc([32, P], BF16)
            w2b = ar.alloc([P], BF16)
            posr = ar.alloc([P], F32)
            posT = ar.alloc([32], F32)
            hidT = ar.alloc([P], BF16)
            kcf = ar.alloc([1, P], F32)
            kcb = ar.alloc([1, P], BF16)
            cosc = ar.alloc([1, 64], F32)
            sinc = ar.alloc([1, 64], F32)
            tc1 = ar.alloc([1, 64], F32)
            tc2 = ar.alloc([1, 64], F32)
            dma(cosc.t[:, 0, :], c_cosc, [], cosc.b)
            dma(sinc.t[:, 0, :], c_sinc, [], sinc.b)
            for which in range(2):
                cname = "nsa_k_cmp" if which == 0 else "nsa_v_cmp"
                proj_fm(l, OFF[cname], P, hTa, S, lambda t0, n: tokT.t[:, t0:t0 + n], tokT.b)
                dma(posr.t[0:32, :], pos_kv[which][l, :, :], [], posr.b)
                po = ps_o[rr("pso", 2)]
                add("pe", lambda e, po=po: e.transpose(po.t[:, 0:32], posr.t[0:32, :], ident_f.t[0:32, 0:32]), posr.b + ident_f.b, po.b)
                add("act", lambda e, po=po: e.copy(posT.t, po.t[:, 0:32]), po.b, posT.b)
                for half in range(2):
                    ws = wst[rr("wst", 2)]
                    dma(ws.t[:, :, :], w1_kv[which][l, half * 2048:(half + 1) * 2048, :].rearrange("(l d) h -> d l h", d=P), [], ws.b)
                    add("pool", lambda e, half=half, ws=ws: e.tensor_copy(w1b.t[:, half * 16:(half + 1) * 16, :], ws.t[:, :, :]),
                        ws.b, w1b.b)
                ws = wst[rr("wst", 2)]
                wsv = ws.t[:, 0, :]
                dma(wsv, w2_kv[which][l, :, :], [], ws.b)
                add("pool", lambda e, wsv=wsv: e.tensor_copy(w2b.t, wsv), ws.b, w2b.b)
                for ll in range(32):
                    src = tokT.t[:, ll:ll + 16 * (NCMP - 1) + 1:16]
                    eng = "dve" if ll % 2 == 0 else "pool"
                    add(eng, lambda e, ll=ll, src=src: e.tensor_scalar(out=blkT.t[:, ll, 0:NCMP], in0=src, scalar1=posT.t[:, ll:ll + 1],
                                                                       scalar2=None, op0=ALU.add), tokT.b + posT.b, blkT.b)
                po = ps_o[rr("pso", 2)]

                def f(e, po=po):
                    ins = None
                    for ll in range(32):
                        ins = e.matmul(po.t[:, 0:NCMP], w1b.t[:, ll, :], blkT.t[:, ll, 0:NCMP], start=(ll == 0), stop=(ll == 31))
                    return ins
                add("pe", f, w1b.b + blkT.b, po.b)
                add("act", lambda e, po=po: e.activation(out=hidT.t[:, 0:NCMP], in_=po.t[:, 0:NCMP], func=AF.Silu), po.b, hidT.b)
                po2 = ps_o[rr("pso", 2)]
                add("pe", lambda e, po2=po2: e.matmul(po2.t[0:NCMP, 0:P], hidT.t[:, 0:NCMP], w2b.t, start=True, stop=True),
                    hidT.b + w2b.b, po2.b)
                if which == 0:
                    add("act", lambda e, po2=po2: e.copy(kcf.t[0:NCMP, 0, :], po2.t[0:NCMP, 0:P]), po2.b, kcf.b)
                    rope_tm(kcf.t[0:NCMP], kcf.b, cosc.t[0:NCMP], sinc.t[0:NCMP], cosc.b + sinc.b, kcb.t[0:NCMP], kcb.b, 64,
                            (view(tc1, tc1.t[0:NCMP]), view(tc2, tc2.t[0:NCMP])))
                    add("pool", lambda e: e.memset(kcT.t, 0.0), [], kcT.b)
                    transpose_to(kcT.t[:, 0:NCMP], kcT.b, [kcb.t[0:NCMP, 0, :]], kcb.b, evac="dve")
                else:
                    add("pool", lambda e: e.memset(vc.t, 0.0), [], vc.b)
                    add("act", lambda e, po2=po2: e.copy(vc.t[0:NCMP, :], po2.t[0:NCMP, 0:P]), po2.b, vc.b)
            for h in range(4):
                gate_to_mixT(l, OFF["nsa_gate"] + h * P, 4 + h)
            pg.barrier()
            if NSA_STOP <= 2:
                return
            ar.reset(mark)
            pbs, pTs, mts = alloc_attn_common()
            mcmp2 = [ar.alloc([P], BF16) for _ in range(2)]
            cmp012 = [ar.alloc([P], F32) for _ in range(2)]
            selb2 = [ar.alloc([32], F32) for _ in range(2)]
            selv2 = [ar.alloc([32], F32) for _ in range(2)]
            ef = ar.alloc([4, P], F32)
            pcf = ef
            pcb = ar.alloc([4, P], BF16)
            ps4 = ar.alloc([P], F32)
            impA = ar.alloc([32], F32)
            imp = ar.alloc([32], F32)
            pcT_b = ar.alloc([4, P], BF16)
            ocmp = ar.alloc([4, P], F32)
            sc = ar.alloc([32], F32)
            sc2 = ar.alloc([32], F32)
            m8a = ar.alloc([8], F32)
            m8b = ar.alloc([8], F32)
            sbias = ar.alloc([32], BF16)
            selT = ar.alloc([4, P], BF16)
            accs = [ar.alloc([P], F32) for _ in range(2)]
            def nsa_sel_row(i, h, nk, nkb, nch):
                st_ = {}

                def A():
                    base = alloc_banks(nch)
                    st_["base"] = base
                    o0 = base * 512

                    def f(e):
                        ins = None
                        for c in range(nch):
                            k0 = c * 512
                            n = min(512, nk - k0)
                            last = (c == nch - 1)
                            e.matmul(ps_s.t[:, o0 + k0:o0 + k0 + n], qT4.t[:, h, i * P:(i + 1) * P], ksT.t[:, k0:k0 + n], start=True, stop=False)
                            ins = e.matmul(ps_s.t[:, o0 + k0:o0 + k0 + n], selT.t[0:8, c, :], e8.t[0:8, 0:n], start=False, stop=not last)
                            if last:
                                ins = e.matmul(ps_s.t[:, o0 + nk - 256:o0 + nk], ident_bf.t[:], mask_c.t[:, 0:256], start=False, stop=True)
                        return ins
                    add("pe", f, qT4.b + ksT.b + selT.b + e8.b + ident_bf.b + mask_c.b, ps_s.b[base:base + nch])

                def B1():
                    pb = pbs[rr("pbf", 2)]
                    st_["pb"] = pb
                    st_["r"] = softmax_row(nk, scale, pb, st_["base"])

                def B2():
                    pb = st_["pb"]
                    rs_, rsb_ = st_["r"]
                    pos_ = ps_o[rr("pso", 2)]
                    pv_accumulate(nkb, pb, vs, pos_, pTs)
                    cs, csb = newsmall()
                    add("dve", lambda e: e.tensor_tensor(out=cs, in0=rs_, in1=bgate.t[:, i, 3 * h + 1:3 * h + 2], op=ALU.mult),
                        [rsb_] + bgate.b, [csb])
                    a_ = accs[h % 2]
                    add("dve", lambda e: e.scalar_tensor_tensor(out=a_.t, in0=pos_.t[:, 0:P], scalar=cs, in1=ocmp.t[:, h, :],
                                                                op0=ALU.mult, op1=ALU.add), pos_.b + [csb] + ocmp.b, a_.b)
                return A, B1, B2, nch

            def nsa_win_row(i, h):
                kb0 = max(0, 2 * i - 4)
                nkbw = 2 * i + 2 - kb0
                nkw = nkbw * P
                moff = (kb0 - (2 * i - 4)) * P
                nchw = (nkw + 511) // 512
                st_ = {}

                def A():
                    basew = alloc_banks(nchw)
                    st_["base"] = basew
                    o0 = basew * 512

                    def f(e):
                        ins = None
                        for c in range(nchw):
                            k0 = c * 512
                            n = min(512, nkw - k0)
                            e.matmul(ps_s.t[:, o0 + k0:o0 + k0 + n], qT4.t[:, h, i * P:(i + 1) * P], kwT.t[:, kb0 * P + k0:kb0 * P + k0 + n],
                                     start=True, stop=False)
                            ins = e.matmul(ps_s.t[:, o0 + k0:o0 + k0 + n], ident_bf.t[:], mask_w.t[:, moff + k0:moff + k0 + n], start=False, stop=True)
                        return ins
                    add("pe", f, qT4.b + kwT.b + ident_bf.b + mask_w.b, ps_s.b[basew:basew + nchw])

                def B1():
                    pb = pbs[rr("pbf", 2)]
                    st_["pb"] = pb
                    st_["r"] = softmax_row(nkw, scale, pb, st_["base"])

                def B2():
                    pb = st_["pb"]
                    rw_, rwb_ = st_["r"]
                    pow_ = ps_o[rr("pso", 2)]
                    pv_accumulate(nkbw, pb, vw, pow_, pTs, kb_off=kb0)
                    cw, cwb = newsmall()
                    add("dve", lambda e: e.tensor_tensor(out=cw, in0=rw_, in1=bgate.t[:, i, 3 * h + 2:3 * h + 3], op=ALU.mult),
                        [rwb_] + bgate.b, [cwb])
                    a_ = accs[h % 2]
                    add("dve", lambda e: e.scalar_tensor_tensor(out=a_.t, in0=pow_.t[:, 0:P], scalar=cw, in1=a_.t, op0=ALU.mult, op1=ALU.add),
                        pow_.b + [cwb] + a_.b, a_.b)
                    finish_head(i, a_.t, a_.b, None, None, 4 + h, mts)
                return A, B1, B2, nchw

            for i in range(NQ):
                nkb = 2 * i + 2
                nk = nkb * P
                nch = (nk + 511) // 512
                mcmp, cmp01, selb, selv = mcmp2[i % 2], cmp012[i % 2], selb2[i % 2], selv2[i % 2]
                dma(mcmp.t, c_mask_cmp[:, i, :], [], mcmp.b)
                dma(cmp01.t, c_cmp01[:, i, :], [], cmp01.b)
                dma(selb.t, c_selbias[:, i, :], [], selb.b)
                dma(selv.t, c_selvalid[:, i, :], [], selv.b)
                pz = ps_o[rr("pso", 2)]

                def f(e, i=i, pz=pz, mcmp=mcmp):
                    ins = None
                    for h in range(4):
                        e.matmul(pz.t[:, h * P:(h + 1) * P], qT4.t[:, h, i * P:(i + 1) * P], kcT.t[:, :], start=True, stop=False)
                        ins = e.matmul(pz.t[:, h * P:(h + 1) * P], ident_bf.t[:], mcmp.t[:, :], start=False, stop=True)
                    return ins
                add("pe", f, qT4.b + kcT.b + ident_bf.b + mcmp.b, pz.b)
                mx4, mx4b = newsmall(4)
                pz3 = pz.t[:, :].rearrange("p (h n) -> p h n", n=P)
                add("dve", lambda e, pz3=pz3, mx4=mx4: e.tensor_reduce(out=mx4, in_=pz3, axis=AX.X, op=ALU.max), pz.b, [mx4b])
                nm4, nm4b = newsmall(4)
                add("dve", lambda e, mx4=mx4, nm4=nm4: e.tensor_scalar(out=nm4, in0=mx4, scalar1=-scale, scalar2=None, op0=ALU.mult),
                    [mx4b], [nm4b])
                for h in range(4):
                    add("act", lambda e, h=h, pz=pz, nm4=nm4: e.activation(out=ef.t[:, h, :], in_=pz.t[:, h * P:(h + 1) * P], func=AF.Exp,
                                                                          bias=nm4[:, h:h + 1], scale=scale), pz.b + [nm4b], ef.b)
                m01 = cmp01.t[:, :].unsqueeze(1).to_broadcast([P, 4, P])
                add("dve", lambda e, m01=m01: e.tensor_tensor(out=ef.t, in0=ef.t, in1=m01, op=ALU.mult), ef.b + cmp01.b, ef.b)
                l4, l4b = newsmall(4)
                add("dve", lambda e, l4=l4: e.tensor_reduce(out=l4, in_=ef.t, axis=AX.X, op=ALU.add), ef.b, [l4b])
                r4, r4b = newsmall(4)
                add("dve", lambda e, l4=l4, r4=r4: e.tensor_scalar(out=r4, in0=l4, scalar1=1e-30, scalar2=None, op0=ALU.max), [l4b], [r4b])
                r4i, r4ib = newsmall(4)
                add("dve", lambda e, r4=r4, r4i=r4i: e.reciprocal(r4i, r4), [r4b], [r4ib])
                add("dve", lambda e, r4i=r4i: e.tensor_tensor(out=pcf.t, in0=ef.t, in1=r4i.unsqueeze(2).to_broadcast([P, 4, P]), op=ALU.mult),
                    ef.b + [r4ib], pcf.b)
                if NSA_STOP <= 2.1:
                    continue
                add("pool", lambda e: e.tensor_copy(pcb.t, pcf.t), pcf.b, pcb.b)
                transpose_to(pcT_b.t, pcT_b.b, [pcb.t[:, h, :] for h in range(4)], pcb.b, evac="act")
                if NSA_STOP <= 2.2:
                    continue
                add("dve", lambda e: e.tensor_reduce(out=ps4.t, in_=pcf.t.rearrange("p h n -> p n h"), axis=AX.X, op=ALU.add),
                    pcf.b, ps4.b)
                ps4v = ps4.t.rearrange("p (s j) -> p s j", j=4)
                add("dve", lambda e, ps4v=ps4v: e.tensor_reduce(out=impA.t, in_=ps4v, axis=AX.X, op=ALU.add), ps4.b, impA.b)
                v3 = ps4v[:, :, 3]
                add("dve", lambda e, v3=v3: e.scalar_tensor_tensor(out=imp.t, in0=v3, scalar=-0.5, in1=impA.t, op0=ALU.mult, op1=ALU.add),
                    ps4.b + impA.b, imp.b)
                add("dve", lambda e, v3=v3: e.scalar_tensor_tensor(out=imp.t[:, 1:32], in0=v3[:, 0:31], scalar=0.5, in1=imp.t[:, 1:32],
                                                                   op0=ALU.mult, op1=ALU.add), ps4.b + imp.b, imp.b)
                if NSA_STOP <= 2.25:
                    continue
                add("dve", lambda e, i=i, selb=selb: e.tensor_tensor(out=sc.t, in0=imp.t, in1=selb.t[:, :], op=ALU.max),
                    imp.b + selb.b, sc.b)
                add("dve", lambda e, i=i, selv=selv: e.tensor_tensor(out=sc.t, in0=sc.t, in1=selv.t[:, :], op=ALU.add), sc.b + selv.b, sc.b)
                if NSA_STOP <= 2.3:
                    continue
                add("dve", lambda e: e.max(out=m8a.t, in_=sc.t), sc.b, m8a.b)
                add("dve", lambda e: e.match_replace(out=sc2.t, in_to_replace=m8a.t, in_values=sc.t, imm_value=-3.0e38),
                    sc.b + m8a.b, sc2.b)
                add("dve", lambda e: e.max(out=m8b.t, in_=sc2.t), sc2.b, m8b.b)
                add("dve", lambda e: e.tensor_scalar(out=sc2.t, in0=sc.t, scalar1=m8b.t[:, 7:8], scalar2=1.0, op0=ALU.is_ge,
                                                     op1=ALU.subtract), sc.b + m8b.b, sc2.b)
                add("dve", lambda e: e.tensor_scalar(out=sbias.t, in0=sc2.t, scalar1=-NEG, scalar2=None, op0=ALU.mult), sc2.b, sbias.b)
                if NSA_STOP <= 2.4:
                    continue
                transpose_to(selT.t[0:8, 0:nch, :], selT.b, [sbias.t[:, c * 8:(c + 1) * 8] for c in range(nch)], sbias.b, evac="dve")
                poc = ps_o[rr("pso", 2)]

                def f(e, poc=poc):
                    ins = None
                    for h in range(4):
                        ins = e.matmul(poc.t[:, h * P:(h + 1) * P], pcT_b.t[:, h, :], vc.t[:, :], start=True, stop=True)
                    return ins
                add("pe", f, pcT_b.b + vc.b, poc.b)
                g0 = bgate.t[:, i, :].rearrange("p (h t) -> p h t", t=3)[:, :, 0:1].to_broadcast([P, 4, P])
                add("dve", lambda e, poc=poc, g0=g0: e.tensor_tensor(out=ocmp.t, in0=poc.t[:, :].rearrange("p (h n) -> p h n", n=P), in1=g0,
                                                                     op=ALU.mult), poc.b + bgate.b, ocmp.b)
                rows = []
                for h in range(4):
                    rows.append(nsa_sel_row(i, h, nk, nkb, nch))
                    rows.append(nsa_win_row(i, h))
                emit_rows(rows)
            pg.barrier()

        def post_phase(l, final, xsrc, ydst):
            ar.reset()
            gbc = ar.alloc([D], F32)
            xin = [ar.alloc([D], F32) for _ in range(2)]
            ytmp = ar.alloc([D], F32)
            junk = ar.alloc([D], BF16)
            for kc in range(16):
                load_wout_chunk(l, kc)
            dma(gbc.t, post_g[l, :].partition_broadcast(P), [], gbc.b)
            for i in range(NQ):
                def f(e, i=i):
                    ins = None
                    for n0 in range(4):
                        for kc in range(16):
                            ins = e.matmul(ps_s.t[:, n0 * 512:(n0 + 1) * 512], mixT.t[:, kc, i * P:(i + 1) * P],
                                           hTa.t[:, kc, n0 * 512:(n0 + 1) * 512], start=(kc == 0), stop=(kc == 15))
                    return ins
                add("pe", f, mixT.b + hTa.b, ps_s.b)
                xt = xin[rr("xin", 2)]
                sap, sbufs = xsrc("q", i)
                dma(xt.t, sap, sbufs, xt.b)
                ss, ssb = newsmall()
                add("act", lambda e, ss=ss: e.activation(out=junk.t, in_=ps_s.t[:, :], func=AF.Square, accum_out=ss),
                    ps_s.b, junk.b + [ssb])
                rs, rsb = rms_scale(ss, ssb, D)
                add("dve", lambda e, rs=rs: e.scalar_tensor_tensor(out=ytmp.t, in0=ps_s.t[:, :], scalar=rs, in1=gbc.t,
                                                                   op0=ALU.mult, op1=ALU.mult),
                    ps_s.b + [rsb] + gbc.b, ytmp.b)
                add("pool", lambda e, xt=xt: e.tensor_tensor(out=ytmp.t, in0=ytmp.t, in1=xt.t, op=ALU.add),
                    ytmp.b + xt.b, ytmp.b)
                if ydst is None:
                    final.append(dma(yout[i * P:(i + 1) * P, :], ytmp.t, ytmp.b, []))
                else:
                    dap, dbufs = ydst(i)
                    dma(dap, ytmp.t, ytmp.b, dbufs)
                    if i % 2 == 1:
                        j = i // 2
                        add_cc(j)
            pg.barrier()

        final = []
        x1b = [Buf() for _ in range(4)]
        gab = [Buf() for _ in range(4)]

        def add_cc(j):
            pg.add("pool", lambda e: e.collective_compute("AllGather", ALU.bypass,
                                                          replica_groups=[[0, 1], [2, 3], [4, 5], [6, 7]],
                                                          ins=[x1own_t[j].ap().opt()], outs=[gath_t[j].ap().opt()]),
                   [x1b[j]], [gab[j]], dma="cc")

        def xsrc_in(which, t):
            if which == "a":
                return xa[t * P:(t + 1) * P, :], []
            return xq[t * P:(t + 1) * P, :], []

        def xsrc_mid(which, t):
            if which == "a":
                r_, i_ = t % 2, t // 2
                j_, k_ = i_ // 2, i_ % 2
                return gath_t[j_].ap()[r_ * 2 * P + k_ * P:r_ * 2 * P + (k_ + 1) * P, :], [gab[j_]]
            j_, k_ = t // 2, t % 2
            return x1own_t[j_].ap()[k_ * P:(k_ + 1) * P, :], [x1b[j_]]

        def ydst_mid(i):
            j_, k_ = i // 2, i % 2
            return x1own_t[j_].ap()[k_ * P:(k_ + 1) * P, :], [x1b[j_]]

        for li, l in enumerate(layers):
            xsrc = xsrc_in if li == 0 else xsrc_mid
            ydst = None if li == len(layers) - 1 else ydst_mid
            phase_norm(l, xsrc)
            for c in range(16):
                mname = MIXERS[c // 4]
                if mname not in mixers:
                    add("pool", lambda e, c=c: e.memset(mixT.t[:, c, :], 0.0), [], [mixT.b[c]])
            if "sb" in mixers:
                mixer_sb(l)
            if "nsa" in mixers:
                mixer_nsa(l)
            if "fox" in mixers:
                mixer_fox(l)
            if "mla" in mixers:
                mixer_mla(l)
            post_phase(l, final, xsrc, ydst)

        with nc.Block() as block:
            pg.emit(block, final)
    return nc


def _consts(r):
    bf = ml_dtypes.bfloat16
    c = {}
    c["c_ident_bf"] = np.eye(P, dtype=np.float32).astype(bf)
    c["c_ident_f"] = np.eye(P, dtype=np.float32)
    p = np.arange(P)[:, None]
    col = np.arange(256)[None, :]
    c["c_mask_c"] = np.where(col <= p + 128 * r, 0.0, NEG).astype(np.float32).astype(bf)
    c["c_mask_s"] = np.where(col < p + 128 * r, 0.0, NEG).astype(np.float32).astype(bf)
    colw = np.arange(768)[None, :]
    c["c_mask_w"] = np.where((colw <= 512 + 128 * r + p) & (colw > 128 * r + p), 0.0, NEG).astype(np.float32).astype(bf)
    qpos = (np.arange(NQ)[None, :] * 2 + r) * P + np.arange(P)[:, None]
    cmp_end = np.arange(P) * 16 + 31
    vis = (cmp_end[None, None, :] <= qpos[:, :, None]) & (np.arange(P)[None, None, :] < NCMP)
    c["c_mask_cmp"] = np.where(vis, 0.0, NEG).astype(np.float32).astype(bf)
    c["c_cmp01"] = vis.astype(np.float32)
    sel = np.arange(32)[None, None, :]
    cur = (qpos // 64)[:, :, None]
    forced = (sel == 0) | (sel == cur) | (sel == cur - 1)
    valid = sel <= cur
    c["c_selbias"] = np.where(forced, 1e6, 0.0).astype(np.float32)
    c["c_selvalid"] = np.where(valid, 0.0, -1e30).astype(np.float32)
    cmp_start = np.arange(NCMP) * 16
    sel_start = np.arange(32) * 64
    ov = np.clip(np.minimum(cmp_start[:, None] + 32, sel_start[None, :] + 64)
                 - np.maximum(cmp_start[:, None], sel_start[None, :]), 0, None)
    c2s = np.zeros((P, 32), np.float32)
    c2s[:NCMP] = (ov / 32).astype(np.float32)
    c["c_c2s"] = c2s
    c["c_e8"] = (np.arange(512)[None, :] // 64 == np.arange(8)[:, None]).astype(np.float32).astype(bf)

    def tables(pos, half):
        inv = (np.float32(10000.0) ** (-np.arange(half, dtype=np.float32) / np.float32(half))).astype(np.float32)
        ang = pos.astype(np.float32)[..., None] * inv
        return np.cos(ang).astype(np.float32), np.sin(ang).astype(np.float32)
    pos_all = np.arange(NT)[None, :] * P + np.arange(P)[:, None]
    c["c_cosa"], c["c_sina"] = tables(pos_all, 64)
    c["c_cosq"], c["c_sinq"] = tables(qpos, 64)
    c["c_cosc"], c["c_sinc"] = tables(cmp_end, 64)
    c["c_cosa32"], c["c_sina32"] = tables(pos_all, 32)
    c["c_cosq32"], c["c_sinq32"] = tables(qpos, 32)
    sel3 = np.zeros((P, 4, P), np.float32)
    for h in range(4):
        for rep in range(3):
            sel3[rep * 32 + h, h, :] = 1.0
    c["c_sel3"] = sel3.astype(bf)
    return c


_WNAMES = ("pre_norm_g", "post_norm_g", "w_in", "b_in", "w_out", "fox_forget_bias",
           "nsa_cmp_pos_k", "nsa_cmp_w1_k", "nsa_cmp_w2_k", "nsa_cmp_pos_v", "nsa_cmp_w1_v", "nsa_cmp_w2_v",
           "mla_q_norm_g", "mla_w_uq", "mla_kv_norm_g", "mla_w_ukv")


def _own_rows(xb, r):
    return np.ascontiguousarray(xb.reshape(NQ, 2, P, D)[:, r].reshape(NQ * P, D))


def run_layers(x, weights, layers, dbg=None, mixers=MIXERS):
    nc = build(layers, dbg, mixers)
    in_maps = []
    for c in range(8):
        b, r = c // 2, c % 2
        m = {"xa": np.ascontiguousarray(x[b]), "xq": _own_rows(x[b], r)}
        for n in _WNAMES:
            m[n] = weights[n]
        m.update(_consts(r))
        in_maps.append(m)
    res = run_bass_kernel_spmd(nc, in_maps, core_ids=list(range(8)))
    out = np.empty((NB, S, D), np.float32)
    for c in range(8):
        b, r = c // 2, c % 2
        out[b].reshape(NQ, 2, P, D)[:, r] = res.results[c]["y"].reshape(NQ, P, D)
    return out, res


def kernel(**inputs):
    x = np.ascontiguousarray(np.asarray(inputs["x"], dtype=np.float32))
    weights = {n: np.ascontiguousarray(np.asarray(inputs[n], dtype=np.float32)) for n in _WNAMES}
    x, _ = run_layers(x, weights, list(range(DEPTH)))
    return x
```
